# Optimizing a Trainium2 kernel written in Bass

```python
import math
import jax, jax.numpy as jnp
from jax import lax
import numpy as np

D_MODEL = 1024
BATCH = 8
SEQ = 2048
DEPTH = 2

MIX_WIDTH = 512
N_BRANCH = 3

RG_WIDTH = MIX_WIDTH
RG_BLOCKS = 8
RG_BLOCK = RG_WIDTH // RG_BLOCKS
CONV_WIDTH = 4
RG_C = 8.0

RW_HEADS = 8
RW_HEAD = 64
RW_WIDTH = RW_HEADS * RW_HEAD
LORA_W = 64
LORA_A = 64
LORA_V = 32
LORA_G = 128
RW_GN_EPS = 64e-5
RW_COLS = 3 * RW_WIDTH + LORA_W + LORA_A + LORA_G

MLA_HEADS = 8
QK_NOPE = 64
QK_ROPE = 32
V_HEAD = 64
Q_LORA = 256
KV_LORA = 128
ROPE_THETA = 10000.0
Q_BLOCK = 128
MLA_COLS = Q_LORA + KV_LORA + QK_ROPE

RG_COLS = 2 * RG_WIDTH
GATE_COLS = N_BRANCH * D_MODEL
IN_COLS = RG_COLS + RW_COLS + MLA_COLS + GATE_COLS

N_KEYS = 128
N_EXPERTS = N_KEYS * N_KEYS
PEER_HEADS = 8
PEER_QDIM = 256
PEER_HALF = PEER_QDIM // 2
PEER_TOPK = 16
TOKEN_BLOCK = 128

ALPHA = (2.0 * DEPTH) ** 0.25
BETA = (8.0 * DEPTH) ** -0.25
LN_EPS = 1e-5

kernel_name = "hybrid_rglru_rwkv7_mla_peer_deepnorm"


def split_last(t, sizes):
    out, o = [], 0
    for s in sizes:
        out.append(t[..., o:o + s])
        o += s
    return out


def layer_norm(x, g, b):
    xf = x.astype(jnp.float32)
    mu = jnp.mean(xf, -1, keepdims=True)
    var = jnp.mean(jnp.square(xf - mu), -1, keepdims=True)
    return ((xf - mu) * lax.rsqrt(var + LN_EPS) * g.astype(jnp.float32) + b.astype(jnp.float32)).astype(x.dtype)


def rms_norm(x, g):
    xf = x.astype(jnp.float32)
    return (xf * lax.rsqrt(jnp.mean(xf * xf, -1, keepdims=True) + 1e-6) * g.astype(jnp.float32)).astype(x.dtype)


def token_shift(p):
    return jnp.pad(p, ((0, 0), (1, 0), (0, 0)))[:, :-1]


def causal_depthwise_conv(x, w, b):
    S = x.shape[1]
    xp = jnp.pad(x, ((0, 0), (CONV_WIDTH - 1, 0), (0, 0)))
    out = b
    for j in range(CONV_WIDTH):
        out = out + w[j] * xp[:, CONV_WIDTH - 1 - j: CONV_WIDTH - 1 - j + S]
    return out


def rg_lru_branch(xb, gb, conv_w, conv_b, wa, ba, wx, bx, log_a_param):
    B, S, _ = xb.shape
    xc = causal_depthwise_conv(xb, conv_w, conv_b)
    xh = xc.reshape(B, S, RG_BLOCKS, RG_BLOCK)
    r = jax.nn.sigmoid(jnp.einsum('bshi,hij->bshj', xh, wa).reshape(B, S, RG_WIDTH) + ba)
    i = jax.nn.sigmoid(jnp.einsum('bshi,hij->bshj', xh, wx).reshape(B, S, RG_WIDTH) + bx)
    log_a = -RG_C * r.astype(jnp.float32) * jax.nn.softplus(-log_a_param.astype(jnp.float32))
    a = jnp.exp(log_a)
    reset = (jnp.arange(S) == 0)[None, :, None]
    mult = jnp.where(reset, 1.0, jnp.sqrt(-jnp.expm1(2.0 * log_a)))
    b_in = mult * (i * xc).astype(jnp.float32)

    def combine(c1, c2):
        a1, h1 = c1
        a2, h2 = c2
        return a1 * a2, a2 * h1 + h2

    _, h = lax.associative_scan(combine, (a, b_in), axis=1)
    return h.astype(xb.dtype) * jax.nn.gelu(gb)


def rwkv7_branch(p, mix, w0, w2, a0, a2, g2, k_k, k_a, r_k, gn_g, gn_b, v_first, vres):
    B, S, _ = p.shape
    p = p + (token_shift(p) - p) * mix
    r, k, v, xw, xa, xg = split_last(p, (RW_WIDTH, RW_WIDTH, RW_WIDTH, LORA_W, LORA_A, LORA_G))
    w = -jax.nn.softplus(-(w0 + jnp.tanh(xw) @ w2)) - 0.5
    a = jax.nn.sigmoid(a0 + xa @ a2)
    g = jax.nn.sigmoid(xg) @ g2
    if vres is None:
        v_first = v
    else:
        v0, v1, v2 = vres
        v = v + (v_first - v) * jax.nn.sigmoid(v0 + (v @ v1) @ v2)
    hd = lambda t: t.reshape(B, S, RW_HEADS, RW_HEAD)
    kk = hd(k * k_k).astype(jnp.float32)
    kk = kk / jnp.maximum(jnp.sqrt(jnp.sum(kk * kk, -1, keepdims=True)), 1e-12)
    k = k * (1.0 + (a - 1.0) * k_a)
    decay = jnp.exp(-jnp.exp(w.astype(jnp.float32)))
    a_h = hd(a).astype(jnp.float32)

    tm = lambda t: jnp.moveaxis(t, 1, 0)
    f32h = lambda t: hd(t).astype(jnp.float32)
    xs = (tm(f32h(r)), tm(f32h(decay)), tm(f32h(k)), tm(f32h(v)), tm(-kk), tm(kk * a_h))

    def step(state, inp):
        r_t, w_t, k_t, v_t, aa_t, bb_t = inp
        sa = jnp.einsum('bhvk,bhk->bhv', state, aa_t)
        state = (state * w_t[:, :, None, :] + sa[..., None] * bb_t[:, :, None, :]
                 + v_t[..., None] * k_t[:, :, None, :])
        return state, jnp.einsum('bhvk,bhk->bhv', state, r_t)

    s0 = jnp.zeros((B, RW_HEADS, RW_HEAD, RW_HEAD), jnp.float32)
    _, ys = lax.scan(step, s0, xs)
    y = jnp.moveaxis(ys, 0, 1)
    mu = jnp.mean(y, -1, keepdims=True)
    var = jnp.mean(jnp.square(y - mu), -1, keepdims=True)
    yn = ((y - mu) * lax.rsqrt(var + RW_GN_EPS)).reshape(B, S, RW_WIDTH)
    yn = (yn * gn_g.astype(jnp.float32) + gn_b.astype(jnp.float32)).astype(p.dtype)
    bonus = jnp.sum(hd(r) * hd(k) * r_k, -1, keepdims=True) * hd(v)
    return (yn + bonus.reshape(B, S, RW_WIDTH)) * g, v_first


def apply_rope(t, cos, sin):
    half = QK_ROPE // 2
    t1, t2 = t[..., :half], t[..., half:]
    return jnp.concatenate([t1 * cos - t2 * sin, t1 * sin + t2 * cos], -1)


def mla_branch(p, q_norm, w_uq, kv_norm, w_ukv):
    B, S, _ = p.shape
    c_q, c_kv, k_rope = split_last(p, (Q_LORA, KV_LORA, QK_ROPE))
    q = (rms_norm(c_q, q_norm) @ w_uq).reshape(B, S, MLA_HEADS, QK_NOPE + QK_ROPE)
    q_nope, q_pe = q[..., :QK_NOPE], q[..., QK_NOPE:]
    kv = (rms_norm(c_kv, kv_norm) @ w_ukv).reshape(B, S, MLA_HEADS, QK_NOPE + V_HEAD)
    k_nope, v = kv[..., :QK_NOPE], kv[..., QK_NOPE:]
    pos = jnp.arange(S, dtype=jnp.float32)
    inv_freq = ROPE_THETA ** (-jnp.arange(QK_ROPE // 2, dtype=jnp.float32) / (QK_ROPE // 2))
    ang = pos[:, None] * inv_freq[None, :]
    cos, sin = jnp.cos(ang).astype(p.dtype), jnp.sin(ang).astype(p.dtype)
    q_pe = apply_rope(q_pe, cos[:, None, :], sin[:, None, :])
    k_pe = apply_rope(k_rope, cos, sin)
    scale = (QK_NOPE + QK_ROPE) ** -0.5
    nb = S // Q_BLOCK
    blk = lambda t: jnp.moveaxis(t.reshape(B, nb, Q_BLOCK, MLA_HEADS, t.shape[-1]), 1, 0)
    starts = jnp.arange(nb, dtype=jnp.int32) * Q_BLOCK
    kpos = jnp.arange(S, dtype=jnp.int32)

    def attend(args):
        qn, qp, start = args
        s = (jnp.einsum('bqhd,bkhd->bhqk', qn, k_nope)
             + jnp.einsum('bqhd,bkd->bhqk', qp, k_pe)).astype(jnp.float32) * scale
        qpos = start + jnp.arange(Q_BLOCK, dtype=jnp.int32)
        s = jnp.where(kpos[None, :] <= qpos[:, None], s, -1e30)
        pr = jax.nn.softmax(s, axis=-1).astype(v.dtype)
        return jnp.einsum('bhqk,bkhd->bqhd', pr, v)

    o = lax.map(attend, (blk(q_nope), blk(q_pe), starts))
    return jnp.moveaxis(o, 0, 1).reshape(B, S, MLA_HEADS * V_HEAD)


def peer_ffn(x, w_query, subkeys, u_table, v_table):
    B, S, D = x.shape
    T = B * S
    xt = x.reshape(T, D)
    q = (xt @ w_query).reshape(T, PEER_HEADS, 2, PEER_HALF)
    s1 = jnp.einsum('thd,nd->thn', q[:, :, 0], subkeys[0])
    s2 = jnp.einsum('thd,nd->thn', q[:, :, 1], subkeys[1])
    v1, i1 = lax.top_k(s1, PEER_TOPK)
    v2, i2 = lax.top_k(s2, PEER_TOPK)
    cand = (v1[..., :, None] + v2[..., None, :]).reshape(T, PEER_HEADS, PEER_TOPK * PEER_TOPK)
    cand_idx = (i1[..., :, None] * N_KEYS + i2[..., None, :]).reshape(T, PEER_HEADS, PEER_TOPK * PEER_TOPK)
    top, pos = lax.top_k(cand, PEER_TOPK)
    idx = jnp.take_along_axis(cand_idx, pos, axis=-1)
    gate = jax.nn.softmax(top.astype(jnp.float32), axis=-1).astype(x.dtype)
    nblk = T // TOKEN_BLOCK

    def expert_block(args):
        xb, ib, gb = args
        u = u_table[ib]
        vv = v_table[ib]
        act = jax.nn.gelu(jnp.einsum('thkd,td->thk', u, xb))
        return jnp.einsum('thk,thkd->td', gb * act, vv)

    y = lax.map(expert_block, (xt.reshape(nblk, TOKEN_BLOCK, D),
                               idx.reshape(nblk, TOKEN_BLOCK, PEER_HEADS, PEER_TOPK),
                               gate.reshape(nblk, TOKEN_BLOCK, PEER_HEADS, PEER_TOPK)))
    return y.reshape(B, S, D)


def setup_inputs(seed: int = 0) -> dict:
    key = jax.random.key(seed)
    ks = iter(jax.random.split(key, 48))
    f32 = jnp.float32
    nrm = lambda shape, scale: jax.random.normal(next(ks), shape, f32) * scale
    uni = lambda shape, lo, hi: jax.random.uniform(next(ks), shape, f32, lo, hi)
    L = DEPTH
    a_c = uni((L, RG_WIDTH), 0.9, 0.999)
    s = a_c ** (1.0 / RG_C)
    return {
        "x": nrm((BATCH, SEQ, D_MODEL), 1.0),
        "w_in": nrm((L, D_MODEL, IN_COLS), D_MODEL ** -0.5),
        "rg_conv_w": nrm((L, CONV_WIDTH, RG_WIDTH), CONV_WIDTH ** -0.5),
        "rg_conv_b": nrm((L, RG_WIDTH), 0.02),
        "rg_wa": nrm((L, RG_BLOCKS, RG_BLOCK, RG_BLOCK), RG_BLOCK ** -0.5),
        "rg_ba": nrm((L, RG_WIDTH), 0.02),
        "rg_wx": nrm((L, RG_BLOCKS, RG_BLOCK, RG_BLOCK), RG_BLOCK ** -0.5),
        "rg_bx": nrm((L, RG_WIDTH), 0.02),
        "rg_log_a": jnp.log(s) - jnp.log1p(-s),
        "rw_mix": uni((L, RW_COLS), 0.0, 1.0),
        "rw_w0": uni((L, RW_WIDTH), -6.0, -1.0),
        "rw_w2": nrm((L, LORA_W, RW_WIDTH), 0.1 * LORA_W ** -0.5),
        "rw_a0": nrm((L, RW_WIDTH), 0.1),
        "rw_a2": nrm((L, LORA_A, RW_WIDTH), 0.1 * LORA_A ** -0.5),
        "rw_g2": nrm((L, LORA_G, RW_WIDTH), LORA_G ** -0.5),
        "rw_v0": nrm((L - 1, RW_WIDTH), 0.1),
        "rw_v1": nrm((L - 1, RW_WIDTH, LORA_V), RW_WIDTH ** -0.5),
        "rw_v2": nrm((L - 1, LORA_V, RW_WIDTH), LORA_V ** -0.5),
        "rw_k_k": 0.85 + nrm((L, RW_WIDTH), 0.02),
        "rw_k_a": 1.0 + nrm((L, RW_WIDTH), 0.02),
        "rw_r_k": nrm((L, RW_HEADS, RW_HEAD), 0.1),
        "rw_gn_g": 1.0 + nrm((L, RW_WIDTH), 0.02),
        "rw_gn_b": nrm((L, RW_WIDTH), 0.02),
        "mla_q_norm": 1.0 + nrm((L, Q_LORA), 0.02),
        "mla_w_uq": nrm((L, Q_LORA, MLA_HEADS * (QK_NOPE + QK_ROPE)), Q_LORA ** -0.5),
        "mla_kv_norm": 1.0 + nrm((L, KV_LORA), 0.02),
        "mla_w_ukv": nrm((L, KV_LORA, MLA_HEADS * (QK_NOPE + V_HEAD)), KV_LORA ** -0.5),
        "w_branch": nrm((L, N_BRANCH, MIX_WIDTH, D_MODEL), BETA * MIX_WIDTH ** -0.5),
        "w_out": nrm((L, D_MODEL, D_MODEL), BETA * D_MODEL ** -0.5),
        "ln1_g": 1.0 + nrm((L, D_MODEL), 0.02),
        "ln1_b": nrm((L, D_MODEL), 0.02),
        "peer_w_query": nrm((L, D_MODEL, PEER_HEADS * PEER_QDIM), D_MODEL ** -0.5),
        "peer_subkeys": nrm((L, 2, N_KEYS, PEER_HALF), PEER_HALF ** -0.5),
        "peer_u": nrm((L, N_EXPERTS, D_MODEL), D_MODEL ** -0.5),
        "peer_v": nrm((L, N_EXPERTS, D_MODEL), BETA * PEER_HEADS ** -0.5),
        "ln2_g": 1.0 + nrm((L, D_MODEL), 0.02),
        "ln2_b": nrm((L, D_MODEL), 0.02),
    }


def reference(x, w_in, rg_conv_w, rg_conv_b, rg_wa, rg_ba, rg_wx, rg_bx, rg_log_a,
              rw_mix, rw_w0, rw_w2, rw_a0, rw_a2, rw_g2, rw_v0, rw_v1, rw_v2,
              rw_k_k, rw_k_a, rw_r_k, rw_gn_g, rw_gn_b,
              mla_q_norm, mla_w_uq, mla_kv_norm, mla_w_ukv,
              w_branch, w_out, ln1_g, ln1_b,
              peer_w_query, peer_subkeys, peer_u, peer_v, ln2_g, ln2_b):
    B, S, D = x.shape
    v_first = None
    for l in range(DEPTH):
        p = x @ w_in[l]
        pa_x, pa_g, pb, pc, pg = split_last(p, (RG_WIDTH, RG_WIDTH, RW_COLS, MLA_COLS, GATE_COLS))
        y_a = rg_lru_branch(pa_x, pa_g, rg_conv_w[l], rg_conv_b[l], rg_wa[l], rg_ba[l],
                            rg_wx[l], rg_bx[l], rg_log_a[l])
        vres = None if l == 0 else (rw_v0[l - 1], rw_v1[l - 1], rw_v2[l - 1])
        y_b, v_first = rwkv7_branch(pb, rw_mix[l], rw_w0[l], rw_w2[l], rw_a0[l], rw_a2[l],
                                    rw_g2[l], rw_k_k[l], rw_k_a[l], rw_r_k[l],
                                    rw_gn_g[l], rw_gn_b[l], v_first, vres)
        y_c = mla_branch(pc, mla_q_norm[l], mla_w_uq[l], mla_kv_norm[l], mla_w_ukv[l])
        y_cat = jnp.stack([y_a, y_b, y_c], axis=2)
        gates = jax.nn.sigmoid(pg).reshape(B, S, N_BRANCH, D)
        merged = jnp.sum(jnp.einsum('bsnc,ncd->bsnd', y_cat, w_branch[l]) * gates, axis=2)
        x = layer_norm(ALPHA * x + merged @ w_out[l], ln1_g[l], ln1_b[l])
        y = peer_ffn(x, peer_w_query[l], peer_subkeys[l], peer_u[l], peer_v[l])
        x = layer_norm(ALPHA * x + y, ln2_g[l], ln2_b[l])
    return x
```

```python
import contextlib
import numpy as np
import concourse.bass as bass
import concourse.mybir as mybir
from concourse.bass_utils import run_bass_kernel_spmd

F32 = mybir.dt.float32
BF16 = mybir.dt.bfloat16
U32 = mybir.dt.uint32
I32 = mybir.dt.int32
AF = mybir.ActivationFunctionType
ALU = mybir.AluOpType
AX = mybir.AxisListType

D = 1024
S = 2048
L = 2
NT = S // 128
IN_COLS = 6304
C_RG, C_RW, C_MLA, C_GATE = 0, 1024, 2816, 3232
ALPHA = (2.0 * L) ** 0.25


class Dep:
    def __init__(self, name):
        self.name = name
        self.st = {}

    def states(self, key):
        if key is None:
            if None not in self.st:
                self.st[None] = [None, {}]
            return list(self.st.values())
        out = []
        if None in self.st:
            out.append(self.st[None])
        if key not in self.st:
            self.st[key] = [None, {}]
        out.append(self.st[key])
        return out


class T:
    def __init__(self, t, name):
        self.t = t
        self.dep = Dep(name)

    def __getitem__(self, idx):
        return self.t[idx]


class Eng:
    def __init__(self, name, h, sem):
        self.name, self.h, self.sem = name, h, sem
        self.count = 0
        self.known = {}


class FW:
    NDS = {"sp": 16, "pool": 12, "act": 6}

    def __init__(self, nc, es):
        self.nc = nc
        self.eng = {}
        for name, h in (("pe", nc.tensor), ("dve", nc.vector), ("act", nc.scalar),
                        ("pool", nc.gpsimd), ("sp", nc.sync)):
            sem = es.enter_context(nc.semaphore("sem_" + name))
            self.eng[name] = Eng(name, h, sem)
        self.dsem = {}
        self.dcnt = {}
        self.drr = {}
        for q, n in self.NDS.items():
            self.dsem[q] = [es.enter_context(nc.semaphore(f"ds_{q}{i}")) for i in range(n)]
            self.dcnt[q] = [0] * n
            self.drr[q] = 0

    def _wait(self, e, sid, sem, val):
        if val > 0 and e.known.get(sid, 0) < val:
            e.h.wait_ge(sem, val)
            e.known[sid] = val

    def _collect(self, reads, writes, ename):
        need = {}

        def add(tok):
            if tok is None:
                return
            sid, sem, val = tok
            if ename == "pe" and sid == "pe":
                return
            if need.get(sid, (None, 0))[1] < val:
                need[sid] = (sem, val)

        rs, ws = [], []
        for r in reads:
            t, k = r if isinstance(r, tuple) else (r, None)
            sts = t.dep.states(k)
            rs.append((k, sts))
            for st in sts:
                add(st[0])
        for w in writes:
            t, k = w if isinstance(w, tuple) else (w, None)
            sts = t.dep.states(k)
            ws.append((k, sts))
            for st in sts:
                add(st[0])
                for tok in st[1].values():
                    add(tok)
        return need, rs, ws

    def _record(self, tok, rs, ws):
        for k, sts in rs:
            for st in (sts if k is None else sts[-1:]):
                st[1][tok[0]] = tok
        for k, sts in ws:
            for st in (sts if k is None else sts[-1:]):
                st[0] = tok
                st[1] = {}

    def op(self, ename, fn, reads=(), writes=()):
        e = self.eng[ename]
        need, rs, ws = self._collect(reads, writes, ename)
        for sid, (sem, val) in need.items():
            self._wait(e, sid, sem, val)
        ins = fn(e.h)
        e.count += 1
        ins.then_inc(e.sem, 1)
        self._record((ename, e.sem, e.count), rs, ws)
        return ins

    def dma(self, q, out, in_, reads=(), writes=(), indirect=None, **kw):
        e = self.eng[q]
        i = self.drr[q]
        self.drr[q] = (i + 1) % len(self.dsem[q])
        sem = self.dsem[q][i]
        sid = ("d", q, i)
        self._wait(e, sid, sem, 16 * self.dcnt[q][i])
        need, rs, ws = self._collect(reads, writes, q)
        for s2, (sm, val) in need.items():
            self._wait(e, s2, sm, val)
        if indirect is None:
            ins = e.h.dma_start(out=out, in_=in_, **kw)
        else:
            ins = e.h.indirect_dma_start(out=out, out_offset=None, in_=in_, in_offset=indirect, **kw)
        self.dcnt[q][i] += 1
        ins.then_inc(sem, 16)
        self._record((sid, sem, 16 * self.dcnt[q][i]), rs, ws)
        return ins

    def barrier(self):
        for e in self.eng.values():
            for o in self.eng.values():
                if o is not e:
                    self._wait(e, o.name, o.sem, o.count)
            for q in self.dsem:
                for i, sem in enumerate(self.dsem[q]):
                    self._wait(e, ("d", q, i), sem, 16 * self.dcnt[q][i])

    def dve(self, fn, r=(), w=()):
        return self.op("dve", fn, r, w)

    def act(self, fn, r=(), w=()):
        return self.op("act", fn, r, w)

    def pe(self, fn, r=(), w=()):
        return self.op("pe", fn, r, w)

    def pool(self, fn, r=(), w=()):
        return self.op("pool", fn, r, w)


class Builder:
    def __init__(self, debug=False, phases=None, nseq=1):
        self.debug = debug
        self.phases = phases
        self.nseq = nseq
        self.nc = bass.Bass("TRN2", target_bir_lowering=False)
        self.inp = {}
        self.scr = {}

    def din(self, name, shape, dt=F32):
        self.inp[name] = self.nc.dram_tensor(name, list(shape), dt, kind="ExternalInput").ap()

    def dscr(self, name, shape, dt, out=False):
        kind = "ExternalOutput" if (out or self.debug) else "Internal"
        ap = self.nc.dram_tensor(name, list(shape), dt, kind=kind).ap()
        self.scr[name] = T(ap, name)
        return self.scr[name]

    def sb(self, es, name, shape, dt=F32):
        self.uid = getattr(self, "uid", 0) + 1
        nm = f"s{self.uid}_{name}"
        return T(es.enter_context(self.nc.sbuf_tensor(nm, list(shape), dt)), nm)

    def declare(self):
        self.din("x", [self.nseq, S, D])
        self.din("w_in", [L, D, IN_COLS])
        self.din("rg_vec", [L, 128, 4, 8])
        self.din("rg_wa", [L, 8, 64, 64])
        self.din("rg_wx", [L, 8, 64, 64])
        self.din("ident", [128, 128])
        self.din("mla_vec", [L, 128, 3])
        self.din("mla_w_uq", [L, 256, 768])
        self.din("mla_w_uq_rot", [L, 256, 256])
        self.din("mla_w_ukv_k", [L, 128, 512])
        self.din("mla_w_ukv_v", [L, 128, 512])
        self.din("rope_cs", [2, 32, S])
        self.din("cmask", [128, 128])
        self.din("rw_mix", [L, 1, 1792])
        self.din("rw_vec", [L, 128, 4, 8])
        self.din("rw_w2a2", [L, 128, 512])
        self.din("rw_g2", [L, 128, 512])
        self.din("rw_v1", [1, 512, 32])
        self.din("rw_v2", [1, 32, 512])
        self.din("rw_masks", [128, 5, 128])
        self.din("onesbd", [128, 128])
        self.dscr("rwT", [7, 512, S], F32)
        self.dscr("vfirst", [512, S], F32)
        self.dscr("rw_wc", [512, NT], F32)
        self.din("w_branch", [L, 3, 512, D])
        self.din("w_out", [L, D, D])
        self.din("peer_wq", [L, D, 2048])
        self.din("peer_skT", [L, 2, 128, 128])
        self.din("iota16", [128, 16])
        for l in range(L):
            if self.phases is None or f"F{l}" in self.phases:
                self.din(f"peer_u{l}", [16384, D])
                self.din(f"peer_v{l}", [16384, D])
            self.dscr(f"x2T_{l}", [D, S], BF16)
        self.dscr("x2_0", [S, D], F32)
        self.din("ln_gb", [L, 4, D])
        for l in range(L):
            self.dscr(f"x1_{l}", [S, D], F32)
            self.dscr(f"x1T_{l}", [D, S], BF16)
        self.dscr("xT0", [D, S], BF16)
        self.dscr("yT", [3, 512, S], BF16)
        self.dscr("out", [self.nseq, S, D], F32, out=True)

    def build(self):
        nc = self.nc
        self.declare()
        with contextlib.ExitStack() as es:
            self.fw = fw = FW(nc, es)
            self.ps = [T(es.enter_context(nc.psum_tensor(f"ps{i}", [128, 512], F32)), f"ps{i}")
                       for i in range(8)]
            self.psi = 0
            self.ident = self.sb(es, "ident", [128, 128])
            fw.dma("sp", self.ident[:], self.inp["ident"][:, :], writes=[self.ident])
            ph = self.phases
            for sq in range(self.nseq):
                x_in = self.inp["x"][sq]
                out_t = T(self.scr["out"].t[sq], f"out{sq}")
                if ph is None or "A" in ph:
                    self.phase_xT(x_in, None, self.scr["xT0"])
                    fw.barrier()
                for l in range(L):
                    xTd = self.scr["xT0"] if l == 0 else self.scr[f"x2T_{l - 1}"]
                    if ph is None or f"B{l}" in ph:
                        self.phase_rg(l, xTd)
                        fw.barrier()
                    if ph is None or f"C{l}" in ph or f"Cp{l}" in ph:
                        self.phase_rwkv_prep(l, xTd)
                        fw.barrier()
                    if ph is None or f"C{l}" in ph or f"Cs{l}" in ph:
                        self.phase_rwkv_scan(l)
                        fw.barrier()
                    if ph is None or f"D{l}" in ph:
                        self.phase_mla(l, xTd)
                        fw.barrier()
                    if ph is None or f"E{l}" in ph:
                        if l == 0:
                            self.phase_merge(l, xTd, x_in, None)
                        else:
                            self.phase_merge(l, xTd, self.scr["x2_0"].t, self.scr["x2_0"])
                        fw.barrier()
                    if ph is None or f"F{l}" in ph:
                        x2d = out_t if l == L - 1 else self.scr["x2_0"]
                        self.phase_peer(l, x2d, self.scr[f"x2T_{l}"])
                        fw.barrier()
            fw.barrier()
        return nc

    def next_ps(self):
        while True:
            p = self.ps[self.psi]
            self.psi = (self.psi + 1) % 8
            if p not in getattr(self, "ps_reserved", ()):
                return p

    def phase_xT(self, x_ap, x_dep, xT_d):
        fw = self.fw
        with contextlib.ExitStack() as es:
            xt = [self.sb(es, f"xa{i}", [128, D]) for i in range(2)]
            xo = [self.sb(es, f"xo{i}", [128, 8, 128], BF16) for i in range(2)]
            for i in range(NT):
                b = xt[i % 2]
                fw.dma("sp", b[:], x_ap[i * 128:(i + 1) * 128, :],
                       reads=[x_dep] if x_dep else [], writes=[b])
                o = xo[i % 2]
                for hh in range(2):
                    ps = self.next_ps()
                    for c in range(4):
                        cc = hh * 4 + c
                        fw.pe(lambda h, ps=ps, c=c, cc=cc, b=b: h.transpose(
                            ps[:, c * 128:(c + 1) * 128], b[:, cc * 128:(cc + 1) * 128], self.ident[:]),
                            r=[b, self.ident], w=[ps])
                    fw.dve(lambda h, ps=ps, o=o, hh=hh: h.tensor_copy(
                        out=o[:, hh * 4:(hh + 1) * 4, :],
                        in_=ps[:, :].rearrange("p (c t) -> p c t", c=4)), r=[ps], w=[(o, hh)])
                fw.dma("sp", xT_d[:, i * 128:(i + 1) * 128].rearrange("(c p) t -> p c t", p=128),
                       o[:], reads=[o], writes=[(xT_d, i)])

    def load_xT(self, es, xT_d):
        xT = self.sb(es, "xT", [128, 8, S], BF16)
        for c in range(8):
            self.fw.dma("sp", xT[:, c, :], xT_d[c * 128:(c + 1) * 128, :], reads=[xT_d], writes=[(xT, c)])
        return xT

    def load_w_cols(self, wt, l, c0, ncols, key=None):
        src = self.inp["w_in"][l, :, c0:c0 + ncols].rearrange("(kc p) c -> p kc c", p=128)
        self.fw.dma("pool", wt[:, :, 0:ncols], src, writes=[(wt, key) if key is not None else wt])

    def proj_fm(self, wt, xT, j0, M, n, ps):
        for kc in range(8):
            self.fw.pe(lambda h, kc=kc: h.matmul(ps[0:M, :], lhsT=wt[:, kc, j0:j0 + M],
                                                  rhs=xT[:, kc, n * 512:(n + 1) * 512],
                                                  start=(kc == 0), stop=(kc == 7)),
                       r=[wt, xT], w=[ps])

    def gelu_tanh(self, es_tmp, x, out, tmp):
        fw = self.fw
        fw.dve(lambda h: h.tensor_tensor(out=tmp[:], in0=x[:], in1=x[:], op=ALU.mult), r=[x], w=[tmp])
        fw.dve(lambda h: h.tensor_scalar(out=tmp[:], in0=tmp[:], scalar1=0.044715, scalar2=1.0,
                                         op0=ALU.mult, op1=ALU.add), r=[tmp], w=[tmp])
        fw.dve(lambda h: h.tensor_tensor(out=tmp[:], in0=tmp[:], in1=x[:], op=ALU.mult), r=[tmp, x], w=[tmp])
        fw.act(lambda h: h.activation(out=tmp[:], in_=tmp[:], func=AF.Sigmoid, scale=1.5957691216057308),
               r=[tmp], w=[tmp])
        fw.dve(lambda h: h.tensor_tensor(out=out[:], in0=tmp[:], in1=x[:], op=ALU.mult), r=[tmp, x], w=[out])

    def phase_rg(self, l, xT_d):
        fw, nc = self.fw, self.nc
        yT = self.scr["yT"]
        with contextlib.ExitStack() as es:
            xT = self.load_xT(es, xT_d)
            vec = self.sb(es, "rgvec", [128, 4, 8])
            fw.dma("sp", vec[:], self.inp["rg_vec"][l], writes=[vec])
            cst = self.sb(es, "rgc", [128, 4, 4])
            wts = [self.sb(es, f"rgw{i}", [128, 8, 256], BF16) for i in range(2)]
            wbd = [self.sb(es, f"rgbd{i}", [128, 2, 128]) for i in range(2)]
            names = ["xb", "gb", "xc", "r", "i", "a", "m", "h"]
            tl = {n: self.sb(es, "rg_" + n, [128, S]) for n in names}
            yo = self.sb(es, "rg_yo", [128, S], BF16)
            xb, gb, xc, r_, i_, a_, m_, h_ = (tl[n] for n in names)
            for c in range(4):
                wt = wts[c % 2]
                bd = wbd[c % 2]
                self.load_w_cols(wt, l, C_RG + c * 128, 128, key=0)
                src = self.inp["w_in"][l, :, C_RG + 512 + c * 128:C_RG + 512 + (c + 1) * 128].rearrange(
                    "(kc p) c -> p kc c", p=128)
                fw.dma("pool", wt[:, :, 128:256], src, writes=[(wt, 1)])
                fw.pool(lambda h: h.memset(bd[:], 0.0), w=[bd])
                for j, nm in enumerate(("rg_wa", "rg_wx")):
                    for hb in range(2):
                        fw.dma("sp", bd[hb * 64:(hb + 1) * 64, j, hb * 64:(hb + 1) * 64],
                               self.inp[nm][l, 2 * c + hb], writes=[bd])
                for j, dst in enumerate((xb, gb)):
                    for n in range(4):
                        ps = self.next_ps()
                        self.proj_fm(wt, xT, j * 128, 128, n, ps)
                        fw.act(lambda h, ps=ps, dst=dst, n=n: h.copy(out=dst[:, n * 512:(n + 1) * 512], in_=ps[:, :]),
                               r=[ps], w=[(dst, n)])
                v = lambda k: vec[:, c, k:k + 1]
                fw.dve(lambda h: h.tensor_scalar(out=xc[:], in0=xb[:], scalar1=v(0), scalar2=v(4),
                                                 op0=ALU.mult, op1=ALU.add), r=[xb, vec], w=[xc])
                for j in range(1, 4):
                    fw.dve(lambda h, j=j: h.scalar_tensor_tensor(out=xc[:, j:], in0=xb[:, 0:S - j], scalar=v(j),
                                                                 in1=xc[:, j:], op0=ALU.mult, op1=ALU.add),
                           r=[xb, xc, vec], w=[xc])
                cc = lambda k: cst[:, c, k:k + 1]
                fw.act(lambda h: h.activation(out=cc(0), in_=v(7), func=AF.Exp, scale=-1.0), r=[vec], w=[cst])
                fw.act(lambda h: h.activation(out=cc(1), in_=cc(0), func=AF.Ln, bias=1.0, scale=1.0), r=[cst], w=[cst])
                fw.dve(lambda h: h.tensor_scalar(out=cc(2), in0=cc(1), scalar1=-8.0, scalar2=None, op0=ALU.mult),
                       r=[cst], w=[cst])
                fw.dve(lambda h: h.tensor_scalar(out=cc(3), in0=cc(1), scalar1=-16.0, scalar2=None, op0=ALU.mult),
                       r=[cst], w=[cst])
                for j, (dst, bk) in enumerate(((r_, 5), (i_, 6))):
                    for n in range(4):
                        ps = self.next_ps()
                        fw.pe(lambda h, ps=ps, j=j, n=n: h.matmul(ps[:, :], lhsT=bd[:, j, :],
                                                                   rhs=xc[:, n * 512:(n + 1) * 512],
                                                                   start=True, stop=True), r=[bd, xc], w=[ps])
                        fw.act(lambda h, ps=ps, dst=dst, n=n, bk=bk: h.activation(
                            out=dst[:, n * 512:(n + 1) * 512], in_=ps[:, :], func=AF.Sigmoid, bias=v(bk), scale=1.0),
                            r=[ps, vec], w=[(dst, n)])
                fw.act(lambda h: h.activation(out=a_[:], in_=r_[:], func=AF.Exp, scale=cc(2)), r=[r_, cst], w=[a_])
                fw.act(lambda h: h.activation(out=m_[:], in_=r_[:], func=AF.Exp, scale=cc(3)), r=[r_, cst], w=[m_])
                fw.act(lambda h: h.activation(out=m_[:], in_=m_[:], func=AF.Sqrt, bias=1.0, scale=-1.0), r=[m_], w=[m_])
                fw.dve(lambda h: h.memset(m_[:, 0:1], 1.0), w=[m_])
                fw.dve(lambda h: h.tensor_tensor(out=i_[:], in0=i_[:], in1=xc[:], op=ALU.mult), r=[i_, xc], w=[i_])
                fw.dve(lambda h: h.tensor_tensor(out=i_[:], in0=i_[:], in1=m_[:], op=ALU.mult), r=[i_, m_], w=[i_])
                fw.dve(lambda h: h.tensor_tensor_scan(out=h_[:], data0=a_[:], data1=i_[:], initial=0.0,
                                                      op0=ALU.mult, op1=ALU.add), r=[a_, i_], w=[h_])
                self.gelu_tanh(es, gb, r_, m_)
                fw.dve(lambda h: h.tensor_tensor(out=yo[:], in0=h_[:], in1=r_[:], op=ALU.mult), r=[h_, r_], w=[yo])
                fw.dma("sp", yT[0, c * 128:(c + 1) * 128, :], yo[:], reads=[yo], writes=[(yT, ("a", c))])


    def phase_rwkv_prep(self, l, xT_d):
        fw = self.fw
        rwT, vfirst = self.scr["rwT"], self.scr["vfirst"]
        with contextlib.ExitStack() as es:
            xTp = self.sb(es, "xTp", [128, 8, S + 2], BF16)
            fw.dve(lambda h: h.memset(xTp[:, :, 0:2], 0.0), w=[(xTp, "z")])
            for c in range(8):
                fw.dma("sp", xTp[:, c, 2:S + 2], xT_d[c * 128:(c + 1) * 128, :], reads=[xT_d], writes=[(xTp, c)])
            mixb = self.sb(es, "mixb", [128, 2, 1792])
            fw.dma("sp", mixb[:, 0, :], self.inp["rw_mix"][l].to_broadcast([128, 1792]), writes=[mixb])
            fw.dve(lambda h: h.tensor_scalar(out=mixb[:, 1, :], in0=mixb[:, 0, :], scalar1=-1.0, scalar2=1.0,
                                             op0=ALU.mult, op1=ALU.add), r=[mixb], w=[mixb])
            vec = self.sb(es, "rwvec", [128, 4, 8])
            fw.dma("sp", vec[:], self.inp["rw_vec"][l], writes=[vec])
            dv = self.sb(es, "rwdv", [128, 4, 2])
            fw.dve(lambda h: h.tensor_scalar(out=dv[:, :, 0:1], in0=vec[:, :, 0:1], scalar1=-1.0, scalar2=None,
                                             op0=ALU.mult), r=[vec], w=[dv])
            fw.dve(lambda h: h.tensor_scalar(out=dv[:, :, 1:2], in0=vec[:, :, 3:4], scalar1=-1.0, scalar2=1.0,
                                             op0=ALU.mult, op1=ALU.add), r=[vec], w=[dv])
            w2a2 = self.sb(es, "w2a2", [128, 512])
            fw.dma("sp", w2a2[:], self.inp["rw_w2a2"][l], writes=[w2a2])
            g2 = self.sb(es, "g2", [128, 512])
            fw.dma("sp", g2[:], self.inp["rw_g2"][l], writes=[g2])
            obd = self.sb(es, "obd", [128, 128])
            fw.dma("sp", obd[:], self.inp["onesbd"][:, :], writes=[obd])
            rmask = self.sb(es, "rmask", [128, S])
            fw.pool(lambda h: h.memset(rmask[:], 1.0), w=[rmask])
            fw.pool(lambda h: h.memset(rmask[:, :].rearrange("p (c t) -> p c t", t=128)[:, :, 0:1], 0.0), w=[rmask])
            wf = [self.sb(es, f"wf{i}", [128, 8, 128]) for i in range(2)]
            wm = [self.sb(es, f"wm{i}", [128, 2, 8, 128], BF16) for i in range(2)]
            self._rw_cnt = 0

            def proj_shift(col0, dst_fn):
                i = self._rw_cnt % 2
                self._rw_cnt += 1
                src = self.inp["w_in"][l, :, C_RW + col0:C_RW + col0 + 128].rearrange("(kc p) c -> p kc c", p=128)
                fw.dma("sp", wf[i][:], src, writes=[wf[i]])
                for j in range(2):
                    fw.dve(lambda h: h.tensor_tensor(
                        out=wm[i][:, j, :, :], in0=wf[i][:],
                        in1=mixb[:, j, col0:col0 + 128][:, None, :].to_broadcast([128, 8, 128]), op=ALU.mult),
                        r=[wf[i], mixb], w=[(wm[i], j)])
                for n in range(4):
                    ps = self.next_ps()
                    for j in range(2):
                        off = 1 + j + n * 512
                        for kc in range(8):
                            fw.pe(lambda h: h.matmul(ps[:, :], lhsT=wm[i][:, j, kc, :], rhs=xTp[:, kc, off:off + 512],
                                                     start=(j == 0 and kc == 0), stop=(j == 1 and kc == 7)),
                                  r=[(wm[i], j), xTp], w=[ps])
                    dst_fn(n, ps)

            nb = 13
            B = [self.sb(es, f"rwb{i}", [128, S]) for i in range(nb)]
            wct = self.sb(es, "wct", [128, NT])
            xwa, sgx = B[11], B[12]
            sl = lambda n: slice(n * 512, (n + 1) * 512)
            def d_xwa(n, ps):
                fw.act(lambda h: h.activation(out=xwa[0:64, sl(n)], in_=ps[0:64, :], func=AF.Tanh), r=[ps], w=[(xwa, n)])
                fw.act(lambda h: h.copy(out=xwa[64:128, sl(n)], in_=ps[64:128, :]), r=[ps], w=[(xwa, n)])
            proj_shift(1536, d_xwa)
            proj_shift(1664, lambda n, ps: fw.act(lambda h: h.activation(out=sgx[:, sl(n)], in_=ps[:, :], func=AF.Sigmoid),
                                                  r=[ps], w=[(sgx, n)]))
            lat = None
            if l > 0:
                lat = self.sb(es, "lat", [32, S])
                v1 = self.sb(es, "v1", [128, 4, 32])
                fw.dma("sp", v1[:], self.inp["rw_v1"][l - 1].rearrange("(c p) j -> p c j", p=128), writes=[v1])
                v2 = self.sb(es, "v2", [32, 512])
                fw.dma("sp", v2[:], self.inp["rw_v2"][l - 1], writes=[v2])
                accb = [self.ps[4 + n] for n in range(4)]
                self.ps_reserved = set(accb)
                for c in range(4):
                    def d_v(n, ps, c=c):
                        fw.act(lambda h: h.copy(out=B[0][:, sl(n)], in_=ps[:, :]), r=[ps], w=[(B[0], n)])
                        fw.pe(lambda h: h.matmul(accb[n][0:32, :], lhsT=v1[:, c, :], rhs=B[0][:, sl(n)],
                                                 start=(c == 0), stop=(c == 3)), r=[v1, (B[0], n)], w=[accb[n]])
                    proj_shift(1024 + c * 128, d_v)
                for n in range(4):
                    fw.act(lambda h: h.copy(out=lat[:, sl(n)], in_=accb[n][0:32, :]), r=[accb[n]], w=[(lat, n)])
                self.ps_reserved = set()
            for c in range(4):
                rT, kT, vT, e2, cum, ep, em, epv, a_, kk, km = B[:11]
                v = lambda k: vec[:, c, k:k + 1]
                cs_ = slice(c * 128, (c + 1) * 128)
                for col0, dst in ((c * 128, rT), (512 + c * 128, kT), (1024 + c * 128, vT)):
                    proj_shift(col0, lambda n, ps, dst=dst: fw.act(
                        lambda h: h.copy(out=dst[:, sl(n)], in_=ps[:, :]), r=[ps], w=[(dst, n)]))
                for n in range(4):
                    ps = self.next_ps()
                    fw.pe(lambda h: h.matmul(ps[:, :], lhsT=w2a2[0:64, cs_], rhs=xwa[0:64, sl(n)], start=True, stop=True),
                          r=[w2a2, (xwa, n)], w=[ps])
                    fw.act(lambda h: h.activation(out=e2[:, sl(n)], in_=ps[:, :], func=AF.Exp, bias=dv[:, c, 0:1], scale=-1.0),
                           r=[ps, dv], w=[(e2, n)])
                    fw.act(lambda h: h.activation(out=e2[:, sl(n)], in_=e2[:, sl(n)], func=AF.Ln, bias=1.0, scale=1.0),
                           r=[(e2, n)], w=[(e2, n)])
                    fw.act(lambda h: h.activation(out=e2[:, sl(n)], in_=e2[:, sl(n)], func=AF.Exp, bias=-0.5, scale=-1.0),
                           r=[(e2, n)], w=[(e2, n)])
                    ps = self.next_ps()
                    fw.pe(lambda h: h.matmul(ps[:, :], lhsT=w2a2[64:128, cs_], rhs=xwa[64:128, sl(n)], start=True, stop=True),
                          r=[w2a2, (xwa, n)], w=[ps])
                    fw.act(lambda h: h.activation(out=a_[:, sl(n)], in_=ps[:, :], func=AF.Sigmoid, bias=v(1), scale=1.0),
                           r=[ps, vec], w=[(a_, n)])
                    ps = self.next_ps()
                    fw.pe(lambda h: h.matmul(ps[:, :], lhsT=g2[:, cs_], rhs=sgx[:, sl(n)], start=True, stop=True),
                          r=[g2, (sgx, n)], w=[ps])
                    fw.act(lambda h: h.copy(out=ep[:, sl(n)], in_=ps[:, :]), r=[ps], w=[(ep, n)])
                    if l > 0:
                        ps = self.next_ps()
                        fw.pe(lambda h: h.matmul(ps[:, :], lhsT=v2[0:32, cs_], rhs=lat[:, sl(n)], start=True, stop=True),
                              r=[v2, (lat, n)], w=[ps])
                        fw.act(lambda h: h.activation(out=em[:, sl(n)], in_=ps[:, :], func=AF.Sigmoid, bias=v(7), scale=1.0),
                               r=[ps, vec], w=[(em, n)])
                fw.dma("sp", rwT[6, cs_, :], ep[:], reads=[ep], writes=[(rwT, (6, c))])
                if l > 0:
                    fw.dma("sp", epv[:], vfirst[cs_, :], reads=[vfirst], writes=[epv])
                    fw.dve(lambda h: h.tensor_tensor(out=epv[:], in0=epv[:], in1=vT[:], op=ALU.subtract), r=[epv, vT], w=[epv])
                    fw.dve(lambda h: h.tensor_tensor(out=epv[:], in0=epv[:], in1=em[:], op=ALU.mult), r=[epv, em], w=[epv])
                    fw.dve(lambda h: h.tensor_tensor(out=vT[:], in0=vT[:], in1=epv[:], op=ALU.add), r=[epv, vT], w=[vT])
                else:
                    fw.dma("sp", vfirst[cs_, :], vT[:], reads=[vT], writes=[(vfirst, c)])
                fw.dma("sp", rwT[4, cs_, :], vT[:], reads=[vT], writes=[(rwT, (4, c))])
                fw.dve(lambda h: h.tensor_tensor_scan(out=cum[:], data0=rmask[:], data1=e2[:], initial=0.0,
                                                      op0=ALU.mult, op1=ALU.add), r=[rmask, e2], w=[cum])
                fw.act(lambda h: h.activation(out=ep[:], in_=cum[:], func=AF.Exp, scale=-1.0), r=[cum], w=[ep])
                fw.act(lambda h: h.activation(out=em[:], in_=cum[:], func=AF.Exp, scale=1.0), r=[cum], w=[em])
                fw.dve(lambda h: h.tensor_tensor(out=epv[:], in0=cum[:], in1=e2[:], op=ALU.subtract), r=[cum, e2], w=[epv])
                fw.act(lambda h: h.activation(out=epv[:], in_=epv[:], func=AF.Exp, scale=-1.0), r=[epv], w=[epv])
                fw.dve(lambda h: h.tensor_scalar(out=kk[:], in0=kT[:], scalar1=v(2), scalar2=None, op0=ALU.mult),
                       r=[kT, vec], w=[kk])
                fw.dve(lambda h: h.tensor_tensor(out=cum[:], in0=kk[:], in1=kk[:], op=ALU.mult), r=[kk], w=[cum])
                for n in range(4):
                    ps = self.next_ps()
                    fw.pe(lambda h: h.matmul(ps[:, :], lhsT=obd[:, :], rhs=cum[:, sl(n)], start=True, stop=True),
                          r=[obd, cum], w=[ps])
                    fw.act(lambda h: h.activation(out=e2[:, sl(n)], in_=ps[:, :], func=AF.Sqrt), r=[ps], w=[(e2, n)])
                fw.dve(lambda h: h.tensor_scalar(out=e2[:], in0=e2[:], scalar1=1e-12, scalar2=None, op0=ALU.max),
                       r=[e2], w=[e2])
                fw.dve(lambda h: h.reciprocal(out=e2[:], in_=e2[:]), r=[e2], w=[e2])
                fw.dve(lambda h: h.tensor_tensor(out=kk[:], in0=kk[:], in1=e2[:], op=ALU.mult), r=[kk, e2], w=[kk])
                fw.dve(lambda h: h.tensor_scalar(out=km[:], in0=a_[:], scalar1=v(3), scalar2=dv[:, c, 1:2],
                                                 op0=ALU.mult, op1=ALU.add), r=[a_, vec, dv], w=[km])
                fw.dve(lambda h: h.tensor_tensor(out=km[:], in0=km[:], in1=kT[:], op=ALU.mult), r=[km, kT], w=[km])
                fw.dve(lambda h: h.scalar_tensor_tensor(out=cum[:], in0=rT[:], scalar=v(4), in1=km[:],
                                                        op0=ALU.mult, op1=ALU.mult), r=[rT, km, vec], w=[cum])
                for n in range(4):
                    ps = self.next_ps()
                    fw.pe(lambda h: h.matmul(ps[:, :], lhsT=obd[:, :], rhs=cum[:, sl(n)], start=True, stop=True),
                          r=[obd, cum], w=[ps])
                    fw.dve(lambda h: h.tensor_tensor(out=e2[:, sl(n)], in0=ps[:, :], in1=vT[:, sl(n)], op=ALU.mult),
                           r=[ps, vT], w=[(e2, n)])
                fw.dma("sp", rwT[5, cs_, :], e2[:], reads=[e2], writes=[(rwT, (5, c))])
                fw.dve(lambda h: h.scalar_tensor_tensor(out=epv[:], in0=kk[:], scalar=-1.0, in1=epv[:],
                                                        op0=ALU.mult, op1=ALU.mult), r=[kk, epv], w=[epv])
                fw.dma("sp", rwT[0, cs_, :], epv[:], reads=[epv], writes=[(rwT, (0, c))])
                fw.dve(lambda h: h.tensor_tensor(out=rT[:], in0=rT[:], in1=ep[:], op=ALU.mult), r=[rT, ep], w=[rT])
                fw.dma("sp", rwT[1, cs_, :], rT[:], reads=[rT], writes=[(rwT, (1, c))])
                fw.dve(lambda h: h.tensor_tensor(out=kk[:], in0=kk[:], in1=a_[:], op=ALU.mult), r=[kk, a_], w=[kk])
                fw.dve(lambda h: h.tensor_tensor(out=kk[:], in0=kk[:], in1=em[:], op=ALU.mult), r=[kk, em], w=[kk])
                fw.dma("sp", rwT[2, cs_, :], kk[:], reads=[kk], writes=[(rwT, (2, c))])
                fw.dve(lambda h: h.tensor_tensor(out=km[:], in0=km[:], in1=em[:], op=ALU.mult), r=[km, em], w=[km])
                fw.dma("sp", rwT[3, cs_, :], km[:], reads=[km], writes=[(rwT, (3, c))])
                fw.dve(lambda h: h.tensor_copy(out=wct[:], in_=ep[:, :].rearrange("p (c t) -> p c t", t=128)[:, :, 127]),
                       r=[ep], w=[wct])
                fw.dma("sp", self.scr["rw_wc"][cs_, :], wct[:], reads=[wct], writes=[(self.scr["rw_wc"], c)])

    def phase_rwkv_scan(self, l):
        fw = self.fw
        rwT, yT = self.scr["rwT"], self.scr["yT"]
        ident = self.ident
        with contextlib.ExitStack() as es:
            masks = self.sb(es, "rwmasks", [128, 5, 128])
            fw.dma("sp", masks[:], self.inp["rw_masks"][:, :, :], writes=[masks])
            vec = self.sb(es, "rwvec2", [128, 4, 8])
            fw.dma("sp", vec[:], self.inp["rw_vec"][l], writes=[vec])
            wc = self.sb(es, "wc", [128, 4, NT])
            fw.dma("sp", wc[:], self.scr["rw_wc"][:, :].rearrange("(c p) n -> p c n", p=128),
                   reads=[self.scr["rw_wc"]], writes=[wc])
            ST = self.sb(es, "ST", [128, 4, 64])
            fw.dve(lambda h: h.memset(ST[:], 0.0), w=[ST])
            fm = [self.sb(es, f"fm{i}", [128, 7, 4, 128]) for i in range(2)]
            tok = [self.sb(es, f"tok{i}", [128, 3, 512]) for i in range(2)]
            M4s = [self.sb(es, f"M4{i}", [128, 4, 128]) for i in range(2)]
            Lbs = [self.sb(es, f"Lb{i}", [128, 2, 128]) for i in range(2)]
            Xs = [self.sb(es, f"X{i}", [128, 7, 128]) for i in range(2)]
            Us = [self.sb(es, f"U{i}", [128, 64]) for i in range(2)]
            tS = [self.sb(es, f"tS{i}", [128, 64]) for i in range(2)]
            Ys = [self.sb(es, f"Y{i}", [128, 8, 64]) for i in range(2)]
            s8 = self.sb(es, "s8", [128, 8])
            r8 = self.sb(es, "r8", [128, 8])
            Yc = self.sb(es, "Yc", [128, 8, 64])
            sq = self.sb(es, "sq", [128, 8, 64])
            o32 = self.sb(es, "o32", [128, 4, 128])
            obs = [self.sb(es, f"ob{i}", [128, 4, 128], BF16) for i in range(2)]
            import os as _os
            _NC = int(_os.environ.get("RW_NC", NT)); _NH = int(_os.environ.get("RW_NH", 8)); _STG = int(_os.environ.get("RW_STAGE", 9))
            for c in range(_NC):
                cs = slice(c * 128, (c + 1) * 128)
                F, TK, Y, ob = fm[c % 2], tok[c % 2], Ys[c % 2], obs[c % 2]
                for q in range(7):
                    fw.dma("sp", F[:, q, :, :], rwT[q, :, cs].rearrange("(ct p) t -> p ct t", p=128),
                           reads=[rwT], writes=[(F, q)])
                for j, q in enumerate((4, 2, 3)):
                    ps = self.next_ps()
                    for ct in range(4):
                        fw.pe(lambda h: h.transpose(ps[:, ct * 128:(ct + 1) * 128], F[:, q, ct, :], ident[:]),
                              r=[(F, q), ident], w=[ps])
                    fw.act(lambda h: h.copy(out=TK[:, j, :], in_=ps[:, :]), r=[ps], w=[(TK, j)])
                for hd in range(_NH):
                    ct = hd // 2
                    hr = slice((hd % 2) * 64, (hd % 2) * 64 + 64)
                    At, Rt, Bt, Kt = (F[hr, q, ct, :] for q in range(4))
                    rA, rR, rB, rK = ((F, q) for q in range(4))
                    Vt = TK[:, 0, hd * 64:(hd + 1) * 64]
                    Bp = TK[:, 1, ct * 128:(ct + 1) * 128]
                    Kp = TK[:, 2, ct * 128:(ct + 1) * 128]
                    M4, Lb, X, U, tmpS = M4s[hd % 2], Lbs[hd % 2], Xs[hd % 2], Us[hd % 2], tS[hd % 2]
                    STh = ST[hr, ct, :]
                    rST = (ST, hd)
                    ps4 = self.next_ps()
                    for i, (lt, rh, rl, rr) in enumerate(((Bt, At, rB, rA), (Kt, At, rK, rA), (Bt, Rt, rB, rR), (Kt, Rt, rK, rR))):
                        fw.pe(lambda h: h.matmul(ps4[:, i * 128:(i + 1) * 128], lhsT=lt, rhs=rh, start=True, stop=True),
                              r=[rl, rr], w=[ps4])
                    fw.dve(lambda h: h.tensor_tensor(out=M4[:], in0=ps4[:, :].rearrange("p (a t) -> p a t", a=4),
                                                     in1=masks[:, 0:4, :], op=ALU.mult), r=[ps4, masks], w=[M4])
                    psL = self.next_ps()
                    fw.pe(lambda h: h.matmul(psL[:, 0:128], lhsT=At, rhs=Bt, start=True, stop=True), r=[rA, rB], w=[psL])
                    fw.dve(lambda h: h.tensor_tensor(out=Lb[:, 0, :], in0=psL[:, 0:128], in1=masks[:, 4, :], op=ALU.mult),
                           r=[psL, masks], w=[(Lb, 0)])
                    xj = lambda j: M4[:, 0, :] if j == 0 else X[:, j, :]
                    rxj = lambda j: M4 if j == 0 else (X, j)
                    if _STG < 1:
                        continue
                    _SQ = int(_os.environ.get("RW_SQ", 6)); _SQ2 = int(_os.environ.get("RW_SQ2", 1))
                    for j in range(_SQ):
                        psq = self.next_ps()
                        fw.pe(lambda h: h.matmul(psq[:, 0:128], lhsT=Lb[:, j % 2, :], rhs=xj(j), start=True, stop=True),
                              r=[(Lb, j % 2), rxj(j)], w=[psq])
                        if j < 5 and _SQ2:
                            psq2 = self.next_ps()
                            fw.pe(lambda h: h.matmul(psq2[:, 0:128], lhsT=xj(j), rhs=Lb[:, j % 2, :], start=True, stop=True),
                                  r=[(Lb, j % 2), rxj(j)], w=[psq2])
                        fw.act(lambda h: h.copy(out=X[:, j + 1, :], in_=psq[:, 0:128]), r=[psq], w=[(X, j + 1)])
                        if j < 5 and _SQ2:
                            fw.act(lambda h: h.copy(out=Lb[:, (j + 1) % 2, :], in_=psq2[:, 0:128]),
                                   r=[psq2], w=[(Lb, (j + 1) % 2)])
                    if _STG < 2:
                        continue
                    psz = self.next_ps()
                    fw.pe(lambda h: h.matmul(psz[:, 0:64], lhsT=M4[:, 1, :], rhs=Vt, start=True, stop=False),
                          r=[M4, (TK, 0)], w=[psz])
                    fw.pe(lambda h: h.matmul(psz[:, 0:64], lhsT=At, rhs=STh, start=False, stop=True),
                          r=[rA, rST], w=[psz])
                    fw.act(lambda h: h.copy(out=U[:], in_=psz[:, 0:64]), r=[psz], w=[U])
                    for j in range(7):
                        psu = self.next_ps()
                        fw.pe(lambda h: h.matmul(psu[:, 0:64], lhsT=xj(j), rhs=U[:], start=True, stop=True),
                              r=[rxj(j), U], w=[psu])
                        fw.dve(lambda h: h.tensor_tensor(out=U[:], in0=psu[:, 0:64], in1=U[:], op=ALU.add),
                               r=[psu, U], w=[U])
                    if _STG < 3:
                        continue
                    psy = self.next_ps()
                    fw.pe(lambda h: h.matmul(psy[:, 0:64], lhsT=Rt, rhs=STh, start=True, stop=False), r=[rR, rST], w=[psy])
                    fw.pe(lambda h: h.matmul(psy[:, 0:64], lhsT=M4[:, 2, :], rhs=U[:], start=False, stop=False),
                          r=[M4, U], w=[psy])
                    fw.pe(lambda h: h.matmul(psy[:, 0:64], lhsT=M4[:, 3, :], rhs=Vt, start=False, stop=True),
                          r=[M4, (TK, 0)], w=[psy])
                    fw.act(lambda h: h.copy(out=Y[:, hd, :], in_=psy[:, 0:64]), r=[psy], w=[(Y, hd)])
                    if _STG < 4:
                        continue
                    pss = self.next_ps()
                    fw.pe(lambda h: h.matmul(pss[:, 0:64], lhsT=Bp, rhs=U[:], start=True, stop=False), r=[(TK, 1), U], w=[pss])
                    fw.pe(lambda h: h.matmul(pss[:, 0:64], lhsT=Kp, rhs=Vt, start=False, stop=True),
                          r=[(TK, 2), (TK, 0)], w=[pss])
                    fw.dve(lambda h: h.tensor_tensor(out=tmpS[hr, :], in0=pss[hr, 0:64], in1=STh, op=ALU.add),
                           r=[pss, rST], w=[tmpS])
                    fw.dve(lambda h: h.tensor_scalar(out=STh, in0=tmpS[hr, :], scalar1=wc[hr, ct, c:c + 1], scalar2=None,
                                                     op0=ALU.mult), r=[tmpS, wc], w=[rST])
                if _STG < 5:
                    continue
                fw.dve(lambda h: h.tensor_reduce(out=s8[:], in_=Y[:], axis=AX.X, op=ALU.add), r=[Y], w=[s8])
                fw.dve(lambda h: h.tensor_scalar(out=s8[:], in0=s8[:], scalar1=1.0 / 64, scalar2=None, op0=ALU.mult),
                       r=[s8], w=[s8])
                fw.dve(lambda h: h.tensor_tensor(out=Yc[:], in0=Y[:], in1=s8[:, :, None].to_broadcast([128, 8, 64]),
                                                 op=ALU.subtract), r=[Y, s8], w=[Yc])
                fw.dve(lambda h: h.tensor_tensor(out=sq[:], in0=Yc[:], in1=Yc[:], op=ALU.mult), r=[Yc], w=[sq])
                fw.dve(lambda h: h.tensor_reduce(out=r8[:], in_=sq[:], axis=AX.X, op=ALU.add), r=[sq], w=[r8])
                fw.act(lambda h: h.activation(out=r8[:], in_=r8[:], func=AF.Sqrt, bias=64e-5, scale=1.0 / 64), r=[r8], w=[r8])
                fw.dve(lambda h: h.reciprocal(out=r8[:], in_=r8[:]), r=[r8], w=[r8])
                fw.dve(lambda h: h.tensor_tensor(out=Yc[:], in0=Yc[:], in1=r8[:, :, None].to_broadcast([128, 8, 64]),
                                                 op=ALU.mult), r=[Yc, r8], w=[Yc])
                pst = self.next_ps()
                ycf = Yc[:, :, :].rearrange("p h d -> p (h d)")
                for ct in range(4):
                    fw.pe(lambda h: h.transpose(pst[:, ct * 128:(ct + 1) * 128], ycf[:, ct * 128:(ct + 1) * 128], ident[:]),
                          r=[Yc, ident], w=[pst])
                for ct in range(4):
                    fw.dve(lambda h: h.tensor_scalar(out=o32[:, ct, :], in0=pst[:, ct * 128:(ct + 1) * 128],
                                                     scalar1=vec[:, ct, 5:6], scalar2=vec[:, ct, 6:7],
                                                     op0=ALU.mult, op1=ALU.add), r=[pst, vec], w=[(o32, ct)])
                fw.dve(lambda h: h.tensor_tensor(out=o32[:], in0=o32[:], in1=F[:, 5, :, :], op=ALU.add), r=[o32, (F, 5)], w=[o32])
                fw.dve(lambda h: h.tensor_tensor(out=ob[:], in0=o32[:], in1=F[:, 6, :, :], op=ALU.mult), r=[o32, (F, 6)], w=[ob])
                fw.dma("sp", yT[1, :, cs].rearrange("(ct p) t -> p ct t", p=128), ob[:], reads=[ob],
                       writes=[(yT, ("b", c))])

    def ln_tile(self, z, st, mv, gbc, bbc, gi):
        fw = self.fw
        for c in range(2):
            fw.dve(lambda h: h.bn_stats(out=st[:, c, :], in_=z[:, c * 512:(c + 1) * 512]), r=[z], w=[(st, c)])
        fw.dve(lambda h: h.bn_aggr(out=mv[:, 0:2], in_=st[:, :, :].rearrange("p a b -> p (a b)")), r=[st], w=[mv])
        fw.act(lambda h: h.activation(out=mv[:, 2:3], in_=mv[:, 1:2], func=AF.Sqrt, bias=1e-5, scale=1.0), r=[mv], w=[mv])
        fw.dve(lambda h: h.reciprocal(out=mv[:, 2:3], in_=mv[:, 2:3]), r=[mv], w=[mv])
        fw.dve(lambda h: h.tensor_scalar(out=z[:], in0=z[:], scalar1=mv[:, 0:1], scalar2=mv[:, 2:3],
                                         op0=ALU.subtract, op1=ALU.mult), r=[z, mv], w=[z])
        fw.dve(lambda h: h.tensor_tensor(out=z[:], in0=z[:], in1=gbc[:, gi, :], op=ALU.mult), r=[z, gbc], w=[z])
        fw.dve(lambda h: h.tensor_tensor(out=z[:], in0=z[:], in1=gbc[:, gi + 1, :], op=ALU.add), r=[z, gbc], w=[z])

    def store_x_and_xT(self, z, ti, x_d, xT_d, xo):
        fw = self.fw
        fw.dma("sp", x_d[ti * 128:(ti + 1) * 128, :], z[:], reads=[z], writes=[(x_d, ti)])
        for hh in range(2):
            ps = self.next_ps()
            for c in range(4):
                cc = hh * 4 + c
                fw.pe(lambda h: h.transpose(ps[:, c * 128:(c + 1) * 128], z[:, cc * 128:(cc + 1) * 128], self.ident[:]),
                      r=[z, self.ident], w=[ps])
            fw.act(lambda h: h.copy(out=xo[:, hh * 4:(hh + 1) * 4, :], in_=ps[:, :].rearrange("p (c t) -> p c t", c=4)),
                   r=[ps], w=[(xo, hh)])
        fw.dma("sp", xT_d[:, ti * 128:(ti + 1) * 128].rearrange("(c p) t -> p c t", p=128), xo[:], reads=[xo],
               writes=[(xT_d, ti)])

    def phase_merge(self, l, xT_d, xres_ap, xres_dep):
        fw = self.fw
        yT = self.scr["yT"]
        x1_d, x1T_d = self.scr[f"x1_{l}"], self.scr[f"x1T_{l}"]
        with contextlib.ExitStack() as es:
            xT = self.load_xT(es, xT_d)
            wg = self.sb(es, "wg", [128, 8, 3072], BF16)
            for b in range(3):
                src = self.inp["w_in"][l, :, C_GATE + b * 1024:C_GATE + (b + 1) * 1024].rearrange("(kc p) c -> p kc c", p=128)
                fw.dma("pool", wg[:, :, b * 1024:(b + 1) * 1024], src, writes=[(wg, b)])
            wb = self.sb(es, "wb", [128, 3, 4, D], BF16)
            for b in range(3):
                fw.dma("pool", wb[:, b, :, :], self.inp["w_branch"][l, b].rearrange("(kc p) c -> p kc c", p=128),
                       writes=[(wb, b)])
            wo = self.sb(es, "wo", [128, 8, D], BF16)
            fw.dma("pool", wo[:], self.inp["w_out"][l].rearrange("(kc p) c -> p kc c", p=128), writes=[wo])
            gbc = self.sb(es, "gbc", [128, 4, D])
            fw.dma("sp", gbc[:], self.inp["ln_gb"][l:l + 1, :, :].to_broadcast([128, 4, D]), writes=[gbc])
            yb = self.sb(es, "yb", [128, 3, 4, 512], BF16)
            mg = self.sb(es, "mg", [128, 8, 512], BF16)
            sg = self.sb(es, "sg", [128, 512])
            acc = self.sb(es, "acc", [128, 512])
            z = [self.sb(es, f"z{i}", [128, D]) for i in range(2)]
            xr = [self.sb(es, f"xr{i}", [128, D]) for i in range(2)]
            xo = [self.sb(es, f"xo{i}", [128, 8, 128], BF16) for i in range(2)]
            st = self.sb(es, "st", [128, 2, 6])
            mv = self.sb(es, "mv", [128, 4])
            for n in range(4):
                ts = slice(n * 512, (n + 1) * 512)
                for b in range(3):
                    fw.dma("sp", yb[:, b, :, :], yT[b, :, ts].rearrange("(c p) t -> p c t", p=128),
                           reads=[yT], writes=[(yb, b)])
                for dt in range(8):
                    for b in range(3):
                        ps = self.next_ps()
                        self.proj_fm(wg, xT, b * 1024 + dt * 128, 128, n, ps)
                        fw.act(lambda h: h.activation(out=sg[:], in_=ps[:, :], func=AF.Sigmoid), r=[ps], w=[sg])
                        ps2 = self.next_ps()
                        for kc in range(4):
                            fw.pe(lambda h: h.matmul(ps2[:, :], lhsT=wb[:, b, kc, dt * 128:(dt + 1) * 128],
                                                     rhs=yb[:, b, kc, :], start=(kc == 0), stop=(kc == 3)),
                                  r=[(wb, b), (yb, b)], w=[ps2])
                        if b == 0:
                            fw.dve(lambda h: h.tensor_tensor(out=acc[:], in0=ps2[:, :], in1=sg[:], op=ALU.mult),
                                   r=[ps2, sg], w=[acc])
                        else:
                            fw.dve(lambda h: h.tensor_tensor(out=sg[:], in0=ps2[:, :], in1=sg[:], op=ALU.mult),
                                   r=[ps2, sg], w=[sg])
                            dst = mg[:, dt, :] if b == 2 else acc[:]
                            fw.dve(lambda h: h.tensor_tensor(out=dst, in0=acc[:], in1=sg[:], op=ALU.add),
                                   r=[acc, sg], w=[(mg, dt)] if b == 2 else [acc])
                for tt in range(4):
                    ti = n * 4 + tt
                    zz, xx = z[ti % 2], xr[ti % 2]
                    fw.dma("sp", xx[:], xres_ap[ti * 128:(ti + 1) * 128, :],
                           reads=[(xres_dep, ti)] if xres_dep else [], writes=[xx])
                    for hf in range(2):
                        ps = self.next_ps()
                        for dt in range(8):
                            fw.pe(lambda h: h.matmul(ps[:, :], lhsT=mg[:, dt, tt * 128:(tt + 1) * 128],
                                                     rhs=wo[:, dt, hf * 512:(hf + 1) * 512], start=(dt == 0), stop=(dt == 7)),
                                  r=[(mg, dt), wo], w=[ps])
                        fw.dve(lambda h: h.scalar_tensor_tensor(out=zz[:, hf * 512:(hf + 1) * 512],
                                                                in0=xx[:, hf * 512:(hf + 1) * 512], scalar=ALPHA,
                                                                in1=ps[:, :], op0=ALU.mult, op1=ALU.add),
                               r=[xx, ps], w=[zz])
                    self.ln_tile(zz, st, mv, gbc, None, 0)
                    self.store_x_and_xT(zz, ti, x1_d, x1T_d, xo[ti % 2])

    def top16(self, vals, idxs, src, scratch, n):
        fw = self.fw
        fw.dve(lambda h: h.max(out=vals[:, 0:8], in_=src), r=[self._t16_src], w=[self._t16_v])
        fw.dve(lambda h: h.max_index(out=idxs[:, 0:8], in_max=vals[:, 0:8], in_values=src),
               r=[self._t16_src, self._t16_v], w=[self._t16_i])
        fw.dve(lambda h: h.match_replace(out=scratch, in_to_replace=vals[:, 0:8], in_values=src, imm_value=-1e30),
               r=[self._t16_src, self._t16_v], w=[self._t16_s])
        fw.dve(lambda h: h.max(out=vals[:, 8:16], in_=scratch), r=[self._t16_s], w=[self._t16_v])
        fw.dve(lambda h: h.max_index(out=idxs[:, 8:16], in_max=vals[:, 8:16], in_values=scratch),
               r=[self._t16_s, self._t16_v], w=[self._t16_i])

    def phase_peer(self, l, x2_d, x2T_d):
        fw = self.fw
        x1_d = self.scr[f"x1_{l}"]
        U_ap, V_ap = self.inp[f"peer_u{l}"], self.inp[f"peer_v{l}"]
        ident = self.ident
        with contextlib.ExitStack() as es:
            wq = self.sb(es, "wq", [128, 8, 2048])
            for kc in range(8):
                fw.dma("sp", wq[:, kc, :], self.inp["peer_wq"][l, kc * 128:(kc + 1) * 128, :], writes=[(wq, kc)])
            skT = self.sb(es, "skT", [128, 2, 128])
            fw.dma("sp", skT[:], self.inp["peer_skT"][l].rearrange("a d n -> d a n"), writes=[skT])
            iota = self.sb(es, "iota16", [128, 16])
            fw.dma("sp", iota[:], self.inp["iota16"][:, :], writes=[iota])
            gbc = self.sb(es, "gbc2", [128, 4, D])
            fw.dma("sp", gbc[:], self.inp["ln_gb"][l:l + 1, :, :].to_broadcast([128, 4, D]), writes=[gbc])
            xts = [self.sb(es, f"pxt{i}", [128, D]) for i in range(2)]
            xTf = self.sb(es, "xTf", [128, 8, 128])
            qT = self.sb(es, "qT", [128, 16, 128])
            sc = self.sb(es, "sc", [128, 16, 128])
            sc2 = self.sb(es, "sc2", [128, 16, 128])
            v16 = self.sb(es, "v16", [128, 16, 16])
            i16 = self.sb(es, "i16", [128, 16, 16], U32)
            i16f = self.sb(es, "i16f", [128, 8, 2, 16])
            cand = self.sb(es, "cand", [128, 8, 256])
            cand2 = self.sb(es, "cand2", [128, 8, 256])
            top = self.sb(es, "top", [128, 8, 16])
            pos = self.sb(es, "pos", [128, 8, 16], U32)
            prc = self.sb(es, "prc", [128, 2, 8, 16], U32)
            prcf = self.sb(es, "prcf", [128, 2, 8, 16])
            oh = self.sb(es, "oh", [128, 128, 16])
            sel = self.sb(es, "sel", [128, 2, 128])
            idxf = self.sb(es, "idxf", [128, 128])
            idx = self.sb(es, "idx", [128, 128], U32)
            gate = self.sb(es, "gate", [128, 8, 16])
            gs = self.sb(es, "gs", [128, 8])
            actv = self.sb(es, "actv", [128, 128])
            gtmp = self.sb(es, "gtmp", [128, 128])
            wgt = self.sb(es, "wgt", [128, 128])
            junk = self.sb(es, "junk", [128, D])
            rows = [self.sb(es, f"rows{i}", [128, D]) for i in range(6)]
            ys = [self.sb(es, f"py{i}", [128, D]) for i in range(2)]
            xo = [self.sb(es, f"pxo{i}", [128, 8, 128], BF16) for i in range(2)]
            st = self.sb(es, "pst", [128, 2, 6])
            mv = self.sb(es, "pmv", [128, 4])
            rr = 0
            for ti in range(NT):
                xt, y = xts[ti % 2], ys[ti % 2]
                fw.dma("sp", xt[:], x1_d[ti * 128:(ti + 1) * 128, :], reads=[(x1_d, ti)], writes=[xt])
                for hh in range(2):
                    ps = self.next_ps()
                    for c in range(4):
                        cc = hh * 4 + c
                        fw.pe(lambda h: h.transpose(ps[:, c * 128:(c + 1) * 128], xt[:, cc * 128:(cc + 1) * 128], ident[:]),
                              r=[xt, ident], w=[ps])
                    fw.act(lambda h: h.copy(out=xTf[:, hh * 4:(hh + 1) * 4, :], in_=ps[:, :].rearrange("p (c t) -> p c t", c=4)),
                           r=[ps], w=[(xTf, hh)])
                for g4 in range(4):
                    ps = self.next_ps()
                    for b4 in range(4):
                        blk = g4 * 4 + b4
                        for kc in range(8):
                            fw.pe(lambda h: h.matmul(ps[:, b4 * 128:(b4 + 1) * 128], lhsT=wq[:, kc, blk * 128:(blk + 1) * 128],
                                                     rhs=xTf[:, kc, :], start=(kc == 0), stop=(kc == 7)),
                                  r=[(wq, kc), xTf], w=[ps])
                    fw.act(lambda h: h.copy(out=qT[:, g4 * 4:(g4 + 1) * 4, :], in_=ps[:, :].rearrange("p (b t) -> p b t", b=4)),
                           r=[ps], w=[(qT, g4)])
                for g4 in range(4):
                    ps = self.next_ps()
                    for b4 in range(4):
                        blk = g4 * 4 + b4
                        fw.pe(lambda h: h.matmul(ps[:, b4 * 128:(b4 + 1) * 128], lhsT=qT[:, blk, :], rhs=skT[:, blk % 2, :],
                                                 start=True, stop=True), r=[(qT, g4), skT], w=[ps])
                    fw.act(lambda h: h.copy(out=sc[:, g4 * 4:(g4 + 1) * 4, :], in_=ps[:, :].rearrange("p (b n) -> p b n", b=4)),
                           r=[ps], w=[(sc, g4)])
                self._t16_src, self._t16_v, self._t16_i, self._t16_s = sc, v16, i16, sc2
                for blk in range(16):
                    self.top16(v16[:, blk, :], i16[:, blk, :], sc[:, blk, :], sc2[:, blk, :], 128)
                v4 = v16[:, :, :].rearrange("p (h a) k -> p h a k", a=2)
                fw.dve(lambda h: h.tensor_tensor(
                    out=cand[:, :, :].rearrange("p h (i j) -> p h i j", i=16),
                    in0=v4[:, :, 0, :][:, :, :, None].to_broadcast([128, 8, 16, 16]),
                    in1=v4[:, :, 1, :][:, :, None, :].to_broadcast([128, 8, 16, 16]), op=ALU.add), r=[v16], w=[cand])
                self._t16_src, self._t16_v, self._t16_i, self._t16_s = cand, top, pos, cand2
                for hd in range(8):
                    self.top16(top[:, hd, :], pos[:, hd, :], cand[:, hd, :], cand2[:, hd, :], 256)
                fw.dve(lambda h: h.tensor_scalar(out=prc[:, 0, :, :], in0=pos[:], scalar1=4, scalar2=None,
                                                 op0=ALU.logical_shift_right), r=[pos], w=[prc])
                fw.dve(lambda h: h.tensor_scalar(out=prc[:, 1, :, :], in0=pos[:], scalar1=15, scalar2=None,
                                                 op0=ALU.bitwise_and), r=[pos], w=[prc])
                fw.dve(lambda h: h.tensor_copy(out=prcf[:], in_=prc[:]), r=[prc], w=[prcf])
                fw.dve(lambda h: h.tensor_copy(out=i16f[:], in_=i16[:, :, :].rearrange("p (h a) k -> p h a k", a=2)),
                       r=[i16], w=[i16f])
                for a in range(2):
                    fw.dve(lambda h: h.tensor_tensor(
                        out=oh[:], in0=iota[:, None, :].to_broadcast([128, 128, 16]),
                        in1=prcf[:, a, :, :].rearrange("p h k -> p (h k)")[:, :, None].to_broadcast([128, 128, 16]),
                        op=ALU.is_equal), r=[iota, prcf], w=[oh])
                    fw.dve(lambda h: h.tensor_tensor(
                        out=oh[:, :, :].rearrange("p (h k) j -> p h k j", h=8),
                        in0=oh[:, :, :].rearrange("p (h k) j -> p h k j", h=8),
                        in1=i16f[:, :, a, :][:, :, None, :].to_broadcast([128, 8, 16, 16]), op=ALU.mult),
                        r=[oh, i16f], w=[oh])
                    fw.dve(lambda h: h.tensor_reduce(out=sel[:, a, :], in_=oh[:], axis=AX.X, op=ALU.add), r=[oh], w=[(sel, a)])
                fw.dve(lambda h: h.scalar_tensor_tensor(out=idxf[:], in0=sel[:, 0, :], scalar=128.0, in1=sel[:, 1, :],
                                                        op0=ALU.mult, op1=ALU.add), r=[sel], w=[idxf])
                fw.dve(lambda h: h.tensor_copy(out=idx[:], in_=idxf[:]), r=[idxf], w=[idx])
                fw.dve(lambda h: h.tensor_tensor(out=gate[:], in0=top[:], in1=top[:, :, 0:1].to_broadcast([128, 8, 16]),
                                                 op=ALU.subtract), r=[top], w=[gate])
                fw.act(lambda h: h.activation(out=gate[:], in_=gate[:], func=AF.Exp), r=[gate], w=[gate])
                fw.dve(lambda h: h.tensor_reduce(out=gs[:], in_=gate[:], axis=AX.X, op=ALU.add), r=[gate], w=[gs])
                fw.dve(lambda h: h.reciprocal(out=gs[:], in_=gs[:]), r=[gs], w=[gs])
                fw.dve(lambda h: h.tensor_tensor(out=gate[:], in0=gate[:], in1=gs[:, :, None].to_broadcast([128, 8, 16]),
                                                 op=ALU.mult), r=[gate, gs], w=[gate])
                for j in range(128):
                    rb = rows[rr % 6]
                    rr += 1
                    fw.dma("pool", rb[:], U_ap[:, :], reads=[idx], writes=[rb],
                           indirect=bass.IndirectOffsetOnAxis(ap=idx[:, j:j + 1], axis=0))
                    fw.dve(lambda h: h.scalar_tensor_tensor(out=junk[:], in0=rb[:], scalar=1.0, in1=xt[:], op0=ALU.mult,
                                                            op1=ALU.mult, accum_out=actv[:, j:j + 1]),
                           r=[rb, xt], w=[junk, (actv, j)])
                self.gelu_tanh(None, actv, wgt, gtmp)
                fw.dve(lambda h: h.tensor_tensor(out=wgt[:], in0=wgt[:], in1=gate[:, :, :].rearrange("p h k -> p (h k)"),
                                                 op=ALU.mult), r=[wgt, gate], w=[wgt])
                for j in range(128):
                    rb = rows[rr % 6]
                    rr += 1
                    fw.dma("pool", rb[:], V_ap[:, :], reads=[idx], writes=[rb],
                           indirect=bass.IndirectOffsetOnAxis(ap=idx[:, j:j + 1], axis=0))
                    if j == 0:
                        fw.dve(lambda h: h.tensor_scalar(out=y[:], in0=rb[:], scalar1=wgt[:, 0:1], scalar2=None, op0=ALU.mult),
                               r=[rb, wgt], w=[y])
                    else:
                        fw.dve(lambda h: h.scalar_tensor_tensor(out=y[:], in0=rb[:], scalar=wgt[:, j:j + 1], in1=y[:],
                                                                op0=ALU.mult, op1=ALU.add), r=[rb, wgt, y], w=[y])
                fw.dve(lambda h: h.scalar_tensor_tensor(out=y[:], in0=xt[:], scalar=ALPHA, in1=y[:], op0=ALU.mult, op1=ALU.add),
                       r=[xt, y], w=[y])
                self.ln_tile(y, st, mv, gbc, None, 2)
                self.store_x_and_xT(y, ti, x2_d, x2T_d, xo[ti % 2])

    def phase_mla(self, l, xT_d):
        fw, nc = self.fw, self.nc
        yT = self.scr["yT"]
        SC = 96.0 ** -0.5
        with contextlib.ExitStack() as es:
            qn = self.sb(es, "qn", [64, 8, S], BF16)
            qp = self.sb(es, "qp", [32, 8, S], BF16)
            kn = self.sb(es, "kn", [64, 8, S], BF16)
            kp = self.sb(es, "kp", [32, S], BF16)
            V = self.sb(es, "V", [128, NT, 8, 65], BF16)
            cmask = self.sb(es, "cmask", [128, 128], BF16)
            identb = self.sb(es, "identb", [128, 128], BF16)
            fw.dma("pool", cmask[:], self.inp["cmask"][:, :], writes=[cmask])
            fw.dve(lambda h: h.tensor_copy(out=identb[:], in_=self.ident[:]), r=[self.ident], w=[identb])
            fw.pool(lambda h: h.memset(V[:], 1.0), w=[V])
            with contextlib.ExitStack() as es2:
                xT = self.load_xT(es2, xT_d)
                wt = self.sb(es2, "mw", [128, 8, 448], BF16)
                self.load_w_cols(wt, l, C_MLA, 416, key=0)
                for hf in range(2):
                    src = self.inp["w_in"][l, :, C_MLA + 384 + (1 - hf) * 16:C_MLA + 384 + (2 - hf) * 16].rearrange(
                        "(kc p) c -> p kc c", p=128)
                    fw.dma("pool", wt[:, :, 416 + hf * 16:416 + (hf + 1) * 16], src, writes=[(wt, 1 + hf)])
                vec = self.sb(es2, "mvec", [128, 3])
                fw.dma("sp", vec[:], self.inp["mla_vec"][l], writes=[vec])
                cs = self.sb(es2, "cs", [32, 2, S])
                fw.dma("sp", cs[:], self.inp["rope_cs"].rearrange("a p t -> p a t"), writes=[cs])
                wuq = self.sb(es2, "wuq", [128, 2, 768], BF16)
                fw.dma("pool", wuq[:], self.inp["mla_w_uq"][l].rearrange("(kc p) c -> p kc c", p=128), writes=[wuq])
                wuqr = self.sb(es2, "wuqr", [128, 2, 256], BF16)
                fw.dma("pool", wuqr[:], self.inp["mla_w_uq_rot"][l].rearrange("(kc p) c -> p kc c", p=128), writes=[wuqr])
                wk = self.sb(es2, "wk", [128, 512], BF16)
                fw.dma("pool", wk[:], self.inp["mla_w_ukv_k"][l], writes=[wk])
                wv = self.sb(es2, "wv", [128, 512], BF16)
                fw.dma("pool", wv[:], self.inp["mla_w_ukv_v"][l], writes=[wv])
                ones = self.sb(es2, "ones", [128, 128])
                fw.pool(lambda h: h.memset(ones[:], 1.0), w=[ones])
                cl = self.sb(es2, "cl", [128, 3, 512])
                sq = self.sb(es2, "sq", [128, 3, 512])
                rs = self.sb(es2, "rs", [128, 2, 512])
                cn = self.sb(es2, "cn", [128, 3, 512], BF16)
                kr = self.sb(es2, "kr", [32, 2, 512])
                tmp32 = self.sb(es2, "tmp32", [32, 2, 512])
                for n in range(4):
                    ts = slice(n * 512, (n + 1) * 512)
                    for j in range(3):
                        ps = self.next_ps()
                        self.proj_fm(wt, xT, j * 128, 128, n, ps)
                        fw.act(lambda h: h.copy(out=cl[:, j, :], in_=ps[:, :]), r=[ps], w=[(cl, j)])
                        fw.dve(lambda h: h.tensor_tensor(out=sq[:, j, :], in0=cl[:, j, :], in1=cl[:, j, :], op=ALU.mult),
                               r=[(cl, j)], w=[(sq, j)])
                    for j in range(2):
                        ps = self.next_ps()
                        self.proj_fm(wt, xT, 384 + j * 32, 32, n, ps)
                        fw.act(lambda h: h.copy(out=kr[:, j, :], in_=ps[0:32, :]), r=[ps], w=[(kr, j)])
                    for g, (tiles, dim) in enumerate((((0, 1), 256.0), ((2,), 128.0))):
                        ps = self.next_ps()
                        for ii, j in enumerate(tiles):
                            fw.pe(lambda h: h.matmul(ps[:, :], lhsT=ones[:, :], rhs=sq[:, j, :],
                                                     start=(ii == 0), stop=(ii == len(tiles) - 1)),
                                  r=[ones, (sq, j)], w=[ps])
                        fw.act(lambda h: h.activation(out=rs[:, g, :], in_=ps[:, :], func=AF.Sqrt, bias=1e-6,
                                                      scale=1.0 / dim), r=[ps], w=[(rs, g)])
                        fw.dve(lambda h: h.reciprocal(out=rs[:, g, :], in_=rs[:, g, :]), r=[(rs, g)], w=[(rs, g)])
                        for j in tiles:
                            fw.dve(lambda h: h.scalar_tensor_tensor(out=cn[:, j, :], in0=cl[:, j, :],
                                                                    scalar=vec[:, j:j + 1], in1=rs[:, g, :],
                                                                    op0=ALU.mult, op1=ALU.mult),
                                   r=[(cl, j), vec, (rs, g)], w=[(cn, j)])
                    fw.dve(lambda h: h.tensor_tensor(out=tmp32[:, 0, :], in0=kr[:, 0, :], in1=cs[:, 0, ts], op=ALU.mult),
                           r=[(kr, 0), cs], w=[(tmp32, 0)])
                    fw.dve(lambda h: h.tensor_tensor(out=tmp32[:, 1, :], in0=kr[:, 1, :], in1=cs[:, 1, ts], op=ALU.mult),
                           r=[(kr, 1), cs], w=[(tmp32, 1)])
                    fw.dve(lambda h: h.tensor_tensor(out=kp[:, ts], in0=tmp32[:, 0, :], in1=tmp32[:, 1, :], op=ALU.add),
                           r=[tmp32], w=[(kp, n)])
                    for hd in range(8):
                        ps = self.next_ps()
                        for kc in range(2):
                            fw.pe(lambda h: h.matmul(ps[0:64, :], lhsT=wuq[:, kc, hd * 96:hd * 96 + 64],
                                                     rhs=cn[:, kc, :], start=(kc == 0), stop=(kc == 1)),
                                  r=[wuq, (cn, kc)], w=[ps])
                        fw.act(lambda h: h.activation(out=qn[:, hd, ts], in_=ps[0:64, :], func=AF.Copy, scale=SC),
                               r=[ps], w=[(qn, (hd, n))])
                        ps = self.next_ps()
                        for kc in range(2):
                            fw.pe(lambda h: h.matmul(ps[0:32, :], lhsT=wuq[:, kc, hd * 96 + 64:hd * 96 + 96],
                                                     rhs=cn[:, kc, :], start=(kc == 0), stop=(kc == 1)),
                                  r=[wuq, (cn, kc)], w=[ps])
                        ps2 = self.next_ps()
                        for kc in range(2):
                            fw.pe(lambda h: h.matmul(ps2[0:32, :], lhsT=wuqr[:, kc, hd * 32:hd * 32 + 32],
                                                     rhs=cn[:, kc, :], start=(kc == 0), stop=(kc == 1)),
                                  r=[wuqr, (cn, kc)], w=[ps2])
                        fw.dve(lambda h: h.tensor_tensor(out=tmp32[:, 0, :], in0=ps[0:32, :], in1=cs[:, 0, ts], op=ALU.mult),
                               r=[ps, cs], w=[(tmp32, 0)])
                        fw.dve(lambda h: h.tensor_tensor(out=tmp32[:, 1, :], in0=ps2[0:32, :], in1=cs[:, 1, ts], op=ALU.mult),
                               r=[ps2, cs], w=[(tmp32, 1)])
                        fw.dve(lambda h: h.scalar_tensor_tensor(out=qp[:, hd, ts], in0=tmp32[:, 0, :], scalar=1.0,
                                                                in1=tmp32[:, 1, :], op0=ALU.mult, op1=ALU.add),
                               r=[tmp32], w=[(qp, (hd, n))])
                        fw.dve(lambda h: h.tensor_scalar(out=qp[:, hd, ts], in0=qp[:, hd, ts], scalar1=SC, scalar2=None,
                                                         op0=ALU.mult), r=[(qp, (hd, n))], w=[(qp, (hd, n))])
                        ps = self.next_ps()
                        fw.pe(lambda h: h.matmul(ps[0:64, :], lhsT=wk[:, hd * 64:(hd + 1) * 64], rhs=cn[:, 2, :],
                                                 start=True, stop=True), r=[wk, (cn, 2)], w=[ps])
                        fw.act(lambda h: h.copy(out=kn[:, hd, ts], in_=ps[0:64, :]), r=[ps], w=[(kn, (hd, n))])
                    for tt in range(4):
                        ti = n * 4 + tt
                        ps = self.next_ps()
                        fw.pe(lambda h: h.matmul(ps[:, :], lhsT=cn[:, 2, tt * 128:(tt + 1) * 128], rhs=wv[:, :],
                                                 start=True, stop=True), r=[(cn, 2), wv], w=[ps])
                        fw.act(lambda h: h.copy(out=V[:, ti, :, 0:64], in_=ps[:, :].rearrange("p (h d) -> p h d", h=8)),
                               r=[ps], w=[(V, ti)])
            fw.barrier()
            with contextlib.ExitStack() as es3:
                PT = self.sb(es3, "PT", [128, NT, 8, 128], BF16)
                rec = self.sb(es3, "rec", [128, 8, 1])
                yc = self.sb(es3, "yc", [128, 8, 64], BF16)
                yct = [self.sb(es3, f"yct{i}", [128, 4, 128], BF16) for i in range(2)]
                for qi in range(NT):
                    qs = slice(qi * 128, (qi + 1) * 128)
                    for kt in range(qi + 1):
                        ks = slice(kt * 128, (kt + 1) * 128)
                        for hg in range(2):
                            ps = self.next_ps()
                            for hh in range(4):
                                hd = hg * 4 + hh
                                o = ps[:, hh * 128:(hh + 1) * 128]
                                fw.pe(lambda h: h.matmul(o, lhsT=kn[:, hd, ks], rhs=qn[:, hd, qs], start=True, stop=False),
                                      r=[kn, qn], w=[ps])
                                fw.pe(lambda h: h.matmul(o, lhsT=kp[:, ks], rhs=qp[:, hd, qs], start=False, stop=True),
                                      r=[kp, qp], w=[ps])
                            fw.act(lambda h: h.activation(
                                out=PT[:, kt, hg * 4:(hg + 1) * 4, :],
                                in_=ps[:, :].rearrange("p (h q) -> p h q", h=4), func=AF.Exp),
                                r=[ps], w=[(PT, (kt, hg))])
                            if kt == qi:
                                fw.dve(lambda h: h.tensor_tensor(
                                    out=PT[:, kt, hg * 4:(hg + 1) * 4, :], in0=PT[:, kt, hg * 4:(hg + 1) * 4, :],
                                    in1=cmask[:, None, :].to_broadcast([128, 4, 128]), op=ALU.mult),
                                    r=[(PT, (kt, hg)), cmask], w=[(PT, (kt, hg))])
                    pss = [self.next_ps(), self.next_ps()]
                    for hd in range(8):
                        ps = pss[hd // 4]
                        o = ps[:, (hd % 4) * 65:(hd % 4) * 65 + 65]
                        for kt in range(qi + 1):
                            fw.pe(lambda h: h.matmul(o, lhsT=PT[:, kt, hd, :], rhs=V[:, kt, hd, :],
                                                     start=(kt == 0), stop=(kt == qi)),
                                  r=[(PT, (kt, hd // 4)), (V, kt)], w=[ps])
                    for hg in range(2):
                        ps = pss[hg]
                        pv = ps[:, 0:260].rearrange("p (h d) -> p h d", h=4)
                        fw.dve(lambda h: h.reciprocal(out=rec[:, hg * 4:(hg + 1) * 4, :], in_=pv[:, :, 64:65]),
                               r=[ps], w=[(rec, hg)])
                        fw.dve(lambda h: h.tensor_tensor(out=yc[:, hg * 4:(hg + 1) * 4, :], in0=pv[:, :, 0:64],
                                                         in1=rec[:, hg * 4:(hg + 1) * 4, :].to_broadcast([128, 4, 64]),
                                                         op=ALU.mult), r=[ps, (rec, hg)], w=[(yc, hg)])
                    o = yct[qi % 2]
                    ycf = yc[:, :, :].rearrange("p h d -> p (h d)")
                    for c in range(4):
                        pst = self.next_ps()
                        pb = pst[:, :].bitcast(BF16)
                        fw.pe(lambda h: h.transpose(pb[:, 0:128], ycf[:, c * 128:(c + 1) * 128], identb[:]),
                              r=[yc, identb], w=[pst])
                        fw.act(lambda h: h.copy(out=o[:, c, :], in_=pb[:, 0:128]), r=[pst], w=[(o, c)])
                    fw.dma("sp", yT[2, :, qs].rearrange("(c p) t -> p c t", p=128), o[:], reads=[o],
                           writes=[(yT, ("c", qi))])


def prep_inputs(inputs, nseq=1):
    f = lambda a: np.ascontiguousarray(np.asarray(a, dtype=np.float32))
    shared = {}
    shared["w_in"] = f(inputs["w_in"])
    rv = np.stack([inputs["rg_conv_w"][:, 0], inputs["rg_conv_w"][:, 1], inputs["rg_conv_w"][:, 2],
                   inputs["rg_conv_w"][:, 3], inputs["rg_conv_b"], inputs["rg_ba"], inputs["rg_bx"],
                   inputs["rg_log_a"]], axis=-1)
    shared["rg_vec"] = f(rv.reshape(L, 4, 128, 8).transpose(0, 2, 1, 3))
    shared["rg_wa"] = f(inputs["rg_wa"])
    shared["rg_wx"] = f(inputs["rg_wx"])
    shared["ident"] = np.eye(128, dtype=np.float32)
    mv = np.stack([inputs["mla_q_norm"][:, :128], inputs["mla_q_norm"][:, 128:], inputs["mla_kv_norm"]], axis=-1)
    shared["mla_vec"] = f(mv)
    wuq = np.asarray(inputs["mla_w_uq"], np.float32)
    shared["mla_w_uq"] = f(wuq)
    w4 = wuq.reshape(L, 256, 8, 96)
    shared["mla_w_uq_rot"] = f(np.concatenate([w4[..., 80:96], w4[..., 64:80]], -1).reshape(L, 256, 256))
    wkv = np.asarray(inputs["mla_w_ukv"], np.float32).reshape(L, 128, 8, 128)
    shared["mla_w_ukv_k"] = f(wkv[..., :64].reshape(L, 128, 512))
    shared["mla_w_ukv_v"] = f(wkv[..., 64:].reshape(L, 128, 512))
    pos = np.arange(S, dtype=np.float32)
    inv = (10000.0 ** (-np.arange(16, dtype=np.float32) / 16)).astype(np.float32)
    ang = pos[None, :] * inv[:, None]
    cs_, sn_ = np.cos(ang).astype(np.float32), np.sin(ang).astype(np.float32)
    shared["rope_cs"] = f(np.stack([np.concatenate([cs_, cs_], 0), np.concatenate([-sn_, sn_], 0)], 0))
    shared["cmask"] = f(np.triu(np.ones((128, 128), np.float32)))
    shared["rw_mix"] = f(inputs["rw_mix"])[:, None, :]
    vz = np.zeros((L, 512), np.float32); vz[1:] = inputs["rw_v0"]
    rwv = np.stack([inputs["rw_w0"], inputs["rw_a0"], inputs["rw_k_k"], inputs["rw_k_a"],
                    np.asarray(inputs["rw_r_k"]).reshape(L, 512), inputs["rw_gn_g"], inputs["rw_gn_b"], vz], axis=-1)
    shared["rw_vec"] = f(rwv.reshape(L, 4, 128, 8).transpose(0, 2, 1, 3))
    shared["rw_w2a2"] = f(np.concatenate([inputs["rw_w2"], inputs["rw_a2"]], axis=1))
    shared["rw_g2"] = f(inputs["rw_g2"])
    shared["rw_v1"] = f(inputs["rw_v1"])
    shared["rw_v2"] = f(inputs["rw_v2"])
    su = np.triu(np.ones((128, 128), np.float32), 1); ui = np.triu(np.ones((128, 128), np.float32))
    shared["rw_masks"] = f(np.stack([su, su, ui, ui, su.T], axis=1))
    obd = np.zeros((128, 128), np.float32); obd[:64, :64] = 1; obd[64:, 64:] = 1
    shared["onesbd"] = obd
    shared["w_branch"] = f(inputs["w_branch"])
    shared["w_out"] = f(inputs["w_out"])
    shared["peer_wq"] = f(inputs["peer_w_query"])
    shared["peer_skT"] = f(np.asarray(inputs["peer_subkeys"]).transpose(0, 1, 3, 2))
    shared["iota16"] = f(np.tile(np.arange(16, dtype=np.float32)[None, :], (128, 1)))
    for l in range(L):
        shared[f"peer_u{l}"] = f(inputs["peer_u"][l])
        shared[f"peer_v{l}"] = f(inputs["peer_v"][l])
    shared["ln_gb"] = f(np.stack([inputs["ln1_g"], inputs["ln1_b"], inputs["ln2_g"], inputs["ln2_b"]], axis=1))
    maps = []
    for b in range(8 // nseq):
        m = dict(shared)
        m["x"] = f(inputs["x"][b * nseq:(b + 1) * nseq])
        maps.append(m)
    return maps


def run(inputs, debug=False, phases=None, cores=8, trace=False, nseq=1):
    bld = Builder(debug=debug, phases=phases, nseq=nseq)
    nc = bld.build()
    maps = prep_inputs(inputs, nseq)[:cores]
    maps = [{k: v for k, v in m.items() if k in bld.inp} for m in maps]
    res = run_bass_kernel_spmd(nc, maps, core_ids=list(range(cores)), trace=trace)
    return res


NCORES = 4


def kernel(**inputs):
    nseq = 8 // NCORES
    res = run(inputs, cores=NCORES, nseq=nseq)
    return np.concatenate([np.asarray(r["out"]) for r in res.results], axis=0).astype(np.float32)
```

```python
import contextlib
import numpy as np
import concourse.bass as bass
import concourse.mybir as mybir
from concourse.bass_utils import run_bass_kernel_spmd

F32 = mybir.dt.float32
BF16 = mybir.dt.bfloat16
U32 = mybir.dt.uint32
I32 = mybir.dt.int32
AF = mybir.ActivationFunctionType
ALU = mybir.AluOpType
AX = mybir.AxisListType

D = 1024
S = 2048
L = 2
NT = S // 128
IN_COLS = 6304
C_RG, C_RW, C_MLA, C_GATE = 0, 1024, 2816, 3232
ALPHA = (2.0 * L) ** 0.25


class Dep:
    def __init__(self, name):
        self.name = name
        self.st = {}

    def states(self, key):
        if key is None:
            if None not in self.st:
                self.st[None] = [None, {}]
            return list(self.st.values())
        out = []
        if None in self.st:
            out.append(self.st[None])
        if key not in self.st:
            self.st[key] = [None, {}]
        out.append(self.st[key])
        return out


class T:
    def __init__(self, t, name):
        self.t = t
        self.dep = Dep(name)

    def __getitem__(self, idx):
        return self.t[idx]


class Eng:
    def __init__(self, name, h, sem):
        self.name, self.h, self.sem = name, h, sem
        self.count = 0
        self.known = {}


class FW:
    NDS = {"sp": 16, "pool": 12, "act": 6}

    def __init__(self, nc, es):
        self.nc = nc
        self.eng = {}
        for name, h in (("pe", nc.tensor), ("dve", nc.vector), ("act", nc.scalar),
                        ("pool", nc.gpsimd), ("sp", nc.sync)):
            sem = es.enter_context(nc.semaphore("sem_" + name))
            self.eng[name] = Eng(name, h, sem)
        self.dsem = {}
        self.dcnt = {}
        self.drr = {}
        for q, n in self.NDS.items():
            self.dsem[q] = [es.enter_context(nc.semaphore(f"ds_{q}{i}")) for i in range(n)]
            self.dcnt[q] = [0] * n
            self.drr[q] = 0

    def _wait(self, e, sid, sem, val):
        if val > 0 and e.known.get(sid, 0) < val:
            e.h.wait_ge(sem, val)
            e.known[sid] = val

    def _collect(self, reads, writes, ename):
        need = {}

        def add(tok):
            if tok is None:
                return
            sid, sem, val = tok
            if ename == "pe" and sid == "pe":
                return
            if need.get(sid, (None, 0))[1] < val:
                need[sid] = (sem, val)

        rs, ws = [], []
        for r in reads:
            t, k = r if isinstance(r, tuple) else (r, None)
            sts = t.dep.states(k)
            rs.append((k, sts))
            for st in sts:
                add(st[0])
        for w in writes:
            t, k = w if isinstance(w, tuple) else (w, None)
            sts = t.dep.states(k)
            ws.append((k, sts))
            for st in sts:
                add(st[0])
                for tok in st[1].values():
                    add(tok)
        return need, rs, ws

    def _record(self, tok, rs, ws):
        for k, sts in rs:
            for st in (sts if k is None else sts[-1:]):
                st[1][tok[0]] = tok
        for k, sts in ws:
            for st in (sts if k is None else sts[-1:]):
                st[0] = tok
                st[1] = {}

    def op(self, ename, fn, reads=(), writes=()):
        e = self.eng[ename]
        need, rs, ws = self._collect(reads, writes, ename)
        for sid, (sem, val) in need.items():
            self._wait(e, sid, sem, val)
        ins = fn(e.h)
        e.count += 1
        ins.then_inc(e.sem, 1)
        self._record((ename, e.sem, e.count), rs, ws)
        return ins

    def dma(self, q, out, in_, reads=(), writes=(), indirect=None, **kw):
        e = self.eng[q]
        i = self.drr[q]
        self.drr[q] = (i + 1) % len(self.dsem[q])
        sem = self.dsem[q][i]
        sid = ("d", q, i)
        self._wait(e, sid, sem, 16 * self.dcnt[q][i])
        need, rs, ws = self._collect(reads, writes, q)
        for s2, (sm, val) in need.items():
            self._wait(e, s2, sm, val)
        if indirect is None:
            ins = e.h.dma_start(out=out, in_=in_, **kw)
        else:
            ins = e.h.indirect_dma_start(out=out, out_offset=None, in_=in_, in_offset=indirect, **kw)
        self.dcnt[q][i] += 1
        ins.then_inc(sem, 16)
        self._record((sid, sem, 16 * self.dcnt[q][i]), rs, ws)
        return ins

    def barrier(self):
        for e in self.eng.values():
            for o in self.eng.values():
                if o is not e:
                    self._wait(e, o.name, o.sem, o.count)
            for q in self.dsem:
                for i, sem in enumerate(self.dsem[q]):
                    self._wait(e, ("d", q, i), sem, 16 * self.dcnt[q][i])

    def dve(self, fn, r=(), w=()):
        return self.op("dve", fn, r, w)

    def act(self, fn, r=(), w=()):
        return self.op("act", fn, r, w)

    def pe(self, fn, r=(), w=()):
        return self.op("pe", fn, r, w)

    def pool(self, fn, r=(), w=()):
        return self.op("pool", fn, r, w)


class Builder:
    def __init__(self, debug=False, phases=None, nseq=1):
        self.debug = debug
        self.phases = phases
        self.nseq = nseq
        self.nc = bass.Bass("TRN2", target_bir_lowering=False)
        self.inp = {}
        self.scr = {}

    def din(self, name, shape, dt=F32):
        self.inp[name] = self.nc.dram_tensor(name, list(shape), dt, kind="ExternalInput").ap()

    def dscr(self, name, shape, dt, out=False):
        kind = "ExternalOutput" if (out or self.debug) else "Internal"
        ap = self.nc.dram_tensor(name, list(shape), dt, kind=kind).ap()
        self.scr[name] = T(ap, name)
        return self.scr[name]

    def sb(self, es, name, shape, dt=F32):
        self.uid = getattr(self, "uid", 0) + 1
        nm = f"s{self.uid}_{name}"
        return T(es.enter_context(self.nc.sbuf_tensor(nm, list(shape), dt)), nm)

    def declare(self):
        self.din("x", [self.nseq, S, D])
        self.din("w_in", [L, D, IN_COLS])
        self.din("rg_vec", [L, 128, 4, 8])
        self.din("rg_wa", [L, 8, 64, 64])
        self.din("rg_wx", [L, 8, 64, 64])
        self.din("ident", [128, 128])
        self.din("mla_vec", [L, 128, 3])
        self.din("mla_w_uq", [L, 256, 768])
        self.din("mla_w_uq_rot", [L, 256, 256])
        self.din("mla_w_ukv_k", [L, 128, 512])
        self.din("mla_w_ukv_v", [L, 128, 512])
        self.din("rope_cs", [2, 32, S])
        self.din("cmask", [128, 128])
        self.din("rw_mix", [L, 1, 1792])
        self.din("rw_vec", [L, 128, 4, 8])
        self.din("rw_w2a2", [L, 128, 512])
        self.din("rw_g2", [L, 128, 512])
        self.din("rw_v1", [1, 512, 32])
        self.din("rw_v2", [1, 32, 512])
        self.din("rw_masks", [128, 5, 128])
        self.din("onesbd", [128, 128])
        self.dscr("rwT", [7, 512, S], F32)
        self.dscr("vfirst", [512, S], F32)
        self.dscr("rw_wc", [512, NT], F32)
        self.din("w_branch", [L, 3, 512, D])
        self.din("w_out", [L, D, D])
        self.din("peer_wq", [L, D, 2048])
        self.din("peer_skT", [L, 2, 128, 128])
        self.din("iota16", [128, 16])
        for l in range(L):
            if self.phases is None or f"F{l}" in self.phases:
                self.din(f"peer_u{l}", [16384, D])
                self.din(f"peer_v{l}", [16384, D])
            self.dscr(f"x2T_{l}", [D, S], BF16)
        self.dscr("x2_0", [S, D], F32)
        self.din("ln_gb", [L, 4, D])
        for l in range(L):
            self.dscr(f"x1_{l}", [S, D], F32)
            self.dscr(f"x1T_{l}", [D, S], BF16)
        self.dscr("xT0", [D, S], BF16)
        self.dscr("yT", [3, 512, S], BF16)
        self.dscr("out", [self.nseq, S, D], F32, out=True)

    def build(self):
        nc = self.nc
        self.declare()
        with contextlib.ExitStack() as es:
            self.fw = fw = FW(nc, es)
            self.ps = [T(es.enter_context(nc.psum_tensor(f"ps{i}", [128, 512], F32)), f"ps{i}")
                       for i in range(8)]
            self.psi = 0
            self.ident = self.sb(es, "ident", [128, 128])
            fw.dma("sp", self.ident[:], self.inp["ident"][:, :], writes=[self.ident])
            ph = self.phases
            for sq in range(self.nseq):
                x_in = self.inp["x"][sq]
                out_t = T(self.scr["out"].t[sq], f"out{sq}")
                if ph is None or "A" in ph:
                    self.phase_xT(x_in, None, self.scr["xT0"])
                    fw.barrier()
                for l in range(L):
                    xTd = self.scr["xT0"] if l == 0 else self.scr[f"x2T_{l - 1}"]
                    if ph is None or f"B{l}" in ph:
                        self.phase_rg(l, xTd)
                        fw.barrier()
                    if ph is None or f"C{l}" in ph or f"Cp{l}" in ph:
                        self.phase_rwkv_prep(l, xTd)
                        fw.barrier()
                    if ph is None or f"C{l}" in ph or f"Cs{l}" in ph:
                        self.phase_rwkv_scan(l)
                        fw.barrier()
                    if ph is None or f"D{l}" in ph:
                        self.phase_mla(l, xTd)
                        fw.barrier()
                    if ph is None or f"E{l}" in ph:
                        if l == 0:
                            self.phase_merge(l, xTd, x_in, None)
                        else:
                            self.phase_merge(l, xTd, self.scr["x2_0"].t, self.scr["x2_0"])
                        fw.barrier()
                    if ph is None or f"F{l}" in ph:
                        x2d = out_t if l == L - 1 else self.scr["x2_0"]
                        self.phase_peer(l, x2d, self.scr[f"x2T_{l}"])
                        fw.barrier()
            fw.barrier()
        return nc

    def next_ps(self):
        while True:
            p = self.ps[self.psi]
            self.psi = (self.psi + 1) % 8
            if p not in getattr(self, "ps_reserved", ()):
                return p

    def phase_xT(self, x_ap, x_dep, xT_d):
        fw = self.fw
        with contextlib.ExitStack() as es:
            xt = [self.sb(es, f"xa{i}", [128, D]) for i in range(2)]
            xo = [self.sb(es, f"xo{i}", [128, 8, 128], BF16) for i in range(2)]
            for i in range(NT):
                b = xt[i % 2]
                fw.dma("sp", b[:], x_ap[i * 128:(i + 1) * 128, :],
                       reads=[x_dep] if x_dep else [], writes=[b])
                o = xo[i % 2]
                for hh in range(2):
                    ps = self.next_ps()
                    for c in range(4):
                        cc = hh * 4 + c
                        fw.pe(lambda h, ps=ps, c=c, cc=cc, b=b: h.transpose(
                            ps[:, c * 128:(c + 1) * 128], b[:, cc * 128:(cc + 1) * 128], self.ident[:]),
                            r=[b, self.ident], w=[ps])
                    fw.dve(lambda h, ps=ps, o=o, hh=hh: h.tensor_copy(
                        out=o[:, hh * 4:(hh + 1) * 4, :],
                        in_=ps[:, :].rearrange("p (c t) -> p c t", c=4)), r=[ps], w=[(o, hh)])
                fw.dma("sp", xT_d[:, i * 128:(i + 1) * 128].rearrange("(c p) t -> p c t", p=128),
                       o[:], reads=[o], writes=[(xT_d, i)])

    def load_xT(self, es, xT_d):
        xT = self.sb(es, "xT", [128, 8, S], BF16)
        for c in range(8):
            self.fw.dma("sp", xT[:, c, :], xT_d[c * 128:(c + 1) * 128, :], reads=[xT_d], writes=[(xT, c)])
        return xT

    def load_w_cols(self, wt, l, c0, ncols, key=None):
        src = self.inp["w_in"][l, :, c0:c0 + ncols].rearrange("(kc p) c -> p kc c", p=128)
        self.fw.dma("pool", wt[:, :, 0:ncols], src, writes=[(wt, key) if key is not None else wt])

    def proj_fm(self, wt, xT, j0, M, n, ps):
        for kc in range(8):
            self.fw.pe(lambda h, kc=kc: h.matmul(ps[0:M, :], lhsT=wt[:, kc, j0:j0 + M],
                                                  rhs=xT[:, kc, n * 512:(n + 1) * 512],
                                                  start=(kc == 0), stop=(kc == 7)),
                       r=[wt, xT], w=[ps])

    def gelu_tanh(self, es_tmp, x, out, tmp):
        fw = self.fw
        fw.dve(lambda h: h.tensor_tensor(out=tmp[:], in0=x[:], in1=x[:], op=ALU.mult), r=[x], w=[tmp])
        fw.dve(lambda h: h.tensor_scalar(out=tmp[:], in0=tmp[:], scalar1=0.044715, scalar2=1.0,
                                         op0=ALU.mult, op1=ALU.add), r=[tmp], w=[tmp])
        fw.dve(lambda h: h.tensor_tensor(out=tmp[:], in0=tmp[:], in1=x[:], op=ALU.mult), r=[tmp, x], w=[tmp])
        fw.act(lambda h: h.activation(out=tmp[:], in_=tmp[:], func=AF.Sigmoid, scale=1.5957691216057308),
               r=[tmp], w=[tmp])
        fw.dve(lambda h: h.tensor_tensor(out=out[:], in0=tmp[:], in1=x[:], op=ALU.mult), r=[tmp, x], w=[out])

    def phase_rg(self, l, xT_d):
        fw, nc = self.fw, self.nc
        yT = self.scr["yT"]
        with contextlib.ExitStack() as es:
            xT = self.load_xT(es, xT_d)
            vec = self.sb(es, "rgvec", [128, 4, 8])
            fw.dma("sp", vec[:], self.inp["rg_vec"][l], writes=[vec])
            cst = self.sb(es, "rgc", [128, 4, 4])
            wts = [self.sb(es, f"rgw{i}", [128, 8, 256], BF16) for i in range(2)]
            wbd = [self.sb(es, f"rgbd{i}", [128, 2, 128]) for i in range(2)]
            names = ["xb", "gb", "xc", "r", "i", "a", "m", "h"]
            tl = {n: self.sb(es, "rg_" + n, [128, S]) for n in names}
            yo = self.sb(es, "rg_yo", [128, S], BF16)
            xb, gb, xc, r_, i_, a_, m_, h_ = (tl[n] for n in names)
            for c in range(4):
                wt = wts[c % 2]
                bd = wbd[c % 2]
                self.load_w_cols(wt, l, C_RG + c * 128, 128, key=0)
                src = self.inp["w_in"][l, :, C_RG + 512 + c * 128:C_RG + 512 + (c + 1) * 128].rearrange(
                    "(kc p) c -> p kc c", p=128)
                fw.dma("pool", wt[:, :, 128:256], src, writes=[(wt, 1)])
                fw.pool(lambda h: h.memset(bd[:], 0.0), w=[bd])
                for j, nm in enumerate(("rg_wa", "rg_wx")):
                    for hb in range(2):
                        fw.dma("sp", bd[hb * 64:(hb + 1) * 64, j, hb * 64:(hb + 1) * 64],
                               self.inp[nm][l, 2 * c + hb], writes=[bd])
                for j, dst in enumerate((xb, gb)):
                    for n in range(4):
                        ps = self.next_ps()
                        self.proj_fm(wt, xT, j * 128, 128, n, ps)
                        fw.act(lambda h, ps=ps, dst=dst, n=n: h.copy(out=dst[:, n * 512:(n + 1) * 512], in_=ps[:, :]),
                               r=[ps], w=[(dst, n)])
                v = lambda k: vec[:, c, k:k + 1]
                fw.dve(lambda h: h.tensor_scalar(out=xc[:], in0=xb[:], scalar1=v(0), scalar2=v(4),
                                                 op0=ALU.mult, op1=ALU.add), r=[xb, vec], w=[xc])
                for j in range(1, 4):
                    fw.dve(lambda h, j=j: h.scalar_tensor_tensor(out=xc[:, j:], in0=xb[:, 0:S - j], scalar=v(j),
                                                                 in1=xc[:, j:], op0=ALU.mult, op1=ALU.add),
                           r=[xb, xc, vec], w=[xc])
                cc = lambda k: cst[:, c, k:k + 1]
                fw.act(lambda h: h.activation(out=cc(0), in_=v(7), func=AF.Exp, scale=-1.0), r=[vec], w=[cst])
                fw.act(lambda h: h.activation(out=cc(1), in_=cc(0), func=AF.Ln, bias=1.0, scale=1.0), r=[cst], w=[cst])
                fw.dve(lambda h: h.tensor_scalar(out=cc(2), in0=cc(1), scalar1=-8.0, scalar2=None, op0=ALU.mult),
                       r=[cst], w=[cst])
                fw.dve(lambda h: h.tensor_scalar(out=cc(3), in0=cc(1), scalar1=-16.0, scalar2=None, op0=ALU.mult),
                       r=[cst], w=[cst])
                for j, (dst, bk) in enumerate(((r_, 5), (i_, 6))):
                    for n in range(4):
                        ps = self.next_ps()
                        fw.pe(lambda h, ps=ps, j=j, n=n: h.matmul(ps[:, :], lhsT=bd[:, j, :],
                                                                   rhs=xc[:, n * 512:(n + 1) * 512],
                                                                   start=True, stop=True), r=[bd, xc], w=[ps])
                        fw.act(lambda h, ps=ps, dst=dst, n=n, bk=bk: h.activation(
                            out=dst[:, n * 512:(n + 1) * 512], in_=ps[:, :], func=AF.Sigmoid, bias=v(bk), scale=1.0),
                            r=[ps, vec], w=[(dst, n)])
                fw.act(lambda h: h.activation(out=a_[:], in_=r_[:], func=AF.Exp, scale=cc(2)), r=[r_, cst], w=[a_])
                fw.act(lambda h: h.activation(out=m_[:], in_=r_[:], func=AF.Exp, scale=cc(3)), r=[r_, cst], w=[m_])
                fw.act(lambda h: h.activation(out=m_[:], in_=m_[:], func=AF.Sqrt, bias=1.0, scale=-1.0), r=[m_], w=[m_])
                fw.dve(lambda h: h.memset(m_[:, 0:1], 1.0), w=[m_])
                fw.dve(lambda h: h.tensor_tensor(out=i_[:], in0=i_[:], in1=xc[:], op=ALU.mult), r=[i_, xc], w=[i_])
                fw.dve(lambda h: h.tensor_tensor(out=i_[:], in0=i_[:], in1=m_[:], op=ALU.mult), r=[i_, m_], w=[i_])
                fw.dve(lambda h: h.tensor_tensor_scan(out=h_[:], data0=a_[:], data1=i_[:], initial=0.0,
                                                      op0=ALU.mult, op1=ALU.add), r=[a_, i_], w=[h_])
                self.gelu_tanh(es, gb, r_, m_)
                fw.dve(lambda h: h.tensor_tensor(out=yo[:], in0=h_[:], in1=r_[:], op=ALU.mult), r=[h_, r_], w=[yo])
                fw.dma("sp", yT[0, c * 128:(c + 1) * 128, :], yo[:], reads=[yo], writes=[(yT, ("a", c))])


    def phase_rwkv_prep(self, l, xT_d):
        fw = self.fw
        rwT, vfirst = self.scr["rwT"], self.scr["vfirst"]
        with contextlib.ExitStack() as es:
            xTp = self.sb(es, "xTp", [128, 8, S + 2], BF16)
            fw.dve(lambda h: h.memset(xTp[:, :, 0:2], 0.0), w=[(xTp, "z")])
            for c in range(8):
                fw.dma("sp", xTp[:, c, 2:S + 2], xT_d[c * 128:(c + 1) * 128, :], reads=[xT_d], writes=[(xTp, c)])
            mixb = self.sb(es, "mixb", [128, 2, 1792])
            fw.dma("sp", mixb[:, 0, :], self.inp["rw_mix"][l].to_broadcast([128, 1792]), writes=[mixb])
            fw.dve(lambda h: h.tensor_scalar(out=mixb[:, 1, :], in0=mixb[:, 0, :], scalar1=-1.0, scalar2=1.0,
                                             op0=ALU.mult, op1=ALU.add), r=[mixb], w=[mixb])
            vec = self.sb(es, "rwvec", [128, 4, 8])
            fw.dma("sp", vec[:], self.inp["rw_vec"][l], writes=[vec])
            dv = self.sb(es, "rwdv", [128, 4, 2])
            fw.dve(lambda h: h.tensor_scalar(out=dv[:, :, 0:1], in0=vec[:, :, 0:1], scalar1=-1.0, scalar2=None,
                                             op0=ALU.mult), r=[vec], w=[dv])
            fw.dve(lambda h: h.tensor_scalar(out=dv[:, :, 1:2], in0=vec[:, :, 3:4], scalar1=-1.0, scalar2=1.0,
                                             op0=ALU.mult, op1=ALU.add), r=[vec], w=[dv])
            w2a2 = self.sb(es, "w2a2", [128, 512])
            fw.dma("sp", w2a2[:], self.inp["rw_w2a2"][l], writes=[w2a2])
            g2 = self.sb(es, "g2", [128, 512])
            fw.dma("sp", g2[:], self.inp["rw_g2"][l], writes=[g2])
            obd = self.sb(es, "obd", [128, 128])
            fw.dma("sp", obd[:], self.inp["onesbd"][:, :], writes=[obd])
            rmask = self.sb(es, "rmask", [128, S])
            fw.pool(lambda h: h.memset(rmask[:], 1.0), w=[rmask])
            fw.pool(lambda h: h.memset(rmask[:, :].rearrange("p (c t) -> p c t", t=128)[:, :, 0:1], 0.0), w=[rmask])
            wf = [self.sb(es, f"wf{i}", [128, 8, 128]) for i in range(2)]
            wm = [self.sb(es, f"wm{i}", [128, 2, 8, 128], BF16) for i in range(2)]
            self._rw_cnt = 0

            def proj_shift(col0, dst_fn):
                i = self._rw_cnt % 2
                self._rw_cnt += 1
                src = self.inp["w_in"][l, :, C_RW + col0:C_RW + col0 + 128].rearrange("(kc p) c -> p kc c", p=128)
                fw.dma("sp", wf[i][:], src, writes=[wf[i]])
                for j in range(2):
                    fw.dve(lambda h: h.tensor_tensor(
                        out=wm[i][:, j, :, :], in0=wf[i][:],
                        in1=mixb[:, j, col0:col0 + 128][:, None, :].to_broadcast([128, 8, 128]), op=ALU.mult),
                        r=[wf[i], mixb], w=[(wm[i], j)])
                for n in range(4):
                    ps = self.next_ps()
                    for j in range(2):
                        off = 1 + j + n * 512
                        for kc in range(8):
                            fw.pe(lambda h: h.matmul(ps[:, :], lhsT=wm[i][:, j, kc, :], rhs=xTp[:, kc, off:off + 512],
                                                     start=(j == 0 and kc == 0), stop=(j == 1 and kc == 7)),
                                  r=[(wm[i], j), xTp], w=[ps])
                    dst_fn(n, ps)

            nb = 13
            B = [self.sb(es, f"rwb{i}", [128, S]) for i in range(nb)]
            wct = self.sb(es, "wct", [128, NT])
            xwa, sgx = B[11], B[12]
            sl = lambda n: slice(n * 512, (n + 1) * 512)
            def d_xwa(n, ps):
                fw.act(lambda h: h.activation(out=xwa[0:64, sl(n)], in_=ps[0:64, :], func=AF.Tanh), r=[ps], w=[(xwa, n)])
                fw.act(lambda h: h.copy(out=xwa[64:128, sl(n)], in_=ps[64:128, :]), r=[ps], w=[(xwa, n)])
            proj_shift(1536, d_xwa)
            proj_shift(1664, lambda n, ps: fw.act(lambda h: h.activation(out=sgx[:, sl(n)], in_=ps[:, :], func=AF.Sigmoid),
                                                  r=[ps], w=[(sgx, n)]))
            lat = None
            if l > 0:
                lat = self.sb(es, "lat", [32, S])
                v1 = self.sb(es, "v1", [128, 4, 32])
                fw.dma("sp", v1[:], self.inp["rw_v1"][l - 1].rearrange("(c p) j -> p c j", p=128), writes=[v1])
                v2 = self.sb(es, "v2", [32, 512])
                fw.dma("sp", v2[:], self.inp["rw_v2"][l - 1], writes=[v2])
                accb = [self.ps[4 + n] for n in range(4)]
                self.ps_reserved = set(accb)
                for c in range(4):
                    def d_v(n, ps, c=c):
                        fw.act(lambda h: h.copy(out=B[0][:, sl(n)], in_=ps[:, :]), r=[ps], w=[(B[0], n)])
                        fw.pe(lambda h: h.matmul(accb[n][0:32, :], lhsT=v1[:, c, :], rhs=B[0][:, sl(n)],
                                                 start=(c == 0), stop=(c == 3)), r=[v1, (B[0], n)], w=[accb[n]])
                    proj_shift(1024 + c * 128, d_v)
                for n in range(4):
                    fw.act(lambda h: h.copy(out=lat[:, sl(n)], in_=accb[n][0:32, :]), r=[accb[n]], w=[(lat, n)])
                self.ps_reserved = set()
            for c in range(4):
                rT, kT, vT, e2, cum, ep, em, epv, a_, kk, km = B[:11]
                v = lambda k: vec[:, c, k:k + 1]
                cs_ = slice(c * 128, (c + 1) * 128)
                for col0, dst in ((c * 128, rT), (512 + c * 128, kT), (1024 + c * 128, vT)):
                    proj_shift(col0, lambda n, ps, dst=dst: fw.act(
                        lambda h: h.copy(out=dst[:, sl(n)], in_=ps[:, :]), r=[ps], w=[(dst, n)]))
                for n in range(4):
                    ps = self.next_ps()
                    fw.pe(lambda h: h.matmul(ps[:, :], lhsT=w2a2[0:64, cs_], rhs=xwa[0:64, sl(n)], start=True, stop=True),
                          r=[w2a2, (xwa, n)], w=[ps])
                    fw.act(lambda h: h.activation(out=e2[:, sl(n)], in_=ps[:, :], func=AF.Exp, bias=dv[:, c, 0:1], scale=-1.0),
                           r=[ps, dv], w=[(e2, n)])
                    fw.act(lambda h: h.activation(out=e2[:, sl(n)], in_=e2[:, sl(n)], func=AF.Ln, bias=1.0, scale=1.0),
                           r=[(e2, n)], w=[(e2, n)])
                    fw.act(lambda h: h.activation(out=e2[:, sl(n)], in_=e2[:, sl(n)], func=AF.Exp, bias=-0.5, scale=-1.0),
                           r=[(e2, n)], w=[(e2, n)])
                    ps = self.next_ps()
                    fw.pe(lambda h: h.matmul(ps[:, :], lhsT=w2a2[64:128, cs_], rhs=xwa[64:128, sl(n)], start=True, stop=True),
                          r=[w2a2, (xwa, n)], w=[ps])
                    fw.act(lambda h: h.activation(out=a_[:, sl(n)], in_=ps[:, :], func=AF.Sigmoid, bias=v(1), scale=1.0),
                           r=[ps, vec], w=[(a_, n)])
                    ps = self.next_ps()
                    fw.pe(lambda h: h.matmul(ps[:, :], lhsT=g2[:, cs_], rhs=sgx[:, sl(n)], start=True, stop=True),
                          r=[g2, (sgx, n)], w=[ps])
                    fw.act(lambda h: h.copy(out=ep[:, sl(n)], in_=ps[:, :]), r=[ps], w=[(ep, n)])
                    if l > 0:
                        ps = self.next_ps()
                        fw.pe(lambda h: h.matmul(ps[:, :], lhsT=v2[0:32, cs_], rhs=lat[:, sl(n)], start=True, stop=True),
                              r=[v2, (lat, n)], w=[ps])
                        fw.act(lambda h: h.activation(out=em[:, sl(n)], in_=ps[:, :], func=AF.Sigmoid, bias=v(7), scale=1.0),
                               r=[ps, vec], w=[(em, n)])
                fw.dma("sp", rwT[6, cs_, :], ep[:], reads=[ep], writes=[(rwT, (6, c))])
                if l > 0:
                    fw.dma("sp", epv[:], vfirst[cs_, :], reads=[vfirst], writes=[epv])
                    fw.dve(lambda h: h.tensor_tensor(out=epv[:], in0=epv[:], in1=vT[:], op=ALU.subtract), r=[epv, vT], w=[epv])
                    fw.dve(lambda h: h.tensor_tensor(out=epv[:], in0=epv[:], in1=em[:], op=ALU.mult), r=[epv, em], w=[epv])
                    fw.dve(lambda h: h.tensor_tensor(out=vT[:], in0=vT[:], in1=epv[:], op=ALU.add), r=[epv, vT], w=[vT])
                else:
                    fw.dma("sp", vfirst[cs_, :], vT[:], reads=[vT], writes=[(vfirst, c)])
                fw.dma("sp", rwT[4, cs_, :], vT[:], reads=[vT], writes=[(rwT, (4, c))])
                fw.dve(lambda h: h.tensor_tensor_scan(out=cum[:], data0=rmask[:], data1=e2[:], initial=0.0,
                                                      op0=ALU.mult, op1=ALU.add), r=[rmask, e2], w=[cum])
                fw.act(lambda h: h.activation(out=ep[:], in_=cum[:], func=AF.Exp, scale=-1.0), r=[cum], w=[ep])
                fw.act(lambda h: h.activation(out=em[:], in_=cum[:], func=AF.Exp, scale=1.0), r=[cum], w=[em])
                fw.dve(lambda h: h.tensor_tensor(out=epv[:], in0=cum[:], in1=e2[:], op=ALU.subtract), r=[cum, e2], w=[epv])
                fw.act(lambda h: h.activation(out=epv[:], in_=epv[:], func=AF.Exp, scale=-1.0), r=[epv], w=[epv])
                fw.dve(lambda h: h.tensor_scalar(out=kk[:], in0=kT[:], scalar1=v(2), scalar2=None, op0=ALU.mult),
                       r=[kT, vec], w=[kk])
                fw.dve(lambda h: h.tensor_tensor(out=cum[:], in0=kk[:], in1=kk[:], op=ALU.mult), r=[kk], w=[cum])
                for n in range(4):
                    ps = self.next_ps()
                    fw.pe(lambda h: h.matmul(ps[:, :], lhsT=obd[:, :], rhs=cum[:, sl(n)], start=True, stop=True),
                          r=[obd, cum], w=[ps])
                    fw.act(lambda h: h.activation(out=e2[:, sl(n)], in_=ps[:, :], func=AF.Sqrt), r=[ps], w=[(e2, n)])
                fw.dve(lambda h: h.tensor_scalar(out=e2[:], in0=e2[:], scalar1=1e-12, scalar2=None, op0=ALU.max),
                       r=[e2], w=[e2])
                fw.dve(lambda h: h.reciprocal(out=e2[:], in_=e2[:]), r=[e2], w=[e2])
                fw.dve(lambda h: h.tensor_tensor(out=kk[:], in0=kk[:], in1=e2[:], op=ALU.mult), r=[kk, e2], w=[kk])
                fw.dve(lambda h: h.tensor_scalar(out=km[:], in0=a_[:], scalar1=v(3), scalar2=dv[:, c, 1:2],
                                                 op0=ALU.mult, op1=ALU.add), r=[a_, vec, dv], w=[km])
                fw.dve(lambda h: h.tensor_tensor(out=km[:], in0=km[:], in1=kT[:], op=ALU.mult), r=[km, kT], w=[km])
                fw.dve(lambda h: h.scalar_tensor_tensor(out=cum[:], in0=rT[:], scalar=v(4), in1=km[:],
                                                        op0=ALU.mult, op1=ALU.mult), r=[rT, km, vec], w=[cum])
                for n in range(4):
                    ps = self.next_ps()
                    fw.pe(lambda h: h.matmul(ps[:, :], lhsT=obd[:, :], rhs=cum[:, sl(n)], start=True, stop=True),
                          r=[obd, cum], w=[ps])
                    fw.dve(lambda h: h.tensor_tensor(out=e2[:, sl(n)], in0=ps[:, :], in1=vT[:, sl(n)], op=ALU.mult),
                           r=[ps, vT], w=[(e2, n)])
                fw.dma("sp", rwT[5, cs_, :], e2[:], reads=[e2], writes=[(rwT, (5, c))])
                fw.dve(lambda h: h.scalar_tensor_tensor(out=epv[:], in0=kk[:], scalar=-1.0, in1=epv[:],
                                                        op0=ALU.mult, op1=ALU.mult), r=[kk, epv], w=[epv])
                fw.dma("sp", rwT[0, cs_, :], epv[:], reads=[epv], writes=[(rwT, (0, c))])
                fw.dve(lambda h: h.tensor_tensor(out=rT[:], in0=rT[:], in1=ep[:], op=ALU.mult), r=[rT, ep], w=[rT])
                fw.dma("sp", rwT[1, cs_, :], rT[:], reads=[rT], writes=[(rwT, (1, c))])
                fw.dve(lambda h: h.tensor_tensor(out=kk[:], in0=kk[:], in1=a_[:], op=ALU.mult), r=[kk, a_], w=[kk])
                fw.dve(lambda h: h.tensor_tensor(out=kk[:], in0=kk[:], in1=em[:], op=ALU.mult), r=[kk, em], w=[kk])
                fw.dma("sp", rwT[2, cs_, :], kk[:], reads=[kk], writes=[(rwT, (2, c))])
                fw.dve(lambda h: h.tensor_tensor(out=km[:], in0=km[:], in1=em[:], op=ALU.mult), r=[km, em], w=[km])
                fw.dma("sp", rwT[3, cs_, :], km[:], reads=[km], writes=[(rwT, (3, c))])
                fw.dve(lambda h: h.tensor_copy(out=wct[:], in_=ep[:, :].rearrange("p (c t) -> p c t", t=128)[:, :, 127]),
                       r=[ep], w=[wct])
                fw.dma("sp", self.scr["rw_wc"][cs_, :], wct[:], reads=[wct], writes=[(self.scr["rw_wc"], c)])

    def phase_rwkv_scan(self, l):
        fw = self.fw
        rwT, yT = self.scr["rwT"], self.scr["yT"]
        ident = self.ident
        with contextlib.ExitStack() as es:
            masks = self.sb(es, "rwmasks", [128, 5, 128])
            fw.dma("sp", masks[:], self.inp["rw_masks"][:, :, :], writes=[masks])
            vec = self.sb(es, "rwvec2", [128, 4, 8])
            fw.dma("sp", vec[:], self.inp["rw_vec"][l], writes=[vec])
            wc = self.sb(es, "wc", [128, 4, NT])
            fw.dma("sp", wc[:], self.scr["rw_wc"][:, :].rearrange("(c p) n -> p c n", p=128),
                   reads=[self.scr["rw_wc"]], writes=[wc])
            ST = self.sb(es, "ST", [128, 4, 64])
            fw.dve(lambda h: h.memset(ST[:], 0.0), w=[ST])
            fm = [self.sb(es, f"fm{i}", [128, 7, 4, 128]) for i in range(2)]
            tok = [self.sb(es, f"tok{i}", [128, 3, 512]) for i in range(2)]
            M4s = [self.sb(es, f"M4{i}", [128, 4, 128]) for i in range(2)]
            Lbs = [self.sb(es, f"Lb{i}", [128, 2, 128]) for i in range(2)]
            Xs = [self.sb(es, f"X{i}", [128, 7, 128]) for i in range(2)]
            Us = [self.sb(es, f"U{i}", [128, 64]) for i in range(2)]
            tS = [self.sb(es, f"tS{i}", [128, 64]) for i in range(2)]
            Ys = [self.sb(es, f"Y{i}", [128, 8, 64]) for i in range(2)]
            s8 = self.sb(es, "s8", [128, 8])
            r8 = self.sb(es, "r8", [128, 8])
            Yc = self.sb(es, "Yc", [128, 8, 64])
            sq = self.sb(es, "sq", [128, 8, 64])
            o32 = self.sb(es, "o32", [128, 4, 128])
            obs = [self.sb(es, f"ob{i}", [128, 4, 128], BF16) for i in range(2)]
            import os as _os
            _NC = int(_os.environ.get("RW_NC", NT)); _NH = int(_os.environ.get("RW_NH", 8)); _STG = int(_os.environ.get("RW_STAGE", 9))
            for c in range(_NC):
                cs = slice(c * 128, (c + 1) * 128)
                F, TK, Y, ob = fm[c % 2], tok[c % 2], Ys[c % 2], obs[c % 2]
                for q in range(7):
                    fw.dma("sp", F[:, q, :, :], rwT[q, :, cs].rearrange("(ct p) t -> p ct t", p=128),
                           reads=[rwT], writes=[(F, q)])
                for j, q in enumerate((4, 2, 3)):
                    ps = self.next_ps()
                    for ct in range(4):
                        fw.pe(lambda h: h.transpose(ps[:, ct * 128:(ct + 1) * 128], F[:, q, ct, :], ident[:]),
                              r=[(F, q), ident], w=[ps])
                    fw.act(lambda h: h.copy(out=TK[:, j, :], in_=ps[:, :]), r=[ps], w=[(TK, j)])
                for hd in range(_NH):
                    ct = hd // 2
                    hr = slice((hd % 2) * 64, (hd % 2) * 64 + 64)
                    At, Rt, Bt, Kt = (F[hr, q, ct, :] for q in range(4))
                    rA, rR, rB, rK = ((F, q) for q in range(4))
                    Vt = TK[:, 0, hd * 64:(hd + 1) * 64]
                    Bp = TK[:, 1, ct * 128:(ct + 1) * 128]
                    Kp = TK[:, 2, ct * 128:(ct + 1) * 128]
                    M4, Lb, X, U, tmpS = M4s[hd % 2], Lbs[hd % 2], Xs[hd % 2], Us[hd % 2], tS[hd % 2]
                    STh = ST[hr, ct, :]
                    rST = (ST, hd)
                    ps4 = self.next_ps()
                    for i, (lt, rh, rl, rr) in enumerate(((Bt, At, rB, rA), (Kt, At, rK, rA), (Bt, Rt, rB, rR), (Kt, Rt, rK, rR))):
                        fw.pe(lambda h: h.matmul(ps4[:, i * 128:(i + 1) * 128], lhsT=lt, rhs=rh, start=True, stop=True),
                              r=[rl, rr], w=[ps4])
                    fw.dve(lambda h: h.tensor_tensor(out=M4[:], in0=ps4[:, :].rearrange("p (a t) -> p a t", a=4),
                                                     in1=masks[:, 0:4, :], op=ALU.mult), r=[ps4, masks], w=[M4])
                    psL = self.next_ps()
                    fw.pe(lambda h: h.matmul(psL[:, 0:128], lhsT=At, rhs=Bt, start=True, stop=True), r=[rA, rB], w=[psL])
                    fw.dve(lambda h: h.tensor_tensor(out=Lb[:, 0, :], in0=psL[:, 0:128], in1=masks[:, 4, :], op=ALU.mult),
                           r=[psL, masks], w=[(Lb, 0)])
                    xj = lambda j: M4[:, 0, :] if j == 0 else X[:, j, :]
                    rxj = lambda j: M4 if j == 0 else (X, j)
                    if _STG < 1:
                        continue
                    _SQ = int(_os.environ.get("RW_SQ", 6)); _SQ2 = int(_os.environ.get("RW_SQ2", 1))
                    for j in range(_SQ):
                        psq = self.next_ps()
                        fw.pe(lambda h: h.matmul(psq[:, 0:128], lhsT=Lb[:, j % 2, :], rhs=xj(j), start=True, stop=True),
                              r=[(Lb, j % 2), rxj(j)], w=[psq])
                        if j < 5 and _SQ2:
                            psq2 = self.next_ps()
                            fw.pe(lambda h: h.matmul(psq2[:, 0:128], lhsT=xj(j), rhs=Lb[:, j % 2, :], start=True, stop=True),
                                  r=[(Lb, j % 2), rxj(j)], w=[psq2])
                        fw.act(lambda h: h.copy(out=X[:, j + 1, :], in_=psq[:, 0:128]), r=[psq], w=[(X, j + 1)])
                        if j < 5 and _SQ2:
                            fw.act(lambda h: h.copy(out=Lb[:, (j + 1) % 2, :], in_=psq2[:, 0:128]),
                                   r=[psq2], w=[(Lb, (j + 1) % 2)])
                    if _STG < 2:
                        continue
                    psz = self.next_ps()
                    fw.pe(lambda h: h.matmul(psz[:, 0:64], lhsT=M4[:, 1, :], rhs=Vt, start=True, stop=False),
                          r=[M4, (TK, 0)], w=[psz])
                    fw.pe(lambda h: h.matmul(psz[:, 0:64], lhsT=At, rhs=STh, start=False, stop=True),
                          r=[rA, rST], w=[psz])
                    fw.act(lambda h: h.copy(out=U[:], in_=psz[:, 0:64]), r=[psz], w=[U])
                    for j in range(7):
                        psu = self.next_ps()
                        fw.pe(lambda h: h.matmul(psu[:, 0:64], lhsT=xj(j), rhs=U[:], start=True, stop=True),
                              r=[rxj(j), U], w=[psu])
                        fw.dve(lambda h: h.tensor_tensor(out=U[:], in0=psu[:, 0:64], in1=U[:], op=ALU.add),
                               r=[psu, U], w=[U])
                    if _STG < 3:
                        continue
                    psy = self.next_ps()
                    fw.pe(lambda h: h.matmul(psy[:, 0:64], lhsT=Rt, rhs=STh, start=True, stop=False), r=[rR, rST], w=[psy])
                    fw.pe(lambda h: h.matmul(psy[:, 0:64], lhsT=M4[:, 2, :], rhs=U[:], start=False, stop=False),
                          r=[M4, U], w=[psy])
                    fw.pe(lambda h: h.matmul(psy[:, 0:64], lhsT=M4[:, 3, :], rhs=Vt, start=False, stop=True),
                          r=[M4, (TK, 0)], w=[psy])
                    fw.act(lambda h: h.copy(out=Y[:, hd, :], in_=psy[:, 0:64]), r=[psy], w=[(Y, hd)])
                    if _STG < 4:
                        continue
                    pss = self.next_ps()
                    fw.pe(lambda h: h.matmul(pss[:, 0:64], lhsT=Bp, rhs=U[:], start=True, stop=False), r=[(TK, 1), U], w=[pss])
                    fw.pe(lambda h: h.matmul(pss[:, 0:64], lhsT=Kp, rhs=Vt, start=False, stop=True),
                          r=[(TK, 2), (TK, 0)], w=[pss])
                    fw.dve(lambda h: h.tensor_tensor(out=tmpS[hr, :], in0=pss[hr, 0:64], in1=STh, op=ALU.add),
                           r=[pss, rST], w=[tmpS])
                    fw.dve(lambda h: h.tensor_scalar(out=STh, in0=tmpS[hr, :], scalar1=wc[hr, ct, c:c + 1], scalar2=None,
                                                     op0=ALU.mult), r=[tmpS, wc], w=[rST])
                if _STG < 5:
                    continue
                fw.dve(lambda h: h.tensor_reduce(out=s8[:], in_=Y[:], axis=AX.X, op=ALU.add), r=[Y], w=[s8])
                fw.dve(lambda h: h.tensor_scalar(out=s8[:], in0=s8[:], scalar1=1.0 / 64, scalar2=None, op0=ALU.mult),
                       r=[s8], w=[s8])
                fw.dve(lambda h: h.tensor_tensor(out=Yc[:], in0=Y[:], in1=s8[:, :, None].to_broadcast([128, 8, 64]),
                                                 op=ALU.subtract), r=[Y, s8], w=[Yc])
                fw.dve(lambda h: h.tensor_tensor(out=sq[:], in0=Yc[:], in1=Yc[:], op=ALU.mult), r=[Yc], w=[sq])
                fw.dve(lambda h: h.tensor_reduce(out=r8[:], in_=sq[:], axis=AX.X, op=ALU.add), r=[sq], w=[r8])
                fw.act(lambda h: h.activation(out=r8[:], in_=r8[:], func=AF.Sqrt, bias=64e-5, scale=1.0 / 64), r=[r8], w=[r8])
                fw.dve(lambda h: h.reciprocal(out=r8[:], in_=r8[:]), r=[r8], w=[r8])
                fw.dve(lambda h: h.tensor_tensor(out=Yc[:], in0=Yc[:], in1=r8[:, :, None].to_broadcast([128, 8, 64]),
                                                 op=ALU.mult), r=[Yc, r8], w=[Yc])
                pst = self.next_ps()
                ycf = Yc[:, :, :].rearrange("p h d -> p (h d)")
                for ct in range(4):
                    fw.pe(lambda h: h.transpose(pst[:, ct * 128:(ct + 1) * 128], ycf[:, ct * 128:(ct + 1) * 128], ident[:]),
                          r=[Yc, ident], w=[pst])
                for ct in range(4):
                    fw.dve(lambda h: h.tensor_scalar(out=o32[:, ct, :], in0=pst[:, ct * 128:(ct + 1) * 128],
                                                     scalar1=vec[:, ct, 5:6], scalar2=vec[:, ct, 6:7],
                                                     op0=ALU.mult, op1=ALU.add), r=[pst, vec], w=[(o32, ct)])
                fw.dve(lambda h: h.tensor_tensor(out=o32[:], in0=o32[:], in1=F[:, 5, :, :], op=ALU.add), r=[o32, (F, 5)], w=[o32])
                fw.dve(lambda h: h.tensor_tensor(out=ob[:], in0=o32[:], in1=F[:, 6, :, :], op=ALU.mult), r=[o32, (F, 6)], w=[ob])
                fw.dma("sp", yT[1, :, cs].rearrange("(ct p) t -> p ct t", p=128), ob[:], reads=[ob],
                       writes=[(yT, ("b", c))])

    def ln_tile(self, z, st, mv, gbc, bbc, gi):
        fw = self.fw
        for c in range(2):
            fw.dve(lambda h: h.bn_stats(out=st[:, c, :], in_=z[:, c * 512:(c + 1) * 512]), r=[z], w=[(st, c)])
        fw.dve(lambda h: h.bn_aggr(out=mv[:, 0:2], in_=st[:, :, :].rearrange("p a b -> p (a b)")), r=[st], w=[mv])
        fw.act(lambda h: h.activation(out=mv[:, 2:3], in_=mv[:, 1:2], func=AF.Sqrt, bias=1e-5, scale=1.0), r=[mv], w=[mv])
        fw.dve(lambda h: h.reciprocal(out=mv[:, 2:3], in_=mv[:, 2:3]), r=[mv], w=[mv])
        fw.dve(lambda h: h.tensor_scalar(out=z[:], in0=z[:], scalar1=mv[:, 0:1], scalar2=mv[:, 2:3],
                                         op0=ALU.subtract, op1=ALU.mult), r=[z, mv], w=[z])
        fw.dve(lambda h: h.tensor_tensor(out=z[:], in0=z[:], in1=gbc[:, gi, :], op=ALU.mult), r=[z, gbc], w=[z])
        fw.dve(lambda h: h.tensor_tensor(out=z[:], in0=z[:], in1=gbc[:, gi + 1, :], op=ALU.add), r=[z, gbc], w=[z])

    def store_x_and_xT(self, z, ti, x_d, xT_d, xo):
        fw = self.fw
        fw.dma("sp", x_d[ti * 128:(ti + 1) * 128, :], z[:], reads=[z], writes=[(x_d, ti)])
        for hh in range(2):
            ps = self.next_ps()
            for c in range(4):
                cc = hh * 4 + c
                fw.pe(lambda h: h.transpose(ps[:, c * 128:(c + 1) * 128], z[:, cc * 128:(cc + 1) * 128], self.ident[:]),
                      r=[z, self.ident], w=[ps])
            fw.act(lambda h: h.copy(out=xo[:, hh * 4:(hh + 1) * 4, :], in_=ps[:, :].rearrange("p (c t) -> p c t", c=4)),
                   r=[ps], w=[(xo, hh)])
        fw.dma("sp", xT_d[:, ti * 128:(ti + 1) * 128].rearrange("(c p) t -> p c t", p=128), xo[:], reads=[xo],
               writes=[(xT_d, ti)])

    def phase_merge(self, l, xT_d, xres_ap, xres_dep):
        fw = self.fw
        yT = self.scr["yT"]
        x1_d, x1T_d = self.scr[f"x1_{l}"], self.scr[f"x1T_{l}"]
        with contextlib.ExitStack() as es:
            xT = self.load_xT(es, xT_d)
            wg = self.sb(es, "wg", [128, 8, 3072], BF16)
            for b in range(3):
                src = self.inp["w_in"][l, :, C_GATE + b * 1024:C_GATE + (b + 1) * 1024].rearrange("(kc p) c -> p kc c", p=128)
                fw.dma("pool", wg[:, :, b * 1024:(b + 1) * 1024], src, writes=[(wg, b)])
            wb = self.sb(es, "wb", [128, 3, 4, D], BF16)
            for b in range(3):
                fw.dma("pool", wb[:, b, :, :], self.inp["w_branch"][l, b].rearrange("(kc p) c -> p kc c", p=128),
                       writes=[(wb, b)])
            wo = self.sb(es, "wo", [128, 8, D], BF16)
            fw.dma("pool", wo[:], self.inp["w_out"][l].rearrange("(kc p) c -> p kc c", p=128), writes=[wo])
            gbc = self.sb(es, "gbc", [128, 4, D])
            fw.dma("sp", gbc[:], self.inp["ln_gb"][l:l + 1, :, :].to_broadcast([128, 4, D]), writes=[gbc])
            yb = self.sb(es, "yb", [128, 3, 4, 512], BF16)
            mg = self.sb(es, "mg", [128, 8, 512], BF16)
            sg = self.sb(es, "sg", [128, 512])
            acc = self.sb(es, "acc", [128, 512])
            z = [self.sb(es, f"z{i}", [128, D]) for i in range(2)]
            xr = [self.sb(es, f"xr{i}", [128, D]) for i in range(2)]
            xo = [self.sb(es, f"xo{i}", [128, 8, 128], BF16) for i in range(2)]
            st = self.sb(es, "st", [128, 2, 6])
            mv = self.sb(es, "mv", [128, 4])
            for n in range(4):
                ts = slice(n * 512, (n + 1) * 512)
                for b in range(3):
                    fw.dma("sp", yb[:, b, :, :], yT[b, :, ts].rearrange("(c p) t -> p c t", p=128),
                           reads=[yT], writes=[(yb, b)])
                for dt in range(8):
                    for b in range(3):
                        ps = self.next_ps()
                        self.proj_fm(wg, xT, b * 1024 + dt * 128, 128, n, ps)
                        fw.act(lambda h: h.activation(out=sg[:], in_=ps[:, :], func=AF.Sigmoid), r=[ps], w=[sg])
                        ps2 = self.next_ps()
                        for kc in range(4):
                            fw.pe(lambda h: h.matmul(ps2[:, :], lhsT=wb[:, b, kc, dt * 128:(dt + 1) * 128],
                                                     rhs=yb[:, b, kc, :], start=(kc == 0), stop=(kc == 3)),
                                  r=[(wb, b), (yb, b)], w=[ps2])
                        if b == 0:
                            fw.dve(lambda h: h.tensor_tensor(out=acc[:], in0=ps2[:, :], in1=sg[:], op=ALU.mult),
                                   r=[ps2, sg], w=[acc])
                        else:
                            fw.dve(lambda h: h.tensor_tensor(out=sg[:], in0=ps2[:, :], in1=sg[:], op=ALU.mult),
                                   r=[ps2, sg], w=[sg])
                            dst = mg[:, dt, :] if b == 2 else acc[:]
                            fw.dve(lambda h: h.tensor_tensor(out=dst, in0=acc[:], in1=sg[:], op=ALU.add),
                                   r=[acc, sg], w=[(mg, dt)] if b == 2 else [acc])
                for tt in range(4):
                    ti = n * 4 + tt
                    zz, xx = z[ti % 2], xr[ti % 2]
                    fw.dma("sp", xx[:], xres_ap[ti * 128:(ti + 1) * 128, :],
                           reads=[(xres_dep, ti)] if xres_dep else [], writes=[xx])
                    for hf in range(2):
                        ps = self.next_ps()
                        for dt in range(8):
                            fw.pe(lambda h: h.matmul(ps[:, :], lhsT=mg[:, dt, tt * 128:(tt + 1) * 128],
                                                     rhs=wo[:, dt, hf * 512:(hf + 1) * 512], start=(dt == 0), stop=(dt == 7)),
                                  r=[(mg, dt), wo], w=[ps])
                        fw.dve(lambda h: h.scalar_tensor_tensor(out=zz[:, hf * 512:(hf + 1) * 512],
                                                                in0=xx[:, hf * 512:(hf + 1) * 512], scalar=ALPHA,
                                                                in1=ps[:, :], op0=ALU.mult, op1=ALU.add),
                               r=[xx, ps], w=[zz])
                    self.ln_tile(zz, st, mv, gbc, None, 0)
                    self.store_x_and_xT(zz, ti, x1_d, x1T_d, xo[ti % 2])

    def top16(self, vals, idxs, src, scratch, n):
        fw = self.fw
        fw.dve(lambda h: h.max(out=vals[:, 0:8], in_=src), r=[self._t16_src], w=[self._t16_v])
        fw.dve(lambda h: h.max_index(out=idxs[:, 0:8], in_max=vals[:, 0:8], in_values=src),
               r=[self._t16_src, self._t16_v], w=[self._t16_i])
        fw.dve(lambda h: h.match_replace(out=scratch, in_to_replace=vals[:, 0:8], in_values=src, imm_value=-1e30),
               r=[self._t16_src, self._t16_v], w=[self._t16_s])
        fw.dve(lambda h: h.max(out=vals[:, 8:16], in_=scratch), r=[self._t16_s], w=[self._t16_v])
        fw.dve(lambda h: h.max_index(out=idxs[:, 8:16], in_max=vals[:, 8:16], in_values=scratch),
               r=[self._t16_s, self._t16_v], w=[self._t16_i])

    def phase_peer(self, l, x2_d, x2T_d):
        fw = self.fw
        x1_d = self.scr[f"x1_{l}"]
        U_ap, V_ap = self.inp[f"peer_u{l}"], self.inp[f"peer_v{l}"]
        ident = self.ident
        with contextlib.ExitStack() as es:
            wq = self.sb(es, "wq", [128, 8, 2048])
            for kc in range(8):
                fw.dma("sp", wq[:, kc, :], self.inp["peer_wq"][l, kc * 128:(kc + 1) * 128, :], writes=[(wq, kc)])
            skT = self.sb(es, "skT", [128, 2, 128])
            fw.dma("sp", skT[:], self.inp["peer_skT"][l].rearrange("a d n -> d a n"), writes=[skT])
            iota = self.sb(es, "iota16", [128, 16])
            fw.dma("sp", iota[:], self.inp["iota16"][:, :], writes=[iota])
            gbc = self.sb(es, "gbc2", [128, 4, D])
            fw.dma("sp", gbc[:], self.inp["ln_gb"][l:l + 1, :, :].to_broadcast([128, 4, D]), writes=[gbc])
            xts = [self.sb(es, f"pxt{i}", [128, D]) for i in range(2)]
            xTf = self.sb(es, "xTf", [128, 8, 128])
            qT = self.sb(es, "qT", [128, 16, 128])
            sc = self.sb(es, "sc", [128, 16, 128])
            sc2 = self.sb(es, "sc2", [128, 16, 128])
            v16 = self.sb(es, "v16", [128, 16, 16])
            i16 = self.sb(es, "i16", [128, 16, 16], U32)
            i16f = self.sb(es, "i16f", [128, 8, 2, 16])
            cand = self.sb(es, "cand", [128, 8, 256])
            cand2 = self.sb(es, "cand2", [128, 8, 256])
            top = self.sb(es, "top", [128, 8, 16])
            pos = self.sb(es, "pos", [128, 8, 16], U32)
            prc = self.sb(es, "prc", [128, 2, 8, 16], U32)
            prcf = self.sb(es, "prcf", [128, 2, 8, 16])
            oh = self.sb(es, "oh", [128, 128, 16])
            sel = self.sb(es, "sel", [128, 2, 128])
            idxf = self.sb(es, "idxf", [128, 128])
            idx = self.sb(es, "idx", [128, 128], U32)
            gate = self.sb(es, "gate", [128, 8, 16])
            gs = self.sb(es, "gs", [128, 8])
            actv = self.sb(es, "actv", [128, 128])
            gtmp = self.sb(es, "gtmp", [128, 128])
            wgt = self.sb(es, "wgt", [128, 128])
            junk = self.sb(es, "junk", [128, D])
            rows = [self.sb(es, f"rows{i}", [128, D]) for i in range(6)]
            ys = [self.sb(es, f"py{i}", [128, D]) for i in range(2)]
            xo = [self.sb(es, f"pxo{i}", [128, 8, 128], BF16) for i in range(2)]
            st = self.sb(es, "pst", [128, 2, 6])
            mv = self.sb(es, "pmv", [128, 4])
            rr = 0
            for ti in range(NT):
                xt, y = xts[ti % 2], ys[ti % 2]
                fw.dma("sp", xt[:], x1_d[ti * 128:(ti + 1) * 128, :], reads=[(x1_d, ti)], writes=[xt])
                for hh in range(2):
                    ps = self.next_ps()
                    for c in range(4):
                        cc = hh * 4 + c
                        fw.pe(lambda h: h.transpose(ps[:, c * 128:(c + 1) * 128], xt[:, cc * 128:(cc + 1) * 128], ident[:]),
                              r=[xt, ident], w=[ps])
                    fw.act(lambda h: h.copy(out=xTf[:, hh * 4:(hh + 1) * 4, :], in_=ps[:, :].rearrange("p (c t) -> p c t", c=4)),
                           r=[ps], w=[(xTf, hh)])
                for g4 in range(4):
                    ps = self.next_ps()
                    for b4 in range(4):
                        blk = g4 * 4 + b4
                        for kc in range(8):
                            fw.pe(lambda h: h.matmul(ps[:, b4 * 128:(b4 + 1) * 128], lhsT=wq[:, kc, blk * 128:(blk + 1) * 128],
                                                     rhs=xTf[:, kc, :], start=(kc == 0), stop=(kc == 7)),
                                  r=[(wq, kc), xTf], w=[ps])
                    fw.act(lambda h: h.copy(out=qT[:, g4 * 4:(g4 + 1) * 4, :], in_=ps[:, :].rearrange("p (b t) -> p b t", b=4)),
                           r=[ps], w=[(qT, g4)])
                for g4 in range(4):
                    ps = self.next_ps()
                    for b4 in range(4):
                        blk = g4 * 4 + b4
                        fw.pe(lambda h: h.matmul(ps[:, b4 * 128:(b4 + 1) * 128], lhsT=qT[:, blk, :], rhs=skT[:, blk % 2, :],
                                                 start=True, stop=True), r=[(qT, g4), skT], w=[ps])
                    fw.act(lambda h: h.copy(out=sc[:, g4 * 4:(g4 + 1) * 4, :], in_=ps[:, :].rearrange("p (b n) -> p b n", b=4)),
                           r=[ps], w=[(sc, g4)])
                self._t16_src, self._t16_v, self._t16_i, self._t16_s = sc, v16, i16, sc2
                for blk in range(16):
                    self.top16(v16[:, blk, :], i16[:, blk, :], sc[:, blk, :], sc2[:, blk, :], 128)
                v4 = v16[:, :, :].rearrange("p (h a) k -> p h a k", a=2)
                fw.dve(lambda h: h.tensor_tensor(
                    out=cand[:, :, :].rearrange("p h (i j) -> p h i j", i=16),
                    in0=v4[:, :, 0, :][:, :, :, None].to_broadcast([128, 8, 16, 16]),
                    in1=v4[:, :, 1, :][:, :, None, :].to_broadcast([128, 8, 16, 16]), op=ALU.add), r=[v16], w=[cand])
                self._t16_src, self._t16_v, self._t16_i, self._t16_s = cand, top, pos, cand2
                for hd in range(8):
                    self.top16(top[:, hd, :], pos[:, hd, :], cand[:, hd, :], cand2[:, hd, :], 256)
                fw.dve(lambda h: h.tensor_scalar(out=prc[:, 0, :, :], in0=pos[:], scalar1=4, scalar2=None,
                                                 op0=ALU.logical_shift_right), r=[pos], w=[prc])
                fw.dve(lambda h: h.tensor_scalar(out=prc[:, 1, :, :], in0=pos[:], scalar1=15, scalar2=None,
                                                 op0=ALU.bitwise_and), r=[pos], w=[prc])
                fw.dve(lambda h: h.tensor_copy(out=prcf[:], in_=prc[:]), r=[prc], w=[prcf])
                fw.dve(lambda h: h.tensor_copy(out=i16f[:], in_=i16[:, :, :].rearrange("p (h a) k -> p h a k", a=2)),
                       r=[i16], w=[i16f])
                for a in range(2):
                    fw.dve(lambda h: h.tensor_tensor(
                        out=oh[:], in0=iota[:, None, :].to_broadcast([128, 128, 16]),
                        in1=prcf[:, a, :, :].rearrange("p h k -> p (h k)")[:, :, None].to_broadcast([128, 128, 16]),
                        op=ALU.is_equal), r=[iota, prcf], w=[oh])
                    fw.dve(lambda h: h.tensor_tensor(
                        out=oh[:, :, :].rearrange("p (h k) j -> p h k j", h=8),
                        in0=oh[:, :, :].rearrange("p (h k) j -> p h k j", h=8),
                        in1=i16f[:, :, a, :][:, :, None, :].to_broadcast([128, 8, 16, 16]), op=ALU.mult),
                        r=[oh, i16f], w=[oh])
                    fw.dve(lambda h: h.tensor_reduce(out=sel[:, a, :], in_=oh[:], axis=AX.X, op=ALU.add), r=[oh], w=[(sel, a)])
                fw.dve(lambda h: h.scalar_tensor_tensor(out=idxf[:], in0=sel[:, 0, :], scalar=128.0, in1=sel[:, 1, :],
                                                        op0=ALU.mult, op1=ALU.add), r=[sel], w=[idxf])
                fw.dve(lambda h: h.tensor_copy(out=idx[:], in_=idxf[:]), r=[idxf], w=[idx])
                fw.dve(lambda h: h.tensor_tensor(out=gate[:], in0=top[:], in1=top[:, :, 0:1].to_broadcast([128, 8, 16]),
                                                 op=ALU.subtract), r=[top], w=[gate])
                fw.act(lambda h: h.activation(out=gate[:], in_=gate[:], func=AF.Exp), r=[gate], w=[gate])
                fw.dve(lambda h: h.tensor_reduce(out=gs[:], in_=gate[:], axis=AX.X, op=ALU.add), r=[gate], w=[gs])
                fw.dve(lambda h: h.reciprocal(out=gs[:], in_=gs[:]), r=[gs], w=[gs])
                fw.dve(lambda h: h.tensor_tensor(out=gate[:], in0=gate[:], in1=gs[:, :, None].to_broadcast([128, 8, 16]),
                                                 op=ALU.mult), r=[gate, gs], w=[gate])
                for j in range(128):
                    rb = rows[rr % 6]
                    rr += 1
                    fw.dma("pool", rb[:], U_ap[:, :], reads=[idx], writes=[rb],
                           indirect=bass.IndirectOffsetOnAxis(ap=idx[:, j:j + 1], axis=0))
                    fw.dve(lambda h: h.scalar_tensor_tensor(out=junk[:], in0=rb[:], scalar=1.0, in1=xt[:], op0=ALU.mult,
                                                            op1=ALU.mult, accum_out=actv[:, j:j + 1]),
                           r=[rb, xt], w=[junk, (actv, j)])
                self.gelu_tanh(None, actv, wgt, gtmp)
                fw.dve(lambda h: h.tensor_tensor(out=wgt[:], in0=wgt[:], in1=gate[:, :, :].rearrange("p h k -> p (h k)"),
                                                 op=ALU.mult), r=[wgt, gate], w=[wgt])
                for j in range(128):
                    rb = rows[rr % 6]
                    rr += 1
                    fw.dma("pool", rb[:], V_ap[:, :], reads=[idx], writes=[rb],
                           indirect=bass.IndirectOffsetOnAxis(ap=idx[:, j:j + 1], axis=0))
                    if j == 0:
                        fw.dve(lambda h: h.tensor_scalar(out=y[:], in0=rb[:], scalar1=wgt[:, 0:1], scalar2=None, op0=ALU.mult),
                               r=[rb, wgt], w=[y])
                    else:
                        fw.dve(lambda h: h.scalar_tensor_tensor(out=y[:], in0=rb[:], scalar=wgt[:, j:j + 1], in1=y[:],
                                                                op0=ALU.mult, op1=ALU.add), r=[rb, wgt, y], w=[y])
                fw.dve(lambda h: h.scalar_tensor_tensor(out=y[:], in0=xt[:], scalar=ALPHA, in1=y[:], op0=ALU.mult, op1=ALU.add),
                       r=[xt, y], w=[y])
                self.ln_tile(y, st, mv, gbc, None, 2)
                self.store_x_and_xT(y, ti, x2_d, x2T_d, xo[ti % 2])

    def phase_mla(self, l, xT_d):
        fw, nc = self.fw, self.nc
        yT = self.scr["yT"]
        SC = 96.0 ** -0.5
        with contextlib.ExitStack() as es:
            qn = self.sb(es, "qn", [64, 8, S], BF16)
            qp = self.sb(es, "qp", [32, 8, S], BF16)
            kn = self.sb(es, "kn", [64, 8, S], BF16)
            kp = self.sb(es, "kp", [32, S], BF16)
            V = self.sb(es, "V", [128, NT, 8, 65], BF16)
            cmask = self.sb(es, "cmask", [128, 128], BF16)
            identb = self.sb(es, "identb", [128, 128], BF16)
            fw.dma("pool", cmask[:], self.inp["cmask"][:, :], writes=[cmask])
            fw.dve(lambda h: h.tensor_copy(out=identb[:], in_=self.ident[:]), r=[self.ident], w=[identb])
            fw.pool(lambda h: h.memset(V[:], 1.0), w=[V])
            with contextlib.ExitStack() as es2:
                xT = self.load_xT(es2, xT_d)
                wt = self.sb(es2, "mw", [128, 8, 448], BF16)
                self.load_w_cols(wt, l, C_MLA, 416, key=0)
                for hf in range(2):
                    src = self.inp["w_in"][l, :, C_MLA + 384 + (1 - hf) * 16:C_MLA + 384 + (2 - hf) * 16].rearrange(
                        "(kc p) c -> p kc c", p=128)
                    fw.dma("pool", wt[:, :, 416 + hf * 16:416 + (hf + 1) * 16], src, writes=[(wt, 1 + hf)])
                vec = self.sb(es2, "mvec", [128, 3])
                fw.dma("sp", vec[:], self.inp["mla_vec"][l], writes=[vec])
                cs = self.sb(es2, "cs", [32, 2, S])
                fw.dma("sp", cs[:], self.inp["rope_cs"].rearrange("a p t -> p a t"), writes=[cs])
                wuq = self.sb(es2, "wuq", [128, 2, 768], BF16)
                fw.dma("pool", wuq[:], self.inp["mla_w_uq"][l].rearrange("(kc p) c -> p kc c", p=128), writes=[wuq])
                wuqr = self.sb(es2, "wuqr", [128, 2, 256], BF16)
                fw.dma("pool", wuqr[:], self.inp["mla_w_uq_rot"][l].rearrange("(kc p) c -> p kc c", p=128), writes=[wuqr])
                wk = self.sb(es2, "wk", [128, 512], BF16)
                fw.dma("pool", wk[:], self.inp["mla_w_ukv_k"][l], writes=[wk])
                wv = self.sb(es2, "wv", [128, 512], BF16)
                fw.dma("pool", wv[:], self.inp["mla_w_ukv_v"][l], writes=[wv])
                ones = self.sb(es2, "ones", [128, 128])
                fw.pool(lambda h: h.memset(ones[:], 1.0), w=[ones])
                cl = self.sb(es2, "cl", [128, 3, 512])
                sq = self.sb(es2, "sq", [128, 3, 512])
                rs = self.sb(es2, "rs", [128, 2, 512])
                cn = self.sb(es2, "cn", [128, 3, 512], BF16)
                kr = self.sb(es2, "kr", [32, 2, 512])
                tmp32 = self.sb(es2, "tmp32", [32, 2, 512])
                for n in range(4):
                    ts = slice(n * 512, (n + 1) * 512)
                    for j in range(3):
                        ps = self.next_ps()
                        self.proj_fm(wt, xT, j * 128, 128, n, ps)
                        fw.act(lambda h: h.copy(out=cl[:, j, :], in_=ps[:, :]), r=[ps], w=[(cl, j)])
                        fw.dve(lambda h: h.tensor_tensor(out=sq[:, j, :], in0=cl[:, j, :], in1=cl[:, j, :], op=ALU.mult),
                               r=[(cl, j)], w=[(sq, j)])
                    for j in range(2):
                        ps = self.next_ps()
                        self.proj_fm(wt, xT, 384 + j * 32, 32, n, ps)
                        fw.act(lambda h: h.copy(out=kr[:, j, :], in_=ps[0:32, :]), r=[ps], w=[(kr, j)])
                    for g, (tiles, dim) in enumerate((((0, 1), 256.0), ((2,), 128.0))):
                        ps = self.next_ps()
                        for ii, j in enumerate(tiles):
                            fw.pe(lambda h: h.matmul(ps[:, :], lhsT=ones[:, :], rhs=sq[:, j, :],
                                                     start=(ii == 0), stop=(ii == len(tiles) - 1)),
                                  r=[ones, (sq, j)], w=[ps])
                        fw.act(lambda h: h.activation(out=rs[:, g, :], in_=ps[:, :], func=AF.Sqrt, bias=1e-6,
                                                      scale=1.0 / dim), r=[ps], w=[(rs, g)])
                        fw.dve(lambda h: h.reciprocal(out=rs[:, g, :], in_=rs[:, g, :]), r=[(rs, g)], w=[(rs, g)])
                        for j in tiles:
                            fw.dve(lambda h: h.scalar_tensor_tensor(out=cn[:, j, :], in0=cl[:, j, :],
                                                                    scalar=vec[:, j:j + 1], in1=rs[:, g, :],
                                                                    op0=ALU.mult, op1=ALU.mult),
                                   r=[(cl, j), vec, (rs, g)], w=[(cn, j)])
                    fw.dve(lambda h: h.tensor_tensor(out=tmp32[:, 0, :], in0=kr[:, 0, :], in1=cs[:, 0, ts], op=ALU.mult),
                           r=[(kr, 0), cs], w=[(tmp32, 0)])
                    fw.dve(lambda h: h.tensor_tensor(out=tmp32[:, 1, :], in0=kr[:, 1, :], in1=cs[:, 1, ts], op=ALU.mult),
                           r=[(kr, 1), cs], w=[(tmp32, 1)])
                    fw.dve(lambda h: h.tensor_tensor(out=kp[:, ts], in0=tmp32[:, 0, :], in1=tmp32[:, 1, :], op=ALU.add),
                           r=[tmp32], w=[(kp, n)])
                    for hd in range(8):
                        ps = self.next_ps()
                        for kc in range(2):
                            fw.pe(lambda h: h.matmul(ps[0:64, :], lhsT=wuq[:, kc, hd * 96:hd * 96 + 64],
                                                     rhs=cn[:, kc, :], start=(kc == 0), stop=(kc == 1)),
                                  r=[wuq, (cn, kc)], w=[ps])
                        fw.act(lambda h: h.activation(out=qn[:, hd, ts], in_=ps[0:64, :], func=AF.Copy, scale=SC),
                               r=[ps], w=[(qn, (hd, n))])
                        ps = self.next_ps()
                        for kc in range(2):
                            fw.pe(lambda h: h.matmul(ps[0:32, :], lhsT=wuq[:, kc, hd * 96 + 64:hd * 96 + 96],
                                                     rhs=cn[:, kc, :], start=(kc == 0), stop=(kc == 1)),
                                  r=[wuq, (cn, kc)], w=[ps])
                        ps2 = self.next_ps()
                        for kc in range(2):
                            fw.pe(lambda h: h.matmul(ps2[0:32, :], lhsT=wuqr[:, kc, hd * 32:hd * 32 + 32],
                                                     rhs=cn[:, kc, :], start=(kc == 0), stop=(kc == 1)),
                                  r=[wuqr, (cn, kc)], w=[ps2])
                        fw.dve(lambda h: h.tensor_tensor(out=tmp32[:, 0, :], in0=ps[0:32, :], in1=cs[:, 0, ts], op=ALU.mult),
                               r=[ps, cs], w=[(tmp32, 0)])
                        fw.dve(lambda h: h.tensor_tensor(out=tmp32[:, 1, :], in0=ps2[0:32, :], in1=cs[:, 1, ts], op=ALU.mult),
                               r=[ps2, cs], w=[(tmp32, 1)])
                        fw.dve(lambda h: h.scalar_tensor_tensor(out=qp[:, hd, ts], in0=tmp32[:, 0, :], scalar=1.0,
                                                                in1=tmp32[:, 1, :], op0=ALU.mult, op1=ALU.add),
                               r=[tmp32], w=[(qp, (hd, n))])
                        fw.dve(lambda h: h.tensor_scalar(out=qp[:, hd, ts], in0=qp[:, hd, ts], scalar1=SC, scalar2=None,
                                                         op0=ALU.mult), r=[(qp, (hd, n))], w=[(qp, (hd, n))])
                        ps = self.next_ps()
                        fw.pe(lambda h: h.matmul(ps[0:64, :], lhsT=wk[:, hd * 64:(hd + 1) * 64], rhs=cn[:, 2, :],
                                                 start=True, stop=True), r=[wk, (cn, 2)], w=[ps])
                        fw.act(lambda h: h.copy(out=kn[:, hd, ts], in_=ps[0:64, :]), r=[ps], w=[(kn, (hd, n))])
                    for tt in range(4):
                        ti = n * 4 + tt
                        ps = self.next_ps()
                        fw.pe(lambda h: h.matmul(ps[:, :], lhsT=cn[:, 2, tt * 128:(tt + 1) * 128], rhs=wv[:, :],
                                                 start=True, stop=True), r=[(cn, 2), wv], w=[ps])
                        fw.act(lambda h: h.copy(out=V[:, ti, :, 0:64], in_=ps[:, :].rearrange("p (h d) -> p h d", h=8)),
                               r=[ps], w=[(V, ti)])
            fw.barrier()
            with contextlib.ExitStack() as es3:
                PT = self.sb(es3, "PT", [128, NT, 8, 128], BF16)
                rec = self.sb(es3, "rec", [128, 8, 1])
                yc = self.sb(es3, "yc", [128, 8, 64], BF16)
                yct = [self.sb(es3, f"yct{i}", [128, 4, 128], BF16) for i in range(2)]
                for qi in range(NT):
                    qs = slice(qi * 128, (qi + 1) * 128)
                    for kt in range(qi + 1):
                        ks = slice(kt * 128, (kt + 1) * 128)
                        for hg in range(2):
                            ps = self.next_ps()
                            for hh in range(4):
                                hd = hg * 4 + hh
                                o = ps[:, hh * 128:(hh + 1) * 128]
                                fw.pe(lambda h: h.matmul(o, lhsT=kn[:, hd, ks], rhs=qn[:, hd, qs], start=True, stop=False),
                                      r=[kn, qn], w=[ps])
                                fw.pe(lambda h: h.matmul(o, lhsT=kp[:, ks], rhs=qp[:, hd, qs], start=False, stop=True),
                                      r=[kp, qp], w=[ps])
                            fw.act(lambda h: h.activation(
                                out=PT[:, kt, hg * 4:(hg + 1) * 4, :],
                                in_=ps[:, :].rearrange("p (h q) -> p h q", h=4), func=AF.Exp),
                                r=[ps], w=[(PT, (kt, hg))])
                            if kt == qi:
                                fw.dve(lambda h: h.tensor_tensor(
                                    out=PT[:, kt, hg * 4:(hg + 1) * 4, :], in0=PT[:, kt, hg * 4:(hg + 1) * 4, :],
                                    in1=cmask[:, None, :].to_broadcast([128, 4, 128]), op=ALU.mult),
                                    r=[(PT, (kt, hg)), cmask], w=[(PT, (kt, hg))])
                    pss = [self.next_ps(), self.next_ps()]
                    for hd in range(8):
                        ps = pss[hd // 4]
                        o = ps[:, (hd % 4) * 65:(hd % 4) * 65 + 65]
                        for kt in range(qi + 1):
                            fw.pe(lambda h: h.matmul(o, lhsT=PT[:, kt, hd, :], rhs=V[:, kt, hd, :],
                                                     start=(kt == 0), stop=(kt == qi)),
                                  r=[(PT, (kt, hd // 4)), (V, kt)], w=[ps])
                    for hg in range(2):
                        ps = pss[hg]
                        pv = ps[:, 0:260].rearrange("p (h d) -> p h d", h=4)
                        fw.dve(lambda h: h.reciprocal(out=rec[:, hg * 4:(hg + 1) * 4, :], in_=pv[:, :, 64:65]),
                               r=[ps], w=[(rec, hg)])
                        fw.dve(lambda h: h.tensor_tensor(out=yc[:, hg * 4:(hg + 1) * 4, :], in0=pv[:, :, 0:64],
                                                         in1=rec[:, hg * 4:(hg + 1) * 4, :].to_broadcast([128, 4, 64]),
                                                         op=ALU.mult), r=[ps, (rec, hg)], w=[(yc, hg)])
                    o = yct[qi % 2]
                    ycf = yc[:, :, :].rearrange("p h d -> p (h d)")
                    for c in range(4):
                        pst = self.next_ps()
                        pb = pst[:, :].bitcast(BF16)
                        fw.pe(lambda h: h.transpose(pb[:, 0:128], ycf[:, c * 128:(c + 1) * 128], identb[:]),
                              r=[yc, identb], w=[pst])
                        fw.act(lambda h: h.copy(out=o[:, c, :], in_=pb[:, 0:128]), r=[pst], w=[(o, c)])
                    fw.dma("sp", yT[2, :, qs].rearrange("(c p) t -> p c t", p=128), o[:], reads=[o],
                           writes=[(yT, ("c", qi))])


def prep_inputs(inputs, nseq=1):
    f = lambda a: np.ascontiguousarray(np.asarray(a, dtype=np.float32))
    shared = {}
    shared["w_in"] = f(inputs["w_in"])
    rv = np.stack([inputs["rg_conv_w"][:, 0], inputs["rg_conv_w"][:, 1], inputs["rg_conv_w"][:, 2],
                   inputs["rg_conv_w"][:, 3], inputs["rg_conv_b"], inputs["rg_ba"], inputs["rg_bx"],
                   inputs["rg_log_a"]], axis=-1)
    shared["rg_vec"] = f(rv.reshape(L, 4, 128, 8).transpose(0, 2, 1, 3))
    shared["rg_wa"] = f(inputs["rg_wa"])
    shared["rg_wx"] = f(inputs["rg_wx"])
    shared["ident"] = np.eye(128, dtype=np.float32)
    mv = np.stack([inputs["mla_q_norm"][:, :128], inputs["mla_q_norm"][:, 128:], inputs["mla_kv_norm"]], axis=-1)
    shared["mla_vec"] = f(mv)
    wuq = np.asarray(inputs["mla_w_uq"], np.float32)
    shared["mla_w_uq"] = f(wuq)
    w4 = wuq.reshape(L, 256, 8, 96)
    shared["mla_w_uq_rot"] = f(np.concatenate([w4[..., 80:96], w4[..., 64:80]], -1).reshape(L, 256, 256))
    wkv = np.asarray(inputs["mla_w_ukv"], np.float32).reshape(L, 128, 8, 128)
    shared["mla_w_ukv_k"] = f(wkv[..., :64].reshape(L, 128, 512))
    shared["mla_w_ukv_v"] = f(wkv[..., 64:].reshape(L, 128, 512))
    pos = np.arange(S, dtype=np.float32)
    inv = (10000.0 ** (-np.arange(16, dtype=np.float32) / 16)).astype(np.float32)
    ang = pos[None, :] * inv[:, None]
    cs_, sn_ = np.cos(ang).astype(np.float32), np.sin(ang).astype(np.float32)
    shared["rope_cs"] = f(np.stack([np.concatenate([cs_, cs_], 0), np.concatenate([-sn_, sn_], 0)], 0))
    shared["cmask"] = f(np.triu(np.ones((128, 128), np.float32)))
    shared["rw_mix"] = f(inputs["rw_mix"])[:, None, :]
    vz = np.zeros((L, 512), np.float32); vz[1:] = inputs["rw_v0"]
    rwv = np.stack([inputs["rw_w0"], inputs["rw_a0"], inputs["rw_k_k"], inputs["rw_k_a"],
                    np.asarray(inputs["rw_r_k"]).reshape(L, 512), inputs["rw_gn_g"], inputs["rw_gn_b"], vz], axis=-1)
    shared["rw_vec"] = f(rwv.reshape(L, 4, 128, 8).transpose(0, 2, 1, 3))
    shared["rw_w2a2"] = f(np.concatenate([inputs["rw_w2"], inputs["rw_a2"]], axis=1))
    shared["rw_g2"] = f(inputs["rw_g2"])
    shared["rw_v1"] = f(inputs["rw_v1"])
    shared["rw_v2"] = f(inputs["rw_v2"])
    su = np.triu(np.ones((128, 128), np.float32), 1); ui = np.triu(np.ones((128, 128), np.float32))
    shared["rw_masks"] = f(np.stack([su, su, ui, ui, su.T], axis=1))
    obd = np.zeros((128, 128), np.float32); obd[:64, :64] = 1; obd[64:, 64:] = 1
    shared["onesbd"] = obd
    shared["w_branch"] = f(inputs["w_branch"])
    shared["w_out"] = f(inputs["w_out"])
    shared["peer_wq"] = f(inputs["peer_w_query"])
    shared["peer_skT"] = f(np.asarray(inputs["peer_subkeys"]).transpose(0, 1, 3, 2))
    shared["iota16"] = f(np.tile(np.arange(16, dtype=np.float32)[None, :], (128, 1)))
    for l in range(L):
        shared[f"peer_u{l}"] = f(inputs["peer_u"][l])
        shared[f"peer_v{l}"] = f(inputs["peer_v"][l])
    shared["ln_gb"] = f(np.stack([inputs["ln1_g"], inputs["ln1_b"], inputs["ln2_g"], inputs["ln2_b"]], axis=1))
    maps = []
    for b in range(8 // nseq):
        m = dict(shared)
        m["x"] = f(inputs["x"][b * nseq:(b + 1) * nseq])
        maps.append(m)
    return maps


def run(inputs, debug=False, phases=None, cores=8, trace=False, nseq=1):
    bld = Builder(debug=debug, phases=phases, nseq=nseq)
    nc = bld.build()
    maps = prep_inputs(inputs, nseq)[:cores]
    maps = [{k: v for k, v in m.items() if k in bld.inp} for m in maps]
    res = run_bass_kernel_spmd(nc, maps, core_ids=list(range(cores)), trace=trace)
    return res


NCORES = 8


def kernel(**inputs):
    nseq = 8 // NCORES
    res = run(inputs, cores=NCORES, nseq=nseq)
    return np.concatenate([np.asarray(r["out"]) for r in res.results], axis=0).astype(np.float32)
```

```python
import contextlib
import numpy as np
import concourse.bass as bass
import concourse.mybir as mybir
from concourse.bass_utils import run_bass_kernel_spmd

F32 = mybir.dt.float32
BF16 = mybir.dt.bfloat16
U32 = mybir.dt.uint32
I32 = mybir.dt.int32
AF = mybir.ActivationFunctionType
ALU = mybir.AluOpType
AX = mybir.AxisListType

D = 1024
S = 2048
L = 2
NT = S // 128
IN_COLS = 6304
C_RG, C_RW, C_MLA, C_GATE = 0, 1024, 2816, 3232
ALPHA = (2.0 * L) ** 0.25


class Dep:
    def __init__(self, name):
        self.name = name
        self.st = {}

    def states(self, key):
        if key is None:
            if None not in self.st:
                self.st[None] = [None, {}]
            return list(self.st.values())
        out = []
        if None in self.st:
            out.append(self.st[None])
        if key not in self.st:
            self.st[key] = [None, {}]
        out.append(self.st[key])
        return out


class T:
    def __init__(self, t, name):
        self.t = t
        self.dep = Dep(name)

    def __getitem__(self, idx):
        return self.t[idx]


class Eng:
    def __init__(self, name, h, sem):
        self.name, self.h, self.sem = name, h, sem
        self.count = 0
        self.known = {}


class FW:
    NDS = {"sp": 16, "pool": 12, "act": 6}

    def __init__(self, nc, es):
        self.nc = nc
        self.eng = {}
        for name, h in (("pe", nc.tensor), ("dve", nc.vector), ("act", nc.scalar),
                        ("pool", nc.gpsimd), ("sp", nc.sync)):
            sem = es.enter_context(nc.semaphore("sem_" + name))
            self.eng[name] = Eng(name, h, sem)
        self.dsem = {}
        self.dcnt = {}
        self.drr = {}
        for q, n in self.NDS.items():
            self.dsem[q] = [es.enter_context(nc.semaphore(f"ds_{q}{i}")) for i in range(n)]
            self.dcnt[q] = [0] * n
            self.drr[q] = 0

    def _wait(self, e, sid, sem, val):
        if val > 0 and e.known.get(sid, 0) < val:
            e.h.wait_ge(sem, val)
            e.known[sid] = val

    def _collect(self, reads, writes, ename):
        need = {}

        def add(tok):
            if tok is None:
                return
            sid, sem, val = tok
            if ename == "pe" and sid == "pe":
                return
            if need.get(sid, (None, 0))[1] < val:
                need[sid] = (sem, val)

        rs, ws = [], []
        for r in reads:
            t, k = r if isinstance(r, tuple) else (r, None)
            sts = t.dep.states(k)
            rs.append((k, sts))
            for st in sts:
                add(st[0])
        for w in writes:
            t, k = w if isinstance(w, tuple) else (w, None)
            sts = t.dep.states(k)
            ws.append((k, sts))
            for st in sts:
                add(st[0])
                for tok in st[1].values():
                    add(tok)
        return need, rs, ws

    def _record(self, tok, rs, ws):
        for k, sts in rs:
            for st in (sts if k is None else sts[-1:]):
                st[1][tok[0]] = tok
        for k, sts in ws:
            for st in (sts if k is None else sts[-1:]):
                st[0] = tok
                st[1] = {}

    def op(self, ename, fn, reads=(), writes=()):
        e = self.eng[ename]
        need, rs, ws = self._collect(reads, writes, ename)
        for sid, (sem, val) in need.items():
            self._wait(e, sid, sem, val)
        ins = fn(e.h)
        e.count += 1
        ins.then_inc(e.sem, 1)
        self._record((ename, e.sem, e.count), rs, ws)
        return ins

    def dma(self, q, out, in_, reads=(), writes=(), indirect=None, **kw):
        e = self.eng[q]
        i = self.drr[q]
        self.drr[q] = (i + 1) % len(self.dsem[q])
        sem = self.dsem[q][i]
        sid = ("d", q, i)
        self._wait(e, sid, sem, 16 * self.dcnt[q][i])
        need, rs, ws = self._collect(reads, writes, q)
        for s2, (sm, val) in need.items():
            self._wait(e, s2, sm, val)
        if indirect is None:
            ins = e.h.dma_start(out=out, in_=in_, **kw)
        else:
            ins = e.h.indirect_dma_start(out=out, out_offset=None, in_=in_, in_offset=indirect, **kw)
        self.dcnt[q][i] += 1
        ins.then_inc(sem, 16)
        self._record((sid, sem, 16 * self.dcnt[q][i]), rs, ws)
        return ins

    def barrier(self):
        for e in self.eng.values():
            for o in self.eng.values():
                if o is not e:
                    self._wait(e, o.name, o.sem, o.count)
            for q in self.dsem:
                for i, sem in enumerate(self.dsem[q]):
                    self._wait(e, ("d", q, i), sem, 16 * self.dcnt[q][i])

    def dve(self, fn, r=(), w=()):
        return self.op("dve", fn, r, w)

    def act(self, fn, r=(), w=()):
        return self.op("act", fn, r, w)

    def pe(self, fn, r=(), w=()):
        return self.op("pe", fn, r, w)

    def pool(self, fn, r=(), w=()):
        return self.op("pool", fn, r, w)


class Builder:
    def __init__(self, debug=False, phases=None, nseq=1):
        self.debug = debug
        self.phases = phases
        self.nseq = nseq
        self.nc = bass.Bass("TRN2", target_bir_lowering=False)
        self.inp = {}
        self.scr = {}

    def din(self, name, shape, dt=F32):
        self.inp[name] = self.nc.dram_tensor(name, list(shape), dt, kind="ExternalInput").ap()

    def dscr(self, name, shape, dt, out=False):
        kind = "ExternalOutput" if (out or self.debug) else "Internal"
        ap = self.nc.dram_tensor(name, list(shape), dt, kind=kind).ap()
        self.scr[name] = T(ap, name)
        return self.scr[name]

    def sb(self, es, name, shape, dt=F32):
        self.uid = getattr(self, "uid", 0) + 1
        nm = f"s{self.uid}_{name}"
        return T(es.enter_context(self.nc.sbuf_tensor(nm, list(shape), dt)), nm)

    def declare(self):
        self.din("x", [self.nseq, S, D])
        self.din("w_in", [L, D, IN_COLS])
        self.din("rg_vec", [L, 128, 4, 8])
        self.din("rg_wa", [L, 8, 64, 64])
        self.din("rg_wx", [L, 8, 64, 64])
        self.din("ident", [128, 128])
        self.din("mla_vec", [L, 128, 3])
        self.din("mla_w_uq", [L, 256, 768])
        self.din("mla_w_uq_rot", [L, 256, 256])
        self.din("mla_w_ukv_k", [L, 128, 512])
        self.din("mla_w_ukv_v", [L, 128, 512])
        self.din("rope_cs", [2, 32, S])
        self.din("cmask", [128, 128])
        self.din("rw_mix", [L, 1, 1792])
        self.din("rw_vec", [L, 128, 4, 8])
        self.din("rw_w2a2", [L, 128, 512])
        self.din("rw_g2", [L, 128, 512])
        self.din("rw_v1", [1, 512, 32])
        self.din("rw_v2", [1, 32, 512])
        self.din("rw_masks", [128, 8, 128])
        self.din("onesbd", [128, 128])
        self.dscr("rwT", [7, 512, S], F32)
        self.dscr("vfirst", [512, S], F32)
        self.dscr("rw_wc", [512, NT], F32)
        self.din("w_branch", [L, 3, 512, D])
        self.din("w_out", [L, D, D])
        self.din("peer_wq", [L, D, 2048])
        self.din("peer_skT", [L, 2, 128, 128])
        self.din("iota16", [128, 16])
        for l in range(L):
            if self.phases is None or f"F{l}" in self.phases:
                self.din(f"peer_u{l}", [16384, D])
                self.din(f"peer_v{l}", [16384, D])
            self.dscr(f"x2T_{l}", [D, S], BF16)
        self.dscr("x2_0", [S, D], F32)
        self.din("ln_gb", [L, 4, D])
        for l in range(L):
            self.dscr(f"x1_{l}", [S, D], F32)
            self.dscr(f"x1T_{l}", [D, S], BF16)
        self.dscr("xT0", [D, S], BF16)
        self.dscr("yT", [3, 512, S], BF16)
        self.dscr("out", [self.nseq, S, D], F32, out=True)

    def build(self):
        nc = self.nc
        self.declare()
        with contextlib.ExitStack() as es:
            self.fw = fw = FW(nc, es)
            self.ps = [T(es.enter_context(nc.psum_tensor(f"ps{i}", [128, 512], F32)), f"ps{i}")
                       for i in range(8)]
            self.psi = 0
            self.ident = self.sb(es, "ident", [128, 128])
            fw.dma("sp", self.ident[:], self.inp["ident"][:, :], writes=[self.ident])
            ph = self.phases
            for sq in range(self.nseq):
                x_in = self.inp["x"][sq]
                out_t = T(self.scr["out"].t[sq], f"out{sq}")
                if ph is None or "A" in ph:
                    self.phase_xT(x_in, None, self.scr["xT0"])
                    fw.barrier()
                for l in range(L):
                    xTd = self.scr["xT0"] if l == 0 else self.scr[f"x2T_{l - 1}"]
                    if ph is None or f"B{l}" in ph:
                        self.phase_rg(l, xTd)
                        fw.barrier()
                    if ph is None or f"C{l}" in ph or f"Cp{l}" in ph:
                        self.phase_rwkv_prep(l, xTd)
                        fw.barrier()
                    if ph is None or f"C{l}" in ph or f"Cs{l}" in ph:
                        self.phase_rwkv_scan(l)
                        fw.barrier()
                    if ph is None or f"D{l}" in ph:
                        self.phase_mla(l, xTd)
                        fw.barrier()
                    if ph is None or f"E{l}" in ph:
                        if l == 0:
                            self.phase_merge(l, xTd, x_in, None)
                        else:
                            self.phase_merge(l, xTd, self.scr["x2_0"].t, self.scr["x2_0"])
                        fw.barrier()
                    if ph is None or f"F{l}" in ph:
                        x2d = out_t if l == L - 1 else self.scr["x2_0"]
                        self.phase_peer(l, x2d, self.scr[f"x2T_{l}"])
                        fw.barrier()
            fw.barrier()
        return nc

    def next_ps(self):
        while True:
            p = self.ps[self.psi]
            self.psi = (self.psi + 1) % 8
            if p not in getattr(self, "ps_reserved", ()):
                return p

    def phase_xT(self, x_ap, x_dep, xT_d):
        fw = self.fw
        with contextlib.ExitStack() as es:
            xt = [self.sb(es, f"xa{i}", [128, D]) for i in range(2)]
            xo = [self.sb(es, f"xo{i}", [128, 8, 128], BF16) for i in range(2)]
            for i in range(NT):
                b = xt[i % 2]
                fw.dma("sp", b[:], x_ap[i * 128:(i + 1) * 128, :],
                       reads=[x_dep] if x_dep else [], writes=[b])
                o = xo[i % 2]
                for hh in range(2):
                    ps = self.next_ps()
                    for c in range(4):
                        cc = hh * 4 + c
                        fw.pe(lambda h, ps=ps, c=c, cc=cc, b=b: h.transpose(
                            ps[:, c * 128:(c + 1) * 128], b[:, cc * 128:(cc + 1) * 128], self.ident[:]),
                            r=[b, self.ident], w=[ps])
                    fw.dve(lambda h, ps=ps, o=o, hh=hh: h.tensor_copy(
                        out=o[:, hh * 4:(hh + 1) * 4, :],
                        in_=ps[:, :].rearrange("p (c t) -> p c t", c=4)), r=[ps], w=[(o, hh)])
                fw.dma("sp", xT_d[:, i * 128:(i + 1) * 128].rearrange("(c p) t -> p c t", p=128),
                       o[:], reads=[o], writes=[(xT_d, i)])

    def load_xT(self, es, xT_d):
        xT = self.sb(es, "xT", [128, 8, S], BF16)
        for c in range(8):
            self.fw.dma("sp", xT[:, c, :], xT_d[c * 128:(c + 1) * 128, :], reads=[xT_d], writes=[(xT, c)])
        return xT

    def load_w_cols(self, wt, l, c0, ncols, key=None):
        src = self.inp["w_in"][l, :, c0:c0 + ncols].rearrange("(kc p) c -> p kc c", p=128)
        self.fw.dma("pool", wt[:, :, 0:ncols], src, writes=[(wt, key) if key is not None else wt])

    def proj_fm(self, wt, xT, j0, M, n, ps):
        for kc in range(8):
            self.fw.pe(lambda h, kc=kc: h.matmul(ps[0:M, :], lhsT=wt[:, kc, j0:j0 + M],
                                                  rhs=xT[:, kc, n * 512:(n + 1) * 512],
                                                  start=(kc == 0), stop=(kc == 7)),
                       r=[wt, xT], w=[ps])

    def gelu_tanh(self, es_tmp, x, out, tmp):
        fw = self.fw
        fw.dve(lambda h: h.tensor_tensor(out=tmp[:], in0=x[:], in1=x[:], op=ALU.mult), r=[x], w=[tmp])
        fw.dve(lambda h: h.tensor_scalar(out=tmp[:], in0=tmp[:], scalar1=0.044715, scalar2=1.0,
                                         op0=ALU.mult, op1=ALU.add), r=[tmp], w=[tmp])
        fw.dve(lambda h: h.tensor_tensor(out=tmp[:], in0=tmp[:], in1=x[:], op=ALU.mult), r=[tmp, x], w=[tmp])
        fw.act(lambda h: h.activation(out=tmp[:], in_=tmp[:], func=AF.Sigmoid, scale=1.5957691216057308),
               r=[tmp], w=[tmp])
        fw.dve(lambda h: h.tensor_tensor(out=out[:], in0=tmp[:], in1=x[:], op=ALU.mult), r=[tmp, x], w=[out])

    def phase_rg(self, l, xT_d):
        fw, nc = self.fw, self.nc
        yT = self.scr["yT"]
        with contextlib.ExitStack() as es:
            xT = self.load_xT(es, xT_d)
            vec = self.sb(es, "rgvec", [128, 4, 8])
            fw.dma("sp", vec[:], self.inp["rg_vec"][l], writes=[vec])
            cst = self.sb(es, "rgc", [128, 4, 4])
            wts = [self.sb(es, f"rgw{i}", [128, 8, 256], BF16) for i in range(2)]
            wbd = [self.sb(es, f"rgbd{i}", [128, 2, 128]) for i in range(2)]
            names = ["xb", "gb", "xc", "r", "i", "a", "m", "h"]
            tl = {n: self.sb(es, "rg_" + n, [128, S]) for n in names}
            yo = self.sb(es, "rg_yo", [128, S], BF16)
            xb, gb, xc, r_, i_, a_, m_, h_ = (tl[n] for n in names)
            for c in range(4):
                wt = wts[c % 2]
                bd = wbd[c % 2]
                self.load_w_cols(wt, l, C_RG + c * 128, 128, key=0)
                src = self.inp["w_in"][l, :, C_RG + 512 + c * 128:C_RG + 512 + (c + 1) * 128].rearrange(
                    "(kc p) c -> p kc c", p=128)
                fw.dma("pool", wt[:, :, 128:256], src, writes=[(wt, 1)])
                fw.pool(lambda h: h.memset(bd[:], 0.0), w=[bd])
                for j, nm in enumerate(("rg_wa", "rg_wx")):
                    for hb in range(2):
                        fw.dma("sp", bd[hb * 64:(hb + 1) * 64, j, hb * 64:(hb + 1) * 64],
                               self.inp[nm][l, 2 * c + hb], writes=[bd])
                for j, dst in enumerate((xb, gb)):
                    for n in range(4):
                        ps = self.next_ps()
                        self.proj_fm(wt, xT, j * 128, 128, n, ps)
                        fw.act(lambda h, ps=ps, dst=dst, n=n: h.copy(out=dst[:, n * 512:(n + 1) * 512], in_=ps[:, :]),
                               r=[ps], w=[(dst, n)])
                v = lambda k: vec[:, c, k:k + 1]
                fw.dve(lambda h: h.tensor_scalar(out=xc[:], in0=xb[:], scalar1=v(0), scalar2=v(4),
                                                 op0=ALU.mult, op1=ALU.add), r=[xb, vec], w=[xc])
                for j in range(1, 4):
                    fw.dve(lambda h, j=j: h.scalar_tensor_tensor(out=xc[:, j:], in0=xb[:, 0:S - j], scalar=v(j),
                                                                 in1=xc[:, j:], op0=ALU.mult, op1=ALU.add),
                           r=[xb, xc, vec], w=[xc])
                cc = lambda k: cst[:, c, k:k + 1]
                fw.act(lambda h: h.activation(out=cc(0), in_=v(7), func=AF.Exp, scale=-1.0), r=[vec], w=[cst])
                fw.act(lambda h: h.activation(out=cc(1), in_=cc(0), func=AF.Ln, bias=1.0, scale=1.0), r=[cst], w=[cst])
                fw.dve(lambda h: h.tensor_scalar(out=cc(2), in0=cc(1), scalar1=-8.0, scalar2=None, op0=ALU.mult),
                       r=[cst], w=[cst])
                fw.dve(lambda h: h.tensor_scalar(out=cc(3), in0=cc(1), scalar1=-16.0, scalar2=None, op0=ALU.mult),
                       r=[cst], w=[cst])
                for j, (dst, bk) in enumerate(((r_, 5), (i_, 6))):
                    for n in range(4):
                        ps = self.next_ps()
                        fw.pe(lambda h, ps=ps, j=j, n=n: h.matmul(ps[:, :], lhsT=bd[:, j, :],
                                                                   rhs=xc[:, n * 512:(n + 1) * 512],
                                                                   start=True, stop=True), r=[bd, xc], w=[ps])
                        fw.act(lambda h, ps=ps, dst=dst, n=n, bk=bk: h.activation(
                            out=dst[:, n * 512:(n + 1) * 512], in_=ps[:, :], func=AF.Sigmoid, bias=v(bk), scale=1.0),
                            r=[ps, vec], w=[(dst, n)])
                fw.act(lambda h: h.activation(out=a_[:], in_=r_[:], func=AF.Exp, scale=cc(2)), r=[r_, cst], w=[a_])
                fw.act(lambda h: h.activation(out=m_[:], in_=r_[:], func=AF.Exp, scale=cc(3)), r=[r_, cst], w=[m_])
                fw.act(lambda h: h.activation(out=m_[:], in_=m_[:], func=AF.Sqrt, bias=1.0, scale=-1.0), r=[m_], w=[m_])
                fw.dve(lambda h: h.memset(m_[:, 0:1], 1.0), w=[m_])
                fw.dve(lambda h: h.tensor_tensor(out=i_[:], in0=i_[:], in1=xc[:], op=ALU.mult), r=[i_, xc], w=[i_])
                fw.dve(lambda h: h.tensor_tensor(out=i_[:], in0=i_[:], in1=m_[:], op=ALU.mult), r=[i_, m_], w=[i_])
                fw.dve(lambda h: h.tensor_tensor_scan(out=h_[:], data0=a_[:], data1=i_[:], initial=0.0,
                                                      op0=ALU.mult, op1=ALU.add), r=[a_, i_], w=[h_])
                self.gelu_tanh(es, gb, r_, m_)
                fw.dve(lambda h: h.tensor_tensor(out=yo[:], in0=h_[:], in1=r_[:], op=ALU.mult), r=[h_, r_], w=[yo])
                fw.dma("sp", yT[0, c * 128:(c + 1) * 128, :], yo[:], reads=[yo], writes=[(yT, ("a", c))])


    def phase_rwkv_prep(self, l, xT_d):
        fw = self.fw
        rwT, vfirst = self.scr["rwT"], self.scr["vfirst"]
        with contextlib.ExitStack() as es:
            xTp = self.sb(es, "xTp", [128, 8, S + 2], BF16)
            fw.dve(lambda h: h.memset(xTp[:, :, 0:2], 0.0), w=[(xTp, "z")])
            for c in range(8):
                fw.dma("sp", xTp[:, c, 2:S + 2], xT_d[c * 128:(c + 1) * 128, :], reads=[xT_d], writes=[(xTp, c)])
            mixb = self.sb(es, "mixb", [128, 2, 1792])
            fw.dma("sp", mixb[:, 0, :], self.inp["rw_mix"][l].to_broadcast([128, 1792]), writes=[mixb])
            fw.dve(lambda h: h.tensor_scalar(out=mixb[:, 1, :], in0=mixb[:, 0, :], scalar1=-1.0, scalar2=1.0,
                                             op0=ALU.mult, op1=ALU.add), r=[mixb], w=[mixb])
            vec = self.sb(es, "rwvec", [128, 4, 8])
            fw.dma("sp", vec[:], self.inp["rw_vec"][l], writes=[vec])
            dv = self.sb(es, "rwdv", [128, 4, 2])
            fw.dve(lambda h: h.tensor_scalar(out=dv[:, :, 0:1], in0=vec[:, :, 0:1], scalar1=-1.0, scalar2=None,
                                             op0=ALU.mult), r=[vec], w=[dv])
            fw.dve(lambda h: h.tensor_scalar(out=dv[:, :, 1:2], in0=vec[:, :, 3:4], scalar1=-1.0, scalar2=1.0,
                                             op0=ALU.mult, op1=ALU.add), r=[vec], w=[dv])
            w2a2 = self.sb(es, "w2a2", [128, 512])
            fw.dma("sp", w2a2[:], self.inp["rw_w2a2"][l], writes=[w2a2])
            g2 = self.sb(es, "g2", [128, 512])
            fw.dma("sp", g2[:], self.inp["rw_g2"][l], writes=[g2])
            obd = self.sb(es, "obd", [128, 128])
            fw.dma("sp", obd[:], self.inp["onesbd"][:, :], writes=[obd])
            rmask = self.sb(es, "rmask", [128, S])
            fw.pool(lambda h: h.memset(rmask[:], 1.0), w=[rmask])
            fw.pool(lambda h: h.memset(rmask[:, :].rearrange("p (c t) -> p c t", t=128)[:, :, 0:1], 0.0), w=[rmask])
            wf = [self.sb(es, f"wf{i}", [128, 8, 128]) for i in range(2)]
            wm = [self.sb(es, f"wm{i}", [128, 2, 8, 128], BF16) for i in range(2)]
            self._rw_cnt = 0

            def proj_shift(col0, dst_fn):
                i = self._rw_cnt % 2
                self._rw_cnt += 1
                src = self.inp["w_in"][l, :, C_RW + col0:C_RW + col0 + 128].rearrange("(kc p) c -> p kc c", p=128)
                fw.dma("sp", wf[i][:], src, writes=[wf[i]])
                for j in range(2):
                    fw.dve(lambda h: h.tensor_tensor(
                        out=wm[i][:, j, :, :], in0=wf[i][:],
                        in1=mixb[:, j, col0:col0 + 128][:, None, :].to_broadcast([128, 8, 128]), op=ALU.mult),
                        r=[wf[i], mixb], w=[(wm[i], j)])
                for n in range(4):
                    ps = self.next_ps()
                    for j in range(2):
                        off = 1 + j + n * 512
                        for kc in range(8):
                            fw.pe(lambda h: h.matmul(ps[:, :], lhsT=wm[i][:, j, kc, :], rhs=xTp[:, kc, off:off + 512],
                                                     start=(j == 0 and kc == 0), stop=(j == 1 and kc == 7)),
                                  r=[(wm[i], j), xTp], w=[ps])
                    dst_fn(n, ps)

            nb = 13
            B = [self.sb(es, f"rwb{i}", [128, S]) for i in range(nb)]
            wct = self.sb(es, "wct", [128, NT])
            xwa, sgx = B[11], B[12]
            sl = lambda n: slice(n * 512, (n + 1) * 512)
            def d_xwa(n, ps):
                fw.act(lambda h: h.activation(out=xwa[0:64, sl(n)], in_=ps[0:64, :], func=AF.Tanh), r=[ps], w=[(xwa, n)])
                fw.act(lambda h: h.copy(out=xwa[64:128, sl(n)], in_=ps[64:128, :]), r=[ps], w=[(xwa, n)])
            proj_shift(1536, d_xwa)
            proj_shift(1664, lambda n, ps: fw.act(lambda h: h.activation(out=sgx[:, sl(n)], in_=ps[:, :], func=AF.Sigmoid),
                                                  r=[ps], w=[(sgx, n)]))
            lat = None
            if l > 0:
                lat = self.sb(es, "lat", [32, S])
                v1 = self.sb(es, "v1", [128, 4, 32])
                fw.dma("sp", v1[:], self.inp["rw_v1"][l - 1].rearrange("(c p) j -> p c j", p=128), writes=[v1])
                v2 = self.sb(es, "v2", [32, 512])
                fw.dma("sp", v2[:], self.inp["rw_v2"][l - 1], writes=[v2])
                accb = [self.ps[4 + n] for n in range(4)]
                self.ps_reserved = set(accb)
                for c in range(4):
                    def d_v(n, ps, c=c):
                        fw.act(lambda h: h.copy(out=B[0][:, sl(n)], in_=ps[:, :]), r=[ps], w=[(B[0], n)])
                        fw.pe(lambda h: h.matmul(accb[n][0:32, :], lhsT=v1[:, c, :], rhs=B[0][:, sl(n)],
                                                 start=(c == 0), stop=(c == 3)), r=[v1, (B[0], n)], w=[accb[n]])
                    proj_shift(1024 + c * 128, d_v)
                for n in range(4):
                    fw.act(lambda h: h.copy(out=lat[:, sl(n)], in_=accb[n][0:32, :]), r=[accb[n]], w=[(lat, n)])
                self.ps_reserved = set()
            for c in range(4):
                rT, kT, vT, e2, cum, ep, em, epv, a_, kk, km = B[:11]
                v = lambda k: vec[:, c, k:k + 1]
                cs_ = slice(c * 128, (c + 1) * 128)
                for col0, dst in ((c * 128, rT), (512 + c * 128, kT), (1024 + c * 128, vT)):
                    proj_shift(col0, lambda n, ps, dst=dst: fw.act(
                        lambda h: h.copy(out=dst[:, sl(n)], in_=ps[:, :]), r=[ps], w=[(dst, n)]))
                for n in range(4):
                    ps = self.next_ps()
                    fw.pe(lambda h: h.matmul(ps[:, :], lhsT=w2a2[0:64, cs_], rhs=xwa[0:64, sl(n)], start=True, stop=True),
                          r=[w2a2, (xwa, n)], w=[ps])
                    fw.act(lambda h: h.activation(out=e2[:, sl(n)], in_=ps[:, :], func=AF.Exp, bias=dv[:, c, 0:1], scale=-1.0),
                           r=[ps, dv], w=[(e2, n)])
                    fw.act(lambda h: h.activation(out=e2[:, sl(n)], in_=e2[:, sl(n)], func=AF.Ln, bias=1.0, scale=1.0),
                           r=[(e2, n)], w=[(e2, n)])
                    fw.act(lambda h: h.activation(out=e2[:, sl(n)], in_=e2[:, sl(n)], func=AF.Exp, bias=-0.5, scale=-1.0),
                           r=[(e2, n)], w=[(e2, n)])
                    ps = self.next_ps()
                    fw.pe(lambda h: h.matmul(ps[:, :], lhsT=w2a2[64:128, cs_], rhs=xwa[64:128, sl(n)], start=True, stop=True),
                          r=[w2a2, (xwa, n)], w=[ps])
                    fw.act(lambda h: h.activation(out=a_[:, sl(n)], in_=ps[:, :], func=AF.Sigmoid, bias=v(1), scale=1.0),
                           r=[ps, vec], w=[(a_, n)])
                    ps = self.next_ps()
                    fw.pe(lambda h: h.matmul(ps[:, :], lhsT=g2[:, cs_], rhs=sgx[:, sl(n)], start=True, stop=True),
                          r=[g2, (sgx, n)], w=[ps])
                    fw.act(lambda h: h.copy(out=ep[:, sl(n)], in_=ps[:, :]), r=[ps], w=[(ep, n)])
                    if l > 0:
                        ps = self.next_ps()
                        fw.pe(lambda h: h.matmul(ps[:, :], lhsT=v2[0:32, cs_], rhs=lat[:, sl(n)], start=True, stop=True),
                              r=[v2, (lat, n)], w=[ps])
                        fw.act(lambda h: h.activation(out=em[:, sl(n)], in_=ps[:, :], func=AF.Sigmoid, bias=v(7), scale=1.0),
                               r=[ps, vec], w=[(em, n)])
                fw.dma("sp", rwT[6, cs_, :], ep[:], reads=[ep], writes=[(rwT, (6, c))])
                if l > 0:
                    fw.dma("sp", epv[:], vfirst[cs_, :], reads=[vfirst], writes=[epv])
                    fw.dve(lambda h: h.tensor_tensor(out=epv[:], in0=epv[:], in1=vT[:], op=ALU.subtract), r=[epv, vT], w=[epv])
                    fw.dve(lambda h: h.tensor_tensor(out=epv[:], in0=epv[:], in1=em[:], op=ALU.mult), r=[epv, em], w=[epv])
                    fw.dve(lambda h: h.tensor_tensor(out=vT[:], in0=vT[:], in1=epv[:], op=ALU.add), r=[epv, vT], w=[vT])
                else:
                    fw.dma("sp", vfirst[cs_, :], vT[:], reads=[vT], writes=[(vfirst, c)])
                fw.dma("sp", rwT[4, cs_, :], vT[:], reads=[vT], writes=[(rwT, (4, c))])
                fw.dve(lambda h: h.tensor_tensor_scan(out=cum[:], data0=rmask[:], data1=e2[:], initial=0.0,
                                                      op0=ALU.mult, op1=ALU.add), r=[rmask, e2], w=[cum])
                fw.act(lambda h: h.activation(out=ep[:], in_=cum[:], func=AF.Exp, scale=-1.0), r=[cum], w=[ep])
                fw.act(lambda h: h.activation(out=em[:], in_=cum[:], func=AF.Exp, scale=1.0), r=[cum], w=[em])
                fw.dve(lambda h: h.tensor_tensor(out=epv[:], in0=cum[:], in1=e2[:], op=ALU.subtract), r=[cum, e2], w=[epv])
                fw.act(lambda h: h.activation(out=epv[:], in_=epv[:], func=AF.Exp, scale=-1.0), r=[epv], w=[epv])
                fw.dve(lambda h: h.tensor_scalar(out=kk[:], in0=kT[:], scalar1=v(2), scalar2=None, op0=ALU.mult),
                       r=[kT, vec], w=[kk])
                fw.dve(lambda h: h.tensor_tensor(out=cum[:], in0=kk[:], in1=kk[:], op=ALU.mult), r=[kk], w=[cum])
                for n in range(4):
                    ps = self.next_ps()
                    fw.pe(lambda h: h.matmul(ps[:, :], lhsT=obd[:, :], rhs=cum[:, sl(n)], start=True, stop=True),
                          r=[obd, cum], w=[ps])
                    fw.act(lambda h: h.activation(out=e2[:, sl(n)], in_=ps[:, :], func=AF.Sqrt), r=[ps], w=[(e2, n)])
                fw.dve(lambda h: h.tensor_scalar(out=e2[:], in0=e2[:], scalar1=1e-12, scalar2=None, op0=ALU.max),
                       r=[e2], w=[e2])
                fw.dve(lambda h: h.reciprocal(out=e2[:], in_=e2[:]), r=[e2], w=[e2])
                fw.dve(lambda h: h.tensor_tensor(out=kk[:], in0=kk[:], in1=e2[:], op=ALU.mult), r=[kk, e2], w=[kk])
                fw.dve(lambda h: h.tensor_scalar(out=km[:], in0=a_[:], scalar1=v(3), scalar2=dv[:, c, 1:2],
                                                 op0=ALU.mult, op1=ALU.add), r=[a_, vec, dv], w=[km])
                fw.dve(lambda h: h.tensor_tensor(out=km[:], in0=km[:], in1=kT[:], op=ALU.mult), r=[km, kT], w=[km])
                fw.dve(lambda h: h.scalar_tensor_tensor(out=cum[:], in0=rT[:], scalar=v(4), in1=km[:],
                                                        op0=ALU.mult, op1=ALU.mult), r=[rT, km, vec], w=[cum])
                for n in range(4):
                    ps = self.next_ps()
                    fw.pe(lambda h: h.matmul(ps[:, :], lhsT=obd[:, :], rhs=cum[:, sl(n)], start=True, stop=True),
                          r=[obd, cum], w=[ps])
                    fw.dve(lambda h: h.tensor_tensor(out=e2[:, sl(n)], in0=ps[:, :], in1=vT[:, sl(n)], op=ALU.mult),
                           r=[ps, vT], w=[(e2, n)])
                fw.dma("sp", rwT[5, cs_, :], e2[:], reads=[e2], writes=[(rwT, (5, c))])
                fw.dve(lambda h: h.scalar_tensor_tensor(out=epv[:], in0=kk[:], scalar=-1.0, in1=epv[:],
                                                        op0=ALU.mult, op1=ALU.mult), r=[kk, epv], w=[epv])
                fw.dma("sp", rwT[0, cs_, :], epv[:], reads=[epv], writes=[(rwT, (0, c))])
                fw.dve(lambda h: h.tensor_tensor(out=rT[:], in0=rT[:], in1=ep[:], op=ALU.mult), r=[rT, ep], w=[rT])
                fw.dma("sp", rwT[1, cs_, :], rT[:], reads=[rT], writes=[(rwT, (1, c))])
                fw.dve(lambda h: h.tensor_tensor(out=kk[:], in0=kk[:], in1=a_[:], op=ALU.mult), r=[kk, a_], w=[kk])
                fw.dve(lambda h: h.tensor_tensor(out=kk[:], in0=kk[:], in1=em[:], op=ALU.mult), r=[kk, em], w=[kk])
                fw.dma("sp", rwT[2, cs_, :], kk[:], reads=[kk], writes=[(rwT, (2, c))])
                fw.dve(lambda h: h.tensor_tensor(out=km[:], in0=km[:], in1=em[:], op=ALU.mult), r=[km, em], w=[km])
                fw.dma("sp", rwT[3, cs_, :], km[:], reads=[km], writes=[(rwT, (3, c))])
                fw.dve(lambda h: h.tensor_copy(out=wct[:], in_=ep[:, :].rearrange("p (c t) -> p c t", t=128)[:, :, 127]),
                       r=[ep], w=[wct])
                fw.dma("sp", self.scr["rw_wc"][cs_, :], wct[:], reads=[wct], writes=[(self.scr["rw_wc"], c)])

    def phase_rwkv_scan(self, l):
        fw = self.fw
        rwT, yT = self.scr["rwT"], self.scr["yT"]
        ident = self.ident
        with contextlib.ExitStack() as es:
            masks = self.sb(es, "rwmasks", [128, 8, 128])
            fw.dma("sp", masks[:], self.inp["rw_masks"][:, :, :], writes=[masks])
            vec = self.sb(es, "rwvec2", [128, 4, 8])
            fw.dma("sp", vec[:], self.inp["rw_vec"][l], writes=[vec])
            wc = self.sb(es, "wc", [128, 4, NT])
            fw.dma("sp", wc[:], self.scr["rw_wc"][:, :].rearrange("(c p) n -> p c n", p=128),
                   reads=[self.scr["rw_wc"]], writes=[wc])
            ST = self.sb(es, "ST", [128, 4, 64])
            fw.dve(lambda h: h.memset(ST[:], 0.0), w=[ST])
            fm = [self.sb(es, f"fm{i}", [128, 7, 4, 128]) for i in range(2)]
            tok = [self.sb(es, f"tok{i}", [128, 3, 512]) for i in range(2)]
            M4s = [self.sb(es, f"M4a{i}", [128, 8, 4, 128]) for i in range(2)]
            Las = [self.sb(es, f"La{i}", [128, 2, 8, 128]) for i in range(2)]
            Xas = [self.sb(es, f"Xa{i}", [128, 7, 8, 128]) for i in range(2)]
            Ua = self.sb(es, "Ua", [128, 8, 64])
            tSa = self.sb(es, "tSa", [128, 4, 64])
            Ys = [self.sb(es, f"Y{i}", [128, 8, 64]) for i in range(2)]
            s8 = self.sb(es, "s8", [128, 8])
            r8 = self.sb(es, "r8", [128, 8])
            Yc = self.sb(es, "Yc", [128, 8, 64])
            sq = self.sb(es, "sq", [128, 8, 64])
            o32 = self.sb(es, "o32", [128, 4, 128])
            obs = [self.sb(es, f"ob{i}", [128, 4, 128], BF16) for i in range(2)]

            def hrow(hd):
                return slice((hd % 2) * 64, (hd % 2) * 64 + 64)

            def pv4(ap3, par):
                return ap3.rearrange("p (c a) t -> p c a t", a=2)[:, :, par, :]

            def prep_stages(c):
                cs = slice(c * 128, (c + 1) * 128)
                F, TK, M4, La, Xa = fm[c % 2], tok[c % 2], M4s[c % 2], Las[c % 2], Xas[c % 2]
                st = []

                def load():
                    for q in range(7):
                        fw.dma("sp", F[:, q, :, :], rwT[q, :, cs].rearrange("(ct p) t -> p ct t", p=128),
                               reads=[rwT], writes=[(F, q)])
                    for j, q in enumerate((4, 2, 3)):
                        ps = self.next_ps()
                        for ct in range(4):
                            fw.pe(lambda h: h.transpose(ps[:, ct * 128:(ct + 1) * 128], F[:, q, ct, :], ident[:]),
                                  r=[(F, q), ident], w=[ps])
                        fw.act(lambda h: h.copy(out=TK[:, j, :], in_=ps[:, :]), r=[ps], w=[(TK, j)])
                st.append(load)

                def products(par):
                    def f():
                        heads = [par + 2 * i for i in range(4)]
                        for hd in heads:
                            ct, hr = hd // 2, hrow(hd)
                            At, Rt, Bt, Kt = (F[hr, q, ct, :] for q in range(4))
                            rA, rR, rB, rK = ((F, q) for q in range(4))
                            ps4 = self.next_ps()
                            for i, (lt, rh, rl, rr) in enumerate(((Bt, At, rB, rA), (Kt, At, rK, rA),
                                                                  (Bt, Rt, rB, rR), (Kt, Rt, rK, rR))):
                                fw.pe(lambda h: h.matmul(ps4[:, i * 128:(i + 1) * 128], lhsT=lt, rhs=rh,
                                                         start=True, stop=True), r=[rl, rr], w=[ps4])
                            fw.dve(lambda h: h.tensor_tensor(out=M4[:, hd, :, :],
                                                             in0=ps4[:, :].rearrange("p (a t) -> p a t", a=4),
                                                             in1=masks[:, 0:4, :], op=ALU.mult),
                                   r=[ps4, masks], w=[(M4, hd)])
                        psL = self.next_ps()
                        for hh, hd in enumerate(heads):
                            ct, hr = hd // 2, hrow(hd)
                            fw.pe(lambda h: h.matmul(psL[:, hh * 128:(hh + 1) * 128], lhsT=F[hr, 0, ct, :],
                                                     rhs=F[hr, 2, ct, :], start=True, stop=True),
                                  r=[(F, 0), (F, 2)], w=[psL])
                        fw.dve(lambda h: h.tensor_tensor(out=pv4(La[:, 0, :, :], par),
                                                         in0=psL[:, :].rearrange("p (a t) -> p a t", a=4),
                                                         in1=masks[:, 4:8, :], op=ALU.mult),
                               r=[psL, masks], w=[(La, (0, par))])
                    return f
                st.append(products(0))
                st.append(products(1))

                def squaring(j):
                    def f():
                        for par in range(2):
                            heads = [par + 2 * i for i in range(4)]
                            psX = self.next_ps()
                            psL2 = self.next_ps() if j < 5 else None
                            for hh, hd in enumerate(heads):
                                Xj = M4[:, hd, 0, :] if j == 0 else Xa[:, j, hd, :]
                                rX = (M4, hd) if j == 0 else (Xa, (j, par))
                                Lj = La[:, j % 2, hd, :]
                                rL = (La, (j % 2, par))
                                fw.pe(lambda h: h.matmul(psX[:, hh * 128:(hh + 1) * 128], lhsT=Lj, rhs=Xj,
                                                         start=True, stop=True), r=[rL, rX], w=[psX])
                                if j < 5:
                                    fw.pe(lambda h: h.matmul(psL2[:, hh * 128:(hh + 1) * 128], lhsT=Xj, rhs=Lj,
                                                             start=True, stop=True), r=[rL, rX], w=[psL2])
                            fw.act(lambda h: h.copy(out=pv4(Xa[:, j + 1, :, :], par),
                                                    in_=psX[:, :].rearrange("p (a t) -> p a t", a=4)),
                                   r=[psX], w=[(Xa, (j + 1, par))])
                            if j < 5:
                                fw.act(lambda h: h.copy(out=pv4(La[:, (j + 1) % 2, :, :], par),
                                                        in_=psL2[:, :].rearrange("p (a t) -> p a t", a=4)),
                                       r=[psL2], w=[(La, ((j + 1) % 2, par))])
                    return f
                for j in range(6):
                    st.append(squaring(j))
                return st

            def chain_stages(c):
                F, TK, M4, Xa, Y = fm[c % 2], tok[c % 2], M4s[c % 2], Xas[c % 2], Ys[c % 2]
                st = []
                Vt = lambda hd: TK[:, 0, hd * 64:(hd + 1) * 64]
                rXj = lambda j, hd: (M4, hd) if j == 0 else (Xa, (j, hd % 2))
                Xj = lambda j, hd: M4[:, hd, 0, :] if j == 0 else Xa[:, j, hd, :]

                def zstage():
                    for par in range(2):
                        psz = self.next_ps()
                        for hh in range(4):
                            hd = par + 2 * hh
                            ct, hr = hd // 2, hrow(hd)
                            o = psz[:, hh * 64:(hh + 1) * 64]
                            fw.pe(lambda h: h.matmul(o, lhsT=M4[:, hd, 1, :], rhs=Vt(hd), start=True, stop=False),
                                  r=[(M4, hd), (TK, 0)], w=[psz])
                            fw.pe(lambda h: h.matmul(o, lhsT=F[hr, 0, ct, :], rhs=ST[hr, ct, :], start=False, stop=True),
                                  r=[(F, 0), ST], w=[psz])
                        fw.act(lambda h: h.copy(out=pv4(Ua[:, :, :], par),
                                                in_=psz[:, 0:256].rearrange("p (a v) -> p a v", a=4)),
                               r=[psz], w=[(Ua, par)])
                st.append(zstage)

                def ustage(j):
                    def f():
                        psu = self.next_ps()
                        for hd in range(8):
                            fw.pe(lambda h: h.matmul(psu[:, hd * 64:(hd + 1) * 64], lhsT=Xj(j, hd), rhs=Ua[:, hd, :],
                                                     start=True, stop=True), r=[rXj(j, hd), Ua], w=[psu])
                        fw.dve(lambda h: h.tensor_tensor(out=Ua[:, :, :].rearrange("p h v -> p (h v)"), in0=psu[:, :],
                                                         in1=Ua[:, :, :].rearrange("p h v -> p (h v)"), op=ALU.add),
                               r=[psu, Ua], w=[Ua])
                    return f
                for j in range(7):
                    st.append(ustage(j))

                def ystage():
                    for par in range(2):
                        psy = self.next_ps()
                        for hh in range(4):
                            hd = par + 2 * hh
                            ct, hr = hd // 2, hrow(hd)
                            o = psy[:, hh * 64:(hh + 1) * 64]
                            fw.pe(lambda h: h.matmul(o, lhsT=F[hr, 1, ct, :], rhs=ST[hr, ct, :], start=True, stop=False),
                                  r=[(F, 1), ST], w=[psy])
                            fw.pe(lambda h: h.matmul(o, lhsT=M4[:, hd, 2, :], rhs=Ua[:, hd, :], start=False, stop=False),
                                  r=[(M4, hd), Ua], w=[psy])
                            fw.pe(lambda h: h.matmul(o, lhsT=M4[:, hd, 3, :], rhs=Vt(hd), start=False, stop=True),
                                  r=[(M4, hd), (TK, 0)], w=[psy])
                        fw.act(lambda h: h.copy(out=pv4(Y[:, :, :], par),
                                                in_=psy[:, 0:256].rearrange("p (a v) -> p a v", a=4)),
                               r=[psy], w=[(Y, par)])
                    pss = self.next_ps()
                    for hd in range(8):
                        ct = hd // 2
                        o = pss[:, hd * 64:(hd + 1) * 64]
                        fw.pe(lambda h: h.matmul(o, lhsT=TK[:, 1, ct * 128:(ct + 1) * 128], rhs=Ua[:, hd, :],
                                                 start=True, stop=False), r=[(TK, 1), Ua], w=[pss])
                        fw.pe(lambda h: h.matmul(o, lhsT=TK[:, 2, ct * 128:(ct + 1) * 128], rhs=Vt(hd),
                                                 start=False, stop=True), r=[(TK, 2), (TK, 0)], w=[pss])
                    pv = pss[:, :].rearrange("p (c a v) -> p c a v", c=4, a=2)
                    for par in range(2):
                        hr = slice(par * 64, par * 64 + 64)
                        fw.dve(lambda h: h.tensor_tensor(out=tSa[hr, :, :], in0=pv[hr, :, par, :], in1=ST[hr, :, :],
                                                         op=ALU.add), r=[pss, ST], w=[(tSa, par)])
                        fw.dve(lambda h: h.tensor_tensor(out=ST[hr, :, :], in0=tSa[hr, :, :],
                                                         in1=wc[hr, :, c:c + 1].to_broadcast([64, 4, 64]), op=ALU.mult),
                               r=[(tSa, par), wc], w=[ST])
                st.append(ystage)
                return st

            def epilogue(c):
                cs = slice(c * 128, (c + 1) * 128)
                F, Y, ob = fm[c % 2], Ys[c % 2], obs[c % 2]
                fw.dve(lambda h: h.tensor_reduce(out=s8[:], in_=Y[:], axis=AX.X, op=ALU.add), r=[Y], w=[s8])
                fw.dve(lambda h: h.tensor_scalar(out=s8[:], in0=s8[:], scalar1=1.0 / 64, scalar2=None, op0=ALU.mult),
                       r=[s8], w=[s8])
                fw.dve(lambda h: h.tensor_tensor(out=Yc[:], in0=Y[:], in1=s8[:, :, None].to_broadcast([128, 8, 64]),
                                                 op=ALU.subtract), r=[Y, s8], w=[Yc])
                fw.dve(lambda h: h.tensor_tensor(out=sq[:], in0=Yc[:], in1=Yc[:], op=ALU.mult), r=[Yc], w=[sq])
                fw.dve(lambda h: h.tensor_reduce(out=r8[:], in_=sq[:], axis=AX.X, op=ALU.add), r=[sq], w=[r8])
                fw.act(lambda h: h.activation(out=r8[:], in_=r8[:], func=AF.Sqrt, bias=64e-5, scale=1.0 / 64), r=[r8], w=[r8])
                fw.dve(lambda h: h.reciprocal(out=r8[:], in_=r8[:]), r=[r8], w=[r8])
                fw.dve(lambda h: h.tensor_tensor(out=Yc[:], in0=Yc[:], in1=r8[:, :, None].to_broadcast([128, 8, 64]),
                                                 op=ALU.mult), r=[Yc, r8], w=[Yc])
                pst = self.next_ps()
                ycf = Yc[:, :, :].rearrange("p h d -> p (h d)")
                for ct in range(4):
                    fw.pe(lambda h: h.transpose(pst[:, ct * 128:(ct + 1) * 128], ycf[:, ct * 128:(ct + 1) * 128], ident[:]),
                          r=[Yc, ident], w=[pst])
                for ct in range(4):
                    fw.dve(lambda h: h.tensor_scalar(out=o32[:, ct, :], in0=pst[:, ct * 128:(ct + 1) * 128],
                                                     scalar1=vec[:, ct, 5:6], scalar2=vec[:, ct, 6:7],
                                                     op0=ALU.mult, op1=ALU.add), r=[pst, vec], w=[(o32, ct)])
                fw.dve(lambda h: h.tensor_tensor(out=o32[:], in0=o32[:], in1=F[:, 5, :, :], op=ALU.add), r=[o32, (F, 5)], w=[o32])
                fw.dve(lambda h: h.tensor_tensor(out=ob[:], in0=o32[:], in1=F[:, 6, :, :], op=ALU.mult), r=[o32, (F, 6)], w=[ob])
                fw.dma("sp", yT[1, :, cs].rearrange("(ct p) t -> p ct t", p=128), ob[:], reads=[ob],
                       writes=[(yT, ("b", c))])

            import os as _os
            _CUT = int(_os.environ.get("RW_CUT", 10 ** 9))
            _cnt = [0]

            def call(f):
                _cnt[0] += 1
                if _cnt[0] <= _CUT:
                    f()
            for f in prep_stages(0):
                call(f)
            for c in range(NT):
                ch = chain_stages(c)
                pr = prep_stages(c + 1) if c + 1 < NT else []
                for i in range(max(len(ch), len(pr))):
                    if i < len(pr):
                        call(pr[i])
                    if i < len(ch):
                        call(ch[i])
                call(lambda: epilogue(c))

    def ln_tile(self, z, st, mv, gbc, bbc, gi):
        fw = self.fw
        for c in range(2):
            fw.dve(lambda h: h.bn_stats(out=st[:, c, :], in_=z[:, c * 512:(c + 1) * 512]), r=[z], w=[(st, c)])
        fw.dve(lambda h: h.bn_aggr(out=mv[:, 0:2], in_=st[:, :, :].rearrange("p a b -> p (a b)")), r=[st], w=[mv])
        fw.act(lambda h: h.activation(out=mv[:, 2:3], in_=mv[:, 1:2], func=AF.Sqrt, bias=1e-5, scale=1.0), r=[mv], w=[mv])
        fw.dve(lambda h: h.reciprocal(out=mv[:, 2:3], in_=mv[:, 2:3]), r=[mv], w=[mv])
        fw.dve(lambda h: h.tensor_scalar(out=z[:], in0=z[:], scalar1=mv[:, 0:1], scalar2=mv[:, 2:3],
                                         op0=ALU.subtract, op1=ALU.mult), r=[z, mv], w=[z])
        fw.dve(lambda h: h.tensor_tensor(out=z[:], in0=z[:], in1=gbc[:, gi, :], op=ALU.mult), r=[z, gbc], w=[z])
        fw.dve(lambda h: h.tensor_tensor(out=z[:], in0=z[:], in1=gbc[:, gi + 1, :], op=ALU.add), r=[z, gbc], w=[z])

    def store_x_and_xT(self, z, ti, x_d, xT_d, xo):
        fw = self.fw
        fw.dma("sp", x_d[ti * 128:(ti + 1) * 128, :], z[:], reads=[z], writes=[(x_d, ti)])
        for hh in range(2):
            ps = self.next_ps()
            for c in range(4):
                cc = hh * 4 + c
                fw.pe(lambda h: h.transpose(ps[:, c * 128:(c + 1) * 128], z[:, cc * 128:(cc + 1) * 128], self.ident[:]),
                      r=[z, self.ident], w=[ps])
            fw.act(lambda h: h.copy(out=xo[:, hh * 4:(hh + 1) * 4, :], in_=ps[:, :].rearrange("p (c t) -> p c t", c=4)),
                   r=[ps], w=[(xo, hh)])
        fw.dma("sp", xT_d[:, ti * 128:(ti + 1) * 128].rearrange("(c p) t -> p c t", p=128), xo[:], reads=[xo],
               writes=[(xT_d, ti)])

    def phase_merge(self, l, xT_d, xres_ap, xres_dep):
        fw = self.fw
        yT = self.scr["yT"]
        x1_d, x1T_d = self.scr[f"x1_{l}"], self.scr[f"x1T_{l}"]
        with contextlib.ExitStack() as es:
            xT = self.load_xT(es, xT_d)
            wg = self.sb(es, "wg", [128, 8, 3072], BF16)
            for b in range(3):
                src = self.inp["w_in"][l, :, C_GATE + b * 1024:C_GATE + (b + 1) * 1024].rearrange("(kc p) c -> p kc c", p=128)
                fw.dma("pool", wg[:, :, b * 1024:(b + 1) * 1024], src, writes=[(wg, b)])
            wb = self.sb(es, "wb", [128, 3, 4, D], BF16)
            for b in range(3):
                fw.dma("pool", wb[:, b, :, :], self.inp["w_branch"][l, b].rearrange("(kc p) c -> p kc c", p=128),
                       writes=[(wb, b)])
            wo = self.sb(es, "wo", [128, 8, D], BF16)
            fw.dma("pool", wo[:], self.inp["w_out"][l].rearrange("(kc p) c -> p kc c", p=128), writes=[wo])
            gbc = self.sb(es, "gbc", [128, 4, D])
            fw.dma("sp", gbc[:], self.inp["ln_gb"][l:l + 1, :, :].to_broadcast([128, 4, D]), writes=[gbc])
            yb = self.sb(es, "yb", [128, 3, 4, 512], BF16)
            mg = self.sb(es, "mg", [128, 8, 512], BF16)
            sg = self.sb(es, "sg", [128, 512])
            acc = self.sb(es, "acc", [128, 512])
            z = [self.sb(es, f"z{i}", [128, D]) for i in range(2)]
            xr = [self.sb(es, f"xr{i}", [128, D]) for i in range(2)]
            xo = [self.sb(es, f"xo{i}", [128, 8, 128], BF16) for i in range(2)]
            st = self.sb(es, "st", [128, 2, 6])
            mv = self.sb(es, "mv", [128, 4])
            for n in range(4):
                ts = slice(n * 512, (n + 1) * 512)
                for b in range(3):
                    fw.dma("sp", yb[:, b, :, :], yT[b, :, ts].rearrange("(c p) t -> p c t", p=128),
                           reads=[yT], writes=[(yb, b)])
                for dt in range(8):
                    for b in range(3):
                        ps = self.next_ps()
                        self.proj_fm(wg, xT, b * 1024 + dt * 128, 128, n, ps)
                        fw.act(lambda h: h.activation(out=sg[:], in_=ps[:, :], func=AF.Sigmoid), r=[ps], w=[sg])
                        ps2 = self.next_ps()
                        for kc in range(4):
                            fw.pe(lambda h: h.matmul(ps2[:, :], lhsT=wb[:, b, kc, dt * 128:(dt + 1) * 128],
                                                     rhs=yb[:, b, kc, :], start=(kc == 0), stop=(kc == 3)),
                                  r=[(wb, b), (yb, b)], w=[ps2])
                        if b == 0:
                            fw.dve(lambda h: h.tensor_tensor(out=acc[:], in0=ps2[:, :], in1=sg[:], op=ALU.mult),
                                   r=[ps2, sg], w=[acc])
                        else:
                            fw.dve(lambda h: h.tensor_tensor(out=sg[:], in0=ps2[:, :], in1=sg[:], op=ALU.mult),
                                   r=[ps2, sg], w=[sg])
                            dst = mg[:, dt, :] if b == 2 else acc[:]
                            fw.dve(lambda h: h.tensor_tensor(out=dst, in0=acc[:], in1=sg[:], op=ALU.add),
                                   r=[acc, sg], w=[(mg, dt)] if b == 2 else [acc])
                for tt in range(4):
                    ti = n * 4 + tt
                    zz, xx = z[ti % 2], xr[ti % 2]
                    fw.dma("sp", xx[:], xres_ap[ti * 128:(ti + 1) * 128, :],
                           reads=[(xres_dep, ti)] if xres_dep else [], writes=[xx])
                    for hf in range(2):
                        ps = self.next_ps()
                        for dt in range(8):
                            fw.pe(lambda h: h.matmul(ps[:, :], lhsT=mg[:, dt, tt * 128:(tt + 1) * 128],
                                                     rhs=wo[:, dt, hf * 512:(hf + 1) * 512], start=(dt == 0), stop=(dt == 7)),
                                  r=[(mg, dt), wo], w=[ps])
                        fw.dve(lambda h: h.scalar_tensor_tensor(out=zz[:, hf * 512:(hf + 1) * 512],
                                                                in0=xx[:, hf * 512:(hf + 1) * 512], scalar=ALPHA,
                                                                in1=ps[:, :], op0=ALU.mult, op1=ALU.add),
                               r=[xx, ps], w=[zz])
                    self.ln_tile(zz, st, mv, gbc, None, 0)
                    self.store_x_and_xT(zz, ti, x1_d, x1T_d, xo[ti % 2])

    def top16(self, vals, idxs, src, scratch, n):
        fw = self.fw
        fw.dve(lambda h: h.max(out=vals[:, 0:8], in_=src), r=[self._t16_src], w=[self._t16_v])
        fw.dve(lambda h: h.max_index(out=idxs[:, 0:8], in_max=vals[:, 0:8], in_values=src),
               r=[self._t16_src, self._t16_v], w=[self._t16_i])
        fw.dve(lambda h: h.match_replace(out=scratch, in_to_replace=vals[:, 0:8], in_values=src, imm_value=-1e30),
               r=[self._t16_src, self._t16_v], w=[self._t16_s])
        fw.dve(lambda h: h.max(out=vals[:, 8:16], in_=scratch), r=[self._t16_s], w=[self._t16_v])
        fw.dve(lambda h: h.max_index(out=idxs[:, 8:16], in_max=vals[:, 8:16], in_values=scratch),
               r=[self._t16_s, self._t16_v], w=[self._t16_i])

    def phase_peer(self, l, x2_d, x2T_d):
        fw = self.fw
        x1_d = self.scr[f"x1_{l}"]
        U_ap, V_ap = self.inp[f"peer_u{l}"], self.inp[f"peer_v{l}"]
        ident = self.ident
        with contextlib.ExitStack() as es:
            wq = self.sb(es, "wq", [128, 8, 2048])
            for kc in range(8):
                fw.dma("sp", wq[:, kc, :], self.inp["peer_wq"][l, kc * 128:(kc + 1) * 128, :], writes=[(wq, kc)])
            skT = self.sb(es, "skT", [128, 2, 128])
            fw.dma("sp", skT[:], self.inp["peer_skT"][l].rearrange("a d n -> d a n"), writes=[skT])
            iota = self.sb(es, "iota16", [128, 16])
            fw.dma("sp", iota[:], self.inp["iota16"][:, :], writes=[iota])
            gbc = self.sb(es, "gbc2", [128, 4, D])
            fw.dma("sp", gbc[:], self.inp["ln_gb"][l:l + 1, :, :].to_broadcast([128, 4, D]), writes=[gbc])
            xts = [self.sb(es, f"pxt{i}", [128, D]) for i in range(2)]
            xTf = self.sb(es, "xTf", [128, 8, 128])
            qT = self.sb(es, "qT", [128, 16, 128])
            sc = self.sb(es, "sc", [128, 16, 128])
            sc2 = self.sb(es, "sc2", [128, 16, 128])
            v16 = self.sb(es, "v16", [128, 16, 16])
            i16 = self.sb(es, "i16", [128, 16, 16], U32)
            i16f = self.sb(es, "i16f", [128, 8, 2, 16])
            cand = self.sb(es, "cand", [128, 8, 256])
            cand2 = self.sb(es, "cand2", [128, 8, 256])
            top = self.sb(es, "top", [128, 8, 16])
            pos = self.sb(es, "pos", [128, 8, 16], U32)
            prc = self.sb(es, "prc", [128, 2, 8, 16], U32)
            prcf = self.sb(es, "prcf", [128, 2, 8, 16])
            oh = self.sb(es, "oh", [128, 128, 16])
            sel = self.sb(es, "sel", [128, 2, 128])
            idxf = self.sb(es, "idxf", [128, 128])
            idx = self.sb(es, "idx", [128, 128], U32)
            gate = self.sb(es, "gate", [128, 8, 16])
            gs = self.sb(es, "gs", [128, 8])
            actv = self.sb(es, "actv", [128, 128])
            gtmp = self.sb(es, "gtmp", [128, 128])
            wgt = self.sb(es, "wgt", [128, 128])
            junk = self.sb(es, "junk", [128, D])
            rows = [self.sb(es, f"rows{i}", [128, D]) for i in range(6)]
            ys = [self.sb(es, f"py{i}", [128, D]) for i in range(2)]
            xo = [self.sb(es, f"pxo{i}", [128, 8, 128], BF16) for i in range(2)]
            st = self.sb(es, "pst", [128, 2, 6])
            mv = self.sb(es, "pmv", [128, 4])
            rr = 0
            for ti in range(NT):
                xt, y = xts[ti % 2], ys[ti % 2]
                fw.dma("sp", xt[:], x1_d[ti * 128:(ti + 1) * 128, :], reads=[(x1_d, ti)], writes=[xt])
                for hh in range(2):
                    ps = self.next_ps()
                    for c in range(4):
                        cc = hh * 4 + c
                        fw.pe(lambda h: h.transpose(ps[:, c * 128:(c + 1) * 128], xt[:, cc * 128:(cc + 1) * 128], ident[:]),
                              r=[xt, ident], w=[ps])
                    fw.act(lambda h: h.copy(out=xTf[:, hh * 4:(hh + 1) * 4, :], in_=ps[:, :].rearrange("p (c t) -> p c t", c=4)),
                           r=[ps], w=[(xTf, hh)])
                for g4 in range(4):
                    ps = self.next_ps()
                    for b4 in range(4):
                        blk = g4 * 4 + b4
                        for kc in range(8):
                            fw.pe(lambda h: h.matmul(ps[:, b4 * 128:(b4 + 1) * 128], lhsT=wq[:, kc, blk * 128:(blk + 1) * 128],
                                                     rhs=xTf[:, kc, :], start=(kc == 0), stop=(kc == 7)),
                                  r=[(wq, kc), xTf], w=[ps])
                    fw.act(lambda h: h.copy(out=qT[:, g4 * 4:(g4 + 1) * 4, :], in_=ps[:, :].rearrange("p (b t) -> p b t", b=4)),
                           r=[ps], w=[(qT, g4)])
                for g4 in range(4):
                    ps = self.next_ps()
                    for b4 in range(4):
                        blk = g4 * 4 + b4
                        fw.pe(lambda h: h.matmul(ps[:, b4 * 128:(b4 + 1) * 128], lhsT=qT[:, blk, :], rhs=skT[:, blk % 2, :],
                                                 start=True, stop=True), r=[(qT, g4), skT], w=[ps])
                    fw.act(lambda h: h.copy(out=sc[:, g4 * 4:(g4 + 1) * 4, :], in_=ps[:, :].rearrange("p (b n) -> p b n", b=4)),
                           r=[ps], w=[(sc, g4)])
                self._t16_src, self._t16_v, self._t16_i, self._t16_s = sc, v16, i16, sc2
                for blk in range(16):
                    self.top16(v16[:, blk, :], i16[:, blk, :], sc[:, blk, :], sc2[:, blk, :], 128)
                v4 = v16[:, :, :].rearrange("p (h a) k -> p h a k", a=2)
                fw.dve(lambda h: h.tensor_tensor(
                    out=cand[:, :, :].rearrange("p h (i j) -> p h i j", i=16),
                    in0=v4[:, :, 0, :][:, :, :, None].to_broadcast([128, 8, 16, 16]),
                    in1=v4[:, :, 1, :][:, :, None, :].to_broadcast([128, 8, 16, 16]), op=ALU.add), r=[v16], w=[cand])
                self._t16_src, self._t16_v, self._t16_i, self._t16_s = cand, top, pos, cand2
                for hd in range(8):
                    self.top16(top[:, hd, :], pos[:, hd, :], cand[:, hd, :], cand2[:, hd, :], 256)
                fw.dve(lambda h: h.tensor_scalar(out=prc[:, 0, :, :], in0=pos[:], scalar1=4, scalar2=None,
                                                 op0=ALU.logical_shift_right), r=[pos], w=[prc])
                fw.dve(lambda h: h.tensor_scalar(out=prc[:, 1, :, :], in0=pos[:], scalar1=15, scalar2=None,
                                                 op0=ALU.bitwise_and), r=[pos], w=[prc])
                fw.dve(lambda h: h.tensor_copy(out=prcf[:], in_=prc[:]), r=[prc], w=[prcf])
                fw.dve(lambda h: h.tensor_copy(out=i16f[:], in_=i16[:, :, :].rearrange("p (h a) k -> p h a k", a=2)),
                       r=[i16], w=[i16f])
                for a in range(2):
                    fw.dve(lambda h: h.tensor_tensor(
                        out=oh[:], in0=iota[:, None, :].to_broadcast([128, 128, 16]),
                        in1=prcf[:, a, :, :].rearrange("p h k -> p (h k)")[:, :, None].to_broadcast([128, 128, 16]),
                        op=ALU.is_equal), r=[iota, prcf], w=[oh])
                    fw.dve(lambda h: h.tensor_tensor(
                        out=oh[:, :, :].rearrange("p (h k) j -> p h k j", h=8),
                        in0=oh[:, :, :].rearrange("p (h k) j -> p h k j", h=8),
                        in1=i16f[:, :, a, :][:, :, None, :].to_broadcast([128, 8, 16, 16]), op=ALU.mult),
                        r=[oh, i16f], w=[oh])
                    fw.dve(lambda h: h.tensor_reduce(out=sel[:, a, :], in_=oh[:], axis=AX.X, op=ALU.add), r=[oh], w=[(sel, a)])
                fw.dve(lambda h: h.scalar_tensor_tensor(out=idxf[:], in0=sel[:, 0, :], scalar=128.0, in1=sel[:, 1, :],
                                                        op0=ALU.mult, op1=ALU.add), r=[sel], w=[idxf])
                fw.dve(lambda h: h.tensor_copy(out=idx[:], in_=idxf[:]), r=[idxf], w=[idx])
                fw.dve(lambda h: h.tensor_tensor(out=gate[:], in0=top[:], in1=top[:, :, 0:1].to_broadcast([128, 8, 16]),
                                                 op=ALU.subtract), r=[top], w=[gate])
                fw.act(lambda h: h.activation(out=gate[:], in_=gate[:], func=AF.Exp), r=[gate], w=[gate])
                fw.dve(lambda h: h.tensor_reduce(out=gs[:], in_=gate[:], axis=AX.X, op=ALU.add), r=[gate], w=[gs])
                fw.dve(lambda h: h.reciprocal(out=gs[:], in_=gs[:]), r=[gs], w=[gs])
                fw.dve(lambda h: h.tensor_tensor(out=gate[:], in0=gate[:], in1=gs[:, :, None].to_broadcast([128, 8, 16]),
                                                 op=ALU.mult), r=[gate, gs], w=[gate])
                for j in range(128):
                    rb = rows[rr % 6]
                    rr += 1
                    fw.dma("pool", rb[:], U_ap[:, :], reads=[idx], writes=[rb],
                           indirect=bass.IndirectOffsetOnAxis(ap=idx[:, j:j + 1], axis=0))
                    fw.dve(lambda h: h.scalar_tensor_tensor(out=junk[:], in0=rb[:], scalar=1.0, in1=xt[:], op0=ALU.mult,
                                                            op1=ALU.mult, accum_out=actv[:, j:j + 1]),
                           r=[rb, xt], w=[junk, (actv, j)])
                self.gelu_tanh(None, actv, wgt, gtmp)
                fw.dve(lambda h: h.tensor_tensor(out=wgt[:], in0=wgt[:], in1=gate[:, :, :].rearrange("p h k -> p (h k)"),
                                                 op=ALU.mult), r=[wgt, gate], w=[wgt])
                for j in range(128):
                    rb = rows[rr % 6]
                    rr += 1
                    fw.dma("pool", rb[:], V_ap[:, :], reads=[idx], writes=[rb],
                           indirect=bass.IndirectOffsetOnAxis(ap=idx[:, j:j + 1], axis=0))
                    if j == 0:
                        fw.dve(lambda h: h.tensor_scalar(out=y[:], in0=rb[:], scalar1=wgt[:, 0:1], scalar2=None, op0=ALU.mult),
                               r=[rb, wgt], w=[y])
                    else:
                        fw.dve(lambda h: h.scalar_tensor_tensor(out=y[:], in0=rb[:], scalar=wgt[:, j:j + 1], in1=y[:],
                                                                op0=ALU.mult, op1=ALU.add), r=[rb, wgt, y], w=[y])
                fw.dve(lambda h: h.scalar_tensor_tensor(out=y[:], in0=xt[:], scalar=ALPHA, in1=y[:], op0=ALU.mult, op1=ALU.add),
                       r=[xt, y], w=[y])
                self.ln_tile(y, st, mv, gbc, None, 2)
                self.store_x_and_xT(y, ti, x2_d, x2T_d, xo[ti % 2])

    def phase_mla(self, l, xT_d):
        fw, nc = self.fw, self.nc
        yT = self.scr["yT"]
        SC = 96.0 ** -0.5
        with contextlib.ExitStack() as es:
            qn = self.sb(es, "qn", [64, 8, S], BF16)
            qp = self.sb(es, "qp", [32, 8, S], BF16)
            kn = self.sb(es, "kn", [64, 8, S], BF16)
            kp = self.sb(es, "kp", [32, S], BF16)
            V = self.sb(es, "V", [128, NT, 8, 65], BF16)
            cmask = self.sb(es, "cmask", [128, 128], BF16)
            identb = self.sb(es, "identb", [128, 128], BF16)
            fw.dma("pool", cmask[:], self.inp["cmask"][:, :], writes=[cmask])
            fw.dve(lambda h: h.tensor_copy(out=identb[:], in_=self.ident[:]), r=[self.ident], w=[identb])
            fw.pool(lambda h: h.memset(V[:], 1.0), w=[V])
            with contextlib.ExitStack() as es2:
                xT = self.load_xT(es2, xT_d)
                wt = self.sb(es2, "mw", [128, 8, 448], BF16)
                self.load_w_cols(wt, l, C_MLA, 416, key=0)
                for hf in range(2):
                    src = self.inp["w_in"][l, :, C_MLA + 384 + (1 - hf) * 16:C_MLA + 384 + (2 - hf) * 16].rearrange(
                        "(kc p) c -> p kc c", p=128)
                    fw.dma("pool", wt[:, :, 416 + hf * 16:416 + (hf + 1) * 16], src, writes=[(wt, 1 + hf)])
                vec = self.sb(es2, "mvec", [128, 3])
                fw.dma("sp", vec[:], self.inp["mla_vec"][l], writes=[vec])
                cs = self.sb(es2, "cs", [32, 2, S])
                fw.dma("sp", cs[:], self.inp["rope_cs"].rearrange("a p t -> p a t"), writes=[cs])
                wuq = self.sb(es2, "wuq", [128, 2, 768], BF16)
                fw.dma("pool", wuq[:], self.inp["mla_w_uq"][l].rearrange("(kc p) c -> p kc c", p=128), writes=[wuq])
                wuqr = self.sb(es2, "wuqr", [128, 2, 256], BF16)
                fw.dma("pool", wuqr[:], self.inp["mla_w_uq_rot"][l].rearrange("(kc p) c -> p kc c", p=128), writes=[wuqr])
                wk = self.sb(es2, "wk", [128, 512], BF16)
                fw.dma("pool", wk[:], self.inp["mla_w_ukv_k"][l], writes=[wk])
                wv = self.sb(es2, "wv", [128, 512], BF16)
                fw.dma("pool", wv[:], self.inp["mla_w_ukv_v"][l], writes=[wv])
                ones = self.sb(es2, "ones", [128, 128])
                fw.pool(lambda h: h.memset(ones[:], 1.0), w=[ones])
                cl = self.sb(es2, "cl", [128, 3, 512])
                sq = self.sb(es2, "sq", [128, 3, 512])
                rs = self.sb(es2, "rs", [128, 2, 512])
                cn = self.sb(es2, "cn", [128, 3, 512], BF16)
                kr = self.sb(es2, "kr", [32, 2, 512])
                tmp32 = self.sb(es2, "tmp32", [32, 2, 512])
                for n in range(4):
                    ts = slice(n * 512, (n + 1) * 512)
                    for j in range(3):
                        ps = self.next_ps()
                        self.proj_fm(wt, xT, j * 128, 128, n, ps)
                        fw.act(lambda h: h.copy(out=cl[:, j, :], in_=ps[:, :]), r=[ps], w=[(cl, j)])
                        fw.dve(lambda h: h.tensor_tensor(out=sq[:, j, :], in0=cl[:, j, :], in1=cl[:, j, :], op=ALU.mult),
                               r=[(cl, j)], w=[(sq, j)])
                    for j in range(2):
                        ps = self.next_ps()
                        self.proj_fm(wt, xT, 384 + j * 32, 32, n, ps)
                        fw.act(lambda h: h.copy(out=kr[:, j, :], in_=ps[0:32, :]), r=[ps], w=[(kr, j)])
                    for g, (tiles, dim) in enumerate((((0, 1), 256.0), ((2,), 128.0))):
                        ps = self.next_ps()
                        for ii, j in enumerate(tiles):
                            fw.pe(lambda h: h.matmul(ps[:, :], lhsT=ones[:, :], rhs=sq[:, j, :],
                                                     start=(ii == 0), stop=(ii == len(tiles) - 1)),
                                  r=[ones, (sq, j)], w=[ps])
                        fw.act(lambda h: h.activation(out=rs[:, g, :], in_=ps[:, :], func=AF.Sqrt, bias=1e-6,
                                                      scale=1.0 / dim), r=[ps], w=[(rs, g)])
                        fw.dve(lambda h: h.reciprocal(out=rs[:, g, :], in_=rs[:, g, :]), r=[(rs, g)], w=[(rs, g)])
                        for j in tiles:
                            fw.dve(lambda h: h.scalar_tensor_tensor(out=cn[:, j, :], in0=cl[:, j, :],
                                                                    scalar=vec[:, j:j + 1], in1=rs[:, g, :],
                                                                    op0=ALU.mult, op1=ALU.mult),
                                   r=[(cl, j), vec, (rs, g)], w=[(cn, j)])
                    fw.dve(lambda h: h.tensor_tensor(out=tmp32[:, 0, :], in0=kr[:, 0, :], in1=cs[:, 0, ts], op=ALU.mult),
                           r=[(kr, 0), cs], w=[(tmp32, 0)])
                    fw.dve(lambda h: h.tensor_tensor(out=tmp32[:, 1, :], in0=kr[:, 1, :], in1=cs[:, 1, ts], op=ALU.mult),
                           r=[(kr, 1), cs], w=[(tmp32, 1)])
                    fw.dve(lambda h: h.tensor_tensor(out=kp[:, ts], in0=tmp32[:, 0, :], in1=tmp32[:, 1, :], op=ALU.add),
                           r=[tmp32], w=[(kp, n)])
                    for hd in range(8):
                        ps = self.next_ps()
                        for kc in range(2):
                            fw.pe(lambda h: h.matmul(ps[0:64, :], lhsT=wuq[:, kc, hd * 96:hd * 96 + 64],
                                                     rhs=cn[:, kc, :], start=(kc == 0), stop=(kc == 1)),
                                  r=[wuq, (cn, kc)], w=[ps])
                        fw.act(lambda h: h.activation(out=qn[:, hd, ts], in_=ps[0:64, :], func=AF.Copy, scale=SC),
                               r=[ps], w=[(qn, (hd, n))])
                        ps = self.next_ps()
                        for kc in range(2):
                            fw.pe(lambda h: h.matmul(ps[0:32, :], lhsT=wuq[:, kc, hd * 96 + 64:hd * 96 + 96],
                                                     rhs=cn[:, kc, :], start=(kc == 0), stop=(kc == 1)),
                                  r=[wuq, (cn, kc)], w=[ps])
                        ps2 = self.next_ps()
                        for kc in range(2):
                            fw.pe(lambda h: h.matmul(ps2[0:32, :], lhsT=wuqr[:, kc, hd * 32:hd * 32 + 32],
                                                     rhs=cn[:, kc, :], start=(kc == 0), stop=(kc == 1)),
                                  r=[wuqr, (cn, kc)], w=[ps2])
                        fw.dve(lambda h: h.tensor_tensor(out=tmp32[:, 0, :], in0=ps[0:32, :], in1=cs[:, 0, ts], op=ALU.mult),
                               r=[ps, cs], w=[(tmp32, 0)])
                        fw.dve(lambda h: h.tensor_tensor(out=tmp32[:, 1, :], in0=ps2[0:32, :], in1=cs[:, 1, ts], op=ALU.mult),
                               r=[ps2, cs], w=[(tmp32, 1)])
                        fw.dve(lambda h: h.scalar_tensor_tensor(out=qp[:, hd, ts], in0=tmp32[:, 0, :], scalar=1.0,
                                                                in1=tmp32[:, 1, :], op0=ALU.mult, op1=ALU.add),
                               r=[tmp32], w=[(qp, (hd, n))])
                        fw.dve(lambda h: h.tensor_scalar(out=qp[:, hd, ts], in0=qp[:, hd, ts], scalar1=SC, scalar2=None,
                                                         op0=ALU.mult), r=[(qp, (hd, n))], w=[(qp, (hd, n))])
                        ps = self.next_ps()
                        fw.pe(lambda h: h.matmul(ps[0:64, :], lhsT=wk[:, hd * 64:(hd + 1) * 64], rhs=cn[:, 2, :],
                                                 start=True, stop=True), r=[wk, (cn, 2)], w=[ps])
                        fw.act(lambda h: h.copy(out=kn[:, hd, ts], in_=ps[0:64, :]), r=[ps], w=[(kn, (hd, n))])
                    for tt in range(4):
                        ti = n * 4 + tt
                        ps = self.next_ps()
                        fw.pe(lambda h: h.matmul(ps[:, :], lhsT=cn[:, 2, tt * 128:(tt + 1) * 128], rhs=wv[:, :],
                                                 start=True, stop=True), r=[(cn, 2), wv], w=[ps])
                        fw.act(lambda h: h.copy(out=V[:, ti, :, 0:64], in_=ps[:, :].rearrange("p (h d) -> p h d", h=8)),
                               r=[ps], w=[(V, ti)])
            fw.barrier()
            with contextlib.ExitStack() as es3:
                PT = self.sb(es3, "PT", [128, NT, 8, 128], BF16)
                rec = self.sb(es3, "rec", [128, 8, 1])
                yc = self.sb(es3, "yc", [128, 8, 64], BF16)
                yct = [self.sb(es3, f"yct{i}", [128, 4, 128], BF16) for i in range(2)]
                for qi in range(NT):
                    qs = slice(qi * 128, (qi + 1) * 128)
                    for kt in range(qi + 1):
                        ks = slice(kt * 128, (kt + 1) * 128)
                        for hg in range(2):
                            ps = self.next_ps()
                            for hh in range(4):
                                hd = hg * 4 + hh
                                o = ps[:, hh * 128:(hh + 1) * 128]
                                fw.pe(lambda h: h.matmul(o, lhsT=kn[:, hd, ks], rhs=qn[:, hd, qs], start=True, stop=False),
                                      r=[kn, qn], w=[ps])
                                fw.pe(lambda h: h.matmul(o, lhsT=kp[:, ks], rhs=qp[:, hd, qs], start=False, stop=True),
                                      r=[kp, qp], w=[ps])
                            fw.act(lambda h: h.activation(
                                out=PT[:, kt, hg * 4:(hg + 1) * 4, :],
                                in_=ps[:, :].rearrange("p (h q) -> p h q", h=4), func=AF.Exp),
                                r=[ps], w=[(PT, (kt, hg))])
                            if kt == qi:
                                fw.dve(lambda h: h.tensor_tensor(
                                    out=PT[:, kt, hg * 4:(hg + 1) * 4, :], in0=PT[:, kt, hg * 4:(hg + 1) * 4, :],
                                    in1=cmask[:, None, :].to_broadcast([128, 4, 128]), op=ALU.mult),
                                    r=[(PT, (kt, hg)), cmask], w=[(PT, (kt, hg))])
                    pss = [self.next_ps(), self.next_ps()]
                    for hd in range(8):
                        ps = pss[hd // 4]
                        o = ps[:, (hd % 4) * 65:(hd % 4) * 65 + 65]
                        for kt in range(qi + 1):
                            fw.pe(lambda h: h.matmul(o, lhsT=PT[:, kt, hd, :], rhs=V[:, kt, hd, :],
                                                     start=(kt == 0), stop=(kt == qi)),
                                  r=[(PT, (kt, hd // 4)), (V, kt)], w=[ps])
                    for hg in range(2):
                        ps = pss[hg]
                        pv = ps[:, 0:260].rearrange("p (h d) -> p h d", h=4)
                        fw.dve(lambda h: h.reciprocal(out=rec[:, hg * 4:(hg + 1) * 4, :], in_=pv[:, :, 64:65]),
                               r=[ps], w=[(rec, hg)])
                        fw.dve(lambda h: h.tensor_tensor(out=yc[:, hg * 4:(hg + 1) * 4, :], in0=pv[:, :, 0:64],
                                                         in1=rec[:, hg * 4:(hg + 1) * 4, :].to_broadcast([128, 4, 64]),
                                                         op=ALU.mult), r=[ps, (rec, hg)], w=[(yc, hg)])
                    o = yct[qi % 2]
                    ycf = yc[:, :, :].rearrange("p h d -> p (h d)")
                    for c in range(4):
                        pst = self.next_ps()
                        pb = pst[:, :].bitcast(BF16)
                        fw.pe(lambda h: h.transpose(pb[:, 0:128], ycf[:, c * 128:(c + 1) * 128], identb[:]),
                              r=[yc, identb], w=[pst])
                        fw.act(lambda h: h.copy(out=o[:, c, :], in_=pb[:, 0:128]), r=[pst], w=[(o, c)])
                    fw.dma("sp", yT[2, :, qs].rearrange("(c p) t -> p c t", p=128), o[:], reads=[o],
                           writes=[(yT, ("c", qi))])


def prep_inputs(inputs, nseq=1):
    f = lambda a: np.ascontiguousarray(np.asarray(a, dtype=np.float32))
    shared = {}
    shared["w_in"] = f(inputs["w_in"])
    rv = np.stack([inputs["rg_conv_w"][:, 0], inputs["rg_conv_w"][:, 1], inputs["rg_conv_w"][:, 2],
                   inputs["rg_conv_w"][:, 3], inputs["rg_conv_b"], inputs["rg_ba"], inputs["rg_bx"],
                   inputs["rg_log_a"]], axis=-1)
    shared["rg_vec"] = f(rv.reshape(L, 4, 128, 8).transpose(0, 2, 1, 3))
    shared["rg_wa"] = f(inputs["rg_wa"])
    shared["rg_wx"] = f(inputs["rg_wx"])
    shared["ident"] = np.eye(128, dtype=np.float32)
    mv = np.stack([inputs["mla_q_norm"][:, :128], inputs["mla_q_norm"][:, 128:], inputs["mla_kv_norm"]], axis=-1)
    shared["mla_vec"] = f(mv)
    wuq = np.asarray(inputs["mla_w_uq"], np.float32)
    shared["mla_w_uq"] = f(wuq)
    w4 = wuq.reshape(L, 256, 8, 96)
    shared["mla_w_uq_rot"] = f(np.concatenate([w4[..., 80:96], w4[..., 64:80]], -1).reshape(L, 256, 256))
    wkv = np.asarray(inputs["mla_w_ukv"], np.float32).reshape(L, 128, 8, 128)
    shared["mla_w_ukv_k"] = f(wkv[..., :64].reshape(L, 128, 512))
    shared["mla_w_ukv_v"] = f(wkv[..., 64:].reshape(L, 128, 512))
    pos = np.arange(S, dtype=np.float32)
    inv = (10000.0 ** (-np.arange(16, dtype=np.float32) / 16)).astype(np.float32)
    ang = pos[None, :] * inv[:, None]
    cs_, sn_ = np.cos(ang).astype(np.float32), np.sin(ang).astype(np.float32)
    shared["rope_cs"] = f(np.stack([np.concatenate([cs_, cs_], 0), np.concatenate([-sn_, sn_], 0)], 0))
    shared["cmask"] = f(np.triu(np.ones((128, 128), np.float32)))
    shared["rw_mix"] = f(inputs["rw_mix"])[:, None, :]
    vz = np.zeros((L, 512), np.float32); vz[1:] = inputs["rw_v0"]
    rwv = np.stack([inputs["rw_w0"], inputs["rw_a0"], inputs["rw_k_k"], inputs["rw_k_a"],
                    np.asarray(inputs["rw_r_k"]).reshape(L, 512), inputs["rw_gn_g"], inputs["rw_gn_b"], vz], axis=-1)
    shared["rw_vec"] = f(rwv.reshape(L, 4, 128, 8).transpose(0, 2, 1, 3))
    shared["rw_w2a2"] = f(np.concatenate([inputs["rw_w2"], inputs["rw_a2"]], axis=1))
    shared["rw_g2"] = f(inputs["rw_g2"])
    shared["rw_v1"] = f(inputs["rw_v1"])
    shared["rw_v2"] = f(inputs["rw_v2"])
    su = np.triu(np.ones((128, 128), np.float32), 1); ui = np.triu(np.ones((128, 128), np.float32))
    shared["rw_masks"] = f(np.stack([su, su, ui, ui, su.T, su.T, su.T, su.T], axis=1))
    obd = np.zeros((128, 128), np.float32); obd[:64, :64] = 1; obd[64:, 64:] = 1
    shared["onesbd"] = obd
    shared["w_branch"] = f(inputs["w_branch"])
    shared["w_out"] = f(inputs["w_out"])
    shared["peer_wq"] = f(inputs["peer_w_query"])
    shared["peer_skT"] = f(np.asarray(inputs["peer_subkeys"]).transpose(0, 1, 3, 2))
    shared["iota16"] = f(np.tile(np.arange(16, dtype=np.float32)[None, :], (128, 1)))
    for l in range(L):
        shared[f"peer_u{l}"] = f(inputs["peer_u"][l])
        shared[f"peer_v{l}"] = f(inputs["peer_v"][l])
    shared["ln_gb"] = f(np.stack([inputs["ln1_g"], inputs["ln1_b"], inputs["ln2_g"], inputs["ln2_b"]], axis=1))
    maps = []
    for b in range(8 // nseq):
        m = dict(shared)
        m["x"] = f(inputs["x"][b * nseq:(b + 1) * nseq])
        maps.append(m)
    return maps


def run(inputs, debug=False, phases=None, cores=8, trace=False, nseq=1):
    bld = Builder(debug=debug, phases=phases, nseq=nseq)
    nc = bld.build()
    maps = prep_inputs(inputs, nseq)[:cores]
    maps = [{k: v for k, v in m.items() if k in bld.inp} for m in maps]
    res = run_bass_kernel_spmd(nc, maps, core_ids=list(range(cores)), trace=trace)
    return res


NCORES = 8


def kernel(**inputs):
    nseq = 8 // NCORES
    res = run(inputs, cores=NCORES, nseq=nseq)
    return np.concatenate([np.asarray(r["out"]) for r in res.results], axis=0).astype(np.float32)
```

```python
import contextlib
import numpy as np
import concourse.bass as bass
import concourse.mybir as mybir
from concourse.bass_utils import run_bass_kernel_spmd

F32 = mybir.dt.float32
BF16 = mybir.dt.bfloat16
U32 = mybir.dt.uint32
I32 = mybir.dt.int32
AF = mybir.ActivationFunctionType
ALU = mybir.AluOpType
AX = mybir.AxisListType

D = 1024
S = 2048
L = 2
NT = S // 128
IN_COLS = 6304
C_RG, C_RW, C_MLA, C_GATE = 0, 1024, 2816, 3232
ALPHA = (2.0 * L) ** 0.25


class Dep:
    def __init__(self, name):
        self.name = name
        self.st = {}

    def states(self, key):
        if key is None:
            if None not in self.st:
                self.st[None] = [None, {}]
            return list(self.st.values())
        out = []
        if None in self.st:
            out.append(self.st[None])
        if key not in self.st:
            self.st[key] = [None, {}]
        out.append(self.st[key])
        return out


class T:
    def __init__(self, t, name):
        self.t = t
        self.dep = Dep(name)

    def __getitem__(self, idx):
        return self.t[idx]


class Eng:
    def __init__(self, name, h, sem):
        self.name, self.h, self.sem = name, h, sem
        self.count = 0
        self.known = {}


class FW:
    NDS = {"sp": 16, "pool": 12, "act": 6}

    def __init__(self, nc, es):
        self.nc = nc
        self.eng = {}
        for name, h in (("pe", nc.tensor), ("dve", nc.vector), ("act", nc.scalar),
                        ("pool", nc.gpsimd), ("sp", nc.sync)):
            sem = es.enter_context(nc.semaphore("sem_" + name))
            self.eng[name] = Eng(name, h, sem)
        self.dsem = {}
        self.dcnt = {}
        self.drr = {}
        for q, n in self.NDS.items():
            self.dsem[q] = [es.enter_context(nc.semaphore(f"ds_{q}{i}")) for i in range(n)]
            self.dcnt[q] = [0] * n
            self.drr[q] = 0

    def _wait(self, e, sid, sem, val):
        if val > 0 and e.known.get(sid, 0) < val:
            e.h.wait_ge(sem, val)
            e.known[sid] = val

    def _collect(self, reads, writes, ename):
        need = {}

        def add(tok):
            if tok is None:
                return
            sid, sem, val = tok
            if ename == "pe" and sid == "pe":
                return
            if need.get(sid, (None, 0))[1] < val:
                need[sid] = (sem, val)

        rs, ws = [], []
        for r in reads:
            t, k = r if isinstance(r, tuple) else (r, None)
            sts = t.dep.states(k)
            rs.append((k, sts))
            for st in sts:
                add(st[0])
        for w in writes:
            t, k = w if isinstance(w, tuple) else (w, None)
            sts = t.dep.states(k)
            ws.append((k, sts))
            for st in sts:
                add(st[0])
                for tok in st[1].values():
                    add(tok)
        return need, rs, ws

    def _record(self, tok, rs, ws):
        for k, sts in rs:
            for st in (sts if k is None else sts[-1:]):
                st[1][tok[0]] = tok
        for k, sts in ws:
            for st in (sts if k is None else sts[-1:]):
                st[0] = tok
                st[1] = {}

    def op(self, ename, fn, reads=(), writes=()):
        e = self.eng[ename]
        need, rs, ws = self._collect(reads, writes, ename)
        for sid, (sem, val) in need.items():
            self._wait(e, sid, sem, val)
        ins = fn(e.h)
        e.count += 1
        ins.then_inc(e.sem, 1)
        self._record((ename, e.sem, e.count), rs, ws)
        return ins

    def dma(self, q, out, in_, reads=(), writes=(), indirect=None, **kw):
        e = self.eng[q]
        i = self.drr[q]
        self.drr[q] = (i + 1) % len(self.dsem[q])
        sem = self.dsem[q][i]
        sid = ("d", q, i)
        self._wait(e, sid, sem, 16 * self.dcnt[q][i])
        need, rs, ws = self._collect(reads, writes, q)
        for s2, (sm, val) in need.items():
            self._wait(e, s2, sm, val)
        if indirect is None:
            ins = e.h.dma_start(out=out, in_=in_, **kw)
        else:
            ins = e.h.indirect_dma_start(out=out, out_offset=None, in_=in_, in_offset=indirect, **kw)
        self.dcnt[q][i] += 1
        ins.then_inc(sem, 16)
        self._record((sid, sem, 16 * self.dcnt[q][i]), rs, ws)
        return ins

    def barrier(self):
        for e in self.eng.values():
            for o in self.eng.values():
                if o is not e:
                    self._wait(e, o.name, o.sem, o.count)
            for q in self.dsem:
                for i, sem in enumerate(self.dsem[q]):
                    self._wait(e, ("d", q, i), sem, 16 * self.dcnt[q][i])

    def dve(self, fn, r=(), w=()):
        return self.op("dve", fn, r, w)

    def act(self, fn, r=(), w=()):
        return self.op("act", fn, r, w)

    def pe(self, fn, r=(), w=()):
        return self.op("pe", fn, r, w)

    def pool(self, fn, r=(), w=()):
        return self.op("pool", fn, r, w)


class Builder:
    def __init__(self, debug=False, phases=None, nseq=1):
        self.debug = debug
        self.phases = phases
        self.nseq = nseq
        self.nc = bass.Bass("TRN2", target_bir_lowering=False)
        self.inp = {}
        self.scr = {}

    def din(self, name, shape, dt=F32):
        self.inp[name] = self.nc.dram_tensor(name, list(shape), dt, kind="ExternalInput").ap()

    def dscr(self, name, shape, dt, out=False):
        kind = "ExternalOutput" if (out or self.debug) else "Internal"
        ap = self.nc.dram_tensor(name, list(shape), dt, kind=kind).ap()
        self.scr[name] = T(ap, name)
        return self.scr[name]

    def sb(self, es, name, shape, dt=F32):
        self.uid = getattr(self, "uid", 0) + 1
        nm = f"s{self.uid}_{name}"
        return T(es.enter_context(self.nc.sbuf_tensor(nm, list(shape), dt)), nm)

    def declare(self):
        self.din("x", [self.nseq, S, D])
        self.din("w_in", [L, D, IN_COLS])
        self.din("rg_vec", [L, 128, 4, 8])
        self.din("rg_wa", [L, 8, 64, 64])
        self.din("rg_wx", [L, 8, 64, 64])
        self.din("ident", [128, 128])
        self.din("mla_vec", [L, 128, 3])
        self.din("mla_w_uq", [L, 256, 768])
        self.din("mla_w_uq_rot", [L, 256, 256])
        self.din("mla_w_ukv_k", [L, 128, 512])
        self.din("mla_w_ukv_v", [L, 128, 512])
        self.din("rope_cs", [2, 32, S])
        self.din("cmask", [128, 128])
        self.din("rw_mix", [L, 1, 1792])
        self.din("rw_vec", [L, 128, 4, 8])
        self.din("rw_w2a2", [L, 128, 512])
        self.din("rw_g2", [L, 128, 512])
        self.din("rw_v1", [1, 512, 32])
        self.din("rw_v2", [1, 32, 512])
        self.din("rw_masks", [128, 8, 128])
        self.din("onesbd", [128, 128])
        self.dscr("rwT", [7, 512, S], F32)
        self.dscr("vfirst", [512, S], F32)
        self.dscr("rw_wc", [512, NT], F32)
        self.din("w_branch", [L, 3, 512, D])
        self.din("w_out", [L, D, D])
        self.din("peer_wq", [L, D, 2048])
        self.din("peer_skT", [L, 2, 128, 128])
        self.din("iota16", [128, 16])
        for l in range(L):
            if self.phases is None or f"F{l}" in self.phases:
                self.din(f"peer_u{l}", [16384, D])
                self.din(f"peer_v{l}", [16384, D])
            self.dscr(f"x2T_{l}", [D, S], BF16)
        self.dscr("x2_0", [S, D], F32)
        self.din("ln_gb", [L, 4, D])
        for l in range(L):
            self.dscr(f"x1_{l}", [S, D], F32)
            self.dscr(f"x1T_{l}", [D, S], BF16)
        self.dscr("xT0", [D, S], BF16)
        self.dscr("yT", [3, 512, S], BF16)
        self.dscr("out", [self.nseq, S, D], F32, out=True)

    def build(self):
        nc = self.nc
        self.declare()
        with contextlib.ExitStack() as es:
            self.fw = fw = FW(nc, es)
            self.ps = [T(es.enter_context(nc.psum_tensor(f"ps{i}", [128, 512], F32)), f"ps{i}")
                       for i in range(8)]
            self.psi = 0
            self.ident = self.sb(es, "ident", [128, 128])
            fw.dma("sp", self.ident[:], self.inp["ident"][:, :], writes=[self.ident])
            ph = self.phases
            for sq in range(self.nseq):
                x_in = self.inp["x"][sq]
                out_t = T(self.scr["out"].t[sq], f"out{sq}")
                if ph is None or "A" in ph:
                    self.phase_xT(x_in, None, self.scr["xT0"])
                    fw.barrier()
                for l in range(L):
                    xTd = self.scr["xT0"] if l == 0 else self.scr[f"x2T_{l - 1}"]
                    if ph is None or f"B{l}" in ph:
                        self.phase_rg(l, xTd)
                        fw.barrier()
                    if ph is None or f"C{l}" in ph or f"Cp{l}" in ph:
                        self.phase_rwkv_prep(l, xTd)
                        fw.barrier()
                    if ph is None or f"C{l}" in ph or f"Cs{l}" in ph:
                        self.phase_rwkv_scan(l)
                        fw.barrier()
                    if ph is None or f"D{l}" in ph:
                        self.phase_mla(l, xTd)
                        fw.barrier()
                    if ph is None or f"E{l}" in ph:
                        if l == 0:
                            self.phase_merge(l, xTd, x_in, None)
                        else:
                            self.phase_merge(l, xTd, self.scr["x2_0"].t, self.scr["x2_0"])
                        fw.barrier()
                    if ph is None or f"F{l}" in ph:
                        x2d = out_t if l == L - 1 else self.scr["x2_0"]
                        self.phase_peer(l, x2d, self.scr[f"x2T_{l}"])
                        fw.barrier()
            fw.barrier()
        return nc

    def next_ps(self):
        while True:
            p = self.ps[self.psi]
            self.psi = (self.psi + 1) % 8
            if p not in getattr(self, "ps_reserved", ()):
                return p

    def phase_xT(self, x_ap, x_dep, xT_d):
        fw = self.fw
        with contextlib.ExitStack() as es:
            xt = [self.sb(es, f"xa{i}", [128, D]) for i in range(2)]
            xo = [self.sb(es, f"xo{i}", [128, 8, 128], BF16) for i in range(2)]
            for i in range(NT):
                b = xt[i % 2]
                fw.dma("sp", b[:], x_ap[i * 128:(i + 1) * 128, :],
                       reads=[x_dep] if x_dep else [], writes=[b])
                o = xo[i % 2]
                for hh in range(2):
                    ps = self.next_ps()
                    for c in range(4):
                        cc = hh * 4 + c
                        fw.pe(lambda h, ps=ps, c=c, cc=cc, b=b: h.transpose(
                            ps[:, c * 128:(c + 1) * 128], b[:, cc * 128:(cc + 1) * 128], self.ident[:]),
                            r=[b, self.ident], w=[ps])
                    fw.dve(lambda h, ps=ps, o=o, hh=hh: h.tensor_copy(
                        out=o[:, hh * 4:(hh + 1) * 4, :],
                        in_=ps[:, :].rearrange("p (c t) -> p c t", c=4)), r=[ps], w=[(o, hh)])
                fw.dma("sp", xT_d[:, i * 128:(i + 1) * 128].rearrange("(c p) t -> p c t", p=128),
                       o[:], reads=[o], writes=[(xT_d, i)])

    def load_xT(self, es, xT_d):
        xT = self.sb(es, "xT", [128, 8, S], BF16)
        for c in range(8):
            self.fw.dma("sp", xT[:, c, :], xT_d[c * 128:(c + 1) * 128, :], reads=[xT_d], writes=[(xT, c)])
        return xT

    def load_w_cols(self, wt, l, c0, ncols, key=None):
        src = self.inp["w_in"][l, :, c0:c0 + ncols].rearrange("(kc p) c -> p kc c", p=128)
        self.fw.dma("pool", wt[:, :, 0:ncols], src, writes=[(wt, key) if key is not None else wt])

    def proj_fm(self, wt, xT, j0, M, n, ps):
        for kc in range(8):
            self.fw.pe(lambda h, kc=kc: h.matmul(ps[0:M, :], lhsT=wt[:, kc, j0:j0 + M],
                                                  rhs=xT[:, kc, n * 512:(n + 1) * 512],
                                                  start=(kc == 0), stop=(kc == 7)),
                       r=[wt, xT], w=[ps])

    def gelu_tanh(self, es_tmp, x, out, tmp):
        fw = self.fw
        fw.dve(lambda h: h.tensor_tensor(out=tmp[:], in0=x[:], in1=x[:], op=ALU.mult), r=[x], w=[tmp])
        fw.dve(lambda h: h.tensor_scalar(out=tmp[:], in0=tmp[:], scalar1=0.044715, scalar2=1.0,
                                         op0=ALU.mult, op1=ALU.add), r=[tmp], w=[tmp])
        fw.dve(lambda h: h.tensor_tensor(out=tmp[:], in0=tmp[:], in1=x[:], op=ALU.mult), r=[tmp, x], w=[tmp])
        fw.act(lambda h: h.activation(out=tmp[:], in_=tmp[:], func=AF.Sigmoid, scale=1.5957691216057308),
               r=[tmp], w=[tmp])
        fw.dve(lambda h: h.tensor_tensor(out=out[:], in0=tmp[:], in1=x[:], op=ALU.mult), r=[tmp, x], w=[out])

    def phase_rg(self, l, xT_d):
        fw, nc = self.fw, self.nc
        yT = self.scr["yT"]
        with contextlib.ExitStack() as es:
            xT = self.load_xT(es, xT_d)
            vec = self.sb(es, "rgvec", [128, 4, 8])
            fw.dma("sp", vec[:], self.inp["rg_vec"][l], writes=[vec])
            cst = self.sb(es, "rgc", [128, 4, 4])
            wts = [self.sb(es, f"rgw{i}", [128, 8, 256], BF16) for i in range(2)]
            wbd = [self.sb(es, f"rgbd{i}", [128, 2, 128]) for i in range(2)]
            names = ["xb", "gb", "xc", "r", "i", "a", "m", "h"]
            tl = {n: self.sb(es, "rg_" + n, [128, S]) for n in names}
            yo = self.sb(es, "rg_yo", [128, S], BF16)
            xb, gb, xc, r_, i_, a_, m_, h_ = (tl[n] for n in names)
            for c in range(4):
                wt = wts[c % 2]
                bd = wbd[c % 2]
                self.load_w_cols(wt, l, C_RG + c * 128, 128, key=0)
                src = self.inp["w_in"][l, :, C_RG + 512 + c * 128:C_RG + 512 + (c + 1) * 128].rearrange(
                    "(kc p) c -> p kc c", p=128)
                fw.dma("pool", wt[:, :, 128:256], src, writes=[(wt, 1)])
                fw.pool(lambda h: h.memset(bd[:], 0.0), w=[bd])
                for j, nm in enumerate(("rg_wa", "rg_wx")):
                    for hb in range(2):
                        fw.dma("sp", bd[hb * 64:(hb + 1) * 64, j, hb * 64:(hb + 1) * 64],
                               self.inp[nm][l, 2 * c + hb], writes=[bd])
                for j, dst in enumerate((xb, gb)):
                    for n in range(4):
                        ps = self.next_ps()
                        self.proj_fm(wt, xT, j * 128, 128, n, ps)
                        fw.act(lambda h, ps=ps, dst=dst, n=n: h.copy(out=dst[:, n * 512:(n + 1) * 512], in_=ps[:, :]),
                               r=[ps], w=[(dst, n)])
                v = lambda k: vec[:, c, k:k + 1]
                fw.dve(lambda h: h.tensor_scalar(out=xc[:], in0=xb[:], scalar1=v(0), scalar2=v(4),
                                                 op0=ALU.mult, op1=ALU.add), r=[xb, vec], w=[xc])
                for j in range(1, 4):
                    fw.dve(lambda h, j=j: h.scalar_tensor_tensor(out=xc[:, j:], in0=xb[:, 0:S - j], scalar=v(j),
                                                                 in1=xc[:, j:], op0=ALU.mult, op1=ALU.add),
                           r=[xb, xc, vec], w=[xc])
                cc = lambda k: cst[:, c, k:k + 1]
                fw.act(lambda h: h.activation(out=cc(0), in_=v(7), func=AF.Exp, scale=-1.0), r=[vec], w=[cst])
                fw.act(lambda h: h.activation(out=cc(1), in_=cc(0), func=AF.Ln, bias=1.0, scale=1.0), r=[cst], w=[cst])
                fw.dve(lambda h: h.tensor_scalar(out=cc(2), in0=cc(1), scalar1=-8.0, scalar2=None, op0=ALU.mult),
                       r=[cst], w=[cst])
                fw.dve(lambda h: h.tensor_scalar(out=cc(3), in0=cc(1), scalar1=-16.0, scalar2=None, op0=ALU.mult),
                       r=[cst], w=[cst])
                for j, (dst, bk) in enumerate(((r_, 5), (i_, 6))):
                    for n in range(4):
                        ps = self.next_ps()
                        fw.pe(lambda h, ps=ps, j=j, n=n: h.matmul(ps[:, :], lhsT=bd[:, j, :],
                                                                   rhs=xc[:, n * 512:(n + 1) * 512],
                                                                   start=True, stop=True), r=[bd, xc], w=[ps])
                        fw.act(lambda h, ps=ps, dst=dst, n=n, bk=bk: h.activation(
                            out=dst[:, n * 512:(n + 1) * 512], in_=ps[:, :], func=AF.Sigmoid, bias=v(bk), scale=1.0),
                            r=[ps, vec], w=[(dst, n)])
                fw.act(lambda h: h.activation(out=a_[:], in_=r_[:], func=AF.Exp, scale=cc(2)), r=[r_, cst], w=[a_])
                fw.act(lambda h: h.activation(out=m_[:], in_=r_[:], func=AF.Exp, scale=cc(3)), r=[r_, cst], w=[m_])
                fw.act(lambda h: h.activation(out=m_[:], in_=m_[:], func=AF.Sqrt, bias=1.0, scale=-1.0), r=[m_], w=[m_])
                fw.dve(lambda h: h.memset(m_[:, 0:1], 1.0), w=[m_])
                fw.dve(lambda h: h.tensor_tensor(out=i_[:], in0=i_[:], in1=xc[:], op=ALU.mult), r=[i_, xc], w=[i_])
                fw.dve(lambda h: h.tensor_tensor(out=i_[:], in0=i_[:], in1=m_[:], op=ALU.mult), r=[i_, m_], w=[i_])
                fw.dve(lambda h: h.tensor_tensor_scan(out=h_[:], data0=a_[:], data1=i_[:], initial=0.0,
                                                      op0=ALU.mult, op1=ALU.add), r=[a_, i_], w=[h_])
                self.gelu_tanh(es, gb, r_, m_)
                fw.dve(lambda h: h.tensor_tensor(out=yo[:], in0=h_[:], in1=r_[:], op=ALU.mult), r=[h_, r_], w=[yo])
                fw.dma("sp", yT[0, c * 128:(c + 1) * 128, :], yo[:], reads=[yo], writes=[(yT, ("a", c))])


    def phase_rwkv_prep(self, l, xT_d):
        fw = self.fw
        rwT, vfirst = self.scr["rwT"], self.scr["vfirst"]
        with contextlib.ExitStack() as es:
            xTp = self.sb(es, "xTp", [128, 8, S + 2], BF16)
            fw.dve(lambda h: h.memset(xTp[:, :, 0:2], 0.0), w=[(xTp, "z")])
            for c in range(8):
                fw.dma("sp", xTp[:, c, 2:S + 2], xT_d[c * 128:(c + 1) * 128, :], reads=[xT_d], writes=[(xTp, c)])
            mixb = self.sb(es, "mixb", [128, 2, 1792])
            fw.dma("sp", mixb[:, 0, :], self.inp["rw_mix"][l].to_broadcast([128, 1792]), writes=[mixb])
            fw.dve(lambda h: h.tensor_scalar(out=mixb[:, 1, :], in0=mixb[:, 0, :], scalar1=-1.0, scalar2=1.0,
                                             op0=ALU.mult, op1=ALU.add), r=[mixb], w=[mixb])
            vec = self.sb(es, "rwvec", [128, 4, 8])
            fw.dma("sp", vec[:], self.inp["rw_vec"][l], writes=[vec])
            dv = self.sb(es, "rwdv", [128, 4, 2])
            fw.dve(lambda h: h.tensor_scalar(out=dv[:, :, 0:1], in0=vec[:, :, 0:1], scalar1=-1.0, scalar2=None,
                                             op0=ALU.mult), r=[vec], w=[dv])
            fw.dve(lambda h: h.tensor_scalar(out=dv[:, :, 1:2], in0=vec[:, :, 3:4], scalar1=-1.0, scalar2=1.0,
                                             op0=ALU.mult, op1=ALU.add), r=[vec], w=[dv])
            w2a2 = self.sb(es, "w2a2", [128, 512])
            fw.dma("sp", w2a2[:], self.inp["rw_w2a2"][l], writes=[w2a2])
            g2 = self.sb(es, "g2", [128, 512])
            fw.dma("sp", g2[:], self.inp["rw_g2"][l], writes=[g2])
            obd = self.sb(es, "obd", [128, 128])
            fw.dma("sp", obd[:], self.inp["onesbd"][:, :], writes=[obd])
            rmask = self.sb(es, "rmask", [128, S])
            fw.pool(lambda h: h.memset(rmask[:], 1.0), w=[rmask])
            fw.pool(lambda h: h.memset(rmask[:, :].rearrange("p (c t) -> p c t", t=128)[:, :, 0:1], 0.0), w=[rmask])
            wf = [self.sb(es, f"wf{i}", [128, 8, 128]) for i in range(2)]
            wm = [self.sb(es, f"wm{i}", [128, 2, 8, 128], BF16) for i in range(2)]
            self._rw_cnt = 0

            def proj_shift(col0, dst_fn):
                i = self._rw_cnt % 2
                self._rw_cnt += 1
                src = self.inp["w_in"][l, :, C_RW + col0:C_RW + col0 + 128].rearrange("(kc p) c -> p kc c", p=128)
                fw.dma("sp", wf[i][:], src, writes=[wf[i]])
                for j in range(2):
                    fw.dve(lambda h: h.tensor_tensor(
                        out=wm[i][:, j, :, :], in0=wf[i][:],
                        in1=mixb[:, j, col0:col0 + 128][:, None, :].to_broadcast([128, 8, 128]), op=ALU.mult),
                        r=[wf[i], mixb], w=[(wm[i], j)])
                for n in range(4):
                    ps = self.next_ps()
                    for j in range(2):
                        off = 1 + j + n * 512
                        for kc in range(8):
                            fw.pe(lambda h: h.matmul(ps[:, :], lhsT=wm[i][:, j, kc, :], rhs=xTp[:, kc, off:off + 512],
                                                     start=(j == 0 and kc == 0), stop=(j == 1 and kc == 7)),
                                  r=[(wm[i], j), xTp], w=[ps])
                    dst_fn(n, ps)

            nb = 13
            B = [self.sb(es, f"rwb{i}", [128, S]) for i in range(nb)]
            wct = self.sb(es, "wct", [128, NT])
            xwa, sgx = B[11], B[12]
            sl = lambda n: slice(n * 512, (n + 1) * 512)
            def d_xwa(n, ps):
                fw.act(lambda h: h.activation(out=xwa[0:64, sl(n)], in_=ps[0:64, :], func=AF.Tanh), r=[ps], w=[(xwa, n)])
                fw.act(lambda h: h.copy(out=xwa[64:128, sl(n)], in_=ps[64:128, :]), r=[ps], w=[(xwa, n)])
            proj_shift(1536, d_xwa)
            proj_shift(1664, lambda n, ps: fw.act(lambda h: h.activation(out=sgx[:, sl(n)], in_=ps[:, :], func=AF.Sigmoid),
                                                  r=[ps], w=[(sgx, n)]))
            lat = None
            if l > 0:
                lat = self.sb(es, "lat", [32, S])
                v1 = self.sb(es, "v1", [128, 4, 32])
                fw.dma("sp", v1[:], self.inp["rw_v1"][l - 1].rearrange("(c p) j -> p c j", p=128), writes=[v1])
                v2 = self.sb(es, "v2", [32, 512])
                fw.dma("sp", v2[:], self.inp["rw_v2"][l - 1], writes=[v2])
                accb = [self.ps[4 + n] for n in range(4)]
                self.ps_reserved = set(accb)
                for c in range(4):
                    def d_v(n, ps, c=c):
                        fw.act(lambda h: h.copy(out=B[0][:, sl(n)], in_=ps[:, :]), r=[ps], w=[(B[0], n)])
                        fw.pe(lambda h: h.matmul(accb[n][0:32, :], lhsT=v1[:, c, :], rhs=B[0][:, sl(n)],
                                                 start=(c == 0), stop=(c == 3)), r=[v1, (B[0], n)], w=[accb[n]])
                    proj_shift(1024 + c * 128, d_v)
                for n in range(4):
                    fw.act(lambda h: h.copy(out=lat[:, sl(n)], in_=accb[n][0:32, :]), r=[accb[n]], w=[(lat, n)])
                self.ps_reserved = set()
            for c in range(4):
                rT, kT, vT, e2, cum, ep, em, epv, a_, kk, km = B[:11]
                v = lambda k: vec[:, c, k:k + 1]
                cs_ = slice(c * 128, (c + 1) * 128)
                for col0, dst in ((c * 128, rT), (512 + c * 128, kT), (1024 + c * 128, vT)):
                    proj_shift(col0, lambda n, ps, dst=dst: fw.act(
                        lambda h: h.copy(out=dst[:, sl(n)], in_=ps[:, :]), r=[ps], w=[(dst, n)]))
                for n in range(4):
                    ps = self.next_ps()
                    fw.pe(lambda h: h.matmul(ps[:, :], lhsT=w2a2[0:64, cs_], rhs=xwa[0:64, sl(n)], start=True, stop=True),
                          r=[w2a2, (xwa, n)], w=[ps])
                    fw.act(lambda h: h.activation(out=e2[:, sl(n)], in_=ps[:, :], func=AF.Exp, bias=dv[:, c, 0:1], scale=-1.0),
                           r=[ps, dv], w=[(e2, n)])
                    fw.act(lambda h: h.activation(out=e2[:, sl(n)], in_=e2[:, sl(n)], func=AF.Ln, bias=1.0, scale=1.0),
                           r=[(e2, n)], w=[(e2, n)])
                    fw.act(lambda h: h.activation(out=e2[:, sl(n)], in_=e2[:, sl(n)], func=AF.Exp, bias=-0.5, scale=-1.0),
                           r=[(e2, n)], w=[(e2, n)])
                    ps = self.next_ps()
                    fw.pe(lambda h: h.matmul(ps[:, :], lhsT=w2a2[64:128, cs_], rhs=xwa[64:128, sl(n)], start=True, stop=True),
                          r=[w2a2, (xwa, n)], w=[ps])
                    fw.act(lambda h: h.activation(out=a_[:, sl(n)], in_=ps[:, :], func=AF.Sigmoid, bias=v(1), scale=1.0),
                           r=[ps, vec], w=[(a_, n)])
                    ps = self.next_ps()
                    fw.pe(lambda h: h.matmul(ps[:, :], lhsT=g2[:, cs_], rhs=sgx[:, sl(n)], start=True, stop=True),
                          r=[g2, (sgx, n)], w=[ps])
                    fw.act(lambda h: h.copy(out=ep[:, sl(n)], in_=ps[:, :]), r=[ps], w=[(ep, n)])
                    if l > 0:
                        ps = self.next_ps()
                        fw.pe(lambda h: h.matmul(ps[:, :], lhsT=v2[0:32, cs_], rhs=lat[:, sl(n)], start=True, stop=True),
                              r=[v2, (lat, n)], w=[ps])
                        fw.act(lambda h: h.activation(out=em[:, sl(n)], in_=ps[:, :], func=AF.Sigmoid, bias=v(7), scale=1.0),
                               r=[ps, vec], w=[(em, n)])
                fw.dma("sp", rwT[6, cs_, :], ep[:], reads=[ep], writes=[(rwT, (6, c))])
                if l > 0:
                    fw.dma("sp", epv[:], vfirst[cs_, :], reads=[vfirst], writes=[epv])
                    fw.dve(lambda h: h.tensor_tensor(out=epv[:], in0=epv[:], in1=vT[:], op=ALU.subtract), r=[epv, vT], w=[epv])
                    fw.dve(lambda h: h.tensor_tensor(out=epv[:], in0=epv[:], in1=em[:], op=ALU.mult), r=[epv, em], w=[epv])
                    fw.dve(lambda h: h.tensor_tensor(out=vT[:], in0=vT[:], in1=epv[:], op=ALU.add), r=[epv, vT], w=[vT])
                else:
                    fw.dma("sp", vfirst[cs_, :], vT[:], reads=[vT], writes=[(vfirst, c)])
                fw.dma("sp", rwT[4, cs_, :], vT[:], reads=[vT], writes=[(rwT, (4, c))])
                fw.dve(lambda h: h.tensor_tensor_scan(out=cum[:], data0=rmask[:], data1=e2[:], initial=0.0,
                                                      op0=ALU.mult, op1=ALU.add), r=[rmask, e2], w=[cum])
                fw.act(lambda h: h.activation(out=ep[:], in_=cum[:], func=AF.Exp, scale=-1.0), r=[cum], w=[ep])
                fw.act(lambda h: h.activation(out=em[:], in_=cum[:], func=AF.Exp, scale=1.0), r=[cum], w=[em])
                fw.dve(lambda h: h.tensor_tensor(out=epv[:], in0=cum[:], in1=e2[:], op=ALU.subtract), r=[cum, e2], w=[epv])
                fw.act(lambda h: h.activation(out=epv[:], in_=epv[:], func=AF.Exp, scale=-1.0), r=[epv], w=[epv])
                fw.dve(lambda h: h.tensor_scalar(out=kk[:], in0=kT[:], scalar1=v(2), scalar2=None, op0=ALU.mult),
                       r=[kT, vec], w=[kk])
                fw.dve(lambda h: h.tensor_tensor(out=cum[:], in0=kk[:], in1=kk[:], op=ALU.mult), r=[kk], w=[cum])
                for n in range(4):
                    ps = self.next_ps()
                    fw.pe(lambda h: h.matmul(ps[:, :], lhsT=obd[:, :], rhs=cum[:, sl(n)], start=True, stop=True),
                          r=[obd, cum], w=[ps])
                    fw.act(lambda h: h.activation(out=e2[:, sl(n)], in_=ps[:, :], func=AF.Sqrt), r=[ps], w=[(e2, n)])
                fw.dve(lambda h: h.tensor_scalar(out=e2[:], in0=e2[:], scalar1=1e-12, scalar2=None, op0=ALU.max),
                       r=[e2], w=[e2])
                fw.dve(lambda h: h.reciprocal(out=e2[:], in_=e2[:]), r=[e2], w=[e2])
                fw.dve(lambda h: h.tensor_tensor(out=kk[:], in0=kk[:], in1=e2[:], op=ALU.mult), r=[kk, e2], w=[kk])
                fw.dve(lambda h: h.tensor_scalar(out=km[:], in0=a_[:], scalar1=v(3), scalar2=dv[:, c, 1:2],
                                                 op0=ALU.mult, op1=ALU.add), r=[a_, vec, dv], w=[km])
                fw.dve(lambda h: h.tensor_tensor(out=km[:], in0=km[:], in1=kT[:], op=ALU.mult), r=[km, kT], w=[km])
                fw.dve(lambda h: h.scalar_tensor_tensor(out=cum[:], in0=rT[:], scalar=v(4), in1=km[:],
                                                        op0=ALU.mult, op1=ALU.mult), r=[rT, km, vec], w=[cum])
                for n in range(4):
                    ps = self.next_ps()
                    fw.pe(lambda h: h.matmul(ps[:, :], lhsT=obd[:, :], rhs=cum[:, sl(n)], start=True, stop=True),
                          r=[obd, cum], w=[ps])
                    fw.dve(lambda h: h.tensor_tensor(out=e2[:, sl(n)], in0=ps[:, :], in1=vT[:, sl(n)], op=ALU.mult),
                           r=[ps, vT], w=[(e2, n)])
                fw.dma("sp", rwT[5, cs_, :], e2[:], reads=[e2], writes=[(rwT, (5, c))])
                fw.dve(lambda h: h.scalar_tensor_tensor(out=epv[:], in0=kk[:], scalar=-1.0, in1=epv[:],
                                                        op0=ALU.mult, op1=ALU.mult), r=[kk, epv], w=[epv])
                fw.dma("sp", rwT[0, cs_, :], epv[:], reads=[epv], writes=[(rwT, (0, c))])
                fw.dve(lambda h: h.tensor_tensor(out=rT[:], in0=rT[:], in1=ep[:], op=ALU.mult), r=[rT, ep], w=[rT])
                fw.dma("sp", rwT[1, cs_, :], rT[:], reads=[rT], writes=[(rwT, (1, c))])
                fw.dve(lambda h: h.tensor_tensor(out=kk[:], in0=kk[:], in1=a_[:], op=ALU.mult), r=[kk, a_], w=[kk])
                fw.dve(lambda h: h.tensor_tensor(out=kk[:], in0=kk[:], in1=em[:], op=ALU.mult), r=[kk, em], w=[kk])
                fw.dma("sp", rwT[2, cs_, :], kk[:], reads=[kk], writes=[(rwT, (2, c))])
                fw.dve(lambda h: h.tensor_tensor(out=km[:], in0=km[:], in1=em[:], op=ALU.mult), r=[km, em], w=[km])
                fw.dma("sp", rwT[3, cs_, :], km[:], reads=[km], writes=[(rwT, (3, c))])
                fw.dve(lambda h: h.tensor_copy(out=wct[:], in_=ep[:, :].rearrange("p (c t) -> p c t", t=128)[:, :, 127]),
                       r=[ep], w=[wct])
                fw.dma("sp", self.scr["rw_wc"][cs_, :], wct[:], reads=[wct], writes=[(self.scr["rw_wc"], c)])

    def phase_rwkv_scan(self, l):
        fw = self.fw
        rwT, yT = self.scr["rwT"], self.scr["yT"]
        ident = self.ident
        with contextlib.ExitStack() as es:
            masks = self.sb(es, "rwmasks", [128, 8, 128])
            fw.dma("sp", masks[:], self.inp["rw_masks"][:, :, :], writes=[masks])
            vec = self.sb(es, "rwvec2", [128, 4, 8])
            fw.dma("sp", vec[:], self.inp["rw_vec"][l], writes=[vec])
            wc = self.sb(es, "wc", [128, 4, NT])
            fw.dma("sp", wc[:], self.scr["rw_wc"][:, :].rearrange("(c p) n -> p c n", p=128),
                   reads=[self.scr["rw_wc"]], writes=[wc])
            ST = self.sb(es, "ST", [128, 4, 64])
            fw.dve(lambda h: h.memset(ST[:], 0.0), w=[ST])
            fm = [self.sb(es, f"fm{i}", [128, 7, 4, 128]) for i in range(2)]
            tok = [self.sb(es, f"tok{i}", [128, 3, 512]) for i in range(2)]
            M4s = [self.sb(es, f"M4a{i}", [128, 8, 4, 128]) for i in range(2)]
            Las = [self.sb(es, f"La{i}", [128, 2, 8, 128]) for i in range(2)]
            Xas = [self.sb(es, f"Xa{i}", [128, 7, 8, 128]) for i in range(2)]
            Ua = self.sb(es, "Ua", [128, 8, 64])
            tSa = self.sb(es, "tSa", [128, 4, 64])
            Ys = [self.sb(es, f"Y{i}", [128, 8, 64]) for i in range(2)]
            s8 = self.sb(es, "s8", [128, 8])
            r8 = self.sb(es, "r8", [128, 8])
            Yc = self.sb(es, "Yc", [128, 8, 64])
            sq = self.sb(es, "sq", [128, 8, 64])
            o32 = self.sb(es, "o32", [128, 4, 128])
            obs = [self.sb(es, f"ob{i}", [128, 4, 128], BF16) for i in range(2)]

            def hrow(hd):
                return slice((hd % 2) * 64, (hd % 2) * 64 + 64)

            def pv4(ap3, par):
                return ap3.rearrange("p (c a) t -> p c a t", a=2)[:, :, par, :]

            def prep_stages(c):
                cs = slice(c * 128, (c + 1) * 128)
                F, TK, M4, La, Xa = fm[c % 2], tok[c % 2], M4s[c % 2], Las[c % 2], Xas[c % 2]
                st = []

                def load():
                    for q in range(7):
                        fw.dma("sp", F[:, q, :, :], rwT[q, :, cs].rearrange("(ct p) t -> p ct t", p=128),
                               reads=[rwT], writes=[(F, q)])
                    for j, q in enumerate((4, 2, 3)):
                        ps = self.next_ps()
                        for ct in range(4):
                            fw.pe(lambda h: h.transpose(ps[:, ct * 128:(ct + 1) * 128], F[:, q, ct, :], ident[:]),
                                  r=[(F, q), ident], w=[ps])
                        fw.act(lambda h: h.copy(out=TK[:, j, :], in_=ps[:, :]), r=[ps], w=[(TK, j)])
                st.append(load)

                def products(par):
                    def f():
                        heads = [par + 2 * i for i in range(4)]
                        for hd in heads:
                            ct, hr = hd // 2, hrow(hd)
                            At, Rt, Bt, Kt = (F[hr, q, ct, :] for q in range(4))
                            rA, rR, rB, rK = ((F, q) for q in range(4))
                            ps4 = self.next_ps()
                            for i, (lt, rh, rl, rr) in enumerate(((Bt, At, rB, rA), (Kt, At, rK, rA),
                                                                  (Bt, Rt, rB, rR), (Kt, Rt, rK, rR))):
                                fw.pe(lambda h: h.matmul(ps4[:, i * 128:(i + 1) * 128], lhsT=lt, rhs=rh,
                                                         start=True, stop=True), r=[rl, rr], w=[ps4])
                            fw.dve(lambda h: h.tensor_tensor(out=M4[:, hd, :, :],
                                                             in0=ps4[:, :].rearrange("p (a t) -> p a t", a=4),
                                                             in1=masks[:, 0:4, :], op=ALU.mult),
                                   r=[ps4, masks], w=[(M4, hd)])
                        psL = self.next_ps()
                        for hh, hd in enumerate(heads):
                            ct, hr = hd // 2, hrow(hd)
                            fw.pe(lambda h: h.matmul(psL[:, hh * 128:(hh + 1) * 128], lhsT=F[hr, 0, ct, :],
                                                     rhs=F[hr, 2, ct, :], start=True, stop=True),
                                  r=[(F, 0), (F, 2)], w=[psL])
                        fw.dve(lambda h: h.tensor_tensor(out=pv4(La[:, 0, :, :], par),
                                                         in0=psL[:, :].rearrange("p (a t) -> p a t", a=4),
                                                         in1=masks[:, 4:8, :], op=ALU.mult),
                               r=[psL, masks], w=[(La, (0, par))])
                    return f
                st.append(products(0))
                st.append(products(1))

                def squaring(j):
                    def f():
                        for par in range(2):
                            heads = [par + 2 * i for i in range(4)]
                            psX = self.next_ps()
                            psL2 = self.next_ps() if j < 5 else None
                            for hh, hd in enumerate(heads):
                                Xj = M4[:, hd, 0, :] if j == 0 else Xa[:, j, hd, :]
                                rX = (M4, hd) if j == 0 else (Xa, (j, par))
                                Lj = La[:, j % 2, hd, :]
                                rL = (La, (j % 2, par))
                                fw.pe(lambda h: h.matmul(psX[:, hh * 128:(hh + 1) * 128], lhsT=Lj, rhs=Xj,
                                                         start=True, stop=True), r=[rL, rX], w=[psX])
                                if j < 5:
                                    fw.pe(lambda h: h.matmul(psL2[:, hh * 128:(hh + 1) * 128], lhsT=Xj, rhs=Lj,
                                                             start=True, stop=True), r=[rL, rX], w=[psL2])
                            fw.act(lambda h: h.copy(out=pv4(Xa[:, j + 1, :, :], par),
                                                    in_=psX[:, :].rearrange("p (a t) -> p a t", a=4)),
                                   r=[psX], w=[(Xa, (j + 1, par))])
                            if j < 5:
                                fw.act(lambda h: h.copy(out=pv4(La[:, (j + 1) % 2, :, :], par),
                                                        in_=psL2[:, :].rearrange("p (a t) -> p a t", a=4)),
                                       r=[psL2], w=[(La, ((j + 1) % 2, par))])
                    return f
                for j in range(6):
                    st.append(squaring(j))
                return st

            def chain_stages(c):
                F, TK, M4, Xa, Y = fm[c % 2], tok[c % 2], M4s[c % 2], Xas[c % 2], Ys[c % 2]
                st = []
                Vt = lambda hd: TK[:, 0, hd * 64:(hd + 1) * 64]
                rXj = lambda j, hd: (M4, hd) if j == 0 else (Xa, (j, hd % 2))
                Xj = lambda j, hd: M4[:, hd, 0, :] if j == 0 else Xa[:, j, hd, :]

                def zstage():
                    for par in range(2):
                        psz = self.next_ps()
                        for hh in range(4):
                            hd = par + 2 * hh
                            ct, hr = hd // 2, hrow(hd)
                            o = psz[:, hh * 64:(hh + 1) * 64]
                            fw.pe(lambda h: h.matmul(o, lhsT=M4[:, hd, 1, :], rhs=Vt(hd), start=True, stop=False),
                                  r=[(M4, hd), (TK, 0)], w=[psz])
                            fw.pe(lambda h: h.matmul(o, lhsT=F[hr, 0, ct, :], rhs=ST[hr, ct, :], start=False, stop=True),
                                  r=[(F, 0), ST], w=[psz])
                        fw.act(lambda h: h.copy(out=pv4(Ua[:, :, :], par),
                                                in_=psz[:, 0:256].rearrange("p (a v) -> p a v", a=4)),
                               r=[psz], w=[(Ua, par)])
                st.append(zstage)

                def ustage(j):
                    def f():
                        psu = self.next_ps()
                        for hd in range(8):
                            fw.pe(lambda h: h.matmul(psu[:, hd * 64:(hd + 1) * 64], lhsT=Xj(j, hd), rhs=Ua[:, hd, :],
                                                     start=True, stop=True), r=[rXj(j, hd), Ua], w=[psu])
                        fw.dve(lambda h: h.tensor_tensor(out=Ua[:, :, :].rearrange("p h v -> p (h v)"), in0=psu[:, :],
                                                         in1=Ua[:, :, :].rearrange("p h v -> p (h v)"), op=ALU.add),
                               r=[psu, Ua], w=[Ua])
                    return f
                for j in range(7):
                    st.append(ustage(j))

                def ystage():
                    for par in range(2):
                        psy = self.next_ps()
                        for hh in range(4):
                            hd = par + 2 * hh
                            ct, hr = hd // 2, hrow(hd)
                            o = psy[:, hh * 64:(hh + 1) * 64]
                            fw.pe(lambda h: h.matmul(o, lhsT=F[hr, 1, ct, :], rhs=ST[hr, ct, :], start=True, stop=False),
                                  r=[(F, 1), ST], w=[psy])
                            fw.pe(lambda h: h.matmul(o, lhsT=M4[:, hd, 2, :], rhs=Ua[:, hd, :], start=False, stop=False),
                                  r=[(M4, hd), Ua], w=[psy])
                            fw.pe(lambda h: h.matmul(o, lhsT=M4[:, hd, 3, :], rhs=Vt(hd), start=False, stop=True),
                                  r=[(M4, hd), (TK, 0)], w=[psy])
                        fw.act(lambda h: h.copy(out=pv4(Y[:, :, :], par),
                                                in_=psy[:, 0:256].rearrange("p (a v) -> p a v", a=4)),
                               r=[psy], w=[(Y, par)])
                    pss = self.next_ps()
                    for hd in range(8):
                        ct = hd // 2
                        o = pss[:, hd * 64:(hd + 1) * 64]
                        fw.pe(lambda h: h.matmul(o, lhsT=TK[:, 1, ct * 128:(ct + 1) * 128], rhs=Ua[:, hd, :],
                                                 start=True, stop=False), r=[(TK, 1), Ua], w=[pss])
                        fw.pe(lambda h: h.matmul(o, lhsT=TK[:, 2, ct * 128:(ct + 1) * 128], rhs=Vt(hd),
                                                 start=False, stop=True), r=[(TK, 2), (TK, 0)], w=[pss])
                    pv = pss[:, :].rearrange("p (c a v) -> p c a v", c=4, a=2)
                    for par in range(2):
                        hr = slice(par * 64, par * 64 + 64)
                        fw.dve(lambda h: h.tensor_tensor(out=tSa[hr, :, :], in0=pv[hr, :, par, :], in1=ST[hr, :, :],
                                                         op=ALU.add), r=[pss, ST], w=[(tSa, par)])
                        fw.dve(lambda h: h.tensor_tensor(out=ST[hr, :, :], in0=tSa[hr, :, :],
                                                         in1=wc[hr, :, c:c + 1].to_broadcast([64, 4, 64]), op=ALU.mult),
                               r=[(tSa, par), wc], w=[ST])
                st.append(ystage)
                return st

            def epilogue(c):
                cs = slice(c * 128, (c + 1) * 128)
                F, Y, ob = fm[c % 2], Ys[c % 2], obs[c % 2]
                fw.dve(lambda h: h.tensor_reduce(out=s8[:], in_=Y[:], axis=AX.X, op=ALU.add), r=[Y], w=[s8])
                fw.dve(lambda h: h.tensor_scalar(out=s8[:], in0=s8[:], scalar1=1.0 / 64, scalar2=None, op0=ALU.mult),
                       r=[s8], w=[s8])
                fw.dve(lambda h: h.tensor_tensor(out=Yc[:], in0=Y[:], in1=s8[:, :, None].to_broadcast([128, 8, 64]),
                                                 op=ALU.subtract), r=[Y, s8], w=[Yc])
                fw.dve(lambda h: h.tensor_tensor(out=sq[:], in0=Yc[:], in1=Yc[:], op=ALU.mult), r=[Yc], w=[sq])
                fw.dve(lambda h: h.tensor_reduce(out=r8[:], in_=sq[:], axis=AX.X, op=ALU.add), r=[sq], w=[r8])
                fw.act(lambda h: h.activation(out=r8[:], in_=r8[:], func=AF.Sqrt, bias=64e-5, scale=1.0 / 64), r=[r8], w=[r8])
                fw.dve(lambda h: h.reciprocal(out=r8[:], in_=r8[:]), r=[r8], w=[r8])
                fw.dve(lambda h: h.tensor_tensor(out=Yc[:], in0=Yc[:], in1=r8[:, :, None].to_broadcast([128, 8, 64]),
                                                 op=ALU.mult), r=[Yc, r8], w=[Yc])
                pst = self.next_ps()
                ycf = Yc[:, :, :].rearrange("p h d -> p (h d)")
                for ct in range(4):
                    fw.pe(lambda h: h.transpose(pst[:, ct * 128:(ct + 1) * 128], ycf[:, ct * 128:(ct + 1) * 128], ident[:]),
                          r=[Yc, ident], w=[pst])
                for ct in range(4):
                    fw.dve(lambda h: h.tensor_scalar(out=o32[:, ct, :], in0=pst[:, ct * 128:(ct + 1) * 128],
                                                     scalar1=vec[:, ct, 5:6], scalar2=vec[:, ct, 6:7],
                                                     op0=ALU.mult, op1=ALU.add), r=[pst, vec], w=[(o32, ct)])
                fw.dve(lambda h: h.tensor_tensor(out=o32[:], in0=o32[:], in1=F[:, 5, :, :], op=ALU.add), r=[o32, (F, 5)], w=[o32])
                fw.dve(lambda h: h.tensor_tensor(out=ob[:], in0=o32[:], in1=F[:, 6, :, :], op=ALU.mult), r=[o32, (F, 6)], w=[ob])
                fw.dma("sp", yT[1, :, cs].rearrange("(ct p) t -> p ct t", p=128), ob[:], reads=[ob],
                       writes=[(yT, ("b", c))])

            import os as _os
            _CUT = int(_os.environ.get("RW_CUT", 10 ** 9))
            _cnt = [0]

            def call(f):
                _cnt[0] += 1
                if _cnt[0] <= _CUT:
                    f()
            for f in prep_stages(0):
                call(f)
            for c in range(NT):
                ch = chain_stages(c)
                pr = prep_stages(c + 1) if c + 1 < NT else []
                for i in range(max(len(ch), len(pr))):
                    if i < len(pr):
                        call(pr[i])
                    if i < len(ch):
                        call(ch[i])
                call(lambda: epilogue(c))

    def ln_tile(self, z, st, mv, gbc, bbc, gi):
        fw = self.fw
        for c in range(2):
            fw.dve(lambda h: h.bn_stats(out=st[:, c, :], in_=z[:, c * 512:(c + 1) * 512]), r=[z], w=[(st, c)])
        fw.dve(lambda h: h.bn_aggr(out=mv[:, 0:2], in_=st[:, :, :].rearrange("p a b -> p (a b)")), r=[st], w=[mv])
        fw.act(lambda h: h.activation(out=mv[:, 2:3], in_=mv[:, 1:2], func=AF.Sqrt, bias=1e-5, scale=1.0), r=[mv], w=[mv])
        fw.dve(lambda h: h.reciprocal(out=mv[:, 2:3], in_=mv[:, 2:3]), r=[mv], w=[mv])
        fw.dve(lambda h: h.tensor_scalar(out=z[:], in0=z[:], scalar1=mv[:, 0:1], scalar2=mv[:, 2:3],
                                         op0=ALU.subtract, op1=ALU.mult), r=[z, mv], w=[z])
        fw.dve(lambda h: h.tensor_tensor(out=z[:], in0=z[:], in1=gbc[:, gi, :], op=ALU.mult), r=[z, gbc], w=[z])
        fw.dve(lambda h: h.tensor_tensor(out=z[:], in0=z[:], in1=gbc[:, gi + 1, :], op=ALU.add), r=[z, gbc], w=[z])

    def store_x_and_xT(self, z, ti, x_d, xT_d, xo):
        fw = self.fw
        fw.dma("sp", x_d[ti * 128:(ti + 1) * 128, :], z[:], reads=[z], writes=[(x_d, ti)])
        for hh in range(2):
            ps = self.next_ps()
            for c in range(4):
                cc = hh * 4 + c
                fw.pe(lambda h: h.transpose(ps[:, c * 128:(c + 1) * 128], z[:, cc * 128:(cc + 1) * 128], self.ident[:]),
                      r=[z, self.ident], w=[ps])
            fw.act(lambda h: h.copy(out=xo[:, hh * 4:(hh + 1) * 4, :], in_=ps[:, :].rearrange("p (c t) -> p c t", c=4)),
                   r=[ps], w=[(xo, hh)])
        fw.dma("sp", xT_d[:, ti * 128:(ti + 1) * 128].rearrange("(c p) t -> p c t", p=128), xo[:], reads=[xo],
               writes=[(xT_d, ti)])

    def phase_merge(self, l, xT_d, xres_ap, xres_dep):
        fw = self.fw
        yT = self.scr["yT"]
        x1_d, x1T_d = self.scr[f"x1_{l}"], self.scr[f"x1T_{l}"]
        with contextlib.ExitStack() as es:
            xT = self.load_xT(es, xT_d)
            wg = self.sb(es, "wg", [128, 8, 3072], BF16)
            for b in range(3):
                src = self.inp["w_in"][l, :, C_GATE + b * 1024:C_GATE + (b + 1) * 1024].rearrange("(kc p) c -> p kc c", p=128)
                fw.dma("pool", wg[:, :, b * 1024:(b + 1) * 1024], src, writes=[(wg, b)])
            wb = self.sb(es, "wb", [128, 3, 4, D], BF16)
            for b in range(3):
                fw.dma("pool", wb[:, b, :, :], self.inp["w_branch"][l, b].rearrange("(kc p) c -> p kc c", p=128),
                       writes=[(wb, b)])
            wo = self.sb(es, "wo", [128, 8, D], BF16)
            fw.dma("pool", wo[:], self.inp["w_out"][l].rearrange("(kc p) c -> p kc c", p=128), writes=[wo])
            gbc = self.sb(es, "gbc", [128, 4, D])
            fw.dma("sp", gbc[:], self.inp["ln_gb"][l:l + 1, :, :].to_broadcast([128, 4, D]), writes=[gbc])
            yb = self.sb(es, "yb", [128, 3, 4, 512], BF16)
            mg = self.sb(es, "mg", [128, 8, 512], BF16)
            sg = self.sb(es, "sg", [128, 512])
            acc = self.sb(es, "acc", [128, 512])
            z = [self.sb(es, f"z{i}", [128, D]) for i in range(2)]
            xr = [self.sb(es, f"xr{i}", [128, D]) for i in range(2)]
            xo = [self.sb(es, f"xo{i}", [128, 8, 128], BF16) for i in range(2)]
            st = self.sb(es, "st", [128, 2, 6])
            mv = self.sb(es, "mv", [128, 4])
            for n in range(4):
                ts = slice(n * 512, (n + 1) * 512)
                for b in range(3):
                    fw.dma("sp", yb[:, b, :, :], yT[b, :, ts].rearrange("(c p) t -> p c t", p=128),
                           reads=[yT], writes=[(yb, b)])
                for dt in range(8):
                    for b in range(3):
                        ps = self.next_ps()
                        self.proj_fm(wg, xT, b * 1024 + dt * 128, 128, n, ps)
                        fw.act(lambda h: h.activation(out=sg[:], in_=ps[:, :], func=AF.Sigmoid), r=[ps], w=[sg])
                        ps2 = self.next_ps()
                        for kc in range(4):
                            fw.pe(lambda h: h.matmul(ps2[:, :], lhsT=wb[:, b, kc, dt * 128:(dt + 1) * 128],
                                                     rhs=yb[:, b, kc, :], start=(kc == 0), stop=(kc == 3)),
                                  r=[(wb, b), (yb, b)], w=[ps2])
                        if b == 0:
                            fw.dve(lambda h: h.tensor_tensor(out=acc[:], in0=ps2[:, :], in1=sg[:], op=ALU.mult),
                                   r=[ps2, sg], w=[acc])
                        else:
                            fw.dve(lambda h: h.tensor_tensor(out=sg[:], in0=ps2[:, :], in1=sg[:], op=ALU.mult),
                                   r=[ps2, sg], w=[sg])
                            dst = mg[:, dt, :] if b == 2 else acc[:]
                            fw.dve(lambda h: h.tensor_tensor(out=dst, in0=acc[:], in1=sg[:], op=ALU.add),
                                   r=[acc, sg], w=[(mg, dt)] if b == 2 else [acc])
                for tt in range(4):
                    ti = n * 4 + tt
                    zz, xx = z[ti % 2], xr[ti % 2]
                    fw.dma("sp", xx[:], xres_ap[ti * 128:(ti + 1) * 128, :],
                           reads=[(xres_dep, ti)] if xres_dep else [], writes=[xx])
                    for hf in range(2):
                        ps = self.next_ps()
                        for dt in range(8):
                            fw.pe(lambda h: h.matmul(ps[:, :], lhsT=mg[:, dt, tt * 128:(tt + 1) * 128],
                                                     rhs=wo[:, dt, hf * 512:(hf + 1) * 512], start=(dt == 0), stop=(dt == 7)),
                                  r=[(mg, dt), wo], w=[ps])
                        fw.dve(lambda h: h.scalar_tensor_tensor(out=zz[:, hf * 512:(hf + 1) * 512],
                                                                in0=xx[:, hf * 512:(hf + 1) * 512], scalar=ALPHA,
                                                                in1=ps[:, :], op0=ALU.mult, op1=ALU.add),
                               r=[xx, ps], w=[zz])
                    self.ln_tile(zz, st, mv, gbc, None, 0)
                    self.store_x_and_xT(zz, ti, x1_d, x1T_d, xo[ti % 2])

    def top16(self, vals, idxs, src, scratch, n):
        fw = self.fw
        fw.dve(lambda h: h.max(out=vals[:, 0:8], in_=src), r=[self._t16_src], w=[self._t16_v])
        fw.dve(lambda h: h.max_index(out=idxs[:, 0:8], in_max=vals[:, 0:8], in_values=src),
               r=[self._t16_src, self._t16_v], w=[self._t16_i])
        fw.dve(lambda h: h.match_replace(out=scratch, in_to_replace=vals[:, 0:8], in_values=src, imm_value=-1e30),
               r=[self._t16_src, self._t16_v], w=[self._t16_s])
        fw.dve(lambda h: h.max(out=vals[:, 8:16], in_=scratch), r=[self._t16_s], w=[self._t16_v])
        fw.dve(lambda h: h.max_index(out=idxs[:, 8:16], in_max=vals[:, 8:16], in_values=scratch),
               r=[self._t16_s, self._t16_v], w=[self._t16_i])

    def phase_peer(self, l, x2_d, x2T_d):
        fw = self.fw
        x1_d = self.scr[f"x1_{l}"]
        U_ap, V_ap = self.inp[f"peer_u{l}"], self.inp[f"peer_v{l}"]
        ident = self.ident
        with contextlib.ExitStack() as es:
            wq = self.sb(es, "wq", [128, 8, 2048])
            for kc in range(8):
                fw.dma("sp", wq[:, kc, :], self.inp["peer_wq"][l, kc * 128:(kc + 1) * 128, :], writes=[(wq, kc)])
            skT = self.sb(es, "skT", [128, 2, 128])
            fw.dma("sp", skT[:], self.inp["peer_skT"][l].rearrange("a d n -> d a n"), writes=[skT])
            iota = self.sb(es, "iota16", [128, 16])
            fw.dma("sp", iota[:], self.inp["iota16"][:, :], writes=[iota])
            gbc = self.sb(es, "gbc2", [128, 4, D])
            fw.dma("sp", gbc[:], self.inp["ln_gb"][l:l + 1, :, :].to_broadcast([128, 4, D]), writes=[gbc])
            xts = [self.sb(es, f"pxt{i}", [128, D]) for i in range(2)]
            xTf = self.sb(es, "xTf", [128, 8, 128])
            qT = self.sb(es, "qT", [128, 16, 128])
            sc = self.sb(es, "sc", [128, 16, 128])
            sc2 = self.sb(es, "sc2", [128, 16, 128])
            v16 = self.sb(es, "v16", [128, 16, 16])
            i16 = self.sb(es, "i16", [128, 16, 16], U32)
            i16f = self.sb(es, "i16f", [128, 8, 2, 16])
            cand = self.sb(es, "cand", [128, 8, 256])
            cand2 = self.sb(es, "cand2", [128, 8, 256])
            top = self.sb(es, "top", [128, 8, 16])
            pos = self.sb(es, "pos", [128, 8, 16], U32)
            prc = self.sb(es, "prc", [128, 2, 8, 16], U32)
            prcf = self.sb(es, "prcf", [128, 2, 8, 16])
            oh = self.sb(es, "oh", [128, 128, 16])
            sel = self.sb(es, "sel", [128, 2, 128])
            idxf = self.sb(es, "idxf", [128, 128])
            idx = self.sb(es, "idx", [128, 128], U32)
            gate = self.sb(es, "gate", [128, 8, 16])
            gs = self.sb(es, "gs", [128, 8])
            actv = self.sb(es, "actv", [128, 128])
            gtmp = self.sb(es, "gtmp", [128, 128])
            wgt = self.sb(es, "wgt", [128, 128])
            junk = self.sb(es, "junk", [128, D])
            rows = [self.sb(es, f"rows{i}", [128, D]) for i in range(8)]
            dgs = [self.sb(es, f"dg{i}", [128, 128]) for i in range(4)]
            accb = [self.ps[6], self.ps[7]]
            self.ps_reserved = set(accb)
            ys = [self.sb(es, f"py{i}", [128, D]) for i in range(2)]
            xo = [self.sb(es, f"pxo{i}", [128, 8, 128], BF16) for i in range(2)]
            st = self.sb(es, "pst", [128, 2, 6])
            mv = self.sb(es, "pmv", [128, 4])
            idxs = [idx, self.sb(es, "idx_b", [128, 128], U32)]
            gates = [gate, self.sb(es, "gate_b", [128, 8, 16])]
            import os as _os
            _NODMA = int(_os.environ.get("PEER_NODMA", 0)); _NODVE = int(_os.environ.get("PEER_NODVE", 0))

            def stage1(ti):
                xt, idx, gate = xts[ti % 2], idxs[ti % 2], gates[ti % 2]
                th = []

                def t_load():
                    fw.dma("sp", xt[:], x1_d[ti * 128:(ti + 1) * 128, :], reads=[(x1_d, ti)], writes=[xt])
                    for hh in range(2):
                        ps = self.next_ps()
                        for c in range(4):
                            cc = hh * 4 + c
                            fw.pe(lambda h: h.transpose(ps[:, c * 128:(c + 1) * 128], xt[:, cc * 128:(cc + 1) * 128], ident[:]),
                                  r=[xt, ident], w=[ps])
                        fw.act(lambda h: h.copy(out=xTf[:, hh * 4:(hh + 1) * 4, :],
                                                in_=ps[:, :].rearrange("p (c t) -> p c t", c=4)), r=[ps], w=[(xTf, hh)])
                th.append(t_load)

                def t_q(g4):
                    def f():
                        ps = self.next_ps()
                        for b4 in range(4):
                            blk = g4 * 4 + b4
                            for kc in range(8):
                                fw.pe(lambda h: h.matmul(ps[:, b4 * 128:(b4 + 1) * 128],
                                                         lhsT=wq[:, kc, blk * 128:(blk + 1) * 128], rhs=xTf[:, kc, :],
                                                         start=(kc == 0), stop=(kc == 7)), r=[(wq, kc), xTf], w=[ps])
                        fw.act(lambda h: h.copy(out=qT[:, g4 * 4:(g4 + 1) * 4, :],
                                                in_=ps[:, :].rearrange("p (b t) -> p b t", b=4)), r=[ps], w=[(qT, g4)])
                    return f
                for g4 in range(4):
                    th.append(t_q(g4))

                def t_s(g4):
                    def f():
                        ps = self.next_ps()
                        for b4 in range(4):
                            blk = g4 * 4 + b4
                            fw.pe(lambda h: h.matmul(ps[:, b4 * 128:(b4 + 1) * 128], lhsT=qT[:, blk, :],
                                                     rhs=skT[:, blk % 2, :], start=True, stop=True),
                                  r=[(qT, g4), skT], w=[ps])
                        fw.act(lambda h: h.copy(out=sc[:, g4 * 4:(g4 + 1) * 4, :],
                                                in_=ps[:, :].rearrange("p (b n) -> p b n", b=4)), r=[ps], w=[(sc, g4)])
                    return f
                for g4 in range(4):
                    th.append(t_s(g4))

                def t_top1(blk):
                    def f():
                        self._t16_src, self._t16_v, self._t16_i, self._t16_s = sc, v16, i16, sc2
                        self.top16(v16[:, blk, :], i16[:, blk, :], sc[:, blk, :], sc2[:, blk, :], 128)
                    return f
                for blk in range(16):
                    th.append(t_top1(blk))

                def t_cand():
                    v4 = v16[:, :, :].rearrange("p (h a) k -> p h a k", a=2)
                    fw.dve(lambda h: h.tensor_tensor(
                        out=cand[:, :, :].rearrange("p h (i j) -> p h i j", i=16),
                        in0=v4[:, :, 0, :][:, :, :, None].to_broadcast([128, 8, 16, 16]),
                        in1=v4[:, :, 1, :][:, :, None, :].to_broadcast([128, 8, 16, 16]), op=ALU.add), r=[v16], w=[cand])
                th.append(t_cand)

                def t_top2(hd):
                    def f():
                        self._t16_src, self._t16_v, self._t16_i, self._t16_s = cand, top, pos, cand2
                        self.top16(top[:, hd, :], pos[:, hd, :], cand[:, hd, :], cand2[:, hd, :], 256)
                    return f
                for hd in range(8):
                    th.append(t_top2(hd))

                def t_idx0():
                    fw.dve(lambda h: h.tensor_scalar(out=prc[:, 0, :, :], in0=pos[:], scalar1=4, scalar2=None,
                                                     op0=ALU.logical_shift_right), r=[pos], w=[prc])
                    fw.dve(lambda h: h.tensor_scalar(out=prc[:, 1, :, :], in0=pos[:], scalar1=15, scalar2=None,
                                                     op0=ALU.bitwise_and), r=[pos], w=[prc])
                    fw.dve(lambda h: h.tensor_copy(out=prcf[:], in_=prc[:]), r=[prc], w=[prcf])
                    fw.dve(lambda h: h.tensor_copy(out=i16f[:], in_=i16[:, :, :].rearrange("p (h a) k -> p h a k", a=2)),
                           r=[i16], w=[i16f])
                th.append(t_idx0)

                def t_idx1(a):
                    def f():
                        fw.dve(lambda h: h.tensor_tensor(
                            out=oh[:], in0=iota[:, None, :].to_broadcast([128, 128, 16]),
                            in1=prcf[:, a, :, :].rearrange("p h k -> p (h k)")[:, :, None].to_broadcast([128, 128, 16]),
                            op=ALU.is_equal), r=[iota, prcf], w=[oh])
                        fw.dve(lambda h: h.tensor_tensor(
                            out=oh[:, :, :].rearrange("p (h k) j -> p h k j", h=8),
                            in0=oh[:, :, :].rearrange("p (h k) j -> p h k j", h=8),
                            in1=i16f[:, :, a, :][:, :, None, :].to_broadcast([128, 8, 16, 16]), op=ALU.mult),
                            r=[oh, i16f], w=[oh])
                        fw.dve(lambda h: h.tensor_reduce(out=sel[:, a, :], in_=oh[:], axis=AX.X, op=ALU.add),
                               r=[oh], w=[(sel, a)])
                    return f
                th.append(t_idx1(0))
                th.append(t_idx1(1))

                def t_fin():
                    fw.dve(lambda h: h.scalar_tensor_tensor(out=idxf[:], in0=sel[:, 0, :], scalar=128.0, in1=sel[:, 1, :],
                                                            op0=ALU.mult, op1=ALU.add), r=[sel], w=[idxf])
                    fw.dve(lambda h: h.tensor_copy(out=idx[:], in_=idxf[:]), r=[idxf], w=[idx])
                    fw.dve(lambda h: h.tensor_tensor(out=gate[:], in0=top[:], in1=top[:, :, 0:1].to_broadcast([128, 8, 16]),
                                                     op=ALU.subtract), r=[top], w=[gate])
                    fw.act(lambda h: h.activation(out=gate[:], in_=gate[:], func=AF.Exp), r=[gate], w=[gate])
                    fw.dve(lambda h: h.tensor_reduce(out=gs[:], in_=gate[:], axis=AX.X, op=ALU.add), r=[gate], w=[gs])
                    fw.dve(lambda h: h.reciprocal(out=gs[:], in_=gs[:]), r=[gs], w=[gs])
                    fw.dve(lambda h: h.tensor_tensor(out=gate[:], in0=gate[:], in1=gs[:, :, None].to_broadcast([128, 8, 16]),
                                                     op=ALU.mult), r=[gate, gs], w=[gate])
                th.append(t_fin)
                return th

            rr = 0
            for f in stage1(0):
                f()
            for ti in range(NT):
                xt, y, idx, gate = xts[ti % 2], ys[ti % 2], idxs[ti % 2], gates[ti % 2]
                nxt = stage1(ti + 1) if ti + 1 < NT else []
                nslot = [0]

                def tick():
                    nslot[0] += 1
                    if nslot[0] % 4 == 0 and nxt:
                        nxt.pop(0)()
                for j in range(128):
                    rb = rows[rr % len(rows)]
                    rr += 1
                    if not _NODMA:
                        fw.dma("pool", rb[:], U_ap[:, :], reads=[idx], writes=[rb],
                               indirect=bass.IndirectOffsetOnAxis(ap=idx[:, j:j + 1], axis=0))
                    fw.dve(lambda h: h.scalar_tensor_tensor(out=junk[:], in0=rb[:], scalar=1.0, in1=xt[:], op0=ALU.mult,
                                                            op1=ALU.mult, accum_out=actv[:, j:j + 1]),
                           r=[rb, xt], w=[(actv, j)])
                    tick()
                self.gelu_tanh(None, actv, wgt, gtmp)
                fw.dve(lambda h: h.tensor_tensor(out=wgt[:], in0=wgt[:], in1=gate[:, :, :].rearrange("p h k -> p (h k)"),
                                                 op=ALU.mult), r=[wgt, gate], w=[wgt])
                for j in range(128):
                    rb = rows[rr % len(rows)]
                    rr += 1
                    if not _NODMA:
                        fw.dma("pool", rb[:], V_ap[:, :], reads=[idx], writes=[rb],
                               indirect=bass.IndirectOffsetOnAxis(ap=idx[:, j:j + 1], axis=0))
                    dg = dgs[j % len(dgs)]
                    fw.act(lambda h: h.activation(out=dg[:], in_=ident[:], func=AF.Copy, scale=wgt[:, j:j + 1]),
                           r=[ident, wgt], w=[dg])
                    for hf in range(2):
                        fw.pe(lambda h: h.matmul(accb[hf][:, :], lhsT=dg[:], rhs=rb[:, hf * 512:(hf + 1) * 512],
                                                 start=(j == 0), stop=(j == 127)), r=[dg, rb], w=[accb[hf]])
                    tick()
                while nxt:
                    nxt.pop(0)()
                for hf in range(2):
                    fw.dve(lambda h: h.scalar_tensor_tensor(out=y[:, hf * 512:(hf + 1) * 512],
                                                            in0=xt[:, hf * 512:(hf + 1) * 512], scalar=ALPHA,
                                                            in1=accb[hf][:, :], op0=ALU.mult, op1=ALU.add),
                           r=[xt, accb[hf]], w=[y])
                self.ln_tile(y, st, mv, gbc, None, 2)
                self.store_x_and_xT(y, ti, x2_d, x2T_d, xo[ti % 2])
            self.ps_reserved = set()

    def phase_mla(self, l, xT_d):
        fw, nc = self.fw, self.nc
        yT = self.scr["yT"]
        SC = 96.0 ** -0.5
        with contextlib.ExitStack() as es:
            qn = self.sb(es, "qn", [64, 8, S], BF16)
            qp = self.sb(es, "qp", [32, 8, S], BF16)
            kn = self.sb(es, "kn", [64, 8, S], BF16)
            kp = self.sb(es, "kp", [32, S], BF16)
            V = self.sb(es, "V", [128, NT, 8, 65], BF16)
            cmask = self.sb(es, "cmask", [128, 128], BF16)
            identb = self.sb(es, "identb", [128, 128], BF16)
            fw.dma("pool", cmask[:], self.inp["cmask"][:, :], writes=[cmask])
            fw.dve(lambda h: h.tensor_copy(out=identb[:], in_=self.ident[:]), r=[self.ident], w=[identb])
            fw.pool(lambda h: h.memset(V[:], 1.0), w=[V])
            with contextlib.ExitStack() as es2:
                xT = self.load_xT(es2, xT_d)
                wt = self.sb(es2, "mw", [128, 8, 448], BF16)
                self.load_w_cols(wt, l, C_MLA, 416, key=0)
                for hf in range(2):
                    src = self.inp["w_in"][l, :, C_MLA + 384 + (1 - hf) * 16:C_MLA + 384 + (2 - hf) * 16].rearrange(
                        "(kc p) c -> p kc c", p=128)
                    fw.dma("pool", wt[:, :, 416 + hf * 16:416 + (hf + 1) * 16], src, writes=[(wt, 1 + hf)])
                vec = self.sb(es2, "mvec", [128, 3])
                fw.dma("sp", vec[:], self.inp["mla_vec"][l], writes=[vec])
                cs = self.sb(es2, "cs", [32, 2, S])
                fw.dma("sp", cs[:], self.inp["rope_cs"].rearrange("a p t -> p a t"), writes=[cs])
                wuq = self.sb(es2, "wuq", [128, 2, 768], BF16)
                fw.dma("pool", wuq[:], self.inp["mla_w_uq"][l].rearrange("(kc p) c -> p kc c", p=128), writes=[wuq])
                wuqr = self.sb(es2, "wuqr", [128, 2, 256], BF16)
                fw.dma("pool", wuqr[:], self.inp["mla_w_uq_rot"][l].rearrange("(kc p) c -> p kc c", p=128), writes=[wuqr])
                wk = self.sb(es2, "wk", [128, 512], BF16)
                fw.dma("pool", wk[:], self.inp["mla_w_ukv_k"][l], writes=[wk])
                wv = self.sb(es2, "wv", [128, 512], BF16)
                fw.dma("pool", wv[:], self.inp["mla_w_ukv_v"][l], writes=[wv])
                ones = self.sb(es2, "ones", [128, 128])
                fw.pool(lambda h: h.memset(ones[:], 1.0), w=[ones])
                cl = self.sb(es2, "cl", [128, 3, 512])
                sq = self.sb(es2, "sq", [128, 3, 512])
                rs = self.sb(es2, "rs", [128, 2, 512])
                cn = self.sb(es2, "cn", [128, 3, 512], BF16)
                kr = self.sb(es2, "kr", [32, 2, 512])
                tmp32 = self.sb(es2, "tmp32", [32, 2, 512])
                for n in range(4):
                    ts = slice(n * 512, (n + 1) * 512)
                    for j in range(3):
                        ps = self.next_ps()
                        self.proj_fm(wt, xT, j * 128, 128, n, ps)
                        fw.act(lambda h: h.copy(out=cl[:, j, :], in_=ps[:, :]), r=[ps], w=[(cl, j)])
                        fw.dve(lambda h: h.tensor_tensor(out=sq[:, j, :], in0=cl[:, j, :], in1=cl[:, j, :], op=ALU.mult),
                               r=[(cl, j)], w=[(sq, j)])
                    for j in range(2):
                        ps = self.next_ps()
                        self.proj_fm(wt, xT, 384 + j * 32, 32, n, ps)
                        fw.act(lambda h: h.copy(out=kr[:, j, :], in_=ps[0:32, :]), r=[ps], w=[(kr, j)])
                    for g, (tiles, dim) in enumerate((((0, 1), 256.0), ((2,), 128.0))):
                        ps = self.next_ps()
                        for ii, j in enumerate(tiles):
                            fw.pe(lambda h: h.matmul(ps[:, :], lhsT=ones[:, :], rhs=sq[:, j, :],
                                                     start=(ii == 0), stop=(ii == len(tiles) - 1)),
                                  r=[ones, (sq, j)], w=[ps])
                        fw.act(lambda h: h.activation(out=rs[:, g, :], in_=ps[:, :], func=AF.Sqrt, bias=1e-6,
                                                      scale=1.0 / dim), r=[ps], w=[(rs, g)])
                        fw.dve(lambda h: h.reciprocal(out=rs[:, g, :], in_=rs[:, g, :]), r=[(rs, g)], w=[(rs, g)])
                        for j in tiles:
                            fw.dve(lambda h: h.scalar_tensor_tensor(out=cn[:, j, :], in0=cl[:, j, :],
                                                                    scalar=vec[:, j:j + 1], in1=rs[:, g, :],
                                                                    op0=ALU.mult, op1=ALU.mult),
                                   r=[(cl, j), vec, (rs, g)], w=[(cn, j)])
                    fw.dve(lambda h: h.tensor_tensor(out=tmp32[:, 0, :], in0=kr[:, 0, :], in1=cs[:, 0, ts], op=ALU.mult),
                           r=[(kr, 0), cs], w=[(tmp32, 0)])
                    fw.dve(lambda h: h.tensor_tensor(out=tmp32[:, 1, :], in0=kr[:, 1, :], in1=cs[:, 1, ts], op=ALU.mult),
                           r=[(kr, 1), cs], w=[(tmp32, 1)])
                    fw.dve(lambda h: h.tensor_tensor(out=kp[:, ts], in0=tmp32[:, 0, :], in1=tmp32[:, 1, :], op=ALU.add),
                           r=[tmp32], w=[(kp, n)])
                    for hd in range(8):
                        ps = self.next_ps()
                        for kc in range(2):
                            fw.pe(lambda h: h.matmul(ps[0:64, :], lhsT=wuq[:, kc, hd * 96:hd * 96 + 64],
                                                     rhs=cn[:, kc, :], start=(kc == 0), stop=(kc == 1)),
                                  r=[wuq, (cn, kc)], w=[ps])
                        fw.act(lambda h: h.activation(out=qn[:, hd, ts], in_=ps[0:64, :], func=AF.Copy, scale=SC),
                               r=[ps], w=[(qn, (hd, n))])
                        ps = self.next_ps()
                        for kc in range(2):
                            fw.pe(lambda h: h.matmul(ps[0:32, :], lhsT=wuq[:, kc, hd * 96 + 64:hd * 96 + 96],
                                                     rhs=cn[:, kc, :], start=(kc == 0), stop=(kc == 1)),
                                  r=[wuq, (cn, kc)], w=[ps])
                        ps2 = self.next_ps()
                        for kc in range(2):
                            fw.pe(lambda h: h.matmul(ps2[0:32, :], lhsT=wuqr[:, kc, hd * 32:hd * 32 + 32],
                                                     rhs=cn[:, kc, :], start=(kc == 0), stop=(kc == 1)),
                                  r=[wuqr, (cn, kc)], w=[ps2])
                        fw.dve(lambda h: h.tensor_tensor(out=tmp32[:, 0, :], in0=ps[0:32, :], in1=cs[:, 0, ts], op=ALU.mult),
                               r=[ps, cs], w=[(tmp32, 0)])
                        fw.dve(lambda h: h.tensor_tensor(out=tmp32[:, 1, :], in0=ps2[0:32, :], in1=cs[:, 1, ts], op=ALU.mult),
                               r=[ps2, cs], w=[(tmp32, 1)])
                        fw.dve(lambda h: h.scalar_tensor_tensor(out=qp[:, hd, ts], in0=tmp32[:, 0, :], scalar=1.0,
                                                                in1=tmp32[:, 1, :], op0=ALU.mult, op1=ALU.add),
                               r=[tmp32], w=[(qp, (hd, n))])
                        fw.dve(lambda h: h.tensor_scalar(out=qp[:, hd, ts], in0=qp[:, hd, ts], scalar1=SC, scalar2=None,
                                                         op0=ALU.mult), r=[(qp, (hd, n))], w=[(qp, (hd, n))])
                        ps = self.next_ps()
                        fw.pe(lambda h: h.matmul(ps[0:64, :], lhsT=wk[:, hd * 64:(hd + 1) * 64], rhs=cn[:, 2, :],
                                                 start=True, stop=True), r=[wk, (cn, 2)], w=[ps])
                        fw.act(lambda h: h.copy(out=kn[:, hd, ts], in_=ps[0:64, :]), r=[ps], w=[(kn, (hd, n))])
                    for tt in range(4):
                        ti = n * 4 + tt
                        ps = self.next_ps()
                        fw.pe(lambda h: h.matmul(ps[:, :], lhsT=cn[:, 2, tt * 128:(tt + 1) * 128], rhs=wv[:, :],
                                                 start=True, stop=True), r=[(cn, 2), wv], w=[ps])
                        fw.act(lambda h: h.copy(out=V[:, ti, :, 0:64], in_=ps[:, :].rearrange("p (h d) -> p h d", h=8)),
                               r=[ps], w=[(V, ti)])
            fw.barrier()
            with contextlib.ExitStack() as es3:
                PT = self.sb(es3, "PT", [128, NT, 8, 128], BF16)
                rec = self.sb(es3, "rec", [128, 8, 1])
                yc = self.sb(es3, "yc", [128, 8, 64], BF16)
                yct = [self.sb(es3, f"yct{i}", [128, 4, 128], BF16) for i in range(2)]
                for qi in range(NT):
                    qs = slice(qi * 128, (qi + 1) * 128)
                    for kt in range(qi + 1):
                        ks = slice(kt * 128, (kt + 1) * 128)
                        for hg in range(2):
                            ps = self.next_ps()
                            for hh in range(4):
                                hd = hg * 4 + hh
                                o = ps[:, hh * 128:(hh + 1) * 128]
                                fw.pe(lambda h: h.matmul(o, lhsT=kn[:, hd, ks], rhs=qn[:, hd, qs], start=True, stop=False),
                                      r=[kn, qn], w=[ps])
                                fw.pe(lambda h: h.matmul(o, lhsT=kp[:, ks], rhs=qp[:, hd, qs], start=False, stop=True),
                                      r=[kp, qp], w=[ps])
                            fw.act(lambda h: h.activation(
                                out=PT[:, kt, hg * 4:(hg + 1) * 4, :],
                                in_=ps[:, :].rearrange("p (h q) -> p h q", h=4), func=AF.Exp),
                                r=[ps], w=[(PT, (kt, hg))])
                            if kt == qi:
                                fw.dve(lambda h: h.tensor_tensor(
                                    out=PT[:, kt, hg * 4:(hg + 1) * 4, :], in0=PT[:, kt, hg * 4:(hg + 1) * 4, :],
                                    in1=cmask[:, None, :].to_broadcast([128, 4, 128]), op=ALU.mult),
                                    r=[(PT, (kt, hg)), cmask], w=[(PT, (kt, hg))])
                    pss = [self.next_ps(), self.next_ps()]
                    for hd in range(8):
                        ps = pss[hd // 4]
                        o = ps[:, (hd % 4) * 65:(hd % 4) * 65 + 65]
                        for kt in range(qi + 1):
                            fw.pe(lambda h: h.matmul(o, lhsT=PT[:, kt, hd, :], rhs=V[:, kt, hd, :],
                                                     start=(kt == 0), stop=(kt == qi)),
                                  r=[(PT, (kt, hd // 4)), (V, kt)], w=[ps])
                    for hg in range(2):
                        ps = pss[hg]
                        pv = ps[:, 0:260].rearrange("p (h d) -> p h d", h=4)
                        fw.dve(lambda h: h.reciprocal(out=rec[:, hg * 4:(hg + 1) * 4, :], in_=pv[:, :, 64:65]),
                               r=[ps], w=[(rec, hg)])
                        fw.dve(lambda h: h.tensor_tensor(out=yc[:, hg * 4:(hg + 1) * 4, :], in0=pv[:, :, 0:64],
                                                         in1=rec[:, hg * 4:(hg + 1) * 4, :].to_broadcast([128, 4, 64]),
                                                         op=ALU.mult), r=[ps, (rec, hg)], w=[(yc, hg)])
                    o = yct[qi % 2]
                    ycf = yc[:, :, :].rearrange("p h d -> p (h d)")
                    for c in range(4):
                        pst = self.next_ps()
                        pb = pst[:, :].bitcast(BF16)
                        fw.pe(lambda h: h.transpose(pb[:, 0:128], ycf[:, c * 128:(c + 1) * 128], identb[:]),
                              r=[yc, identb], w=[pst])
                        fw.act(lambda h: h.copy(out=o[:, c, :], in_=pb[:, 0:128]), r=[pst], w=[(o, c)])
                    fw.dma("sp", yT[2, :, qs].rearrange("(c p) t -> p c t", p=128), o[:], reads=[o],
                           writes=[(yT, ("c", qi))])


def prep_inputs(inputs, nseq=1):
    f = lambda a: np.ascontiguousarray(np.asarray(a, dtype=np.float32))
    shared = {}
    shared["w_in"] = f(inputs["w_in"])
    rv = np.stack([inputs["rg_conv_w"][:, 0], inputs["rg_conv_w"][:, 1], inputs["rg_conv_w"][:, 2],
                   inputs["rg_conv_w"][:, 3], inputs["rg_conv_b"], inputs["rg_ba"], inputs["rg_bx"],
                   inputs["rg_log_a"]], axis=-1)
    shared["rg_vec"] = f(rv.reshape(L, 4, 128, 8).transpose(0, 2, 1, 3))
    shared["rg_wa"] = f(inputs["rg_wa"])
    shared["rg_wx"] = f(inputs["rg_wx"])
    shared["ident"] = np.eye(128, dtype=np.float32)
    mv = np.stack([inputs["mla_q_norm"][:, :128], inputs["mla_q_norm"][:, 128:], inputs["mla_kv_norm"]], axis=-1)
    shared["mla_vec"] = f(mv)
    wuq = np.asarray(inputs["mla_w_uq"], np.float32)
    shared["mla_w_uq"] = f(wuq)
    w4 = wuq.reshape(L, 256, 8, 96)
    shared["mla_w_uq_rot"] = f(np.concatenate([w4[..., 80:96], w4[..., 64:80]], -1).reshape(L, 256, 256))
    wkv = np.asarray(inputs["mla_w_ukv"], np.float32).reshape(L, 128, 8, 128)
    shared["mla_w_ukv_k"] = f(wkv[..., :64].reshape(L, 128, 512))
    shared["mla_w_ukv_v"] = f(wkv[..., 64:].reshape(L, 128, 512))
    pos = np.arange(S, dtype=np.float32)
    inv = (10000.0 ** (-np.arange(16, dtype=np.float32) / 16)).astype(np.float32)
    ang = pos[None, :] * inv[:, None]
    cs_, sn_ = np.cos(ang).astype(np.float32), np.sin(ang).astype(np.float32)
    shared["rope_cs"] = f(np.stack([np.concatenate([cs_, cs_], 0), np.concatenate([-sn_, sn_], 0)], 0))
    shared["cmask"] = f(np.triu(np.ones((128, 128), np.float32)))
    shared["rw_mix"] = f(inputs["rw_mix"])[:, None, :]
    vz = np.zeros((L, 512), np.float32); vz[1:] = inputs["rw_v0"]
    rwv = np.stack([inputs["rw_w0"], inputs["rw_a0"], inputs["rw_k_k"], inputs["rw_k_a"],
                    np.asarray(inputs["rw_r_k"]).reshape(L, 512), inputs["rw_gn_g"], inputs["rw_gn_b"], vz], axis=-1)
    shared["rw_vec"] = f(rwv.reshape(L, 4, 128, 8).transpose(0, 2, 1, 3))
    shared["rw_w2a2"] = f(np.concatenate([inputs["rw_w2"], inputs["rw_a2"]], axis=1))
    shared["rw_g2"] = f(inputs["rw_g2"])
    shared["rw_v1"] = f(inputs["rw_v1"])
    shared["rw_v2"] = f(inputs["rw_v2"])
    su = np.triu(np.ones((128, 128), np.float32), 1); ui = np.triu(np.ones((128, 128), np.float32))
    shared["rw_masks"] = f(np.stack([su, su, ui, ui, su.T, su.T, su.T, su.T], axis=1))
    obd = np.zeros((128, 128), np.float32); obd[:64, :64] = 1; obd[64:, 64:] = 1
    shared["onesbd"] = obd
    shared["w_branch"] = f(inputs["w_branch"])
    shared["w_out"] = f(inputs["w_out"])
    shared["peer_wq"] = f(inputs["peer_w_query"])
    shared["peer_skT"] = f(np.asarray(inputs["peer_subkeys"]).transpose(0, 1, 3, 2))
    shared["iota16"] = f(np.tile(np.arange(16, dtype=np.float32)[None, :], (128, 1)))
    for l in range(L):
        shared[f"peer_u{l}"] = f(inputs["peer_u"][l])
        shared[f"peer_v{l}"] = f(inputs["peer_v"][l])
    shared["ln_gb"] = f(np.stack([inputs["ln1_g"], inputs["ln1_b"], inputs["ln2_g"], inputs["ln2_b"]], axis=1))
    maps = []
    for b in range(8 // nseq):
        m = dict(shared)
        m["x"] = f(inputs["x"][b * nseq:(b + 1) * nseq])
        maps.append(m)
    return maps


def run(inputs, debug=False, phases=None, cores=8, trace=False, nseq=1):
    bld = Builder(debug=debug, phases=phases, nseq=nseq)
    nc = bld.build()
    maps = prep_inputs(inputs, nseq)[:cores]
    maps = [{k: v for k, v in m.items() if k in bld.inp} for m in maps]
    res = run_bass_kernel_spmd(nc, maps, core_ids=list(range(cores)), trace=trace)
    return res


NCORES = 8


def kernel(**inputs):
    nseq = 8 // NCORES
    res = run(inputs, cores=NCORES, nseq=nseq)
    return np.concatenate([np.asarray(r["out"]) for r in res.results], axis=0).astype(np.float32)
```

```python
import contextlib
import numpy as np
import concourse.bass as bass
import concourse.mybir as mybir
from concourse.bass_utils import run_bass_kernel_spmd

F32 = mybir.dt.float32
BF16 = mybir.dt.bfloat16
U32 = mybir.dt.uint32
I32 = mybir.dt.int32
AF = mybir.ActivationFunctionType
ALU = mybir.AluOpType
AX = mybir.AxisListType

D = 1024
S = 2048
L = 2
NT = S // 128
IN_COLS = 6304
C_RG, C_RW, C_MLA, C_GATE = 0, 1024, 2816, 3232
ALPHA = (2.0 * L) ** 0.25


class Dep:
    def __init__(self, name):
        self.name = name
        self.st = {}

    def states(self, key):
        if key is None:
            if None not in self.st:
                self.st[None] = [None, {}]
            return list(self.st.values())
        out = []
        if None in self.st:
            out.append(self.st[None])
        if key not in self.st:
            self.st[key] = [None, {}]
        out.append(self.st[key])
        return out


class T:
    def __init__(self, t, name):
        self.t = t
        self.dep = Dep(name)

    def __getitem__(self, idx):
        return self.t[idx]


class Eng:
    def __init__(self, name, h, sem):
        self.name, self.h, self.sem = name, h, sem
        self.count = 0
        self.known = {}


class FW:
    NDS = {"sp": 16, "pool": 12, "act": 6}

    def __init__(self, nc, es):
        self.nc = nc
        self.eng = {}
        for name, h in (("pe", nc.tensor), ("dve", nc.vector), ("act", nc.scalar),
                        ("pool", nc.gpsimd), ("sp", nc.sync)):
            sem = es.enter_context(nc.semaphore("sem_" + name))
            self.eng[name] = Eng(name, h, sem)
        self.dsem = {}
        self.dcnt = {}
        self.drr = {}
        for q, n in self.NDS.items():
            self.dsem[q] = [es.enter_context(nc.semaphore(f"ds_{q}{i}")) for i in range(n)]
            self.dcnt[q] = [0] * n
            self.drr[q] = 0

    def _wait(self, e, sid, sem, val):
        if val > 0 and e.known.get(sid, 0) < val:
            e.h.wait_ge(sem, val)
            e.known[sid] = val

    def _collect(self, reads, writes, ename):
        need = {}

        def add(tok):
            if tok is None:
                return
            sid, sem, val = tok
            if ename == "pe" and sid == "pe":
                return
            if need.get(sid, (None, 0))[1] < val:
                need[sid] = (sem, val)

        rs, ws = [], []
        for r in reads:
            t, k = r if isinstance(r, tuple) else (r, None)
            sts = t.dep.states(k)
            rs.append((k, sts))
            for st in sts:
                add(st[0])
        for w in writes:
            t, k = w if isinstance(w, tuple) else (w, None)
            sts = t.dep.states(k)
            ws.append((k, sts))
            for st in sts:
                add(st[0])
                for tok in st[1].values():
                    add(tok)
        return need, rs, ws

    def _record(self, tok, rs, ws):
        for k, sts in rs:
            for st in (sts if k is None else sts[-1:]):
                st[1][tok[0]] = tok
        for k, sts in ws:
            for st in (sts if k is None else sts[-1:]):
                st[0] = tok
                st[1] = {}

    def op(self, ename, fn, reads=(), writes=()):
        e = self.eng[ename]
        need, rs, ws = self._collect(reads, writes, ename)
        for sid, (sem, val) in need.items():
            self._wait(e, sid, sem, val)
        ins = fn(e.h)
        e.count += 1
        ins.then_inc(e.sem, 1)
        self._record((ename, e.sem, e.count), rs, ws)
        return ins

    def dma(self, q, out, in_, reads=(), writes=(), indirect=None, **kw):
        e = self.eng[q]
        i = self.drr[q]
        self.drr[q] = (i + 1) % len(self.dsem[q])
        sem = self.dsem[q][i]
        sid = ("d", q, i)
        self._wait(e, sid, sem, 16 * self.dcnt[q][i])
        need, rs, ws = self._collect(reads, writes, q)
        for s2, (sm, val) in need.items():
            self._wait(e, s2, sm, val)
        if indirect is None:
            ins = e.h.dma_start(out=out, in_=in_, **kw)
        else:
            ins = e.h.indirect_dma_start(out=out, out_offset=None, in_=in_, in_offset=indirect, **kw)
        self.dcnt[q][i] += 1
        ins.then_inc(sem, 16)
        self._record((sid, sem, 16 * self.dcnt[q][i]), rs, ws)
        return ins

    def barrier(self):
        for e in self.eng.values():
            for o in self.eng.values():
                if o is not e:
                    self._wait(e, o.name, o.sem, o.count)
            for q in self.dsem:
                for i, sem in enumerate(self.dsem[q]):
                    self._wait(e, ("d", q, i), sem, 16 * self.dcnt[q][i])

    def dve(self, fn, r=(), w=()):
        return self.op("dve", fn, r, w)

    def act(self, fn, r=(), w=()):
        return self.op("act", fn, r, w)

    def pe(self, fn, r=(), w=()):
        return self.op("pe", fn, r, w)

    def pool(self, fn, r=(), w=()):
        return self.op("pool", fn, r, w)


class Builder:
    def __init__(self, debug=False, phases=None, nseq=1):
        self.debug = debug
        self.phases = phases
        self.nseq = nseq
        self.nc = bass.Bass("TRN2", target_bir_lowering=False)
        self.inp = {}
        self.scr = {}

    def din(self, name, shape, dt=F32):
        self.inp[name] = self.nc.dram_tensor(name, list(shape), dt, kind="ExternalInput").ap()

    def dscr(self, name, shape, dt, out=False):
        kind = "ExternalOutput" if (out or self.debug) else "Internal"
        ap = self.nc.dram_tensor(name, list(shape), dt, kind=kind).ap()
        self.scr[name] = T(ap, name)
        return self.scr[name]

    def sb(self, es, name, shape, dt=F32):
        self.uid = getattr(self, "uid", 0) + 1
        nm = f"s{self.uid}_{name}"
        return T(es.enter_context(self.nc.sbuf_tensor(nm, list(shape), dt)), nm)

    def declare(self):
        self.din("x", [self.nseq, S, D])
        self.din("w_in", [L, D, IN_COLS])
        self.din("rg_vec", [L, 128, 4, 8])
        self.din("rg_wa", [L, 8, 64, 64])
        self.din("rg_wx", [L, 8, 64, 64])
        self.din("ident", [128, 128])
        self.din("mla_vec", [L, 128, 3])
        self.din("mla_w_uq", [L, 256, 768])
        self.din("mla_w_uq_rot", [L, 256, 256])
        self.din("mla_w_ukv_k", [L, 128, 512])
        self.din("mla_w_ukv_v", [L, 128, 512])
        self.din("rope_cs", [2, 32, S])
        self.din("cmask", [128, 128])
        self.din("rw_mix", [L, 1, 1792])
        self.din("rw_vec", [L, 128, 4, 8])
        self.din("rw_w2a2", [L, 128, 512])
        self.din("rw_g2", [L, 128, 512])
        self.din("rw_v1", [1, 512, 32])
        self.din("rw_v2", [1, 32, 512])
        self.din("rw_masks", [128, 8, 128])
        self.din("onesbd", [128, 128])
        self.dscr("rwT", [7, 512, S], F32)
        self.dscr("vfirst", [512, S], F32)
        self.dscr("rw_wc", [512, NT], F32)
        self.din("w_branch", [L, 3, 512, D])
        self.din("w_out", [L, D, D])
        self.din("peer_wq", [L, D, 2048])
        self.din("peer_skT", [L, 2, 128, 128])
        self.din("iota16", [128, 16])
        for l in range(L):
            if self.phases is None or f"F{l}" in self.phases:
                self.din(f"peer_u{l}", [16384, D])
                self.din(f"peer_v{l}", [16384, D])
                self.scr[f"uvb{l}"] = T(self.nc.dram_tensor(f"uvb{l}", [16384, 2 * D], BF16, kind="Internal").ap(), f"uvb{l}")
            self.dscr(f"x2T_{l}", [D, S], BF16)
        self.dscr("x2_0", [S, D], F32)
        self.din("ln_gb", [L, 4, D])
        for l in range(L):
            self.dscr(f"x1_{l}", [S, D], F32)
            self.dscr(f"x1T_{l}", [D, S], BF16)
        self.dscr("xT0", [D, S], BF16)
        self.dscr("yT", [3, 512, S], BF16)
        self.dscr("out", [self.nseq, S, D], F32, out=True)

    def build(self):
        nc = self.nc
        self.declare()
        with contextlib.ExitStack() as es:
            self.fw = fw = FW(nc, es)
            self.ps = [T(es.enter_context(nc.psum_tensor(f"ps{i}", [128, 512], F32)), f"ps{i}")
                       for i in range(8)]
            self.psi = 0
            self.ident = self.sb(es, "ident", [128, 128])
            fw.dma("sp", self.ident[:], self.inp["ident"][:, :], writes=[self.ident])
            ph = self.phases
            if ph is None or any(p.startswith("F") for p in ph):
                self.phase_tables([l for l in range(L) if ph is None or f"F{l}" in ph])
                fw.barrier()
            for sq in range(self.nseq):
                x_in = self.inp["x"][sq]
                out_t = T(self.scr["out"].t[sq], f"out{sq}")
                if ph is None or "A" in ph:
                    self.phase_xT(x_in, None, self.scr["xT0"])
                    fw.barrier()
                for l in range(L):
                    xTd = self.scr["xT0"] if l == 0 else self.scr[f"x2T_{l - 1}"]
                    if ph is None or f"B{l}" in ph:
                        self.phase_rg(l, xTd)
                        fw.barrier()
                    if ph is None or f"C{l}" in ph or f"Cp{l}" in ph:
                        self.phase_rwkv_prep(l, xTd)
                        fw.barrier()
                    if ph is None or f"C{l}" in ph or f"Cs{l}" in ph:
                        self.phase_rwkv_scan(l)
                        fw.barrier()
                    if ph is None or f"D{l}" in ph:
                        self.phase_mla(l, xTd)
                        fw.barrier()
                    if ph is None or f"E{l}" in ph:
                        if l == 0:
                            self.phase_merge(l, xTd, x_in, None)
                        else:
                            self.phase_merge(l, xTd, self.scr["x2_0"].t, self.scr["x2_0"])
                        fw.barrier()
                    if ph is None or f"F{l}" in ph:
                        x2d = out_t if l == L - 1 else self.scr["x2_0"]
                        self.phase_peer(l, x2d, self.scr[f"x2T_{l}"])
                        fw.barrier()
            fw.barrier()
        return nc

    def next_ps(self):
        while True:
            p = self.ps[self.psi]
            self.psi = (self.psi + 1) % 8
            if p not in getattr(self, "ps_reserved", ()):
                return p

    def phase_xT(self, x_ap, x_dep, xT_d):
        fw = self.fw
        with contextlib.ExitStack() as es:
            xt = [self.sb(es, f"xa{i}", [128, D]) for i in range(2)]
            xo = [self.sb(es, f"xo{i}", [128, 8, 128], BF16) for i in range(2)]
            for i in range(NT):
                b = xt[i % 2]
                fw.dma("sp", b[:], x_ap[i * 128:(i + 1) * 128, :],
                       reads=[x_dep] if x_dep else [], writes=[b])
                o = xo[i % 2]
                for hh in range(2):
                    ps = self.next_ps()
                    for c in range(4):
                        cc = hh * 4 + c
                        fw.pe(lambda h, ps=ps, c=c, cc=cc, b=b: h.transpose(
                            ps[:, c * 128:(c + 1) * 128], b[:, cc * 128:(cc + 1) * 128], self.ident[:]),
                            r=[b, self.ident], w=[ps])
                    fw.dve(lambda h, ps=ps, o=o, hh=hh: h.tensor_copy(
                        out=o[:, hh * 4:(hh + 1) * 4, :],
                        in_=ps[:, :].rearrange("p (c t) -> p c t", c=4)), r=[ps], w=[(o, hh)])
                fw.dma("sp", xT_d[:, i * 128:(i + 1) * 128].rearrange("(c p) t -> p c t", p=128),
                       o[:], reads=[o], writes=[(xT_d, i)])

    def load_xT(self, es, xT_d):
        xT = self.sb(es, "xT", [128, 8, S], BF16)
        for c in range(8):
            self.fw.dma("sp", xT[:, c, :], xT_d[c * 128:(c + 1) * 128, :], reads=[xT_d], writes=[(xT, c)])
        return xT

    def load_w_cols(self, wt, l, c0, ncols, key=None):
        src = self.inp["w_in"][l, :, c0:c0 + ncols].rearrange("(kc p) c -> p kc c", p=128)
        self.fw.dma("pool", wt[:, :, 0:ncols], src, writes=[(wt, key) if key is not None else wt])

    def proj_fm(self, wt, xT, j0, M, n, ps):
        for kc in range(8):
            self.fw.pe(lambda h, kc=kc: h.matmul(ps[0:M, :], lhsT=wt[:, kc, j0:j0 + M],
                                                  rhs=xT[:, kc, n * 512:(n + 1) * 512],
                                                  start=(kc == 0), stop=(kc == 7)),
                       r=[wt, xT], w=[ps])

    def gelu_tanh(self, es_tmp, x, out, tmp):
        fw = self.fw
        fw.dve(lambda h: h.tensor_tensor(out=tmp[:], in0=x[:], in1=x[:], op=ALU.mult), r=[x], w=[tmp])
        fw.dve(lambda h: h.tensor_scalar(out=tmp[:], in0=tmp[:], scalar1=0.044715, scalar2=1.0,
                                         op0=ALU.mult, op1=ALU.add), r=[tmp], w=[tmp])
        fw.dve(lambda h: h.tensor_tensor(out=tmp[:], in0=tmp[:], in1=x[:], op=ALU.mult), r=[tmp, x], w=[tmp])
        fw.act(lambda h: h.activation(out=tmp[:], in_=tmp[:], func=AF.Sigmoid, scale=1.5957691216057308),
               r=[tmp], w=[tmp])
        fw.dve(lambda h: h.tensor_tensor(out=out[:], in0=tmp[:], in1=x[:], op=ALU.mult), r=[tmp, x], w=[out])

    def phase_rg(self, l, xT_d):
        fw, nc = self.fw, self.nc
        yT = self.scr["yT"]
        with contextlib.ExitStack() as es:
            xT = self.load_xT(es, xT_d)
            vec = self.sb(es, "rgvec", [128, 4, 8])
            fw.dma("sp", vec[:], self.inp["rg_vec"][l], writes=[vec])
            cst = self.sb(es, "rgc", [128, 4, 4])
            wts = [self.sb(es, f"rgw{i}", [128, 8, 256], BF16) for i in range(2)]
            wbd = [self.sb(es, f"rgbd{i}", [128, 2, 128]) for i in range(2)]
            names = ["xb", "gb", "xc", "r", "i", "a", "m", "h"]
            tl = {n: self.sb(es, "rg_" + n, [128, S]) for n in names}
            yo = self.sb(es, "rg_yo", [128, S], BF16)
            xb, gb, xc, r_, i_, a_, m_, h_ = (tl[n] for n in names)
            for c in range(4):
                wt = wts[c % 2]
                bd = wbd[c % 2]
                self.load_w_cols(wt, l, C_RG + c * 128, 128, key=0)
                src = self.inp["w_in"][l, :, C_RG + 512 + c * 128:C_RG + 512 + (c + 1) * 128].rearrange(
                    "(kc p) c -> p kc c", p=128)
                fw.dma("pool", wt[:, :, 128:256], src, writes=[(wt, 1)])
                fw.pool(lambda h: h.memset(bd[:], 0.0), w=[bd])
                for j, nm in enumerate(("rg_wa", "rg_wx")):
                    for hb in range(2):
                        fw.dma("sp", bd[hb * 64:(hb + 1) * 64, j, hb * 64:(hb + 1) * 64],
                               self.inp[nm][l, 2 * c + hb], writes=[bd])
                for j, dst in enumerate((xb, gb)):
                    for n in range(4):
                        ps = self.next_ps()
                        self.proj_fm(wt, xT, j * 128, 128, n, ps)
                        fw.act(lambda h, ps=ps, dst=dst, n=n: h.copy(out=dst[:, n * 512:(n + 1) * 512], in_=ps[:, :]),
                               r=[ps], w=[(dst, n)])
                v = lambda k: vec[:, c, k:k + 1]
                fw.dve(lambda h: h.tensor_scalar(out=xc[:], in0=xb[:], scalar1=v(0), scalar2=v(4),
                                                 op0=ALU.mult, op1=ALU.add), r=[xb, vec], w=[xc])
                for j in range(1, 4):
                    fw.dve(lambda h, j=j: h.scalar_tensor_tensor(out=xc[:, j:], in0=xb[:, 0:S - j], scalar=v(j),
                                                                 in1=xc[:, j:], op0=ALU.mult, op1=ALU.add),
                           r=[xb, xc, vec], w=[xc])
                cc = lambda k: cst[:, c, k:k + 1]
                fw.act(lambda h: h.activation(out=cc(0), in_=v(7), func=AF.Exp, scale=-1.0), r=[vec], w=[cst])
                fw.act(lambda h: h.activation(out=cc(1), in_=cc(0), func=AF.Ln, bias=1.0, scale=1.0), r=[cst], w=[cst])
                fw.dve(lambda h: h.tensor_scalar(out=cc(2), in0=cc(1), scalar1=-8.0, scalar2=None, op0=ALU.mult),
                       r=[cst], w=[cst])
                fw.dve(lambda h: h.tensor_scalar(out=cc(3), in0=cc(1), scalar1=-16.0, scalar2=None, op0=ALU.mult),
                       r=[cst], w=[cst])
                for j, (dst, bk) in enumerate(((r_, 5), (i_, 6))):
                    for n in range(4):
                        ps = self.next_ps()
                        fw.pe(lambda h, ps=ps, j=j, n=n: h.matmul(ps[:, :], lhsT=bd[:, j, :],
                                                                   rhs=xc[:, n * 512:(n + 1) * 512],
                                                                   start=True, stop=True), r=[bd, xc], w=[ps])
                        fw.act(lambda h, ps=ps, dst=dst, n=n, bk=bk: h.activation(
                            out=dst[:, n * 512:(n + 1) * 512], in_=ps[:, :], func=AF.Sigmoid, bias=v(bk), scale=1.0),
                            r=[ps, vec], w=[(dst, n)])
                fw.act(lambda h: h.activation(out=a_[:], in_=r_[:], func=AF.Exp, scale=cc(2)), r=[r_, cst], w=[a_])
                fw.act(lambda h: h.activation(out=m_[:], in_=r_[:], func=AF.Exp, scale=cc(3)), r=[r_, cst], w=[m_])
                fw.act(lambda h: h.activation(out=m_[:], in_=m_[:], func=AF.Sqrt, bias=1.0, scale=-1.0), r=[m_], w=[m_])
                fw.dve(lambda h: h.memset(m_[:, 0:1], 1.0), w=[m_])
                fw.dve(lambda h: h.tensor_tensor(out=i_[:], in0=i_[:], in1=xc[:], op=ALU.mult), r=[i_, xc], w=[i_])
                fw.dve(lambda h: h.tensor_tensor(out=i_[:], in0=i_[:], in1=m_[:], op=ALU.mult), r=[i_, m_], w=[i_])
                fw.dve(lambda h: h.tensor_tensor_scan(out=h_[:], data0=a_[:], data1=i_[:], initial=0.0,
                                                      op0=ALU.mult, op1=ALU.add), r=[a_, i_], w=[h_])
                self.gelu_tanh(es, gb, r_, m_)
                fw.dve(lambda h: h.tensor_tensor(out=yo[:], in0=h_[:], in1=r_[:], op=ALU.mult), r=[h_, r_], w=[yo])
                fw.dma("sp", yT[0, c * 128:(c + 1) * 128, :], yo[:], reads=[yo], writes=[(yT, ("a", c))])


    def phase_rwkv_prep(self, l, xT_d):
        fw = self.fw
        rwT, vfirst = self.scr["rwT"], self.scr["vfirst"]
        with contextlib.ExitStack() as es:
            xTp = self.sb(es, "xTp", [128, 8, S + 2], BF16)
            fw.dve(lambda h: h.memset(xTp[:, :, 0:2], 0.0), w=[(xTp, "z")])
            for c in range(8):
                fw.dma("sp", xTp[:, c, 2:S + 2], xT_d[c * 128:(c + 1) * 128, :], reads=[xT_d], writes=[(xTp, c)])
            mixb = self.sb(es, "mixb", [128, 2, 1792])
            fw.dma("sp", mixb[:, 0, :], self.inp["rw_mix"][l].to_broadcast([128, 1792]), writes=[mixb])
            fw.dve(lambda h: h.tensor_scalar(out=mixb[:, 1, :], in0=mixb[:, 0, :], scalar1=-1.0, scalar2=1.0,
                                             op0=ALU.mult, op1=ALU.add), r=[mixb], w=[mixb])
            vec = self.sb(es, "rwvec", [128, 4, 8])
            fw.dma("sp", vec[:], self.inp["rw_vec"][l], writes=[vec])
            dv = self.sb(es, "rwdv", [128, 4, 2])
            fw.dve(lambda h: h.tensor_scalar(out=dv[:, :, 0:1], in0=vec[:, :, 0:1], scalar1=-1.0, scalar2=None,
                                             op0=ALU.mult), r=[vec], w=[dv])
            fw.dve(lambda h: h.tensor_scalar(out=dv[:, :, 1:2], in0=vec[:, :, 3:4], scalar1=-1.0, scalar2=1.0,
                                             op0=ALU.mult, op1=ALU.add), r=[vec], w=[dv])
            w2a2 = self.sb(es, "w2a2", [128, 512])
            fw.dma("sp", w2a2[:], self.inp["rw_w2a2"][l], writes=[w2a2])
            g2 = self.sb(es, "g2", [128, 512])
            fw.dma("sp", g2[:], self.inp["rw_g2"][l], writes=[g2])
            obd = self.sb(es, "obd", [128, 128])
            fw.dma("sp", obd[:], self.inp["onesbd"][:, :], writes=[obd])
            rmask = self.sb(es, "rmask", [128, S])
            fw.pool(lambda h: h.memset(rmask[:], 1.0), w=[rmask])
            fw.pool(lambda h: h.memset(rmask[:, :].rearrange("p (c t) -> p c t", t=128)[:, :, 0:1], 0.0), w=[rmask])
            wf = [self.sb(es, f"wf{i}", [128, 8, 128]) for i in range(2)]
            wm = [self.sb(es, f"wm{i}", [128, 2, 8, 128], BF16) for i in range(2)]
            self._rw_cnt = 0

            def proj_shift(col0, dst_fn):
                i = self._rw_cnt % 2
                self._rw_cnt += 1
                src = self.inp["w_in"][l, :, C_RW + col0:C_RW + col0 + 128].rearrange("(kc p) c -> p kc c", p=128)
                fw.dma("sp", wf[i][:], src, writes=[wf[i]])
                for j in range(2):
                    fw.dve(lambda h: h.tensor_tensor(
                        out=wm[i][:, j, :, :], in0=wf[i][:],
                        in1=mixb[:, j, col0:col0 + 128][:, None, :].to_broadcast([128, 8, 128]), op=ALU.mult),
                        r=[wf[i], mixb], w=[(wm[i], j)])
                for n in range(4):
                    ps = self.next_ps()
                    for j in range(2):
                        off = 1 + j + n * 512
                        for kc in range(8):
                            fw.pe(lambda h: h.matmul(ps[:, :], lhsT=wm[i][:, j, kc, :], rhs=xTp[:, kc, off:off + 512],
                                                     start=(j == 0 and kc == 0), stop=(j == 1 and kc == 7)),
                                  r=[(wm[i], j), xTp], w=[ps])
                    dst_fn(n, ps)

            nb = 13
            B = [self.sb(es, f"rwb{i}", [128, S]) for i in range(nb)]
            wct = self.sb(es, "wct", [128, NT])
            xwa, sgx = B[11], B[12]
            sl = lambda n: slice(n * 512, (n + 1) * 512)
            def d_xwa(n, ps):
                fw.act(lambda h: h.activation(out=xwa[0:64, sl(n)], in_=ps[0:64, :], func=AF.Tanh), r=[ps], w=[(xwa, n)])
                fw.act(lambda h: h.copy(out=xwa[64:128, sl(n)], in_=ps[64:128, :]), r=[ps], w=[(xwa, n)])
            proj_shift(1536, d_xwa)
            proj_shift(1664, lambda n, ps: fw.act(lambda h: h.activation(out=sgx[:, sl(n)], in_=ps[:, :], func=AF.Sigmoid),
                                                  r=[ps], w=[(sgx, n)]))
            lat = None
            if l > 0:
                lat = self.sb(es, "lat", [32, S])
                v1 = self.sb(es, "v1", [128, 4, 32])
                fw.dma("sp", v1[:], self.inp["rw_v1"][l - 1].rearrange("(c p) j -> p c j", p=128), writes=[v1])
                v2 = self.sb(es, "v2", [32, 512])
                fw.dma("sp", v2[:], self.inp["rw_v2"][l - 1], writes=[v2])
                accb = [self.ps[4 + n] for n in range(4)]
                self.ps_reserved = set(accb)
                for c in range(4):
                    def d_v(n, ps, c=c):
                        fw.act(lambda h: h.copy(out=B[0][:, sl(n)], in_=ps[:, :]), r=[ps], w=[(B[0], n)])
                        fw.pe(lambda h: h.matmul(accb[n][0:32, :], lhsT=v1[:, c, :], rhs=B[0][:, sl(n)],
                                                 start=(c == 0), stop=(c == 3)), r=[v1, (B[0], n)], w=[accb[n]])
                    proj_shift(1024 + c * 128, d_v)
                for n in range(4):
                    fw.act(lambda h: h.copy(out=lat[:, sl(n)], in_=accb[n][0:32, :]), r=[accb[n]], w=[(lat, n)])
                self.ps_reserved = set()
            for c in range(4):
                rT, kT, vT, e2, cum, ep, em, epv, a_, kk, km = B[:11]
                v = lambda k: vec[:, c, k:k + 1]
                cs_ = slice(c * 128, (c + 1) * 128)
                for col0, dst in ((c * 128, rT), (512 + c * 128, kT), (1024 + c * 128, vT)):
                    proj_shift(col0, lambda n, ps, dst=dst: fw.act(
                        lambda h: h.copy(out=dst[:, sl(n)], in_=ps[:, :]), r=[ps], w=[(dst, n)]))
                for n in range(4):
                    ps = self.next_ps()
                    fw.pe(lambda h: h.matmul(ps[:, :], lhsT=w2a2[0:64, cs_], rhs=xwa[0:64, sl(n)], start=True, stop=True),
                          r=[w2a2, (xwa, n)], w=[ps])
                    fw.act(lambda h: h.activation(out=e2[:, sl(n)], in_=ps[:, :], func=AF.Exp, bias=dv[:, c, 0:1], scale=-1.0),
                           r=[ps, dv], w=[(e2, n)])
                    fw.act(lambda h: h.activation(out=e2[:, sl(n)], in_=e2[:, sl(n)], func=AF.Ln, bias=1.0, scale=1.0),
                           r=[(e2, n)], w=[(e2, n)])
                    fw.act(lambda h: h.activation(out=e2[:, sl(n)], in_=e2[:, sl(n)], func=AF.Exp, bias=-0.5, scale=-1.0),
                           r=[(e2, n)], w=[(e2, n)])
                    ps = self.next_ps()
                    fw.pe(lambda h: h.matmul(ps[:, :], lhsT=w2a2[64:128, cs_], rhs=xwa[64:128, sl(n)], start=True, stop=True),
                          r=[w2a2, (xwa, n)], w=[ps])
                    fw.act(lambda h: h.activation(out=a_[:, sl(n)], in_=ps[:, :], func=AF.Sigmoid, bias=v(1), scale=1.0),
                           r=[ps, vec], w=[(a_, n)])
                    ps = self.next_ps()
                    fw.pe(lambda h: h.matmul(ps[:, :], lhsT=g2[:, cs_], rhs=sgx[:, sl(n)], start=True, stop=True),
                          r=[g2, (sgx, n)], w=[ps])
                    fw.act(lambda h: h.copy(out=ep[:, sl(n)], in_=ps[:, :]), r=[ps], w=[(ep, n)])
                    if l > 0:
                        ps = self.next_ps()
                        fw.pe(lambda h: h.matmul(ps[:, :], lhsT=v2[0:32, cs_], rhs=lat[:, sl(n)], start=True, stop=True),
                              r=[v2, (lat, n)], w=[ps])
                        fw.act(lambda h: h.activation(out=em[:, sl(n)], in_=ps[:, :], func=AF.Sigmoid, bias=v(7), scale=1.0),
                               r=[ps, vec], w=[(em, n)])
                fw.dma("sp", rwT[6, cs_, :], ep[:], reads=[ep], writes=[(rwT, (6, c))])
                if l > 0:
                    fw.dma("sp", epv[:], vfirst[cs_, :], reads=[vfirst], writes=[epv])
                    fw.dve(lambda h: h.tensor_tensor(out=epv[:], in0=epv[:], in1=vT[:], op=ALU.subtract), r=[epv, vT], w=[epv])
                    fw.dve(lambda h: h.tensor_tensor(out=epv[:], in0=epv[:], in1=em[:], op=ALU.mult), r=[epv, em], w=[epv])
                    fw.dve(lambda h: h.tensor_tensor(out=vT[:], in0=vT[:], in1=epv[:], op=ALU.add), r=[epv, vT], w=[vT])
                else:
                    fw.dma("sp", vfirst[cs_, :], vT[:], reads=[vT], writes=[(vfirst, c)])
                fw.dma("sp", rwT[4, cs_, :], vT[:], reads=[vT], writes=[(rwT, (4, c))])
                fw.dve(lambda h: h.tensor_tensor_scan(out=cum[:], data0=rmask[:], data1=e2[:], initial=0.0,
                                                      op0=ALU.mult, op1=ALU.add), r=[rmask, e2], w=[cum])
                fw.act(lambda h: h.activation(out=ep[:], in_=cum[:], func=AF.Exp, scale=-1.0), r=[cum], w=[ep])
                fw.act(lambda h: h.activation(out=em[:], in_=cum[:], func=AF.Exp, scale=1.0), r=[cum], w=[em])
                fw.dve(lambda h: h.tensor_tensor(out=epv[:], in0=cum[:], in1=e2[:], op=ALU.subtract), r=[cum, e2], w=[epv])
                fw.act(lambda h: h.activation(out=epv[:], in_=epv[:], func=AF.Exp, scale=-1.0), r=[epv], w=[epv])
                fw.dve(lambda h: h.tensor_scalar(out=kk[:], in0=kT[:], scalar1=v(2), scalar2=None, op0=ALU.mult),
                       r=[kT, vec], w=[kk])
                fw.dve(lambda h: h.tensor_tensor(out=cum[:], in0=kk[:], in1=kk[:], op=ALU.mult), r=[kk], w=[cum])
                for n in range(4):
                    ps = self.next_ps()
                    fw.pe(lambda h: h.matmul(ps[:, :], lhsT=obd[:, :], rhs=cum[:, sl(n)], start=True, stop=True),
                          r=[obd, cum], w=[ps])
                    fw.act(lambda h: h.activation(out=e2[:, sl(n)], in_=ps[:, :], func=AF.Sqrt), r=[ps], w=[(e2, n)])
                fw.dve(lambda h: h.tensor_scalar(out=e2[:], in0=e2[:], scalar1=1e-12, scalar2=None, op0=ALU.max),
                       r=[e2], w=[e2])
                fw.dve(lambda h: h.reciprocal(out=e2[:], in_=e2[:]), r=[e2], w=[e2])
                fw.dve(lambda h: h.tensor_tensor(out=kk[:], in0=kk[:], in1=e2[:], op=ALU.mult), r=[kk, e2], w=[kk])
                fw.dve(lambda h: h.tensor_scalar(out=km[:], in0=a_[:], scalar1=v(3), scalar2=dv[:, c, 1:2],
                                                 op0=ALU.mult, op1=ALU.add), r=[a_, vec, dv], w=[km])
                fw.dve(lambda h: h.tensor_tensor(out=km[:], in0=km[:], in1=kT[:], op=ALU.mult), r=[km, kT], w=[km])
                fw.dve(lambda h: h.scalar_tensor_tensor(out=cum[:], in0=rT[:], scalar=v(4), in1=km[:],
                                                        op0=ALU.mult, op1=ALU.mult), r=[rT, km, vec], w=[cum])
                for n in range(4):
                    ps = self.next_ps()
                    fw.pe(lambda h: h.matmul(ps[:, :], lhsT=obd[:, :], rhs=cum[:, sl(n)], start=True, stop=True),
                          r=[obd, cum], w=[ps])
                    fw.dve(lambda h: h.tensor_tensor(out=e2[:, sl(n)], in0=ps[:, :], in1=vT[:, sl(n)], op=ALU.mult),
                           r=[ps, vT], w=[(e2, n)])
                fw.dma("sp", rwT[5, cs_, :], e2[:], reads=[e2], writes=[(rwT, (5, c))])
                fw.dve(lambda h: h.scalar_tensor_tensor(out=epv[:], in0=kk[:], scalar=-1.0, in1=epv[:],
                                                        op0=ALU.mult, op1=ALU.mult), r=[kk, epv], w=[epv])
                fw.dma("sp", rwT[0, cs_, :], epv[:], reads=[epv], writes=[(rwT, (0, c))])
                fw.dve(lambda h: h.tensor_tensor(out=rT[:], in0=rT[:], in1=ep[:], op=ALU.mult), r=[rT, ep], w=[rT])
                fw.dma("sp", rwT[1, cs_, :], rT[:], reads=[rT], writes=[(rwT, (1, c))])
                fw.dve(lambda h: h.tensor_tensor(out=kk[:], in0=kk[:], in1=a_[:], op=ALU.mult), r=[kk, a_], w=[kk])
                fw.dve(lambda h: h.tensor_tensor(out=kk[:], in0=kk[:], in1=em[:], op=ALU.mult), r=[kk, em], w=[kk])
                fw.dma("sp", rwT[2, cs_, :], kk[:], reads=[kk], writes=[(rwT, (2, c))])
                fw.dve(lambda h: h.tensor_tensor(out=km[:], in0=km[:], in1=em[:], op=ALU.mult), r=[km, em], w=[km])
                fw.dma("sp", rwT[3, cs_, :], km[:], reads=[km], writes=[(rwT, (3, c))])
                fw.dve(lambda h: h.tensor_copy(out=wct[:], in_=ep[:, :].rearrange("p (c t) -> p c t", t=128)[:, :, 127]),
                       r=[ep], w=[wct])
                fw.dma("sp", self.scr["rw_wc"][cs_, :], wct[:], reads=[wct], writes=[(self.scr["rw_wc"], c)])

    def phase_rwkv_scan(self, l):
        fw = self.fw
        rwT, yT = self.scr["rwT"], self.scr["yT"]
        ident = self.ident
        with contextlib.ExitStack() as es:
            masks = self.sb(es, "rwmasks", [128, 8, 128])
            fw.dma("sp", masks[:], self.inp["rw_masks"][:, :, :], writes=[masks])
            vec = self.sb(es, "rwvec2", [128, 4, 8])
            fw.dma("sp", vec[:], self.inp["rw_vec"][l], writes=[vec])
            wc = self.sb(es, "wc", [128, 4, NT])
            fw.dma("sp", wc[:], self.scr["rw_wc"][:, :].rearrange("(c p) n -> p c n", p=128),
                   reads=[self.scr["rw_wc"]], writes=[wc])
            ST = self.sb(es, "ST", [128, 4, 64])
            fw.dve(lambda h: h.memset(ST[:], 0.0), w=[ST])
            fm = [self.sb(es, f"fm{i}", [128, 7, 4, 128]) for i in range(2)]
            tok = [self.sb(es, f"tok{i}", [128, 3, 512]) for i in range(2)]
            M4s = [self.sb(es, f"M4a{i}", [128, 8, 4, 128]) for i in range(2)]
            Las = [self.sb(es, f"La{i}", [128, 2, 8, 128]) for i in range(2)]
            Xas = [self.sb(es, f"Xa{i}", [128, 7, 8, 128]) for i in range(2)]
            Ua = self.sb(es, "Ua", [128, 8, 64])
            tSa = self.sb(es, "tSa", [128, 4, 64])
            Ys = [self.sb(es, f"Y{i}", [128, 8, 64]) for i in range(2)]
            s8 = self.sb(es, "s8", [128, 8])
            r8 = self.sb(es, "r8", [128, 8])
            Yc = self.sb(es, "Yc", [128, 8, 64])
            sq = self.sb(es, "sq", [128, 8, 64])
            o32 = self.sb(es, "o32", [128, 4, 128])
            obs = [self.sb(es, f"ob{i}", [128, 4, 128], BF16) for i in range(2)]

            def hrow(hd):
                return slice((hd % 2) * 64, (hd % 2) * 64 + 64)

            def pv4(ap3, par):
                return ap3.rearrange("p (c a) t -> p c a t", a=2)[:, :, par, :]

            def prep_stages(c):
                cs = slice(c * 128, (c + 1) * 128)
                F, TK, M4, La, Xa = fm[c % 2], tok[c % 2], M4s[c % 2], Las[c % 2], Xas[c % 2]
                st = []

                def load():
                    for q in range(7):
                        fw.dma("sp", F[:, q, :, :], rwT[q, :, cs].rearrange("(ct p) t -> p ct t", p=128),
                               reads=[rwT], writes=[(F, q)])
                    for j, q in enumerate((4, 2, 3)):
                        ps = self.next_ps()
                        for ct in range(4):
                            fw.pe(lambda h: h.transpose(ps[:, ct * 128:(ct + 1) * 128], F[:, q, ct, :], ident[:]),
                                  r=[(F, q), ident], w=[ps])
                        fw.act(lambda h: h.copy(out=TK[:, j, :], in_=ps[:, :]), r=[ps], w=[(TK, j)])
                st.append(load)

                def products(par):
                    def f():
                        heads = [par + 2 * i for i in range(4)]
                        for hd in heads:
                            ct, hr = hd // 2, hrow(hd)
                            At, Rt, Bt, Kt = (F[hr, q, ct, :] for q in range(4))
                            rA, rR, rB, rK = ((F, q) for q in range(4))
                            ps4 = self.next_ps()
                            for i, (lt, rh, rl, rr) in enumerate(((Bt, At, rB, rA), (Kt, At, rK, rA),
                                                                  (Bt, Rt, rB, rR), (Kt, Rt, rK, rR))):
                                fw.pe(lambda h: h.matmul(ps4[:, i * 128:(i + 1) * 128], lhsT=lt, rhs=rh,
                                                         start=True, stop=True), r=[rl, rr], w=[ps4])
                            fw.dve(lambda h: h.tensor_tensor(out=M4[:, hd, :, :],
                                                             in0=ps4[:, :].rearrange("p (a t) -> p a t", a=4),
                                                             in1=masks[:, 0:4, :], op=ALU.mult),
                                   r=[ps4, masks], w=[(M4, hd)])
                        psL = self.next_ps()
                        for hh, hd in enumerate(heads):
                            ct, hr = hd // 2, hrow(hd)
                            fw.pe(lambda h: h.matmul(psL[:, hh * 128:(hh + 1) * 128], lhsT=F[hr, 0, ct, :],
                                                     rhs=F[hr, 2, ct, :], start=True, stop=True),
                                  r=[(F, 0), (F, 2)], w=[psL])
                        fw.dve(lambda h: h.tensor_tensor(out=pv4(La[:, 0, :, :], par),
                                                         in0=psL[:, :].rearrange("p (a t) -> p a t", a=4),
                                                         in1=masks[:, 4:8, :], op=ALU.mult),
                               r=[psL, masks], w=[(La, (0, par))])
                    return f
                st.append(products(0))
                st.append(products(1))

                def squaring(j):
                    def f():
                        for par in range(2):
                            heads = [par + 2 * i for i in range(4)]
                            psX = self.next_ps()
                            psL2 = self.next_ps() if j < 5 else None
                            for hh, hd in enumerate(heads):
                                Xj = M4[:, hd, 0, :] if j == 0 else Xa[:, j, hd, :]
                                rX = (M4, hd) if j == 0 else (Xa, (j, par))
                                Lj = La[:, j % 2, hd, :]
                                rL = (La, (j % 2, par))
                                fw.pe(lambda h: h.matmul(psX[:, hh * 128:(hh + 1) * 128], lhsT=Lj, rhs=Xj,
                                                         start=True, stop=True), r=[rL, rX], w=[psX])
                                if j < 5:
                                    fw.pe(lambda h: h.matmul(psL2[:, hh * 128:(hh + 1) * 128], lhsT=Xj, rhs=Lj,
                                                             start=True, stop=True), r=[rL, rX], w=[psL2])
                            fw.act(lambda h: h.copy(out=pv4(Xa[:, j + 1, :, :], par),
                                                    in_=psX[:, :].rearrange("p (a t) -> p a t", a=4)),
                                   r=[psX], w=[(Xa, (j + 1, par))])
                            if j < 5:
                                fw.act(lambda h: h.copy(out=pv4(La[:, (j + 1) % 2, :, :], par),
                                                        in_=psL2[:, :].rearrange("p (a t) -> p a t", a=4)),
                                       r=[psL2], w=[(La, ((j + 1) % 2, par))])
                    return f
                for j in range(6):
                    st.append(squaring(j))
                return st

            def chain_stages(c):
                F, TK, M4, Xa, Y = fm[c % 2], tok[c % 2], M4s[c % 2], Xas[c % 2], Ys[c % 2]
                st = []
                Vt = lambda hd: TK[:, 0, hd * 64:(hd + 1) * 64]
                rXj = lambda j, hd: (M4, hd) if j == 0 else (Xa, (j, hd % 2))
                Xj = lambda j, hd: M4[:, hd, 0, :] if j == 0 else Xa[:, j, hd, :]

                def zstage():
                    for par in range(2):
                        psz = self.next_ps()
                        for hh in range(4):
                            hd = par + 2 * hh
                            ct, hr = hd // 2, hrow(hd)
                            o = psz[:, hh * 64:(hh + 1) * 64]
                            fw.pe(lambda h: h.matmul(o, lhsT=M4[:, hd, 1, :], rhs=Vt(hd), start=True, stop=False),
                                  r=[(M4, hd), (TK, 0)], w=[psz])
                            fw.pe(lambda h: h.matmul(o, lhsT=F[hr, 0, ct, :], rhs=ST[hr, ct, :], start=False, stop=True),
                                  r=[(F, 0), ST], w=[psz])
                        fw.act(lambda h: h.copy(out=pv4(Ua[:, :, :], par),
                                                in_=psz[:, 0:256].rearrange("p (a v) -> p a v", a=4)),
                               r=[psz], w=[(Ua, par)])
                st.append(zstage)

                def ustage(j):
                    def f():
                        psu = self.next_ps()
                        for hd in range(8):
                            fw.pe(lambda h: h.matmul(psu[:, hd * 64:(hd + 1) * 64], lhsT=Xj(j, hd), rhs=Ua[:, hd, :],
                                                     start=True, stop=True), r=[rXj(j, hd), Ua], w=[psu])
                        fw.dve(lambda h: h.tensor_tensor(out=Ua[:, :, :].rearrange("p h v -> p (h v)"), in0=psu[:, :],
                                                         in1=Ua[:, :, :].rearrange("p h v -> p (h v)"), op=ALU.add),
                               r=[psu, Ua], w=[Ua])
                    return f
                for j in range(7):
                    st.append(ustage(j))

                def ystage():
                    for par in range(2):
                        psy = self.next_ps()
                        for hh in range(4):
                            hd = par + 2 * hh
                            ct, hr = hd // 2, hrow(hd)
                            o = psy[:, hh * 64:(hh + 1) * 64]
                            fw.pe(lambda h: h.matmul(o, lhsT=F[hr, 1, ct, :], rhs=ST[hr, ct, :], start=True, stop=False),
                                  r=[(F, 1), ST], w=[psy])
                            fw.pe(lambda h: h.matmul(o, lhsT=M4[:, hd, 2, :], rhs=Ua[:, hd, :], start=False, stop=False),
                                  r=[(M4, hd), Ua], w=[psy])
                            fw.pe(lambda h: h.matmul(o, lhsT=M4[:, hd, 3, :], rhs=Vt(hd), start=False, stop=True),
                                  r=[(M4, hd), (TK, 0)], w=[psy])
                        fw.act(lambda h: h.copy(out=pv4(Y[:, :, :], par),
                                                in_=psy[:, 0:256].rearrange("p (a v) -> p a v", a=4)),
                               r=[psy], w=[(Y, par)])
                    pss = self.next_ps()
                    for hd in range(8):
                        ct = hd // 2
                        o = pss[:, hd * 64:(hd + 1) * 64]
                        fw.pe(lambda h: h.matmul(o, lhsT=TK[:, 1, ct * 128:(ct + 1) * 128], rhs=Ua[:, hd, :],
                                                 start=True, stop=False), r=[(TK, 1), Ua], w=[pss])
                        fw.pe(lambda h: h.matmul(o, lhsT=TK[:, 2, ct * 128:(ct + 1) * 128], rhs=Vt(hd),
                                                 start=False, stop=True), r=[(TK, 2), (TK, 0)], w=[pss])
                    pv = pss[:, :].rearrange("p (c a v) -> p c a v", c=4, a=2)
                    for par in range(2):
                        hr = slice(par * 64, par * 64 + 64)
                        fw.dve(lambda h: h.tensor_tensor(out=tSa[hr, :, :], in0=pv[hr, :, par, :], in1=ST[hr, :, :],
                                                         op=ALU.add), r=[pss, ST], w=[(tSa, par)])
                        fw.dve(lambda h: h.tensor_tensor(out=ST[hr, :, :], in0=tSa[hr, :, :],
                                                         in1=wc[hr, :, c:c + 1].to_broadcast([64, 4, 64]), op=ALU.mult),
                               r=[(tSa, par), wc], w=[ST])
                st.append(ystage)
                return st

            def epilogue(c):
                cs = slice(c * 128, (c + 1) * 128)
                F, Y, ob = fm[c % 2], Ys[c % 2], obs[c % 2]
                fw.dve(lambda h: h.tensor_reduce(out=s8[:], in_=Y[:], axis=AX.X, op=ALU.add), r=[Y], w=[s8])
                fw.dve(lambda h: h.tensor_scalar(out=s8[:], in0=s8[:], scalar1=1.0 / 64, scalar2=None, op0=ALU.mult),
                       r=[s8], w=[s8])
                fw.dve(lambda h: h.tensor_tensor(out=Yc[:], in0=Y[:], in1=s8[:, :, None].to_broadcast([128, 8, 64]),
                                                 op=ALU.subtract), r=[Y, s8], w=[Yc])
                fw.dve(lambda h: h.tensor_tensor(out=sq[:], in0=Yc[:], in1=Yc[:], op=ALU.mult), r=[Yc], w=[sq])
                fw.dve(lambda h: h.tensor_reduce(out=r8[:], in_=sq[:], axis=AX.X, op=ALU.add), r=[sq], w=[r8])
                fw.act(lambda h: h.activation(out=r8[:], in_=r8[:], func=AF.Sqrt, bias=64e-5, scale=1.0 / 64), r=[r8], w=[r8])
                fw.dve(lambda h: h.reciprocal(out=r8[:], in_=r8[:]), r=[r8], w=[r8])
                fw.dve(lambda h: h.tensor_tensor(out=Yc[:], in0=Yc[:], in1=r8[:, :, None].to_broadcast([128, 8, 64]),
                                                 op=ALU.mult), r=[Yc, r8], w=[Yc])
                pst = self.next_ps()
                ycf = Yc[:, :, :].rearrange("p h d -> p (h d)")
                for ct in range(4):
                    fw.pe(lambda h: h.transpose(pst[:, ct * 128:(ct + 1) * 128], ycf[:, ct * 128:(ct + 1) * 128], ident[:]),
                          r=[Yc, ident], w=[pst])
                for ct in range(4):
                    fw.dve(lambda h: h.tensor_scalar(out=o32[:, ct, :], in0=pst[:, ct * 128:(ct + 1) * 128],
                                                     scalar1=vec[:, ct, 5:6], scalar2=vec[:, ct, 6:7],
                                                     op0=ALU.mult, op1=ALU.add), r=[pst, vec], w=[(o32, ct)])
                fw.dve(lambda h: h.tensor_tensor(out=o32[:], in0=o32[:], in1=F[:, 5, :, :], op=ALU.add), r=[o32, (F, 5)], w=[o32])
                fw.dve(lambda h: h.tensor_tensor(out=ob[:], in0=o32[:], in1=F[:, 6, :, :], op=ALU.mult), r=[o32, (F, 6)], w=[ob])
                fw.dma("sp", yT[1, :, cs].rearrange("(ct p) t -> p ct t", p=128), ob[:], reads=[ob],
                       writes=[(yT, ("b", c))])

            import os as _os
            _CUT = int(_os.environ.get("RW_CUT", 10 ** 9))
            _cnt = [0]

            def call(f):
                _cnt[0] += 1
                if _cnt[0] <= _CUT:
                    f()
            for f in prep_stages(0):
                call(f)
            for c in range(NT):
                ch = chain_stages(c)
                pr = prep_stages(c + 1) if c + 1 < NT else []
                for i in range(max(len(ch), len(pr))):
                    if i < len(pr):
                        call(pr[i])
                    if i < len(ch):
                        call(ch[i])
                call(lambda: epilogue(c))

    def ln_tile(self, z, st, mv, gbc, bbc, gi):
        fw = self.fw
        for c in range(2):
            fw.dve(lambda h: h.bn_stats(out=st[:, c, :], in_=z[:, c * 512:(c + 1) * 512]), r=[z], w=[(st, c)])
        fw.dve(lambda h: h.bn_aggr(out=mv[:, 0:2], in_=st[:, :, :].rearrange("p a b -> p (a b)")), r=[st], w=[mv])
        fw.act(lambda h: h.activation(out=mv[:, 2:3], in_=mv[:, 1:2], func=AF.Sqrt, bias=1e-5, scale=1.0), r=[mv], w=[mv])
        fw.dve(lambda h: h.reciprocal(out=mv[:, 2:3], in_=mv[:, 2:3]), r=[mv], w=[mv])
        fw.dve(lambda h: h.tensor_scalar(out=z[:], in0=z[:], scalar1=mv[:, 0:1], scalar2=mv[:, 2:3],
                                         op0=ALU.subtract, op1=ALU.mult), r=[z, mv], w=[z])
        fw.dve(lambda h: h.tensor_tensor(out=z[:], in0=z[:], in1=gbc[:, gi, :], op=ALU.mult), r=[z, gbc], w=[z])
        fw.dve(lambda h: h.tensor_tensor(out=z[:], in0=z[:], in1=gbc[:, gi + 1, :], op=ALU.add), r=[z, gbc], w=[z])

    def store_x_and_xT(self, z, ti, x_d, xT_d, xo):
        fw = self.fw
        fw.dma("sp", x_d[ti * 128:(ti + 1) * 128, :], z[:], reads=[z], writes=[(x_d, ti)])
        for hh in range(2):
            ps = self.next_ps()
            for c in range(4):
                cc = hh * 4 + c
                fw.pe(lambda h: h.transpose(ps[:, c * 128:(c + 1) * 128], z[:, cc * 128:(cc + 1) * 128], self.ident[:]),
                      r=[z, self.ident], w=[ps])
            fw.act(lambda h: h.copy(out=xo[:, hh * 4:(hh + 1) * 4, :], in_=ps[:, :].rearrange("p (c t) -> p c t", c=4)),
                   r=[ps], w=[(xo, hh)])
        fw.dma("sp", xT_d[:, ti * 128:(ti + 1) * 128].rearrange("(c p) t -> p c t", p=128), xo[:], reads=[xo],
               writes=[(xT_d, ti)])

    def phase_merge(self, l, xT_d, xres_ap, xres_dep):
        fw = self.fw
        yT = self.scr["yT"]
        x1_d, x1T_d = self.scr[f"x1_{l}"], self.scr[f"x1T_{l}"]
        with contextlib.ExitStack() as es:
            xT = self.load_xT(es, xT_d)
            wg = self.sb(es, "wg", [128, 8, 3072], BF16)
            for b in range(3):
                src = self.inp["w_in"][l, :, C_GATE + b * 1024:C_GATE + (b + 1) * 1024].rearrange("(kc p) c -> p kc c", p=128)
                fw.dma("pool", wg[:, :, b * 1024:(b + 1) * 1024], src, writes=[(wg, b)])
            wb = self.sb(es, "wb", [128, 3, 4, D], BF16)
            for b in range(3):
                fw.dma("pool", wb[:, b, :, :], self.inp["w_branch"][l, b].rearrange("(kc p) c -> p kc c", p=128),
                       writes=[(wb, b)])
            wo = self.sb(es, "wo", [128, 8, D], BF16)
            fw.dma("pool", wo[:], self.inp["w_out"][l].rearrange("(kc p) c -> p kc c", p=128), writes=[wo])
            gbc = self.sb(es, "gbc", [128, 4, D])
            fw.dma("sp", gbc[:], self.inp["ln_gb"][l:l + 1, :, :].to_broadcast([128, 4, D]), writes=[gbc])
            yb = self.sb(es, "yb", [128, 3, 4, 512], BF16)
            mg = self.sb(es, "mg", [128, 8, 512], BF16)
            sg = self.sb(es, "sg", [128, 512])
            acc = self.sb(es, "acc", [128, 512])
            z = [self.sb(es, f"z{i}", [128, D]) for i in range(2)]
            xr = [self.sb(es, f"xr{i}", [128, D]) for i in range(2)]
            xo = [self.sb(es, f"xo{i}", [128, 8, 128], BF16) for i in range(2)]
            st = self.sb(es, "st", [128, 2, 6])
            mv = self.sb(es, "mv", [128, 4])
            for n in range(4):
                ts = slice(n * 512, (n + 1) * 512)
                for b in range(3):
                    fw.dma("sp", yb[:, b, :, :], yT[b, :, ts].rearrange("(c p) t -> p c t", p=128),
                           reads=[yT], writes=[(yb, b)])
                for dt in range(8):
                    for b in range(3):
                        ps = self.next_ps()
                        self.proj_fm(wg, xT, b * 1024 + dt * 128, 128, n, ps)
                        fw.act(lambda h: h.activation(out=sg[:], in_=ps[:, :], func=AF.Sigmoid), r=[ps], w=[sg])
                        ps2 = self.next_ps()
                        for kc in range(4):
                            fw.pe(lambda h: h.matmul(ps2[:, :], lhsT=wb[:, b, kc, dt * 128:(dt + 1) * 128],
                                                     rhs=yb[:, b, kc, :], start=(kc == 0), stop=(kc == 3)),
                                  r=[(wb, b), (yb, b)], w=[ps2])
                        if b == 0:
                            fw.dve(lambda h: h.tensor_tensor(out=acc[:], in0=ps2[:, :], in1=sg[:], op=ALU.mult),
                                   r=[ps2, sg], w=[acc])
                        else:
                            fw.dve(lambda h: h.tensor_tensor(out=sg[:], in0=ps2[:, :], in1=sg[:], op=ALU.mult),
                                   r=[ps2, sg], w=[sg])
                            dst = mg[:, dt, :] if b == 2 else acc[:]
                            fw.dve(lambda h: h.tensor_tensor(out=dst, in0=acc[:], in1=sg[:], op=ALU.add),
                                   r=[acc, sg], w=[(mg, dt)] if b == 2 else [acc])
                for tt in range(4):
                    ti = n * 4 + tt
                    zz, xx = z[ti % 2], xr[ti % 2]
                    fw.dma("sp", xx[:], xres_ap[ti * 128:(ti + 1) * 128, :],
                           reads=[(xres_dep, ti)] if xres_dep else [], writes=[xx])
                    for hf in range(2):
                        ps = self.next_ps()
                        for dt in range(8):
                            fw.pe(lambda h: h.matmul(ps[:, :], lhsT=mg[:, dt, tt * 128:(tt + 1) * 128],
                                                     rhs=wo[:, dt, hf * 512:(hf + 1) * 512], start=(dt == 0), stop=(dt == 7)),
                                  r=[(mg, dt), wo], w=[ps])
                        fw.dve(lambda h: h.scalar_tensor_tensor(out=zz[:, hf * 512:(hf + 1) * 512],
                                                                in0=xx[:, hf * 512:(hf + 1) * 512], scalar=ALPHA,
                                                                in1=ps[:, :], op0=ALU.mult, op1=ALU.add),
                               r=[xx, ps], w=[zz])
                    self.ln_tile(zz, st, mv, gbc, None, 0)
                    self.store_x_and_xT(zz, ti, x1_d, x1T_d, xo[ti % 2])

    def phase_tables(self, layers):
        fw = self.fw
        with contextlib.ExitStack() as es:
            ins = [self.sb(es, f"tin{i}", [128, 4, D]) for i in range(3)]
            outs = [self.sb(es, f"tout{i}", [128, 4, D], BF16) for i in range(3)]
            k = 0
            for l in layers:
                for half, nm in enumerate((f"peer_u{l}", f"peer_v{l}")):
                    src = self.inp[nm].rearrange("(p i) d -> p i d", p=128)
                    dstT = self.scr[f"uvb{l}"]
                    dst = dstT.t.rearrange("(p i) d -> p i d", p=128)[:, :, half * D:(half + 1) * D]
                    for i0 in range(0, 128, 4):
                        a, b = ins[k % 3], outs[k % 3]
                        fw.dma("sp", a[:], src[:, i0:i0 + 4, :], writes=[a])
                        if k % 2 == 0:
                            fw.act(lambda h: h.copy(out=b[:], in_=a[:]), r=[a], w=[b])
                        else:
                            fw.dve(lambda h: h.tensor_copy(out=b[:], in_=a[:]), r=[a], w=[b])
                        fw.dma("pool", dst[:, i0:i0 + 4, :], b[:], reads=[b], writes=[(dstT, (half, i0))])
                        k += 1

    def top16(self, vals, idxs, src, scratch, n):
        fw = self.fw
        fw.dve(lambda h: h.max(out=vals[:, 0:8], in_=src), r=[self._t16_src], w=[self._t16_v])
        fw.dve(lambda h: h.max_index(out=idxs[:, 0:8], in_max=vals[:, 0:8], in_values=src),
               r=[self._t16_src, self._t16_v], w=[self._t16_i])
        fw.dve(lambda h: h.match_replace(out=scratch, in_to_replace=vals[:, 0:8], in_values=src, imm_value=-1e30),
               r=[self._t16_src, self._t16_v], w=[self._t16_s])
        fw.dve(lambda h: h.max(out=vals[:, 8:16], in_=scratch), r=[self._t16_s], w=[self._t16_v])
        fw.dve(lambda h: h.max_index(out=idxs[:, 8:16], in_max=vals[:, 8:16], in_values=scratch),
               r=[self._t16_s, self._t16_v], w=[self._t16_i])

    def phase_peer(self, l, x2_d, x2T_d):
        fw = self.fw
        x1_d = self.scr[f"x1_{l}"]
        UV_ap = self.scr[f"uvb{l}"].t
        ident = self.ident
        with contextlib.ExitStack() as es:
            wq = self.sb(es, "wq", [128, 8, 2048])
            for kc in range(8):
                fw.dma("sp", wq[:, kc, :], self.inp["peer_wq"][l, kc * 128:(kc + 1) * 128, :], writes=[(wq, kc)])
            skT = self.sb(es, "skT", [128, 2, 128])
            fw.dma("sp", skT[:], self.inp["peer_skT"][l].rearrange("a d n -> d a n"), writes=[skT])
            iota = self.sb(es, "iota16", [128, 16])
            fw.dma("sp", iota[:], self.inp["iota16"][:, :], writes=[iota])
            gbc = self.sb(es, "gbc2", [128, 2, D])
            fw.dma("sp", gbc[:], self.inp["ln_gb"][l:l + 1, 2:4, :].to_broadcast([128, 2, D]), writes=[gbc])
            xts = [self.sb(es, f"pxt{i}", [128, D]) for i in range(2)]
            xTf = self.sb(es, "xTf", [128, 8, 128])
            qT = self.sb(es, "qT", [128, 16, 128])
            sc = self.sb(es, "sc", [128, 16, 128])
            sc2 = self.sb(es, "sc2", [128, 16, 128])
            v16 = self.sb(es, "v16", [128, 16, 16])
            i16 = self.sb(es, "i16", [128, 16, 16], U32)
            i16f = self.sb(es, "i16f", [128, 8, 2, 16])
            cand = self.sb(es, "cand", [128, 8, 256])
            cand2 = T(sc2.t[:, :, :].rearrange("p (h a) n -> p h (a n)", a=2), "cand2")
            cand2.dep = sc2.dep
            top = self.sb(es, "top", [128, 8, 16])
            pos = self.sb(es, "pos", [128, 8, 16], U32)
            prc = self.sb(es, "prc", [128, 2, 8, 16], U32)
            prcf = self.sb(es, "prcf", [128, 2, 8, 16])
            oh = self.sb(es, "oh", [128, 128, 16])
            sel = self.sb(es, "sel", [128, 2, 128])
            idxf = self.sb(es, "idxf", [128, 128])
            idx = self.sb(es, "idx", [128, 128], U32)
            gate = self.sb(es, "gate", [128, 8, 16])
            gs = self.sb(es, "gs", [128, 8])
            actv = self.sb(es, "actv", [128, 128])
            gtmp = self.sb(es, "gtmp", [128, 128])
            wgt = self.sb(es, "wgt", [128, 128])
            junk = self.sb(es, "junk", [128, D])
            rows = [self.sb(es, f"rows{i}", [128, 2 * D], BF16) for i in range(12)]
            dgs = [self.sb(es, f"dg{i}", [128, 128], BF16) for i in range(4)]
            accb = [self.ps[6], self.ps[7]]
            self.ps_reserved = set(accb)
            ys = [self.sb(es, f"py{i}", [128, D]) for i in range(2)]
            xo = [self.sb(es, f"pxo{i}", [128, 8, 128], BF16) for i in range(2)]
            st = self.sb(es, "pst", [128, 2, 6])
            mv = self.sb(es, "pmv", [128, 4])
            idxs = [idx, self.sb(es, "idx_b", [128, 128], U32)]
            gates = [gate, self.sb(es, "gate_b", [128, 8, 16])]
            import os as _os
            _NODMA = int(_os.environ.get("PEER_NODMA", 0)); _NODVE = int(_os.environ.get("PEER_NODVE", 0))

            def stage1(ti):
                xt, idx, gate = xts[ti % 2], idxs[ti % 2], gates[ti % 2]
                th = []

                def t_load():
                    fw.dma("sp", xt[:], x1_d[ti * 128:(ti + 1) * 128, :], reads=[(x1_d, ti)], writes=[xt])
                    for hh in range(2):
                        ps = self.next_ps()
                        for c in range(4):
                            cc = hh * 4 + c
                            fw.pe(lambda h: h.transpose(ps[:, c * 128:(c + 1) * 128], xt[:, cc * 128:(cc + 1) * 128], ident[:]),
                                  r=[xt, ident], w=[ps])
                        fw.act(lambda h: h.copy(out=xTf[:, hh * 4:(hh + 1) * 4, :],
                                                in_=ps[:, :].rearrange("p (c t) -> p c t", c=4)), r=[ps], w=[(xTf, hh)])
                th.append(t_load)

                def t_q(g4):
                    def f():
                        ps = self.next_ps()
                        for b4 in range(4):
                            blk = g4 * 4 + b4
                            for kc in range(8):
                                fw.pe(lambda h: h.matmul(ps[:, b4 * 128:(b4 + 1) * 128],
                                                         lhsT=wq[:, kc, blk * 128:(blk + 1) * 128], rhs=xTf[:, kc, :],
                                                         start=(kc == 0), stop=(kc == 7)), r=[(wq, kc), xTf], w=[ps])
                        fw.act(lambda h: h.copy(out=qT[:, g4 * 4:(g4 + 1) * 4, :],
                                                in_=ps[:, :].rearrange("p (b t) -> p b t", b=4)), r=[ps], w=[(qT, g4)])
                    return f
                for g4 in range(4):
                    th.append(t_q(g4))

                def t_s(g4):
                    def f():
                        ps = self.next_ps()
                        for b4 in range(4):
                            blk = g4 * 4 + b4
                            fw.pe(lambda h: h.matmul(ps[:, b4 * 128:(b4 + 1) * 128], lhsT=qT[:, blk, :],
                                                     rhs=skT[:, blk % 2, :], start=True, stop=True),
                                  r=[(qT, g4), skT], w=[ps])
                        fw.act(lambda h: h.copy(out=sc[:, g4 * 4:(g4 + 1) * 4, :],
                                                in_=ps[:, :].rearrange("p (b n) -> p b n", b=4)), r=[ps], w=[(sc, g4)])
                    return f
                for g4 in range(4):
                    th.append(t_s(g4))

                def t_top1(blk):
                    def f():
                        self._t16_src, self._t16_v, self._t16_i, self._t16_s = sc, v16, i16, sc2
                        self.top16(v16[:, blk, :], i16[:, blk, :], sc[:, blk, :], sc2[:, blk, :], 128)
                    return f
                for blk in range(16):
                    th.append(t_top1(blk))

                def t_cand():
                    v4 = v16[:, :, :].rearrange("p (h a) k -> p h a k", a=2)
                    fw.dve(lambda h: h.tensor_tensor(
                        out=cand[:, :, :].rearrange("p h (i j) -> p h i j", i=16),
                        in0=v4[:, :, 0, :][:, :, :, None].to_broadcast([128, 8, 16, 16]),
                        in1=v4[:, :, 1, :][:, :, None, :].to_broadcast([128, 8, 16, 16]), op=ALU.add), r=[v16], w=[cand])
                th.append(t_cand)

                def t_top2(hd):
                    def f():
                        self._t16_src, self._t16_v, self._t16_i, self._t16_s = cand, top, pos, cand2
                        self.top16(top[:, hd, :], pos[:, hd, :], cand[:, hd, :], cand2[:, hd, :], 256)
                    return f
                for hd in range(8):
                    th.append(t_top2(hd))

                def t_idx0():
                    fw.dve(lambda h: h.tensor_scalar(out=prc[:, 0, :, :], in0=pos[:], scalar1=4, scalar2=None,
                                                     op0=ALU.logical_shift_right), r=[pos], w=[prc])
                    fw.dve(lambda h: h.tensor_scalar(out=prc[:, 1, :, :], in0=pos[:], scalar1=15, scalar2=None,
                                                     op0=ALU.bitwise_and), r=[pos], w=[prc])
                    fw.dve(lambda h: h.tensor_copy(out=prcf[:], in_=prc[:]), r=[prc], w=[prcf])
                    fw.dve(lambda h: h.tensor_copy(out=i16f[:], in_=i16[:, :, :].rearrange("p (h a) k -> p h a k", a=2)),
                           r=[i16], w=[i16f])
                th.append(t_idx0)

                def t_idx1(a):
                    def f():
                        fw.dve(lambda h: h.tensor_tensor(
                            out=oh[:], in0=iota[:, None, :].to_broadcast([128, 128, 16]),
                            in1=prcf[:, a, :, :].rearrange("p h k -> p (h k)")[:, :, None].to_broadcast([128, 128, 16]),
                            op=ALU.is_equal), r=[iota, prcf], w=[oh])
                        fw.dve(lambda h: h.tensor_tensor(
                            out=oh[:, :, :].rearrange("p (h k) j -> p h k j", h=8),
                            in0=oh[:, :, :].rearrange("p (h k) j -> p h k j", h=8),
                            in1=i16f[:, :, a, :][:, :, None, :].to_broadcast([128, 8, 16, 16]), op=ALU.mult),
                            r=[oh, i16f], w=[oh])
                        fw.dve(lambda h: h.tensor_reduce(out=sel[:, a, :], in_=oh[:], axis=AX.X, op=ALU.add),
                               r=[oh], w=[(sel, a)])
                    return f
                th.append(t_idx1(0))
                th.append(t_idx1(1))

                def t_fin():
                    fw.dve(lambda h: h.scalar_tensor_tensor(out=idxf[:], in0=sel[:, 0, :], scalar=128.0, in1=sel[:, 1, :],
                                                            op0=ALU.mult, op1=ALU.add), r=[sel], w=[idxf])
                    fw.dve(lambda h: h.tensor_copy(out=idx[:], in_=idxf[:]), r=[idxf], w=[idx])
                    fw.dve(lambda h: h.tensor_tensor(out=gate[:], in0=top[:], in1=top[:, :, 0:1].to_broadcast([128, 8, 16]),
                                                     op=ALU.subtract), r=[top], w=[gate])
                    fw.act(lambda h: h.activation(out=gate[:], in_=gate[:], func=AF.Exp), r=[gate], w=[gate])
                    fw.dve(lambda h: h.tensor_reduce(out=gs[:], in_=gate[:], axis=AX.X, op=ALU.add), r=[gate], w=[gs])
                    fw.dve(lambda h: h.reciprocal(out=gs[:], in_=gs[:]), r=[gs], w=[gs])
                    fw.dve(lambda h: h.tensor_tensor(out=gate[:], in0=gate[:], in1=gs[:, :, None].to_broadcast([128, 8, 16]),
                                                     op=ALU.mult), r=[gate, gs], w=[gate])
                th.append(t_fin)
                return th

            rr = 0
            for f in stage1(0):
                f()
            for ti in range(NT):
                xt, y, idx, gate = xts[ti % 2], ys[ti % 2], idxs[ti % 2], gates[ti % 2]
                nxt = stage1(ti + 1) if ti + 1 < NT else []
                nslot = [0]

                def tick():
                    nslot[0] += 1
                    if nslot[0] % 2 == 0 and nxt:
                        nxt.pop(0)()
                gflat = gate[:, :, :].rearrange("p h k -> p (h k)")
                for g in range(16):
                    gs_ = slice(g * 8, (g + 1) * 8)
                    grp = []
                    for j in range(g * 8, (g + 1) * 8):
                        rb = rows[rr % len(rows)]
                        rr += 1
                        grp.append((j, rb))
                        fw.dma("pool", rb[:], UV_ap[:, :], reads=[idx], writes=[rb],
                               indirect=bass.IndirectOffsetOnAxis(ap=idx[:, j:j + 1], axis=0))
                        fw.dve(lambda h: h.scalar_tensor_tensor(out=junk[:], in0=rb[:, 0:D], scalar=1.0, in1=xt[:],
                                                                op0=ALU.mult, op1=ALU.mult, accum_out=actv[:, j:j + 1]),
                               r=[rb, xt], w=[(actv, g)])
                        tick()
                    a8, t8, w8 = actv[:, gs_], gtmp[:, gs_], wgt[:, gs_]
                    fw.dve(lambda h: h.tensor_tensor(out=t8, in0=a8, in1=a8, op=ALU.mult), r=[(actv, g)], w=[(gtmp, g)])
                    fw.dve(lambda h: h.tensor_scalar(out=t8, in0=t8, scalar1=0.044715, scalar2=1.0, op0=ALU.mult, op1=ALU.add),
                           r=[(gtmp, g)], w=[(gtmp, g)])
                    fw.dve(lambda h: h.tensor_tensor(out=t8, in0=t8, in1=a8, op=ALU.mult), r=[(gtmp, g), (actv, g)], w=[(gtmp, g)])
                    fw.act(lambda h: h.activation(out=t8, in_=t8, func=AF.Sigmoid, scale=1.5957691216057308),
                           r=[(gtmp, g)], w=[(gtmp, g)])
                    fw.dve(lambda h: h.tensor_tensor(out=t8, in0=t8, in1=a8, op=ALU.mult), r=[(gtmp, g), (actv, g)], w=[(gtmp, g)])
                    fw.dve(lambda h: h.tensor_tensor(out=w8, in0=t8, in1=gflat[:, gs_], op=ALU.mult),
                           r=[(gtmp, g), gate], w=[(wgt, g)])
                    for j, rb in grp:
                        dg = dgs[j % len(dgs)]
                        fw.act(lambda h: h.activation(out=dg[:], in_=ident[:], func=AF.Copy, scale=wgt[:, j:j + 1]),
                               r=[ident, (wgt, g)], w=[dg])
                        for hf in range(2):
                            fw.pe(lambda h: h.matmul(accb[hf][:, :], lhsT=dg[:], rhs=rb[:, D + hf * 512:D + (hf + 1) * 512],
                                                     start=(j == 0), stop=(j == 127)), r=[dg, rb], w=[accb[hf]])
                while nxt:
                    nxt.pop(0)()
                for hf in range(2):
                    fw.dve(lambda h: h.scalar_tensor_tensor(out=y[:, hf * 512:(hf + 1) * 512],
                                                            in0=xt[:, hf * 512:(hf + 1) * 512], scalar=ALPHA,
                                                            in1=accb[hf][:, :], op0=ALU.mult, op1=ALU.add),
                           r=[xt, accb[hf]], w=[y])
                self.ln_tile(y, st, mv, gbc, None, 0)
                self.store_x_and_xT(y, ti, x2_d, x2T_d, xo[ti % 2])
            self.ps_reserved = set()

    def phase_mla(self, l, xT_d):
        fw, nc = self.fw, self.nc
        yT = self.scr["yT"]
        SC = 96.0 ** -0.5
        with contextlib.ExitStack() as es:
            qn = self.sb(es, "qn", [64, 8, S], BF16)
            qp = self.sb(es, "qp", [32, 8, S], BF16)
            kn = self.sb(es, "kn", [64, 8, S], BF16)
            kp = self.sb(es, "kp", [32, S], BF16)
            V = self.sb(es, "V", [128, NT, 8, 65], BF16)
            cmask = self.sb(es, "cmask", [128, 128], BF16)
            identb = self.sb(es, "identb", [128, 128], BF16)
            fw.dma("pool", cmask[:], self.inp["cmask"][:, :], writes=[cmask])
            fw.dve(lambda h: h.tensor_copy(out=identb[:], in_=self.ident[:]), r=[self.ident], w=[identb])
            fw.pool(lambda h: h.memset(V[:], 1.0), w=[V])
            with contextlib.ExitStack() as es2:
                xT = self.load_xT(es2, xT_d)
                wt = self.sb(es2, "mw", [128, 8, 448], BF16)
                self.load_w_cols(wt, l, C_MLA, 416, key=0)
                for hf in range(2):
                    src = self.inp["w_in"][l, :, C_MLA + 384 + (1 - hf) * 16:C_MLA + 384 + (2 - hf) * 16].rearrange(
                        "(kc p) c -> p kc c", p=128)
                    fw.dma("pool", wt[:, :, 416 + hf * 16:416 + (hf + 1) * 16], src, writes=[(wt, 1 + hf)])
                vec = self.sb(es2, "mvec", [128, 3])
                fw.dma("sp", vec[:], self.inp["mla_vec"][l], writes=[vec])
                cs = self.sb(es2, "cs", [32, 2, S])
                fw.dma("sp", cs[:], self.inp["rope_cs"].rearrange("a p t -> p a t"), writes=[cs])
                wuq = self.sb(es2, "wuq", [128, 2, 768], BF16)
                fw.dma("pool", wuq[:], self.inp["mla_w_uq"][l].rearrange("(kc p) c -> p kc c", p=128), writes=[wuq])
                wuqr = self.sb(es2, "wuqr", [128, 2, 256], BF16)
                fw.dma("pool", wuqr[:], self.inp["mla_w_uq_rot"][l].rearrange("(kc p) c -> p kc c", p=128), writes=[wuqr])
                wk = self.sb(es2, "wk", [128, 512], BF16)
                fw.dma("pool", wk[:], self.inp["mla_w_ukv_k"][l], writes=[wk])
                wv = self.sb(es2, "wv", [128, 512], BF16)
                fw.dma("pool", wv[:], self.inp["mla_w_ukv_v"][l], writes=[wv])
                ones = self.sb(es2, "ones", [128, 128])
                fw.pool(lambda h: h.memset(ones[:], 1.0), w=[ones])
                cl = self.sb(es2, "cl", [128, 3, 512])
                sq = self.sb(es2, "sq", [128, 3, 512])
                rs = self.sb(es2, "rs", [128, 2, 512])
                cn = self.sb(es2, "cn", [128, 3, 512], BF16)
                kr = self.sb(es2, "kr", [32, 2, 512])
                tmp32 = self.sb(es2, "tmp32", [32, 2, 512])
                for n in range(4):
                    ts = slice(n * 512, (n + 1) * 512)
                    for j in range(3):
                        ps = self.next_ps()
                        self.proj_fm(wt, xT, j * 128, 128, n, ps)
                        fw.act(lambda h: h.copy(out=cl[:, j, :], in_=ps[:, :]), r=[ps], w=[(cl, j)])
                        fw.dve(lambda h: h.tensor_tensor(out=sq[:, j, :], in0=cl[:, j, :], in1=cl[:, j, :], op=ALU.mult),
                               r=[(cl, j)], w=[(sq, j)])
                    for j in range(2):
                        ps = self.next_ps()
                        self.proj_fm(wt, xT, 384 + j * 32, 32, n, ps)
                        fw.act(lambda h: h.copy(out=kr[:, j, :], in_=ps[0:32, :]), r=[ps], w=[(kr, j)])
                    for g, (tiles, dim) in enumerate((((0, 1), 256.0), ((2,), 128.0))):
                        ps = self.next_ps()
                        for ii, j in enumerate(tiles):
                            fw.pe(lambda h: h.matmul(ps[:, :], lhsT=ones[:, :], rhs=sq[:, j, :],
                                                     start=(ii == 0), stop=(ii == len(tiles) - 1)),
                                  r=[ones, (sq, j)], w=[ps])
                        fw.act(lambda h: h.activation(out=rs[:, g, :], in_=ps[:, :], func=AF.Sqrt, bias=1e-6,
                                                      scale=1.0 / dim), r=[ps], w=[(rs, g)])
                        fw.dve(lambda h: h.reciprocal(out=rs[:, g, :], in_=rs[:, g, :]), r=[(rs, g)], w=[(rs, g)])
                        for j in tiles:
                            fw.dve(lambda h: h.scalar_tensor_tensor(out=cn[:, j, :], in0=cl[:, j, :],
                                                                    scalar=vec[:, j:j + 1], in1=rs[:, g, :],
                                                                    op0=ALU.mult, op1=ALU.mult),
                                   r=[(cl, j), vec, (rs, g)], w=[(cn, j)])
                    fw.dve(lambda h: h.tensor_tensor(out=tmp32[:, 0, :], in0=kr[:, 0, :], in1=cs[:, 0, ts], op=ALU.mult),
                           r=[(kr, 0), cs], w=[(tmp32, 0)])
                    fw.dve(lambda h: h.tensor_tensor(out=tmp32[:, 1, :], in0=kr[:, 1, :], in1=cs[:, 1, ts], op=ALU.mult),
                           r=[(kr, 1), cs], w=[(tmp32, 1)])
                    fw.dve(lambda h: h.tensor_tensor(out=kp[:, ts], in0=tmp32[:, 0, :], in1=tmp32[:, 1, :], op=ALU.add),
                           r=[tmp32], w=[(kp, n)])
                    for hd in range(8):
                        ps = self.next_ps()
                        for kc in range(2):
                            fw.pe(lambda h: h.matmul(ps[0:64, :], lhsT=wuq[:, kc, hd * 96:hd * 96 + 64],
                                                     rhs=cn[:, kc, :], start=(kc == 0), stop=(kc == 1)),
                                  r=[wuq, (cn, kc)], w=[ps])
                        fw.act(lambda h: h.activation(out=qn[:, hd, ts], in_=ps[0:64, :], func=AF.Copy, scale=SC),
                               r=[ps], w=[(qn, (hd, n))])
                        ps = self.next_ps()
                        for kc in range(2):
                            fw.pe(lambda h: h.matmul(ps[0:32, :], lhsT=wuq[:, kc, hd * 96 + 64:hd * 96 + 96],
                                                     rhs=cn[:, kc, :], start=(kc == 0), stop=(kc == 1)),
                                  r=[wuq, (cn, kc)], w=[ps])
                        ps2 = self.next_ps()
                        for kc in range(2):
                            fw.pe(lambda h: h.matmul(ps2[0:32, :], lhsT=wuqr[:, kc, hd * 32:hd * 32 + 32],
                                                     rhs=cn[:, kc, :], start=(kc == 0), stop=(kc == 1)),
                                  r=[wuqr, (cn, kc)], w=[ps2])
                        fw.dve(lambda h: h.tensor_tensor(out=tmp32[:, 0, :], in0=ps[0:32, :], in1=cs[:, 0, ts], op=ALU.mult),
                               r=[ps, cs], w=[(tmp32, 0)])
                        fw.dve(lambda h: h.tensor_tensor(out=tmp32[:, 1, :], in0=ps2[0:32, :], in1=cs[:, 1, ts], op=ALU.mult),
                               r=[ps2, cs], w=[(tmp32, 1)])
                        fw.dve(lambda h: h.scalar_tensor_tensor(out=qp[:, hd, ts], in0=tmp32[:, 0, :], scalar=1.0,
                                                                in1=tmp32[:, 1, :], op0=ALU.mult, op1=ALU.add),
                               r=[tmp32], w=[(qp, (hd, n))])
                        fw.dve(lambda h: h.tensor_scalar(out=qp[:, hd, ts], in0=qp[:, hd, ts], scalar1=SC, scalar2=None,
                                                         op0=ALU.mult), r=[(qp, (hd, n))], w=[(qp, (hd, n))])
                        ps = self.next_ps()
                        fw.pe(lambda h: h.matmul(ps[0:64, :], lhsT=wk[:, hd * 64:(hd + 1) * 64], rhs=cn[:, 2, :],
                                                 start=True, stop=True), r=[wk, (cn, 2)], w=[ps])
                        fw.act(lambda h: h.copy(out=kn[:, hd, ts], in_=ps[0:64, :]), r=[ps], w=[(kn, (hd, n))])
                    for tt in range(4):
                        ti = n * 4 + tt
                        ps = self.next_ps()
                        fw.pe(lambda h: h.matmul(ps[:, :], lhsT=cn[:, 2, tt * 128:(tt + 1) * 128], rhs=wv[:, :],
                                                 start=True, stop=True), r=[(cn, 2), wv], w=[ps])
                        fw.act(lambda h: h.copy(out=V[:, ti, :, 0:64], in_=ps[:, :].rearrange("p (h d) -> p h d", h=8)),
                               r=[ps], w=[(V, ti)])
            fw.barrier()
            with contextlib.ExitStack() as es3:
                PT = self.sb(es3, "PT", [128, NT, 8, 128], BF16)
                rec = self.sb(es3, "rec", [128, 8, 1])
                yc = self.sb(es3, "yc", [128, 8, 64], BF16)
                yct = [self.sb(es3, f"yct{i}", [128, 4, 128], BF16) for i in range(2)]
                for qi in range(NT):
                    qs = slice(qi * 128, (qi + 1) * 128)
                    for kt in range(qi + 1):
                        ks = slice(kt * 128, (kt + 1) * 128)
                        for hg in range(2):
                            ps = self.next_ps()
                            for hh in range(4):
                                hd = hg * 4 + hh
                                o = ps[:, hh * 128:(hh + 1) * 128]
                                fw.pe(lambda h: h.matmul(o, lhsT=kn[:, hd, ks], rhs=qn[:, hd, qs], start=True, stop=False),
                                      r=[kn, qn], w=[ps])
                                fw.pe(lambda h: h.matmul(o, lhsT=kp[:, ks], rhs=qp[:, hd, qs], start=False, stop=True),
                                      r=[kp, qp], w=[ps])
                            fw.act(lambda h: h.activation(
                                out=PT[:, kt, hg * 4:(hg + 1) * 4, :],
                                in_=ps[:, :].rearrange("p (h q) -> p h q", h=4), func=AF.Exp),
                                r=[ps], w=[(PT, (kt, hg))])
                            if kt == qi:
                                fw.dve(lambda h: h.tensor_tensor(
                                    out=PT[:, kt, hg * 4:(hg + 1) * 4, :], in0=PT[:, kt, hg * 4:(hg + 1) * 4, :],
                                    in1=cmask[:, None, :].to_broadcast([128, 4, 128]), op=ALU.mult),
                                    r=[(PT, (kt, hg)), cmask], w=[(PT, (kt, hg))])
                    pss = [self.next_ps(), self.next_ps()]
                    for hd in range(8):
                        ps = pss[hd // 4]
                        o = ps[:, (hd % 4) * 65:(hd % 4) * 65 + 65]
                        for kt in range(qi + 1):
                            fw.pe(lambda h: h.matmul(o, lhsT=PT[:, kt, hd, :], rhs=V[:, kt, hd, :],
                                                     start=(kt == 0), stop=(kt == qi)),
                                  r=[(PT, (kt, hd // 4)), (V, kt)], w=[ps])
                    for hg in range(2):
                        ps = pss[hg]
                        pv = ps[:, 0:260].rearrange("p (h d) -> p h d", h=4)
                        fw.dve(lambda h: h.reciprocal(out=rec[:, hg * 4:(hg + 1) * 4, :], in_=pv[:, :, 64:65]),
                               r=[ps], w=[(rec, hg)])
                        fw.dve(lambda h: h.tensor_tensor(out=yc[:, hg * 4:(hg + 1) * 4, :], in0=pv[:, :, 0:64],
                                                         in1=rec[:, hg * 4:(hg + 1) * 4, :].to_broadcast([128, 4, 64]),
                                                         op=ALU.mult), r=[ps, (rec, hg)], w=[(yc, hg)])
                    o = yct[qi % 2]
                    ycf = yc[:, :, :].rearrange("p h d -> p (h d)")
                    for c in range(4):
                        pst = self.next_ps()
                        pb = pst[:, :].bitcast(BF16)
                        fw.pe(lambda h: h.transpose(pb[:, 0:128], ycf[:, c * 128:(c + 1) * 128], identb[:]),
                              r=[yc, identb], w=[pst])
                        fw.act(lambda h: h.copy(out=o[:, c, :], in_=pb[:, 0:128]), r=[pst], w=[(o, c)])
                    fw.dma("sp", yT[2, :, qs].rearrange("(c p) t -> p c t", p=128), o[:], reads=[o],
                           writes=[(yT, ("c", qi))])


def prep_inputs(inputs, nseq=1):
    f = lambda a: np.ascontiguousarray(np.asarray(a, dtype=np.float32))
    shared = {}
    shared["w_in"] = f(inputs["w_in"])
    rv = np.stack([inputs["rg_conv_w"][:, 0], inputs["rg_conv_w"][:, 1], inputs["rg_conv_w"][:, 2],
                   inputs["rg_conv_w"][:, 3], inputs["rg_conv_b"], inputs["rg_ba"], inputs["rg_bx"],
                   inputs["rg_log_a"]], axis=-1)
    shared["rg_vec"] = f(rv.reshape(L, 4, 128, 8).transpose(0, 2, 1, 3))
    shared["rg_wa"] = f(inputs["rg_wa"])
    shared["rg_wx"] = f(inputs["rg_wx"])
    shared["ident"] = np.eye(128, dtype=np.float32)
    mv = np.stack([inputs["mla_q_norm"][:, :128], inputs["mla_q_norm"][:, 128:], inputs["mla_kv_norm"]], axis=-1)
    shared["mla_vec"] = f(mv)
    wuq = np.asarray(inputs["mla_w_uq"], np.float32)
    shared["mla_w_uq"] = f(wuq)
    w4 = wuq.reshape(L, 256, 8, 96)
    shared["mla_w_uq_rot"] = f(np.concatenate([w4[..., 80:96], w4[..., 64:80]], -1).reshape(L, 256, 256))
    wkv = np.asarray(inputs["mla_w_ukv"], np.float32).reshape(L, 128, 8, 128)
    shared["mla_w_ukv_k"] = f(wkv[..., :64].reshape(L, 128, 512))
    shared["mla_w_ukv_v"] = f(wkv[..., 64:].reshape(L, 128, 512))
    pos = np.arange(S, dtype=np.float32)
    inv = (10000.0 ** (-np.arange(16, dtype=np.float32) / 16)).astype(np.float32)
    ang = pos[None, :] * inv[:, None]
    cs_, sn_ = np.cos(ang).astype(np.float32), np.sin(ang).astype(np.float32)
    shared["rope_cs"] = f(np.stack([np.concatenate([cs_, cs_], 0), np.concatenate([-sn_, sn_], 0)], 0))
    shared["cmask"] = f(np.triu(np.ones((128, 128), np.float32)))
    shared["rw_mix"] = f(inputs["rw_mix"])[:, None, :]
    vz = np.zeros((L, 512), np.float32); vz[1:] = inputs["rw_v0"]
    rwv = np.stack([inputs["rw_w0"], inputs["rw_a0"], inputs["rw_k_k"], inputs["rw_k_a"],
                    np.asarray(inputs["rw_r_k"]).reshape(L, 512), inputs["rw_gn_g"], inputs["rw_gn_b"], vz], axis=-1)
    shared["rw_vec"] = f(rwv.reshape(L, 4, 128, 8).transpose(0, 2, 1, 3))
    shared["rw_w2a2"] = f(np.concatenate([inputs["rw_w2"], inputs["rw_a2"]], axis=1))
    shared["rw_g2"] = f(inputs["rw_g2"])
    shared["rw_v1"] = f(inputs["rw_v1"])
    shared["rw_v2"] = f(inputs["rw_v2"])
    su = np.triu(np.ones((128, 128), np.float32), 1); ui = np.triu(np.ones((128, 128), np.float32))
    shared["rw_masks"] = f(np.stack([su, su, ui, ui, su.T, su.T, su.T, su.T], axis=1))
    obd = np.zeros((128, 128), np.float32); obd[:64, :64] = 1; obd[64:, 64:] = 1
    shared["onesbd"] = obd
    shared["w_branch"] = f(inputs["w_branch"])
    shared["w_out"] = f(inputs["w_out"])
    shared["peer_wq"] = f(inputs["peer_w_query"])
    shared["peer_skT"] = f(np.asarray(inputs["peer_subkeys"]).transpose(0, 1, 3, 2))
    shared["iota16"] = f(np.tile(np.arange(16, dtype=np.float32)[None, :], (128, 1)))
    for l in range(L):
        shared[f"peer_u{l}"] = f(inputs["peer_u"][l])
        shared[f"peer_v{l}"] = f(inputs["peer_v"][l])
    shared["ln_gb"] = f(np.stack([inputs["ln1_g"], inputs["ln1_b"], inputs["ln2_g"], inputs["ln2_b"]], axis=1))
    maps = []
    for b in range(8 // nseq):
        m = dict(shared)
        m["x"] = f(inputs["x"][b * nseq:(b + 1) * nseq])
        maps.append(m)
    return maps


def run(inputs, debug=False, phases=None, cores=8, trace=False, nseq=1):
    bld = Builder(debug=debug, phases=phases, nseq=nseq)
    nc = bld.build()
    maps = prep_inputs(inputs, nseq)[:cores]
    maps = [{k: v for k, v in m.items() if k in bld.inp} for m in maps]
    res = run_bass_kernel_spmd(nc, maps, core_ids=list(range(cores)), trace=trace)
    return res


NCORES = 8


def kernel(**inputs):
    nseq = 8 // NCORES
    res = run(inputs, cores=NCORES, nseq=nseq)
    return np.concatenate([np.asarray(r["out"]) for r in res.results], axis=0).astype(np.float32)
```

```python
import contextlib
import numpy as np
import concourse.bass as bass
import concourse.mybir as mybir
from concourse.bass_utils import run_bass_kernel_spmd

F32 = mybir.dt.float32
BF16 = mybir.dt.bfloat16
U32 = mybir.dt.uint32
I32 = mybir.dt.int32
AF = mybir.ActivationFunctionType
ALU = mybir.AluOpType
AX = mybir.AxisListType

D = 1024
S = 2048
L = 2
NT = S // 128
IN_COLS = 6304
C_RG, C_RW, C_MLA, C_GATE = 0, 1024, 2816, 3232
ALPHA = (2.0 * L) ** 0.25


class Dep:
    def __init__(self, name):
        self.name = name
        self.st = {}

    def states(self, key):
        if key is None:
            if None not in self.st:
                self.st[None] = [None, {}]
            return list(self.st.values())
        out = []
        if None in self.st:
            out.append(self.st[None])
        if key not in self.st:
            self.st[key] = [None, {}]
        out.append(self.st[key])
        return out


class T:
    def __init__(self, t, name):
        self.t = t
        self.dep = Dep(name)

    def __getitem__(self, idx):
        return self.t[idx]


class Eng:
    def __init__(self, name, h, sem):
        self.name, self.h, self.sem = name, h, sem
        self.count = 0
        self.known = {}


class FW:
    NDS = {"sp": 16, "pool": 24, "act": 4}

    def __init__(self, nc, es):
        self.nc = nc
        self.eng = {}
        for name, h in (("pe", nc.tensor), ("dve", nc.vector), ("act", nc.scalar),
                        ("pool", nc.gpsimd), ("sp", nc.sync)):
            sem = es.enter_context(nc.semaphore("sem_" + name))
            self.eng[name] = Eng(name, h, sem)
        self.dsem = {}
        self.dcnt = {}
        self.drr = {}
        for q, n in self.NDS.items():
            self.dsem[q] = [es.enter_context(nc.semaphore(f"ds_{q}{i}")) for i in range(n)]
            self.dcnt[q] = [0] * n
            self.drr[q] = 0

    def _wait(self, e, sid, sem, val):
        if val > 0 and e.known.get(sid, 0) < val:
            e.h.wait_ge(sem, val)
            e.known[sid] = val

    def _collect(self, reads, writes, ename):
        need = {}

        def add(tok):
            if tok is None:
                return
            sid, sem, val = tok
            if ename == "pe" and sid == "pe":
                return
            if need.get(sid, (None, 0))[1] < val:
                need[sid] = (sem, val)

        rs, ws = [], []
        for r in reads:
            t, k = r if isinstance(r, tuple) else (r, None)
            sts = t.dep.states(k)
            rs.append((k, sts))
            for st in sts:
                add(st[0])
        for w in writes:
            t, k = w if isinstance(w, tuple) else (w, None)
            sts = t.dep.states(k)
            ws.append((k, sts))
            for st in sts:
                add(st[0])
                for tok in st[1].values():
                    add(tok)
        return need, rs, ws

    def _record(self, tok, rs, ws):
        for k, sts in rs:
            for st in (sts if k is None else sts[-1:]):
                st[1][tok[0]] = tok
        for k, sts in ws:
            for st in (sts if k is None else sts[-1:]):
                st[0] = tok
                st[1] = {}

    def op(self, ename, fn, reads=(), writes=()):
        e = self.eng[ename]
        need, rs, ws = self._collect(reads, writes, ename)
        for sid, (sem, val) in need.items():
            self._wait(e, sid, sem, val)
        ins = fn(e.h)
        e.count += 1
        ins.then_inc(e.sem, 1)
        self._record((ename, e.sem, e.count), rs, ws)
        return ins

    def dma(self, q, out, in_, reads=(), writes=(), indirect=None, **kw):
        e = self.eng[q]
        i = self.drr[q]
        self.drr[q] = (i + 1) % len(self.dsem[q])
        sem = self.dsem[q][i]
        sid = ("d", q, i)
        self._wait(e, sid, sem, 16 * self.dcnt[q][i])
        need, rs, ws = self._collect(reads, writes, q)
        for s2, (sm, val) in need.items():
            self._wait(e, s2, sm, val)
        if indirect is None:
            ins = e.h.dma_start(out=out, in_=in_, **kw)
        else:
            ins = e.h.indirect_dma_start(out=out, out_offset=None, in_=in_, in_offset=indirect, **kw)
        self.dcnt[q][i] += 1
        ins.then_inc(sem, 16)
        self._record((sid, sem, 16 * self.dcnt[q][i]), rs, ws)
        return ins

    def barrier(self):
        for e in self.eng.values():
            for o in self.eng.values():
                if o is not e:
                    self._wait(e, o.name, o.sem, o.count)
            for q in self.dsem:
                for i, sem in enumerate(self.dsem[q]):
                    self._wait(e, ("d", q, i), sem, 16 * self.dcnt[q][i])

    def dve(self, fn, r=(), w=()):
        return self.op("dve", fn, r, w)

    def act(self, fn, r=(), w=()):
        return self.op("act", fn, r, w)

    def pe(self, fn, r=(), w=()):
        return self.op("pe", fn, r, w)

    def pool(self, fn, r=(), w=()):
        return self.op("pool", fn, r, w)


class Builder:
    def __init__(self, debug=False, phases=None, nseq=1):
        self.debug = debug
        self.phases = phases
        self.nseq = nseq
        self.nc = bass.Bass("TRN2", target_bir_lowering=False)
        self.inp = {}
        self.scr = {}

    def din(self, name, shape, dt=F32):
        self.inp[name] = self.nc.dram_tensor(name, list(shape), dt, kind="ExternalInput").ap()

    def dscr(self, name, shape, dt, out=False):
        kind = "ExternalOutput" if (out or self.debug) else "Internal"
        ap = self.nc.dram_tensor(name, list(shape), dt, kind=kind).ap()
        self.scr[name] = T(ap, name)
        return self.scr[name]

    def sb(self, es, name, shape, dt=F32):
        self.uid = getattr(self, "uid", 0) + 1
        nm = f"s{self.uid}_{name}"
        return T(es.enter_context(self.nc.sbuf_tensor(nm, list(shape), dt)), nm)

    def declare(self):
        self.din("x", [self.nseq, S, D])
        self.din("w_in", [L, D, IN_COLS])
        self.din("rg_vec", [L, 128, 4, 8])
        self.din("rg_wa", [L, 8, 64, 64])
        self.din("rg_wx", [L, 8, 64, 64])
        self.din("ident", [128, 128])
        self.din("mla_vec", [L, 128, 3])
        self.din("mla_w_uq", [L, 256, 768])
        self.din("mla_w_uq_rot", [L, 256, 256])
        self.din("mla_w_ukv_k", [L, 128, 512])
        self.din("mla_w_ukv_v", [L, 128, 512])
        self.din("rope_cs", [2, 32, S])
        self.din("cmask", [128, 128])
        self.din("rw_mix", [L, 1, 1792])
        self.din("rw_vec", [L, 128, 4, 8])
        self.din("rw_w2a2", [L, 128, 512])
        self.din("rw_g2", [L, 128, 512])
        self.din("rw_v1", [1, 512, 32])
        self.din("rw_v2", [1, 32, 512])
        self.din("rw_masks", [128, 8, 128])
        self.din("onesbd", [128, 128])
        self.dscr("rwT", [7, 512, S], F32)
        self.dscr("vfirst", [512, S], F32)
        self.dscr("rw_wc", [512, NT], F32)
        self.din("w_branch", [L, 3, 512, D])
        self.din("w_out", [L, D, D])
        self.din("peer_wq", [L, D, 2048])
        self.din("peer_skT", [L, 2, 128, 128])
        self.din("iota16", [128, 16])
        for l in range(L):
            if self.phases is None or f"F{l}" in self.phases:
                self.din(f"peer_u{l}", [16384, D])
                self.din(f"peer_v{l}", [16384, D])
                self.scr[f"uvb{l}"] = T(self.nc.dram_tensor(f"uvb{l}", [16384, 2 * D], BF16, kind="Internal").ap(), f"uvb{l}")
            self.dscr(f"x2T_{l}", [D, S], BF16)
        self.dscr("x2_0", [S, D], F32)
        self.din("ln_gb", [L, 4, D])
        for l in range(L):
            self.dscr(f"x1_{l}", [S, D], F32)
            self.dscr(f"x1T_{l}", [D, S], BF16)
        self.dscr("xT0", [D, S], BF16)
        self.dscr("yT", [3, 512, S], BF16)
        self.dscr("out", [self.nseq, S, D], F32, out=True)

    def build(self):
        nc = self.nc
        self.declare()
        with contextlib.ExitStack() as es:
            self.fw = fw = FW(nc, es)
            self.ps = [T(es.enter_context(nc.psum_tensor(f"ps{i}", [128, 512], F32)), f"ps{i}")
                       for i in range(8)]
            self.psi = 0
            self.ident = self.sb(es, "ident", [128, 128])
            fw.dma("sp", self.ident[:], self.inp["ident"][:, :], writes=[self.ident])
            ph = self.phases
            if ph is None or any(p.startswith("F") for p in ph):
                self.phase_tables([l for l in range(L) if ph is None or f"F{l}" in ph])
                fw.barrier()
            for sq in range(self.nseq):
                x_in = self.inp["x"][sq]
                out_t = T(self.scr["out"].t[sq], f"out{sq}")
                if ph is None or "A" in ph:
                    self.phase_xT(x_in, None, self.scr["xT0"])
                    fw.barrier()
                for l in range(L):
                    xTd = self.scr["xT0"] if l == 0 else self.scr[f"x2T_{l - 1}"]
                    if ph is None or f"B{l}" in ph:
                        self.phase_rg(l, xTd)
                        fw.barrier()
                    if ph is None or f"C{l}" in ph or f"Cp{l}" in ph:
                        self.phase_rwkv_prep(l, xTd)
                        fw.barrier()
                    if ph is None or f"C{l}" in ph or f"Cs{l}" in ph:
                        self.phase_rwkv_scan(l)
                        fw.barrier()
                    if ph is None or f"D{l}" in ph:
                        self.phase_mla(l, xTd)
                        fw.barrier()
                    if ph is None or f"E{l}" in ph:
                        if l == 0:
                            self.phase_merge(l, xTd, x_in, None)
                        else:
                            self.phase_merge(l, xTd, self.scr["x2_0"].t, self.scr["x2_0"])
                        fw.barrier()
                    if ph is None or f"F{l}" in ph:
                        x2d = out_t if l == L - 1 else self.scr["x2_0"]
                        self.phase_peer(l, x2d, self.scr[f"x2T_{l}"])
                        fw.barrier()
            fw.barrier()
        return nc

    def next_ps(self):
        while True:
            p = self.ps[self.psi]
            self.psi = (self.psi + 1) % 8
            if p not in getattr(self, "ps_reserved", ()):
                return p

    def phase_xT(self, x_ap, x_dep, xT_d):
        fw = self.fw
        with contextlib.ExitStack() as es:
            xt = [self.sb(es, f"xa{i}", [128, D]) for i in range(2)]
            xo = [self.sb(es, f"xo{i}", [128, 8, 128], BF16) for i in range(2)]
            for i in range(NT):
                b = xt[i % 2]
                fw.dma("sp", b[:], x_ap[i * 128:(i + 1) * 128, :],
                       reads=[x_dep] if x_dep else [], writes=[b])
                o = xo[i % 2]
                for hh in range(2):
                    ps = self.next_ps()
                    for c in range(4):
                        cc = hh * 4 + c
                        fw.pe(lambda h, ps=ps, c=c, cc=cc, b=b: h.transpose(
                            ps[:, c * 128:(c + 1) * 128], b[:, cc * 128:(cc + 1) * 128], self.ident[:]),
                            r=[b, self.ident], w=[ps])
                    fw.dve(lambda h, ps=ps, o=o, hh=hh: h.tensor_copy(
                        out=o[:, hh * 4:(hh + 1) * 4, :],
                        in_=ps[:, :].rearrange("p (c t) -> p c t", c=4)), r=[ps], w=[(o, hh)])
                fw.dma("sp", xT_d[:, i * 128:(i + 1) * 128].rearrange("(c p) t -> p c t", p=128),
                       o[:], reads=[o], writes=[(xT_d, i)])

    def load_xT(self, es, xT_d):
        xT = self.sb(es, "xT", [128, 8, S], BF16)
        for c in range(8):
            self.fw.dma("sp", xT[:, c, :], xT_d[c * 128:(c + 1) * 128, :], reads=[xT_d], writes=[(xT, c)])
        return xT

    def load_w_cols(self, wt, l, c0, ncols, key=None):
        src = self.inp["w_in"][l, :, c0:c0 + ncols].rearrange("(kc p) c -> p kc c", p=128)
        self.fw.dma("pool", wt[:, :, 0:ncols], src, writes=[(wt, key) if key is not None else wt])

    def proj_fm(self, wt, xT, j0, M, n, ps):
        for kc in range(8):
            self.fw.pe(lambda h, kc=kc: h.matmul(ps[0:M, :], lhsT=wt[:, kc, j0:j0 + M],
                                                  rhs=xT[:, kc, n * 512:(n + 1) * 512],
                                                  start=(kc == 0), stop=(kc == 7)),
                       r=[wt, xT], w=[ps])

    def gelu_tanh(self, es_tmp, x, out, tmp):
        fw = self.fw
        fw.dve(lambda h: h.tensor_tensor(out=tmp[:], in0=x[:], in1=x[:], op=ALU.mult), r=[x], w=[tmp])
        fw.dve(lambda h: h.tensor_scalar(out=tmp[:], in0=tmp[:], scalar1=0.044715, scalar2=1.0,
                                         op0=ALU.mult, op1=ALU.add), r=[tmp], w=[tmp])
        fw.dve(lambda h: h.tensor_tensor(out=tmp[:], in0=tmp[:], in1=x[:], op=ALU.mult), r=[tmp, x], w=[tmp])
        fw.act(lambda h: h.activation(out=tmp[:], in_=tmp[:], func=AF.Sigmoid, scale=1.5957691216057308),
               r=[tmp], w=[tmp])
        fw.dve(lambda h: h.tensor_tensor(out=out[:], in0=tmp[:], in1=x[:], op=ALU.mult), r=[tmp, x], w=[out])

    def phase_rg(self, l, xT_d):
        fw, nc = self.fw, self.nc
        yT = self.scr["yT"]
        with contextlib.ExitStack() as es:
            xT = self.load_xT(es, xT_d)
            vec = self.sb(es, "rgvec", [128, 4, 8])
            fw.dma("sp", vec[:], self.inp["rg_vec"][l], writes=[vec])
            cst = self.sb(es, "rgc", [128, 4, 4])
            wts = [self.sb(es, f"rgw{i}", [128, 8, 256], BF16) for i in range(2)]
            wbd = [self.sb(es, f"rgbd{i}", [128, 2, 128]) for i in range(2)]
            names = ["xb", "gb", "xc", "r", "i", "a", "m", "h"]
            tl = {n: self.sb(es, "rg_" + n, [128, S]) for n in names}
            yo = self.sb(es, "rg_yo", [128, S], BF16)
            xb, gb, xc, r_, i_, a_, m_, h_ = (tl[n] for n in names)
            for c in range(4):
                wt = wts[c % 2]
                bd = wbd[c % 2]
                self.load_w_cols(wt, l, C_RG + c * 128, 128, key=0)
                src = self.inp["w_in"][l, :, C_RG + 512 + c * 128:C_RG + 512 + (c + 1) * 128].rearrange(
                    "(kc p) c -> p kc c", p=128)
                fw.dma("pool", wt[:, :, 128:256], src, writes=[(wt, 1)])
                fw.pool(lambda h: h.memset(bd[:], 0.0), w=[bd])
                for j, nm in enumerate(("rg_wa", "rg_wx")):
                    for hb in range(2):
                        fw.dma("sp", bd[hb * 64:(hb + 1) * 64, j, hb * 64:(hb + 1) * 64],
                               self.inp[nm][l, 2 * c + hb], writes=[bd])
                for j, dst in enumerate((xb, gb)):
                    for n in range(4):
                        ps = self.next_ps()
                        self.proj_fm(wt, xT, j * 128, 128, n, ps)
                        fw.act(lambda h, ps=ps, dst=dst, n=n: h.copy(out=dst[:, n * 512:(n + 1) * 512], in_=ps[:, :]),
                               r=[ps], w=[(dst, n)])
                v = lambda k: vec[:, c, k:k + 1]
                fw.dve(lambda h: h.tensor_scalar(out=xc[:], in0=xb[:], scalar1=v(0), scalar2=v(4),
                                                 op0=ALU.mult, op1=ALU.add), r=[xb, vec], w=[xc])
                for j in range(1, 4):
                    fw.dve(lambda h, j=j: h.scalar_tensor_tensor(out=xc[:, j:], in0=xb[:, 0:S - j], scalar=v(j),
                                                                 in1=xc[:, j:], op0=ALU.mult, op1=ALU.add),
                           r=[xb, xc, vec], w=[xc])
                cc = lambda k: cst[:, c, k:k + 1]
                fw.act(lambda h: h.activation(out=cc(0), in_=v(7), func=AF.Exp, scale=-1.0), r=[vec], w=[cst])
                fw.act(lambda h: h.activation(out=cc(1), in_=cc(0), func=AF.Ln, bias=1.0, scale=1.0), r=[cst], w=[cst])
                fw.dve(lambda h: h.tensor_scalar(out=cc(2), in0=cc(1), scalar1=-8.0, scalar2=None, op0=ALU.mult),
                       r=[cst], w=[cst])
                fw.dve(lambda h: h.tensor_scalar(out=cc(3), in0=cc(1), scalar1=-16.0, scalar2=None, op0=ALU.mult),
                       r=[cst], w=[cst])
                for j, (dst, bk) in enumerate(((r_, 5), (i_, 6))):
                    for n in range(4):
                        ps = self.next_ps()
                        fw.pe(lambda h, ps=ps, j=j, n=n: h.matmul(ps[:, :], lhsT=bd[:, j, :],
                                                                   rhs=xc[:, n * 512:(n + 1) * 512],
                                                                   start=True, stop=True), r=[bd, xc], w=[ps])
                        fw.act(lambda h, ps=ps, dst=dst, n=n, bk=bk: h.activation(
                            out=dst[:, n * 512:(n + 1) * 512], in_=ps[:, :], func=AF.Sigmoid, bias=v(bk), scale=1.0),
                            r=[ps, vec], w=[(dst, n)])
                fw.act(lambda h: h.activation(out=a_[:], in_=r_[:], func=AF.Exp, scale=cc(2)), r=[r_, cst], w=[a_])
                fw.act(lambda h: h.activation(out=m_[:], in_=r_[:], func=AF.Exp, scale=cc(3)), r=[r_, cst], w=[m_])
                fw.act(lambda h: h.activation(out=m_[:], in_=m_[:], func=AF.Sqrt, bias=1.0, scale=-1.0), r=[m_], w=[m_])
                fw.dve(lambda h: h.memset(m_[:, 0:1], 1.0), w=[m_])
                fw.dve(lambda h: h.tensor_tensor(out=i_[:], in0=i_[:], in1=xc[:], op=ALU.mult), r=[i_, xc], w=[i_])
                fw.dve(lambda h: h.tensor_tensor(out=i_[:], in0=i_[:], in1=m_[:], op=ALU.mult), r=[i_, m_], w=[i_])
                fw.dve(lambda h: h.tensor_tensor_scan(out=h_[:], data0=a_[:], data1=i_[:], initial=0.0,
                                                      op0=ALU.mult, op1=ALU.add), r=[a_, i_], w=[h_])
                self.gelu_tanh(es, gb, r_, m_)
                fw.dve(lambda h: h.tensor_tensor(out=yo[:], in0=h_[:], in1=r_[:], op=ALU.mult), r=[h_, r_], w=[yo])
                fw.dma("sp", yT[0, c * 128:(c + 1) * 128, :], yo[:], reads=[yo], writes=[(yT, ("a", c))])


    def phase_rwkv_prep(self, l, xT_d):
        fw = self.fw
        rwT, vfirst = self.scr["rwT"], self.scr["vfirst"]
        with contextlib.ExitStack() as es:
            xTp = self.sb(es, "xTp", [128, 8, S + 2], BF16)
            fw.dve(lambda h: h.memset(xTp[:, :, 0:2], 0.0), w=[(xTp, "z")])
            for c in range(8):
                fw.dma("sp", xTp[:, c, 2:S + 2], xT_d[c * 128:(c + 1) * 128, :], reads=[xT_d], writes=[(xTp, c)])
            mixb = self.sb(es, "mixb", [128, 2, 1792])
            fw.dma("sp", mixb[:, 0, :], self.inp["rw_mix"][l].to_broadcast([128, 1792]), writes=[mixb])
            fw.dve(lambda h: h.tensor_scalar(out=mixb[:, 1, :], in0=mixb[:, 0, :], scalar1=-1.0, scalar2=1.0,
                                             op0=ALU.mult, op1=ALU.add), r=[mixb], w=[mixb])
            vec = self.sb(es, "rwvec", [128, 4, 8])
            fw.dma("sp", vec[:], self.inp["rw_vec"][l], writes=[vec])
            dv = self.sb(es, "rwdv", [128, 4, 2])
            fw.dve(lambda h: h.tensor_scalar(out=dv[:, :, 0:1], in0=vec[:, :, 0:1], scalar1=-1.0, scalar2=None,
                                             op0=ALU.mult), r=[vec], w=[dv])
            fw.dve(lambda h: h.tensor_scalar(out=dv[:, :, 1:2], in0=vec[:, :, 3:4], scalar1=-1.0, scalar2=1.0,
                                             op0=ALU.mult, op1=ALU.add), r=[vec], w=[dv])
            w2a2 = self.sb(es, "w2a2", [128, 512])
            fw.dma("sp", w2a2[:], self.inp["rw_w2a2"][l], writes=[w2a2])
            g2 = self.sb(es, "g2", [128, 512])
            fw.dma("sp", g2[:], self.inp["rw_g2"][l], writes=[g2])
            obd = self.sb(es, "obd", [128, 128])
            fw.dma("sp", obd[:], self.inp["onesbd"][:, :], writes=[obd])
            rmask = self.sb(es, "rmask", [128, S])
            fw.pool(lambda h: h.memset(rmask[:], 1.0), w=[rmask])
            fw.pool(lambda h: h.memset(rmask[:, :].rearrange("p (c t) -> p c t", t=128)[:, :, 0:1], 0.0), w=[rmask])
            wf = [self.sb(es, f"wf{i}", [128, 8, 128]) for i in range(2)]
            wm = [self.sb(es, f"wm{i}", [128, 2, 8, 128], BF16) for i in range(2)]
            self._rw_cnt = 0

            def proj_shift(col0, dst_fn):
                i = self._rw_cnt % 2
                self._rw_cnt += 1
                src = self.inp["w_in"][l, :, C_RW + col0:C_RW + col0 + 128].rearrange("(kc p) c -> p kc c", p=128)
                fw.dma("sp", wf[i][:], src, writes=[wf[i]])
                for j in range(2):
                    fw.dve(lambda h: h.tensor_tensor(
                        out=wm[i][:, j, :, :], in0=wf[i][:],
                        in1=mixb[:, j, col0:col0 + 128][:, None, :].to_broadcast([128, 8, 128]), op=ALU.mult),
                        r=[wf[i], mixb], w=[(wm[i], j)])
                for n in range(4):
                    ps = self.next_ps()
                    for j in range(2):
                        off = 1 + j + n * 512
                        for kc in range(8):
                            fw.pe(lambda h: h.matmul(ps[:, :], lhsT=wm[i][:, j, kc, :], rhs=xTp[:, kc, off:off + 512],
                                                     start=(j == 0 and kc == 0), stop=(j == 1 and kc == 7)),
                                  r=[(wm[i], j), xTp], w=[ps])
                    dst_fn(n, ps)

            nb = 13
            B = [self.sb(es, f"rwb{i}", [128, S]) for i in range(nb)]
            wct = self.sb(es, "wct", [128, NT])
            xwa, sgx = B[11], B[12]
            sl = lambda n: slice(n * 512, (n + 1) * 512)
            def d_xwa(n, ps):
                fw.act(lambda h: h.activation(out=xwa[0:64, sl(n)], in_=ps[0:64, :], func=AF.Tanh), r=[ps], w=[(xwa, n)])
                fw.act(lambda h: h.copy(out=xwa[64:128, sl(n)], in_=ps[64:128, :]), r=[ps], w=[(xwa, n)])
            proj_shift(1536, d_xwa)
            proj_shift(1664, lambda n, ps: fw.act(lambda h: h.activation(out=sgx[:, sl(n)], in_=ps[:, :], func=AF.Sigmoid),
                                                  r=[ps], w=[(sgx, n)]))
            lat = None
            if l > 0:
                lat = self.sb(es, "lat", [32, S])
                v1 = self.sb(es, "v1", [128, 4, 32])
                fw.dma("sp", v1[:], self.inp["rw_v1"][l - 1].rearrange("(c p) j -> p c j", p=128), writes=[v1])
                v2 = self.sb(es, "v2", [32, 512])
                fw.dma("sp", v2[:], self.inp["rw_v2"][l - 1], writes=[v2])
                accb = [self.ps[4 + n] for n in range(4)]
                self.ps_reserved = set(accb)
                for c in range(4):
                    def d_v(n, ps, c=c):
                        fw.act(lambda h: h.copy(out=B[0][:, sl(n)], in_=ps[:, :]), r=[ps], w=[(B[0], n)])
                        fw.pe(lambda h: h.matmul(accb[n][0:32, :], lhsT=v1[:, c, :], rhs=B[0][:, sl(n)],
                                                 start=(c == 0), stop=(c == 3)), r=[v1, (B[0], n)], w=[accb[n]])
                    proj_shift(1024 + c * 128, d_v)
                for n in range(4):
                    fw.act(lambda h: h.copy(out=lat[:, sl(n)], in_=accb[n][0:32, :]), r=[accb[n]], w=[(lat, n)])
                self.ps_reserved = set()
            for c in range(4):
                rT, kT, vT, e2, cum, ep, em, epv, a_, kk, km = B[:11]
                v = lambda k: vec[:, c, k:k + 1]
                cs_ = slice(c * 128, (c + 1) * 128)
                for col0, dst in ((c * 128, rT), (512 + c * 128, kT), (1024 + c * 128, vT)):
                    proj_shift(col0, lambda n, ps, dst=dst: fw.act(
                        lambda h: h.copy(out=dst[:, sl(n)], in_=ps[:, :]), r=[ps], w=[(dst, n)]))
                for n in range(4):
                    ps = self.next_ps()
                    fw.pe(lambda h: h.matmul(ps[:, :], lhsT=w2a2[0:64, cs_], rhs=xwa[0:64, sl(n)], start=True, stop=True),
                          r=[w2a2, (xwa, n)], w=[ps])
                    fw.act(lambda h: h.activation(out=e2[:, sl(n)], in_=ps[:, :], func=AF.Exp, bias=dv[:, c, 0:1], scale=-1.0),
                           r=[ps, dv], w=[(e2, n)])
                    fw.act(lambda h: h.activation(out=e2[:, sl(n)], in_=e2[:, sl(n)], func=AF.Ln, bias=1.0, scale=1.0),
                           r=[(e2, n)], w=[(e2, n)])
                    fw.act(lambda h: h.activation(out=e2[:, sl(n)], in_=e2[:, sl(n)], func=AF.Exp, bias=-0.5, scale=-1.0),
                           r=[(e2, n)], w=[(e2, n)])
                    ps = self.next_ps()
                    fw.pe(lambda h: h.matmul(ps[:, :], lhsT=w2a2[64:128, cs_], rhs=xwa[64:128, sl(n)], start=True, stop=True),
                          r=[w2a2, (xwa, n)], w=[ps])
                    fw.act(lambda h: h.activation(out=a_[:, sl(n)], in_=ps[:, :], func=AF.Sigmoid, bias=v(1), scale=1.0),
                           r=[ps, vec], w=[(a_, n)])
                    ps = self.next_ps()
                    fw.pe(lambda h: h.matmul(ps[:, :], lhsT=g2[:, cs_], rhs=sgx[:, sl(n)], start=True, stop=True),
                          r=[g2, (sgx, n)], w=[ps])
                    fw.act(lambda h: h.copy(out=ep[:, sl(n)], in_=ps[:, :]), r=[ps], w=[(ep, n)])
                    if l > 0:
                        ps = self.next_ps()
                        fw.pe(lambda h: h.matmul(ps[:, :], lhsT=v2[0:32, cs_], rhs=lat[:, sl(n)], start=True, stop=True),
                              r=[v2, (lat, n)], w=[ps])
                        fw.act(lambda h: h.activation(out=em[:, sl(n)], in_=ps[:, :], func=AF.Sigmoid, bias=v(7), scale=1.0),
                               r=[ps, vec], w=[(em, n)])
                fw.dma("sp", rwT[6, cs_, :], ep[:], reads=[ep], writes=[(rwT, (6, c))])
                if l > 0:
                    fw.dma("sp", epv[:], vfirst[cs_, :], reads=[vfirst], writes=[epv])
                    fw.dve(lambda h: h.tensor_tensor(out=epv[:], in0=epv[:], in1=vT[:], op=ALU.subtract), r=[epv, vT], w=[epv])
                    fw.dve(lambda h: h.tensor_tensor(out=epv[:], in0=epv[:], in1=em[:], op=ALU.mult), r=[epv, em], w=[epv])
                    fw.dve(lambda h: h.tensor_tensor(out=vT[:], in0=vT[:], in1=epv[:], op=ALU.add), r=[epv, vT], w=[vT])
                else:
                    fw.dma("sp", vfirst[cs_, :], vT[:], reads=[vT], writes=[(vfirst, c)])
                fw.dma("sp", rwT[4, cs_, :], vT[:], reads=[vT], writes=[(rwT, (4, c))])
                fw.dve(lambda h: h.tensor_tensor_scan(out=cum[:], data0=rmask[:], data1=e2[:], initial=0.0,
                                                      op0=ALU.mult, op1=ALU.add), r=[rmask, e2], w=[cum])
                fw.act(lambda h: h.activation(out=ep[:], in_=cum[:], func=AF.Exp, scale=-1.0), r=[cum], w=[ep])
                fw.act(lambda h: h.activation(out=em[:], in_=cum[:], func=AF.Exp, scale=1.0), r=[cum], w=[em])
                fw.dve(lambda h: h.tensor_tensor(out=epv[:], in0=cum[:], in1=e2[:], op=ALU.subtract), r=[cum, e2], w=[epv])
                fw.act(lambda h: h.activation(out=epv[:], in_=epv[:], func=AF.Exp, scale=-1.0), r=[epv], w=[epv])
                fw.dve(lambda h: h.tensor_scalar(out=kk[:], in0=kT[:], scalar1=v(2), scalar2=None, op0=ALU.mult),
                       r=[kT, vec], w=[kk])
                fw.dve(lambda h: h.tensor_tensor(out=cum[:], in0=kk[:], in1=kk[:], op=ALU.mult), r=[kk], w=[cum])
                for n in range(4):
                    ps = self.next_ps()
                    fw.pe(lambda h: h.matmul(ps[:, :], lhsT=obd[:, :], rhs=cum[:, sl(n)], start=True, stop=True),
                          r=[obd, cum], w=[ps])
                    fw.act(lambda h: h.activation(out=e2[:, sl(n)], in_=ps[:, :], func=AF.Sqrt), r=[ps], w=[(e2, n)])
                fw.dve(lambda h: h.tensor_scalar(out=e2[:], in0=e2[:], scalar1=1e-12, scalar2=None, op0=ALU.max),
                       r=[e2], w=[e2])
                fw.dve(lambda h: h.reciprocal(out=e2[:], in_=e2[:]), r=[e2], w=[e2])
                fw.dve(lambda h: h.tensor_tensor(out=kk[:], in0=kk[:], in1=e2[:], op=ALU.mult), r=[kk, e2], w=[kk])
                fw.dve(lambda h: h.tensor_scalar(out=km[:], in0=a_[:], scalar1=v(3), scalar2=dv[:, c, 1:2],
                                                 op0=ALU.mult, op1=ALU.add), r=[a_, vec, dv], w=[km])
                fw.dve(lambda h: h.tensor_tensor(out=km[:], in0=km[:], in1=kT[:], op=ALU.mult), r=[km, kT], w=[km])
                fw.dve(lambda h: h.scalar_tensor_tensor(out=cum[:], in0=rT[:], scalar=v(4), in1=km[:],
                                                        op0=ALU.mult, op1=ALU.mult), r=[rT, km, vec], w=[cum])
                for n in range(4):
                    ps = self.next_ps()
                    fw.pe(lambda h: h.matmul(ps[:, :], lhsT=obd[:, :], rhs=cum[:, sl(n)], start=True, stop=True),
                          r=[obd, cum], w=[ps])
                    fw.dve(lambda h: h.tensor_tensor(out=e2[:, sl(n)], in0=ps[:, :], in1=vT[:, sl(n)], op=ALU.mult),
                           r=[ps, vT], w=[(e2, n)])
                fw.dma("sp", rwT[5, cs_, :], e2[:], reads=[e2], writes=[(rwT, (5, c))])
                fw.dve(lambda h: h.scalar_tensor_tensor(out=epv[:], in0=kk[:], scalar=-1.0, in1=epv[:],
                                                        op0=ALU.mult, op1=ALU.mult), r=[kk, epv], w=[epv])
                fw.dma("sp", rwT[0, cs_, :], epv[:], reads=[epv], writes=[(rwT, (0, c))])
                fw.dve(lambda h: h.tensor_tensor(out=rT[:], in0=rT[:], in1=ep[:], op=ALU.mult), r=[rT, ep], w=[rT])
                fw.dma("sp", rwT[1, cs_, :], rT[:], reads=[rT], writes=[(rwT, (1, c))])
                fw.dve(lambda h: h.tensor_tensor(out=kk[:], in0=kk[:], in1=a_[:], op=ALU.mult), r=[kk, a_], w=[kk])
                fw.dve(lambda h: h.tensor_tensor(out=kk[:], in0=kk[:], in1=em[:], op=ALU.mult), r=[kk, em], w=[kk])
                fw.dma("sp", rwT[2, cs_, :], kk[:], reads=[kk], writes=[(rwT, (2, c))])
                fw.dve(lambda h: h.tensor_tensor(out=km[:], in0=km[:], in1=em[:], op=ALU.mult), r=[km, em], w=[km])
                fw.dma("sp", rwT[3, cs_, :], km[:], reads=[km], writes=[(rwT, (3, c))])
                fw.dve(lambda h: h.tensor_copy(out=wct[:], in_=ep[:, :].rearrange("p (c t) -> p c t", t=128)[:, :, 127]),
                       r=[ep], w=[wct])
                fw.dma("sp", self.scr["rw_wc"][cs_, :], wct[:], reads=[wct], writes=[(self.scr["rw_wc"], c)])

    def phase_rwkv_scan(self, l):
        fw = self.fw
        rwT, yT = self.scr["rwT"], self.scr["yT"]
        ident = self.ident
        with contextlib.ExitStack() as es:
            masks = self.sb(es, "rwmasks", [128, 8, 128])
            fw.dma("sp", masks[:], self.inp["rw_masks"][:, :, :], writes=[masks])
            vec = self.sb(es, "rwvec2", [128, 4, 8])
            fw.dma("sp", vec[:], self.inp["rw_vec"][l], writes=[vec])
            wc = self.sb(es, "wc", [128, 4, NT])
            fw.dma("sp", wc[:], self.scr["rw_wc"][:, :].rearrange("(c p) n -> p c n", p=128),
                   reads=[self.scr["rw_wc"]], writes=[wc])
            ST = self.sb(es, "ST", [128, 4, 64])
            fw.dve(lambda h: h.memset(ST[:], 0.0), w=[ST])
            fm = [self.sb(es, f"fm{i}", [128, 7, 4, 128]) for i in range(2)]
            tok = [self.sb(es, f"tok{i}", [128, 3, 512]) for i in range(2)]
            M4s = [self.sb(es, f"M4a{i}", [128, 8, 4, 128]) for i in range(2)]
            Las = [self.sb(es, f"La{i}", [128, 2, 8, 128]) for i in range(2)]
            Xas = [self.sb(es, f"Xa{i}", [128, 7, 8, 128]) for i in range(2)]
            Ua = self.sb(es, "Ua", [128, 8, 64])
            tSa = self.sb(es, "tSa", [128, 4, 64])
            Ys = [self.sb(es, f"Y{i}", [128, 8, 64]) for i in range(2)]
            s8 = self.sb(es, "s8", [128, 8])
            r8 = self.sb(es, "r8", [128, 8])
            Yc = self.sb(es, "Yc", [128, 8, 64])
            sq = self.sb(es, "sq", [128, 8, 64])
            o32 = self.sb(es, "o32", [128, 4, 128])
            obs = [self.sb(es, f"ob{i}", [128, 4, 128], BF16) for i in range(2)]

            def hrow(hd):
                return slice((hd % 2) * 64, (hd % 2) * 64 + 64)

            def pv4(ap3, par):
                return ap3.rearrange("p (c a) t -> p c a t", a=2)[:, :, par, :]

            def prep_stages(c):
                cs = slice(c * 128, (c + 1) * 128)
                F, TK, M4, La, Xa = fm[c % 2], tok[c % 2], M4s[c % 2], Las[c % 2], Xas[c % 2]
                st = []

                def load():
                    for q in range(7):
                        fw.dma("sp", F[:, q, :, :], rwT[q, :, cs].rearrange("(ct p) t -> p ct t", p=128),
                               reads=[rwT], writes=[(F, q)])
                    for j, q in enumerate((4, 2, 3)):
                        ps = self.next_ps()
                        for ct in range(4):
                            fw.pe(lambda h: h.transpose(ps[:, ct * 128:(ct + 1) * 128], F[:, q, ct, :], ident[:]),
                                  r=[(F, q), ident], w=[ps])
                        fw.act(lambda h: h.copy(out=TK[:, j, :], in_=ps[:, :]), r=[ps], w=[(TK, j)])
                st.append(load)

                def products(par):
                    def f():
                        heads = [par + 2 * i for i in range(4)]
                        for hd in heads:
                            ct, hr = hd // 2, hrow(hd)
                            At, Rt, Bt, Kt = (F[hr, q, ct, :] for q in range(4))
                            rA, rR, rB, rK = ((F, q) for q in range(4))
                            ps4 = self.next_ps()
                            for i, (lt, rh, rl, rr) in enumerate(((Bt, At, rB, rA), (Kt, At, rK, rA),
                                                                  (Bt, Rt, rB, rR), (Kt, Rt, rK, rR))):
                                fw.pe(lambda h: h.matmul(ps4[:, i * 128:(i + 1) * 128], lhsT=lt, rhs=rh,
                                                         start=True, stop=True), r=[rl, rr], w=[ps4])
                            fw.dve(lambda h: h.tensor_tensor(out=M4[:, hd, :, :],
                                                             in0=ps4[:, :].rearrange("p (a t) -> p a t", a=4),
                                                             in1=masks[:, 0:4, :], op=ALU.mult),
                                   r=[ps4, masks], w=[(M4, hd)])
                        psL = self.next_ps()
                        for hh, hd in enumerate(heads):
                            ct, hr = hd // 2, hrow(hd)
                            fw.pe(lambda h: h.matmul(psL[:, hh * 128:(hh + 1) * 128], lhsT=F[hr, 0, ct, :],
                                                     rhs=F[hr, 2, ct, :], start=True, stop=True),
                                  r=[(F, 0), (F, 2)], w=[psL])
                        fw.dve(lambda h: h.tensor_tensor(out=pv4(La[:, 0, :, :], par),
                                                         in0=psL[:, :].rearrange("p (a t) -> p a t", a=4),
                                                         in1=masks[:, 4:8, :], op=ALU.mult),
                               r=[psL, masks], w=[(La, (0, par))])
                    return f
                st.append(products(0))
                st.append(products(1))

                def squaring(j):
                    def f():
                        for par in range(2):
                            heads = [par + 2 * i for i in range(4)]
                            psX = self.next_ps()
                            psL2 = self.next_ps() if j < 5 else None
                            for hh, hd in enumerate(heads):
                                Xj = M4[:, hd, 0, :] if j == 0 else Xa[:, j, hd, :]
                                rX = (M4, hd) if j == 0 else (Xa, (j, par))
                                Lj = La[:, j % 2, hd, :]
                                rL = (La, (j % 2, par))
                                fw.pe(lambda h: h.matmul(psX[:, hh * 128:(hh + 1) * 128], lhsT=Lj, rhs=Xj,
                                                         start=True, stop=True), r=[rL, rX], w=[psX])
                                if j < 5:
                                    fw.pe(lambda h: h.matmul(psL2[:, hh * 128:(hh + 1) * 128], lhsT=Xj, rhs=Lj,
                                                             start=True, stop=True), r=[rL, rX], w=[psL2])
                            fw.act(lambda h: h.copy(out=pv4(Xa[:, j + 1, :, :], par),
                                                    in_=psX[:, :].rearrange("p (a t) -> p a t", a=4)),
                                   r=[psX], w=[(Xa, (j + 1, par))])
                            if j < 5:
                                fw.act(lambda h: h.copy(out=pv4(La[:, (j + 1) % 2, :, :], par),
                                                        in_=psL2[:, :].rearrange("p (a t) -> p a t", a=4)),
                                       r=[psL2], w=[(La, ((j + 1) % 2, par))])
                    return f
                for j in range(6):
                    st.append(squaring(j))
                return st

            def chain_stages(c):
                F, TK, M4, Xa, Y = fm[c % 2], tok[c % 2], M4s[c % 2], Xas[c % 2], Ys[c % 2]
                st = []
                Vt = lambda hd: TK[:, 0, hd * 64:(hd + 1) * 64]
                rXj = lambda j, hd: (M4, hd) if j == 0 else (Xa, (j, hd % 2))
                Xj = lambda j, hd: M4[:, hd, 0, :] if j == 0 else Xa[:, j, hd, :]

                def zstage():
                    for par in range(2):
                        psz = self.next_ps()
                        for hh in range(4):
                            hd = par + 2 * hh
                            ct, hr = hd // 2, hrow(hd)
                            o = psz[:, hh * 64:(hh + 1) * 64]
                            fw.pe(lambda h: h.matmul(o, lhsT=M4[:, hd, 1, :], rhs=Vt(hd), start=True, stop=False),
                                  r=[(M4, hd), (TK, 0)], w=[psz])
                            fw.pe(lambda h: h.matmul(o, lhsT=F[hr, 0, ct, :], rhs=ST[hr, ct, :], start=False, stop=True),
                                  r=[(F, 0), ST], w=[psz])
                        fw.act(lambda h: h.copy(out=pv4(Ua[:, :, :], par),
                                                in_=psz[:, 0:256].rearrange("p (a v) -> p a v", a=4)),
                               r=[psz], w=[(Ua, par)])
                st.append(zstage)

                def ustage(j):
                    def f():
                        psu = self.next_ps()
                        for hd in range(8):
                            fw.pe(lambda h: h.matmul(psu[:, hd * 64:(hd + 1) * 64], lhsT=Xj(j, hd), rhs=Ua[:, hd, :],
                                                     start=True, stop=True), r=[rXj(j, hd), Ua], w=[psu])
                        fw.dve(lambda h: h.tensor_tensor(out=Ua[:, :, :].rearrange("p h v -> p (h v)"), in0=psu[:, :],
                                                         in1=Ua[:, :, :].rearrange("p h v -> p (h v)"), op=ALU.add),
                               r=[psu, Ua], w=[Ua])
                    return f
                for j in range(7):
                    st.append(ustage(j))

                def ystage():
                    for par in range(2):
                        psy = self.next_ps()
                        for hh in range(4):
                            hd = par + 2 * hh
                            ct, hr = hd // 2, hrow(hd)
                            o = psy[:, hh * 64:(hh + 1) * 64]
                            fw.pe(lambda h: h.matmul(o, lhsT=F[hr, 1, ct, :], rhs=ST[hr, ct, :], start=True, stop=False),
                                  r=[(F, 1), ST], w=[psy])
                            fw.pe(lambda h: h.matmul(o, lhsT=M4[:, hd, 2, :], rhs=Ua[:, hd, :], start=False, stop=False),
                                  r=[(M4, hd), Ua], w=[psy])
                            fw.pe(lambda h: h.matmul(o, lhsT=M4[:, hd, 3, :], rhs=Vt(hd), start=False, stop=True),
                                  r=[(M4, hd), (TK, 0)], w=[psy])
                        fw.act(lambda h: h.copy(out=pv4(Y[:, :, :], par),
                                                in_=psy[:, 0:256].rearrange("p (a v) -> p a v", a=4)),
                               r=[psy], w=[(Y, par)])
                    pss = self.next_ps()
                    for hd in range(8):
                        ct = hd // 2
                        o = pss[:, hd * 64:(hd + 1) * 64]
                        fw.pe(lambda h: h.matmul(o, lhsT=TK[:, 1, ct * 128:(ct + 1) * 128], rhs=Ua[:, hd, :],
                                                 start=True, stop=False), r=[(TK, 1), Ua], w=[pss])
                        fw.pe(lambda h: h.matmul(o, lhsT=TK[:, 2, ct * 128:(ct + 1) * 128], rhs=Vt(hd),
                                                 start=False, stop=True), r=[(TK, 2), (TK, 0)], w=[pss])
                    pv = pss[:, :].rearrange("p (c a v) -> p c a v", c=4, a=2)
                    for par in range(2):
                        hr = slice(par * 64, par * 64 + 64)
                        fw.dve(lambda h: h.tensor_tensor(out=tSa[hr, :, :], in0=pv[hr, :, par, :], in1=ST[hr, :, :],
                                                         op=ALU.add), r=[pss, ST], w=[(tSa, par)])
                        fw.dve(lambda h: h.tensor_tensor(out=ST[hr, :, :], in0=tSa[hr, :, :],
                                                         in1=wc[hr, :, c:c + 1].to_broadcast([64, 4, 64]), op=ALU.mult),
                               r=[(tSa, par), wc], w=[ST])
                st.append(ystage)
                return st

            def epilogue(c):
                cs = slice(c * 128, (c + 1) * 128)
                F, Y, ob = fm[c % 2], Ys[c % 2], obs[c % 2]
                fw.dve(lambda h: h.tensor_reduce(out=s8[:], in_=Y[:], axis=AX.X, op=ALU.add), r=[Y], w=[s8])
                fw.dve(lambda h: h.tensor_scalar(out=s8[:], in0=s8[:], scalar1=1.0 / 64, scalar2=None, op0=ALU.mult),
                       r=[s8], w=[s8])
                fw.dve(lambda h: h.tensor_tensor(out=Yc[:], in0=Y[:], in1=s8[:, :, None].to_broadcast([128, 8, 64]),
                                                 op=ALU.subtract), r=[Y, s8], w=[Yc])
                fw.dve(lambda h: h.tensor_tensor(out=sq[:], in0=Yc[:], in1=Yc[:], op=ALU.mult), r=[Yc], w=[sq])
                fw.dve(lambda h: h.tensor_reduce(out=r8[:], in_=sq[:], axis=AX.X, op=ALU.add), r=[sq], w=[r8])
                fw.act(lambda h: h.activation(out=r8[:], in_=r8[:], func=AF.Sqrt, bias=64e-5, scale=1.0 / 64), r=[r8], w=[r8])
                fw.dve(lambda h: h.reciprocal(out=r8[:], in_=r8[:]), r=[r8], w=[r8])
                fw.dve(lambda h: h.tensor_tensor(out=Yc[:], in0=Yc[:], in1=r8[:, :, None].to_broadcast([128, 8, 64]),
                                                 op=ALU.mult), r=[Yc, r8], w=[Yc])
                pst = self.next_ps()
                ycf = Yc[:, :, :].rearrange("p h d -> p (h d)")
                for ct in range(4):
                    fw.pe(lambda h: h.transpose(pst[:, ct * 128:(ct + 1) * 128], ycf[:, ct * 128:(ct + 1) * 128], ident[:]),
                          r=[Yc, ident], w=[pst])
                for ct in range(4):
                    fw.dve(lambda h: h.tensor_scalar(out=o32[:, ct, :], in0=pst[:, ct * 128:(ct + 1) * 128],
                                                     scalar1=vec[:, ct, 5:6], scalar2=vec[:, ct, 6:7],
                                                     op0=ALU.mult, op1=ALU.add), r=[pst, vec], w=[(o32, ct)])
                fw.dve(lambda h: h.tensor_tensor(out=o32[:], in0=o32[:], in1=F[:, 5, :, :], op=ALU.add), r=[o32, (F, 5)], w=[o32])
                fw.dve(lambda h: h.tensor_tensor(out=ob[:], in0=o32[:], in1=F[:, 6, :, :], op=ALU.mult), r=[o32, (F, 6)], w=[ob])
                fw.dma("sp", yT[1, :, cs].rearrange("(ct p) t -> p ct t", p=128), ob[:], reads=[ob],
                       writes=[(yT, ("b", c))])

            import os as _os
            _CUT = int(_os.environ.get("RW_CUT", 10 ** 9))
            _cnt = [0]

            def call(f):
                _cnt[0] += 1
                if _cnt[0] <= _CUT:
                    f()
            for f in prep_stages(0):
                call(f)
            for c in range(NT):
                ch = chain_stages(c)
                pr = prep_stages(c + 1) if c + 1 < NT else []
                for i in range(max(len(ch), len(pr))):
                    if i < len(pr):
                        call(pr[i])
                    if i < len(ch):
                        call(ch[i])
                call(lambda: epilogue(c))

    def ln_tile(self, z, st, mv, gbc, bbc, gi):
        fw = self.fw
        for c in range(2):
            fw.dve(lambda h: h.bn_stats(out=st[:, c, :], in_=z[:, c * 512:(c + 1) * 512]), r=[z], w=[(st, c)])
        fw.dve(lambda h: h.bn_aggr(out=mv[:, 0:2], in_=st[:, :, :].rearrange("p a b -> p (a b)")), r=[st], w=[mv])
        fw.act(lambda h: h.activation(out=mv[:, 2:3], in_=mv[:, 1:2], func=AF.Sqrt, bias=1e-5, scale=1.0), r=[mv], w=[mv])
        fw.dve(lambda h: h.reciprocal(out=mv[:, 2:3], in_=mv[:, 2:3]), r=[mv], w=[mv])
        fw.dve(lambda h: h.tensor_scalar(out=z[:], in0=z[:], scalar1=mv[:, 0:1], scalar2=mv[:, 2:3],
                                         op0=ALU.subtract, op1=ALU.mult), r=[z, mv], w=[z])
        fw.dve(lambda h: h.tensor_tensor(out=z[:], in0=z[:], in1=gbc[:, gi, :], op=ALU.mult), r=[z, gbc], w=[z])
        fw.dve(lambda h: h.tensor_tensor(out=z[:], in0=z[:], in1=gbc[:, gi + 1, :], op=ALU.add), r=[z, gbc], w=[z])

    def store_x_and_xT(self, z, ti, x_d, xT_d, xo):
        fw = self.fw
        fw.dma("sp", x_d[ti * 128:(ti + 1) * 128, :], z[:], reads=[z], writes=[(x_d, ti)])
        for hh in range(2):
            ps = self.next_ps()
            for c in range(4):
                cc = hh * 4 + c
                fw.pe(lambda h: h.transpose(ps[:, c * 128:(c + 1) * 128], z[:, cc * 128:(cc + 1) * 128], self.ident[:]),
                      r=[z, self.ident], w=[ps])
            fw.act(lambda h: h.copy(out=xo[:, hh * 4:(hh + 1) * 4, :], in_=ps[:, :].rearrange("p (c t) -> p c t", c=4)),
                   r=[ps], w=[(xo, hh)])
        fw.dma("sp", xT_d[:, ti * 128:(ti + 1) * 128].rearrange("(c p) t -> p c t", p=128), xo[:], reads=[xo],
               writes=[(xT_d, ti)])

    def phase_merge(self, l, xT_d, xres_ap, xres_dep):
        fw = self.fw
        yT = self.scr["yT"]
        x1_d, x1T_d = self.scr[f"x1_{l}"], self.scr[f"x1T_{l}"]
        with contextlib.ExitStack() as es:
            xT = self.load_xT(es, xT_d)
            wg = self.sb(es, "wg", [128, 8, 3072], BF16)
            for b in range(3):
                src = self.inp["w_in"][l, :, C_GATE + b * 1024:C_GATE + (b + 1) * 1024].rearrange("(kc p) c -> p kc c", p=128)
                fw.dma("pool", wg[:, :, b * 1024:(b + 1) * 1024], src, writes=[(wg, b)])
            wb = self.sb(es, "wb", [128, 3, 4, D], BF16)
            for b in range(3):
                fw.dma("pool", wb[:, b, :, :], self.inp["w_branch"][l, b].rearrange("(kc p) c -> p kc c", p=128),
                       writes=[(wb, b)])
            wo = self.sb(es, "wo", [128, 8, D], BF16)
            fw.dma("pool", wo[:], self.inp["w_out"][l].rearrange("(kc p) c -> p kc c", p=128), writes=[wo])
            gbc = self.sb(es, "gbc", [128, 4, D])
            fw.dma("sp", gbc[:], self.inp["ln_gb"][l:l + 1, :, :].to_broadcast([128, 4, D]), writes=[gbc])
            yb = self.sb(es, "yb", [128, 3, 4, 512], BF16)
            mg = self.sb(es, "mg", [128, 8, 512], BF16)
            sg = self.sb(es, "sg", [128, 512])
            acc = self.sb(es, "acc", [128, 512])
            z = [self.sb(es, f"z{i}", [128, D]) for i in range(2)]
            xr = [self.sb(es, f"xr{i}", [128, D]) for i in range(2)]
            xo = [self.sb(es, f"xo{i}", [128, 8, 128], BF16) for i in range(2)]
            st = self.sb(es, "st", [128, 2, 6])
            mv = self.sb(es, "mv", [128, 4])
            for n in range(4):
                ts = slice(n * 512, (n + 1) * 512)
                for b in range(3):
                    fw.dma("sp", yb[:, b, :, :], yT[b, :, ts].rearrange("(c p) t -> p c t", p=128),
                           reads=[yT], writes=[(yb, b)])
                for dt in range(8):
                    for b in range(3):
                        ps = self.next_ps()
                        self.proj_fm(wg, xT, b * 1024 + dt * 128, 128, n, ps)
                        fw.act(lambda h: h.activation(out=sg[:], in_=ps[:, :], func=AF.Sigmoid), r=[ps], w=[sg])
                        ps2 = self.next_ps()
                        for kc in range(4):
                            fw.pe(lambda h: h.matmul(ps2[:, :], lhsT=wb[:, b, kc, dt * 128:(dt + 1) * 128],
                                                     rhs=yb[:, b, kc, :], start=(kc == 0), stop=(kc == 3)),
                                  r=[(wb, b), (yb, b)], w=[ps2])
                        if b == 0:
                            fw.dve(lambda h: h.tensor_tensor(out=acc[:], in0=ps2[:, :], in1=sg[:], op=ALU.mult),
                                   r=[ps2, sg], w=[acc])
                        else:
                            fw.dve(lambda h: h.tensor_tensor(out=sg[:], in0=ps2[:, :], in1=sg[:], op=ALU.mult),
                                   r=[ps2, sg], w=[sg])
                            dst = mg[:, dt, :] if b == 2 else acc[:]
                            fw.dve(lambda h: h.tensor_tensor(out=dst, in0=acc[:], in1=sg[:], op=ALU.add),
                                   r=[acc, sg], w=[(mg, dt)] if b == 2 else [acc])
                for tt in range(4):
                    ti = n * 4 + tt
                    zz, xx = z[ti % 2], xr[ti % 2]
                    fw.dma("sp", xx[:], xres_ap[ti * 128:(ti + 1) * 128, :],
                           reads=[(xres_dep, ti)] if xres_dep else [], writes=[xx])
                    for hf in range(2):
                        ps = self.next_ps()
                        for dt in range(8):
                            fw.pe(lambda h: h.matmul(ps[:, :], lhsT=mg[:, dt, tt * 128:(tt + 1) * 128],
                                                     rhs=wo[:, dt, hf * 512:(hf + 1) * 512], start=(dt == 0), stop=(dt == 7)),
                                  r=[(mg, dt), wo], w=[ps])
                        fw.dve(lambda h: h.scalar_tensor_tensor(out=zz[:, hf * 512:(hf + 1) * 512],
                                                                in0=xx[:, hf * 512:(hf + 1) * 512], scalar=ALPHA,
                                                                in1=ps[:, :], op0=ALU.mult, op1=ALU.add),
                               r=[xx, ps], w=[zz])
                    self.ln_tile(zz, st, mv, gbc, None, 0)
                    self.store_x_and_xT(zz, ti, x1_d, x1T_d, xo[ti % 2])

    def phase_tables(self, layers):
        fw = self.fw
        with contextlib.ExitStack() as es:
            ins = [self.sb(es, f"tin{i}", [128, 4, D]) for i in range(3)]
            outs = [self.sb(es, f"tout{i}", [128, 4, D], BF16) for i in range(3)]
            k = 0
            for l in layers:
                for half, nm in enumerate((f"peer_u{l}", f"peer_v{l}")):
                    src = self.inp[nm].rearrange("(p i) d -> p i d", p=128)
                    dstT = self.scr[f"uvb{l}"]
                    dst = dstT.t.rearrange("(p i) d -> p i d", p=128)[:, :, half * D:(half + 1) * D]
                    for i0 in range(0, 128, 4):
                        a, b = ins[k % 3], outs[k % 3]
                        fw.dma("sp", a[:], src[:, i0:i0 + 4, :], writes=[a])
                        if k % 2 == 0:
                            fw.act(lambda h: h.copy(out=b[:], in_=a[:]), r=[a], w=[b])
                        else:
                            fw.dve(lambda h: h.tensor_copy(out=b[:], in_=a[:]), r=[a], w=[b])
                        fw.dma("pool", dst[:, i0:i0 + 4, :], b[:], reads=[b], writes=[(dstT, (half, i0))])
                        k += 1

    def top16(self, vals, idxs, src, scratch, n):
        fw = self.fw
        fw.dve(lambda h: h.max(out=vals[:, 0:8], in_=src), r=[self._t16_src], w=[self._t16_v])
        fw.dve(lambda h: h.max_index(out=idxs[:, 0:8], in_max=vals[:, 0:8], in_values=src),
               r=[self._t16_src, self._t16_v], w=[self._t16_i])
        fw.dve(lambda h: h.match_replace(out=scratch, in_to_replace=vals[:, 0:8], in_values=src, imm_value=-1e30),
               r=[self._t16_src, self._t16_v], w=[self._t16_s])
        fw.dve(lambda h: h.max(out=vals[:, 8:16], in_=scratch), r=[self._t16_s], w=[self._t16_v])
        fw.dve(lambda h: h.max_index(out=idxs[:, 8:16], in_max=vals[:, 8:16], in_values=scratch),
               r=[self._t16_s, self._t16_v], w=[self._t16_i])

    def phase_peer(self, l, x2_d, x2T_d):
        fw = self.fw
        x1_d = self.scr[f"x1_{l}"]
        UV_ap = self.scr[f"uvb{l}"].t
        ident = self.ident
        with contextlib.ExitStack() as es:
            wq = self.sb(es, "wq", [128, 8, 2048], BF16)
            for kc in range(8):
                fw.dma("pool", wq[:, kc, :], self.inp["peer_wq"][l, kc * 128:(kc + 1) * 128, :], writes=[(wq, kc)])
            skT = self.sb(es, "skT", [128, 2, 128])
            fw.dma("sp", skT[:], self.inp["peer_skT"][l].rearrange("a d n -> d a n"), writes=[skT])
            iota = self.sb(es, "iota16", [128, 16])
            fw.dma("sp", iota[:], self.inp["iota16"][:, :], writes=[iota])
            gbc = self.sb(es, "gbc2", [128, 2, D])
            fw.dma("sp", gbc[:], self.inp["ln_gb"][l:l + 1, 2:4, :].to_broadcast([128, 2, D]), writes=[gbc])
            xts = [self.sb(es, f"pxt{i}", [128, D]) for i in range(2)]
            xTf = self.sb(es, "xTf", [128, 8, 128], BF16)
            qT = self.sb(es, "qT", [128, 16, 128])
            sc = self.sb(es, "sc", [128, 16, 128])
            sc2 = self.sb(es, "sc2", [128, 16, 128])
            v16 = self.sb(es, "v16", [128, 16, 16])
            i16 = self.sb(es, "i16", [128, 16, 16], U32)
            i16f = self.sb(es, "i16f", [128, 8, 2, 16])
            cand = self.sb(es, "cand", [128, 8, 256])
            cand2 = T(sc2.t[:, :, :].rearrange("p (h a) n -> p h (a n)", a=2), "cand2")
            cand2.dep = sc2.dep
            top = self.sb(es, "top", [128, 8, 16])
            pos = self.sb(es, "pos", [128, 8, 16], U32)
            prc = self.sb(es, "prc", [128, 2, 8, 16], U32)
            prcf = self.sb(es, "prcf", [128, 2, 8, 16])
            oh = self.sb(es, "oh", [128, 128, 16])
            sel = self.sb(es, "sel", [128, 2, 128])
            idxf = self.sb(es, "idxf", [128, 128])
            idx = self.sb(es, "idx", [128, 128], U32)
            gate = self.sb(es, "gate", [128, 8, 16])
            gs = self.sb(es, "gs", [128, 8])
            actv = self.sb(es, "actv", [128, 128])
            gtmp = self.sb(es, "gtmp", [128, 128])
            wgt = self.sb(es, "wgt", [128, 128])
            junk = self.sb(es, "junk", [128, D])
            rows = [self.sb(es, f"rows{i}", [128, 2 * D], BF16) for i in range(20)]
            dgs = [self.sb(es, f"dg{i}", [128, 128], BF16) for i in range(4)]
            accb = [self.ps[6], self.ps[7]]
            self.ps_reserved = set(accb)
            ys = [self.sb(es, f"py{i}", [128, D]) for i in range(2)]
            xo = [self.sb(es, f"pxo{i}", [128, 8, 128], BF16) for i in range(2)]
            st = self.sb(es, "pst", [128, 2, 6])
            mv = self.sb(es, "pmv", [128, 4])
            idxs = [idx, self.sb(es, "idx_b", [128, 128], U32)]
            gates = [gate, self.sb(es, "gate_b", [128, 8, 16])]
            import os as _os
            _NODMA = int(_os.environ.get("PEER_NODMA", 0)); _NODVE = int(_os.environ.get("PEER_NODVE", 0))

            def stage1(ti):
                xt, idx, gate = xts[ti % 2], idxs[ti % 2], gates[ti % 2]
                th = []

                def t_load():
                    fw.dma("sp", xt[:], x1_d[ti * 128:(ti + 1) * 128, :], reads=[(x1_d, ti)], writes=[xt])
                    for hh in range(2):
                        ps = self.next_ps()
                        for c in range(4):
                            cc = hh * 4 + c
                            fw.pe(lambda h: h.transpose(ps[:, c * 128:(c + 1) * 128], xt[:, cc * 128:(cc + 1) * 128], ident[:]),
                                  r=[xt, ident], w=[ps])
                        fw.act(lambda h: h.copy(out=xTf[:, hh * 4:(hh + 1) * 4, :],
                                                in_=ps[:, :].rearrange("p (c t) -> p c t", c=4)), r=[ps], w=[(xTf, hh)])
                th.append(t_load)

                def t_q(g4):
                    def f():
                        ps = self.next_ps()
                        for b4 in range(4):
                            blk = g4 * 4 + b4
                            for kc in range(8):
                                fw.pe(lambda h: h.matmul(ps[:, b4 * 128:(b4 + 1) * 128],
                                                         lhsT=wq[:, kc, blk * 128:(blk + 1) * 128], rhs=xTf[:, kc, :],
                                                         start=(kc == 0), stop=(kc == 7)), r=[(wq, kc), xTf], w=[ps])
                        fw.act(lambda h: h.copy(out=qT[:, g4 * 4:(g4 + 1) * 4, :],
                                                in_=ps[:, :].rearrange("p (b t) -> p b t", b=4)), r=[ps], w=[(qT, g4)])
                    return f
                for g4 in range(4):
                    th.append(t_q(g4))

                def t_s(g4):
                    def f():
                        ps = self.next_ps()
                        for b4 in range(4):
                            blk = g4 * 4 + b4
                            fw.pe(lambda h: h.matmul(ps[:, b4 * 128:(b4 + 1) * 128], lhsT=qT[:, blk, :],
                                                     rhs=skT[:, blk % 2, :], start=True, stop=True),
                                  r=[(qT, g4), skT], w=[ps])
                        fw.act(lambda h: h.copy(out=sc[:, g4 * 4:(g4 + 1) * 4, :],
                                                in_=ps[:, :].rearrange("p (b n) -> p b n", b=4)), r=[ps], w=[(sc, g4)])
                    return f
                for g4 in range(4):
                    th.append(t_s(g4))

                def t_top1(blk):
                    def f():
                        self._t16_src, self._t16_v, self._t16_i, self._t16_s = sc, v16, i16, sc2
                        self.top16(v16[:, blk, :], i16[:, blk, :], sc[:, blk, :], sc2[:, blk, :], 128)
                    return f
                for blk in range(16):
                    th.append(t_top1(blk))

                def t_cand():
                    v4 = v16[:, :, :].rearrange("p (h a) k -> p h a k", a=2)
                    fw.dve(lambda h: h.tensor_tensor(
                        out=cand[:, :, :].rearrange("p h (i j) -> p h i j", i=16),
                        in0=v4[:, :, 0, :][:, :, :, None].to_broadcast([128, 8, 16, 16]),
                        in1=v4[:, :, 1, :][:, :, None, :].to_broadcast([128, 8, 16, 16]), op=ALU.add), r=[v16], w=[cand])
                th.append(t_cand)

                def t_top2(hd):
                    def f():
                        self._t16_src, self._t16_v, self._t16_i, self._t16_s = cand, top, pos, cand2
                        self.top16(top[:, hd, :], pos[:, hd, :], cand[:, hd, :], cand2[:, hd, :], 256)
                    return f
                for hd in range(8):
                    th.append(t_top2(hd))

                def t_idx0():
                    fw.dve(lambda h: h.tensor_scalar(out=prc[:, 0, :, :], in0=pos[:], scalar1=4, scalar2=None,
                                                     op0=ALU.logical_shift_right), r=[pos], w=[prc])
                    fw.dve(lambda h: h.tensor_scalar(out=prc[:, 1, :, :], in0=pos[:], scalar1=15, scalar2=None,
                                                     op0=ALU.bitwise_and), r=[pos], w=[prc])
                    fw.dve(lambda h: h.tensor_copy(out=prcf[:], in_=prc[:]), r=[prc], w=[prcf])
                    fw.dve(lambda h: h.tensor_copy(out=i16f[:], in_=i16[:, :, :].rearrange("p (h a) k -> p h a k", a=2)),
                           r=[i16], w=[i16f])
                th.append(t_idx0)

                def t_idx1(a):
                    def f():
                        fw.dve(lambda h: h.tensor_tensor(
                            out=oh[:], in0=iota[:, None, :].to_broadcast([128, 128, 16]),
                            in1=prcf[:, a, :, :].rearrange("p h k -> p (h k)")[:, :, None].to_broadcast([128, 128, 16]),
                            op=ALU.is_equal), r=[iota, prcf], w=[oh])
                        fw.dve(lambda h: h.tensor_tensor(
                            out=oh[:, :, :].rearrange("p (h k) j -> p h k j", h=8),
                            in0=oh[:, :, :].rearrange("p (h k) j -> p h k j", h=8),
                            in1=i16f[:, :, a, :][:, :, None, :].to_broadcast([128, 8, 16, 16]), op=ALU.mult),
                            r=[oh, i16f], w=[oh])
                        fw.dve(lambda h: h.tensor_reduce(out=sel[:, a, :], in_=oh[:], axis=AX.X, op=ALU.add),
                               r=[oh], w=[(sel, a)])
                    return f
                th.append(t_idx1(0))
                th.append(t_idx1(1))

                def t_fin():
                    fw.dve(lambda h: h.scalar_tensor_tensor(out=idxf[:], in0=sel[:, 0, :], scalar=128.0, in1=sel[:, 1, :],
                                                            op0=ALU.mult, op1=ALU.add), r=[sel], w=[idxf])
                    fw.dve(lambda h: h.tensor_copy(out=idx[:], in_=idxf[:]), r=[idxf], w=[idx])
                    fw.dve(lambda h: h.tensor_tensor(out=gate[:], in0=top[:], in1=top[:, :, 0:1].to_broadcast([128, 8, 16]),
                                                     op=ALU.subtract), r=[top], w=[gate])
                    fw.act(lambda h: h.activation(out=gate[:], in_=gate[:], func=AF.Exp), r=[gate], w=[gate])
                    fw.dve(lambda h: h.tensor_reduce(out=gs[:], in_=gate[:], axis=AX.X, op=ALU.add), r=[gate], w=[gs])
                    fw.dve(lambda h: h.reciprocal(out=gs[:], in_=gs[:]), r=[gs], w=[gs])
                    fw.dve(lambda h: h.tensor_tensor(out=gate[:], in0=gate[:], in1=gs[:, :, None].to_broadcast([128, 8, 16]),
                                                     op=ALU.mult), r=[gate, gs], w=[gate])
                th.append(t_fin)
                return th

            rr = 0
            for f in stage1(0):
                f()
            for ti in range(NT):
                xt, y, idx, gate = xts[ti % 2], ys[ti % 2], idxs[ti % 2], gates[ti % 2]
                nxt = stage1(ti + 1) if ti + 1 < NT else []
                nslot = [0]

                def tick():
                    nslot[0] += 1
                    if nslot[0] % 2 == 0 and nxt:
                        nxt.pop(0)()
                gflat = gate[:, :, :].rearrange("p h k -> p (h k)")
                GP = 4
                for g in range(128 // GP):
                    gs_ = slice(g * GP, (g + 1) * GP)
                    grp = []
                    for j in range(g * GP, (g + 1) * GP):
                        rb = rows[rr % len(rows)]
                        rr += 1
                        grp.append((j, rb))
                        fw.dma("pool", rb[:], UV_ap[:, :], reads=[idx], writes=[rb],
                               indirect=bass.IndirectOffsetOnAxis(ap=idx[:, j:j + 1], axis=0))
                        fw.dve(lambda h: h.scalar_tensor_tensor(out=junk[:], in0=rb[:, 0:D], scalar=1.0, in1=xt[:],
                                                                op0=ALU.mult, op1=ALU.mult, accum_out=actv[:, j:j + 1]),
                               r=[rb, xt], w=[(actv, g)])
                        tick()
                    a8, t8, w8 = actv[:, gs_], gtmp[:, gs_], wgt[:, gs_]
                    fw.dve(lambda h: h.tensor_tensor(out=t8, in0=a8, in1=a8, op=ALU.mult), r=[(actv, g)], w=[(gtmp, g)])
                    fw.dve(lambda h: h.tensor_scalar(out=t8, in0=t8, scalar1=0.044715, scalar2=1.0, op0=ALU.mult, op1=ALU.add),
                           r=[(gtmp, g)], w=[(gtmp, g)])
                    fw.dve(lambda h: h.tensor_tensor(out=t8, in0=t8, in1=a8, op=ALU.mult), r=[(gtmp, g), (actv, g)], w=[(gtmp, g)])
                    fw.act(lambda h: h.activation(out=t8, in_=t8, func=AF.Sigmoid, scale=1.5957691216057308),
                           r=[(gtmp, g)], w=[(gtmp, g)])
                    fw.dve(lambda h: h.tensor_tensor(out=t8, in0=t8, in1=a8, op=ALU.mult), r=[(gtmp, g), (actv, g)], w=[(gtmp, g)])
                    fw.dve(lambda h: h.tensor_tensor(out=w8, in0=t8, in1=gflat[:, gs_], op=ALU.mult),
                           r=[(gtmp, g), gate], w=[(wgt, g)])
                    for j, rb in grp:
                        dg = dgs[j % len(dgs)]
                        fw.act(lambda h: h.activation(out=dg[:], in_=ident[:], func=AF.Copy, scale=wgt[:, j:j + 1]),
                               r=[ident, (wgt, g)], w=[dg])
                        for hf in range(2):
                            fw.pe(lambda h: h.matmul(accb[hf][:, :], lhsT=dg[:], rhs=rb[:, D + hf * 512:D + (hf + 1) * 512],
                                                     start=(j == 0), stop=(j == 127)), r=[dg, rb], w=[accb[hf]])
                while nxt:
                    nxt.pop(0)()
                for hf in range(2):
                    fw.dve(lambda h: h.scalar_tensor_tensor(out=y[:, hf * 512:(hf + 1) * 512],
                                                            in0=xt[:, hf * 512:(hf + 1) * 512], scalar=ALPHA,
                                                            in1=accb[hf][:, :], op0=ALU.mult, op1=ALU.add),
                           r=[xt, accb[hf]], w=[y])
                self.ln_tile(y, st, mv, gbc, None, 0)
                self.store_x_and_xT(y, ti, x2_d, x2T_d, xo[ti % 2])
            self.ps_reserved = set()

    def phase_mla(self, l, xT_d):
        fw, nc = self.fw, self.nc
        yT = self.scr["yT"]
        SC = 96.0 ** -0.5
        with contextlib.ExitStack() as es:
            qn = self.sb(es, "qn", [64, 8, S], BF16)
            qp = self.sb(es, "qp", [32, 8, S], BF16)
            kn = self.sb(es, "kn", [64, 8, S], BF16)
            kp = self.sb(es, "kp", [32, S], BF16)
            V = self.sb(es, "V", [128, NT, 8, 65], BF16)
            cmask = self.sb(es, "cmask", [128, 128], BF16)
            identb = self.sb(es, "identb", [128, 128], BF16)
            fw.dma("pool", cmask[:], self.inp["cmask"][:, :], writes=[cmask])
            fw.dve(lambda h: h.tensor_copy(out=identb[:], in_=self.ident[:]), r=[self.ident], w=[identb])
            fw.pool(lambda h: h.memset(V[:], 1.0), w=[V])
            with contextlib.ExitStack() as es2:
                xT = self.load_xT(es2, xT_d)
                wt = self.sb(es2, "mw", [128, 8, 448], BF16)
                self.load_w_cols(wt, l, C_MLA, 416, key=0)
                for hf in range(2):
                    src = self.inp["w_in"][l, :, C_MLA + 384 + (1 - hf) * 16:C_MLA + 384 + (2 - hf) * 16].rearrange(
                        "(kc p) c -> p kc c", p=128)
                    fw.dma("pool", wt[:, :, 416 + hf * 16:416 + (hf + 1) * 16], src, writes=[(wt, 1 + hf)])
                vec = self.sb(es2, "mvec", [128, 3])
                fw.dma("sp", vec[:], self.inp["mla_vec"][l], writes=[vec])
                cs = self.sb(es2, "cs", [32, 2, S])
                fw.dma("sp", cs[:], self.inp["rope_cs"].rearrange("a p t -> p a t"), writes=[cs])
                wuq = self.sb(es2, "wuq", [128, 2, 768], BF16)
                fw.dma("pool", wuq[:], self.inp["mla_w_uq"][l].rearrange("(kc p) c -> p kc c", p=128), writes=[wuq])
                wuqr = self.sb(es2, "wuqr", [128, 2, 256], BF16)
                fw.dma("pool", wuqr[:], self.inp["mla_w_uq_rot"][l].rearrange("(kc p) c -> p kc c", p=128), writes=[wuqr])
                wk = self.sb(es2, "wk", [128, 512], BF16)
                fw.dma("pool", wk[:], self.inp["mla_w_ukv_k"][l], writes=[wk])
                wv = self.sb(es2, "wv", [128, 512], BF16)
                fw.dma("pool", wv[:], self.inp["mla_w_ukv_v"][l], writes=[wv])
                ones = self.sb(es2, "ones", [128, 128])
                fw.pool(lambda h: h.memset(ones[:], 1.0), w=[ones])
                cl = self.sb(es2, "cl", [128, 3, 512])
                sq = self.sb(es2, "sq", [128, 3, 512])
                rs = self.sb(es2, "rs", [128, 2, 512])
                cn = self.sb(es2, "cn", [128, 3, 512], BF16)
                kr = self.sb(es2, "kr", [32, 2, 512])
                tmp32 = self.sb(es2, "tmp32", [32, 2, 512])
                for n in range(4):
                    ts = slice(n * 512, (n + 1) * 512)
                    for j in range(3):
                        ps = self.next_ps()
                        self.proj_fm(wt, xT, j * 128, 128, n, ps)
                        fw.act(lambda h: h.copy(out=cl[:, j, :], in_=ps[:, :]), r=[ps], w=[(cl, j)])
                        fw.dve(lambda h: h.tensor_tensor(out=sq[:, j, :], in0=cl[:, j, :], in1=cl[:, j, :], op=ALU.mult),
                               r=[(cl, j)], w=[(sq, j)])
                    for j in range(2):
                        ps = self.next_ps()
                        self.proj_fm(wt, xT, 384 + j * 32, 32, n, ps)
                        fw.act(lambda h: h.copy(out=kr[:, j, :], in_=ps[0:32, :]), r=[ps], w=[(kr, j)])
                    for g, (tiles, dim) in enumerate((((0, 1), 256.0), ((2,), 128.0))):
                        ps = self.next_ps()
                        for ii, j in enumerate(tiles):
                            fw.pe(lambda h: h.matmul(ps[:, :], lhsT=ones[:, :], rhs=sq[:, j, :],
                                                     start=(ii == 0), stop=(ii == len(tiles) - 1)),
                                  r=[ones, (sq, j)], w=[ps])
                        fw.act(lambda h: h.activation(out=rs[:, g, :], in_=ps[:, :], func=AF.Sqrt, bias=1e-6,
                                                      scale=1.0 / dim), r=[ps], w=[(rs, g)])
                        fw.dve(lambda h: h.reciprocal(out=rs[:, g, :], in_=rs[:, g, :]), r=[(rs, g)], w=[(rs, g)])
                        for j in tiles:
                            fw.dve(lambda h: h.scalar_tensor_tensor(out=cn[:, j, :], in0=cl[:, j, :],
                                                                    scalar=vec[:, j:j + 1], in1=rs[:, g, :],
                                                                    op0=ALU.mult, op1=ALU.mult),
                                   r=[(cl, j), vec, (rs, g)], w=[(cn, j)])
                    fw.dve(lambda h: h.tensor_tensor(out=tmp32[:, 0, :], in0=kr[:, 0, :], in1=cs[:, 0, ts], op=ALU.mult),
                           r=[(kr, 0), cs], w=[(tmp32, 0)])
                    fw.dve(lambda h: h.tensor_tensor(out=tmp32[:, 1, :], in0=kr[:, 1, :], in1=cs[:, 1, ts], op=ALU.mult),
                           r=[(kr, 1), cs], w=[(tmp32, 1)])
                    fw.dve(lambda h: h.tensor_tensor(out=kp[:, ts], in0=tmp32[:, 0, :], in1=tmp32[:, 1, :], op=ALU.add),
                           r=[tmp32], w=[(kp, n)])
                    for hd in range(8):
                        ps = self.next_ps()
                        for kc in range(2):
                            fw.pe(lambda h: h.matmul(ps[0:64, :], lhsT=wuq[:, kc, hd * 96:hd * 96 + 64],
                                                     rhs=cn[:, kc, :], start=(kc == 0), stop=(kc == 1)),
                                  r=[wuq, (cn, kc)], w=[ps])
                        fw.act(lambda h: h.activation(out=qn[:, hd, ts], in_=ps[0:64, :], func=AF.Copy, scale=SC),
                               r=[ps], w=[(qn, (hd, n))])
                        ps = self.next_ps()
                        for kc in range(2):
                            fw.pe(lambda h: h.matmul(ps[0:32, :], lhsT=wuq[:, kc, hd * 96 + 64:hd * 96 + 96],
                                                     rhs=cn[:, kc, :], start=(kc == 0), stop=(kc == 1)),
                                  r=[wuq, (cn, kc)], w=[ps])
                        ps2 = self.next_ps()
                        for kc in range(2):
                            fw.pe(lambda h: h.matmul(ps2[0:32, :], lhsT=wuqr[:, kc, hd * 32:hd * 32 + 32],
                                                     rhs=cn[:, kc, :], start=(kc == 0), stop=(kc == 1)),
                                  r=[wuqr, (cn, kc)], w=[ps2])
                        fw.dve(lambda h: h.tensor_tensor(out=tmp32[:, 0, :], in0=ps[0:32, :], in1=cs[:, 0, ts], op=ALU.mult),
                               r=[ps, cs], w=[(tmp32, 0)])
                        fw.dve(lambda h: h.tensor_tensor(out=tmp32[:, 1, :], in0=ps2[0:32, :], in1=cs[:, 1, ts], op=ALU.mult),
                               r=[ps2, cs], w=[(tmp32, 1)])
                        fw.dve(lambda h: h.scalar_tensor_tensor(out=qp[:, hd, ts], in0=tmp32[:, 0, :], scalar=1.0,
                                                                in1=tmp32[:, 1, :], op0=ALU.mult, op1=ALU.add),
                               r=[tmp32], w=[(qp, (hd, n))])
                        fw.dve(lambda h: h.tensor_scalar(out=qp[:, hd, ts], in0=qp[:, hd, ts], scalar1=SC, scalar2=None,
                                                         op0=ALU.mult), r=[(qp, (hd, n))], w=[(qp, (hd, n))])
                        ps = self.next_ps()
                        fw.pe(lambda h: h.matmul(ps[0:64, :], lhsT=wk[:, hd * 64:(hd + 1) * 64], rhs=cn[:, 2, :],
                                                 start=True, stop=True), r=[wk, (cn, 2)], w=[ps])
                        fw.act(lambda h: h.copy(out=kn[:, hd, ts], in_=ps[0:64, :]), r=[ps], w=[(kn, (hd, n))])
                    for tt in range(4):
                        ti = n * 4 + tt
                        ps = self.next_ps()
                        fw.pe(lambda h: h.matmul(ps[:, :], lhsT=cn[:, 2, tt * 128:(tt + 1) * 128], rhs=wv[:, :],
                                                 start=True, stop=True), r=[(cn, 2), wv], w=[ps])
                        fw.act(lambda h: h.copy(out=V[:, ti, :, 0:64], in_=ps[:, :].rearrange("p (h d) -> p h d", h=8)),
                               r=[ps], w=[(V, ti)])
            fw.barrier()
            with contextlib.ExitStack() as es3:
                PT = self.sb(es3, "PT", [128, NT, 8, 128], BF16)
                rec = self.sb(es3, "rec", [128, 8, 1])
                yc = self.sb(es3, "yc", [128, 8, 64], BF16)
                yct = [self.sb(es3, f"yct{i}", [128, 4, 128], BF16) for i in range(2)]
                for qi in range(NT):
                    qs = slice(qi * 128, (qi + 1) * 128)
                    for kt in range(qi + 1):
                        ks = slice(kt * 128, (kt + 1) * 128)
                        for hg in range(2):
                            ps = self.next_ps()
                            for hh in range(4):
                                hd = hg * 4 + hh
                                o = ps[:, hh * 128:(hh + 1) * 128]
                                fw.pe(lambda h: h.matmul(o, lhsT=kn[:, hd, ks], rhs=qn[:, hd, qs], start=True, stop=False),
                                      r=[kn, qn], w=[ps])
                                fw.pe(lambda h: h.matmul(o, lhsT=kp[:, ks], rhs=qp[:, hd, qs], start=False, stop=True),
                                      r=[kp, qp], w=[ps])
                            fw.act(lambda h: h.activation(
                                out=PT[:, kt, hg * 4:(hg + 1) * 4, :],
                                in_=ps[:, :].rearrange("p (h q) -> p h q", h=4), func=AF.Exp),
                                r=[ps], w=[(PT, (kt, hg))])
                            if kt == qi:
                                fw.dve(lambda h: h.tensor_tensor(
                                    out=PT[:, kt, hg * 4:(hg + 1) * 4, :], in0=PT[:, kt, hg * 4:(hg + 1) * 4, :],
                                    in1=cmask[:, None, :].to_broadcast([128, 4, 128]), op=ALU.mult),
                                    r=[(PT, (kt, hg)), cmask], w=[(PT, (kt, hg))])
                    pss = [self.next_ps(), self.next_ps()]
                    for hd in range(8):
                        ps = pss[hd // 4]
                        o = ps[:, (hd % 4) * 65:(hd % 4) * 65 + 65]
                        for kt in range(qi + 1):
                            fw.pe(lambda h: h.matmul(o, lhsT=PT[:, kt, hd, :], rhs=V[:, kt, hd, :],
                                                     start=(kt == 0), stop=(kt == qi)),
                                  r=[(PT, (kt, hd // 4)), (V, kt)], w=[ps])
                    for hg in range(2):
                        ps = pss[hg]
                        pv = ps[:, 0:260].rearrange("p (h d) -> p h d", h=4)
                        fw.dve(lambda h: h.reciprocal(out=rec[:, hg * 4:(hg + 1) * 4, :], in_=pv[:, :, 64:65]),
                               r=[ps], w=[(rec, hg)])
                        fw.dve(lambda h: h.tensor_tensor(out=yc[:, hg * 4:(hg + 1) * 4, :], in0=pv[:, :, 0:64],
                                                         in1=rec[:, hg * 4:(hg + 1) * 4, :].to_broadcast([128, 4, 64]),
                                                         op=ALU.mult), r=[ps, (rec, hg)], w=[(yc, hg)])
                    o = yct[qi % 2]
                    ycf = yc[:, :, :].rearrange("p h d -> p (h d)")
                    for c in range(4):
                        pst = self.next_ps()
                        pb = pst[:, :].bitcast(BF16)
                        fw.pe(lambda h: h.transpose(pb[:, 0:128], ycf[:, c * 128:(c + 1) * 128], identb[:]),
                              r=[yc, identb], w=[pst])
                        fw.act(lambda h: h.copy(out=o[:, c, :], in_=pb[:, 0:128]), r=[pst], w=[(o, c)])
                    fw.dma("sp", yT[2, :, qs].rearrange("(c p) t -> p c t", p=128), o[:], reads=[o],
                           writes=[(yT, ("c", qi))])


def prep_inputs(inputs, nseq=1):
    f = lambda a: np.ascontiguousarray(np.asarray(a, dtype=np.float32))
    shared = {}
    shared["w_in"] = f(inputs["w_in"])
    rv = np.stack([inputs["rg_conv_w"][:, 0], inputs["rg_conv_w"][:, 1], inputs["rg_conv_w"][:, 2],
                   inputs["rg_conv_w"][:, 3], inputs["rg_conv_b"], inputs["rg_ba"], inputs["rg_bx"],
                   inputs["rg_log_a"]], axis=-1)
    shared["rg_vec"] = f(rv.reshape(L, 4, 128, 8).transpose(0, 2, 1, 3))
    shared["rg_wa"] = f(inputs["rg_wa"])
    shared["rg_wx"] = f(inputs["rg_wx"])
    shared["ident"] = np.eye(128, dtype=np.float32)
    mv = np.stack([inputs["mla_q_norm"][:, :128], inputs["mla_q_norm"][:, 128:], inputs["mla_kv_norm"]], axis=-1)
    shared["mla_vec"] = f(mv)
    wuq = np.asarray(inputs["mla_w_uq"], np.float32)
    shared["mla_w_uq"] = f(wuq)
    w4 = wuq.reshape(L, 256, 8, 96)
    shared["mla_w_uq_rot"] = f(np.concatenate([w4[..., 80:96], w4[..., 64:80]], -1).reshape(L, 256, 256))
    wkv = np.asarray(inputs["mla_w_ukv"], np.float32).reshape(L, 128, 8, 128)
    shared["mla_w_ukv_k"] = f(wkv[..., :64].reshape(L, 128, 512))
    shared["mla_w_ukv_v"] = f(wkv[..., 64:].reshape(L, 128, 512))
    pos = np.arange(S, dtype=np.float32)
    inv = (10000.0 ** (-np.arange(16, dtype=np.float32) / 16)).astype(np.float32)
    ang = pos[None, :] * inv[:, None]
    cs_, sn_ = np.cos(ang).astype(np.float32), np.sin(ang).astype(np.float32)
    shared["rope_cs"] = f(np.stack([np.concatenate([cs_, cs_], 0), np.concatenate([-sn_, sn_], 0)], 0))
    shared["cmask"] = f(np.triu(np.ones((128, 128), np.float32)))
    shared["rw_mix"] = f(inputs["rw_mix"])[:, None, :]
    vz = np.zeros((L, 512), np.float32); vz[1:] = inputs["rw_v0"]
    rwv = np.stack([inputs["rw_w0"], inputs["rw_a0"], inputs["rw_k_k"], inputs["rw_k_a"],
                    np.asarray(inputs["rw_r_k"]).reshape(L, 512), inputs["rw_gn_g"], inputs["rw_gn_b"], vz], axis=-1)
    shared["rw_vec"] = f(rwv.reshape(L, 4, 128, 8).transpose(0, 2, 1, 3))
    shared["rw_w2a2"] = f(np.concatenate([inputs["rw_w2"], inputs["rw_a2"]], axis=1))
    shared["rw_g2"] = f(inputs["rw_g2"])
    shared["rw_v1"] = f(inputs["rw_v1"])
    shared["rw_v2"] = f(inputs["rw_v2"])
    su = np.triu(np.ones((128, 128), np.float32), 1); ui = np.triu(np.ones((128, 128), np.float32))
    shared["rw_masks"] = f(np.stack([su, su, ui, ui, su.T, su.T, su.T, su.T], axis=1))
    obd = np.zeros((128, 128), np.float32); obd[:64, :64] = 1; obd[64:, 64:] = 1
    shared["onesbd"] = obd
    shared["w_branch"] = f(inputs["w_branch"])
    shared["w_out"] = f(inputs["w_out"])
    shared["peer_wq"] = f(inputs["peer_w_query"])
    shared["peer_skT"] = f(np.asarray(inputs["peer_subkeys"]).transpose(0, 1, 3, 2))
    shared["iota16"] = f(np.tile(np.arange(16, dtype=np.float32)[None, :], (128, 1)))
    for l in range(L):
        shared[f"peer_u{l}"] = f(inputs["peer_u"][l])
        shared[f"peer_v{l}"] = f(inputs["peer_v"][l])
    shared["ln_gb"] = f(np.stack([inputs["ln1_g"], inputs["ln1_b"], inputs["ln2_g"], inputs["ln2_b"]], axis=1))
    maps = []
    for b in range(8 // nseq):
        m = dict(shared)
        m["x"] = f(inputs["x"][b * nseq:(b + 1) * nseq])
        maps.append(m)
    return maps


def run(inputs, debug=False, phases=None, cores=8, trace=False, nseq=1):
    bld = Builder(debug=debug, phases=phases, nseq=nseq)
    nc = bld.build()
    maps = prep_inputs(inputs, nseq)[:cores]
    maps = [{k: v for k, v in m.items() if k in bld.inp} for m in maps]
    res = run_bass_kernel_spmd(nc, maps, core_ids=list(range(cores)), trace=trace)
    return res


NCORES = 8


def kernel(**inputs):
    nseq = 8 // NCORES
    res = run(inputs, cores=NCORES, nseq=nseq)
    return np.concatenate([np.asarray(r["out"]) for r in res.results], axis=0).astype(np.float32)
```
